# Optimizing a Trainium2 kernel written in Bass

```python
import math
import jax
import jax.numpy as jnp
from jax import lax
import numpy as np

D_MODEL = 1024
BATCH = 2
SEQ = 16384
DEPTH = 2
DEC_BATCH = 4
DEC_SEQ = 4096
PAST_LEN = 128

GRID_W = 64
QB = 128
ROPE_THETA = 10000.0
EPS = 1e-6
N_BRANCH = 4
BRANCH_W = 512
H_A = 4
DH_A = 64
H_B = 4
Q_LORA = 384
KV_LORA = 256
NOPE_B = 64
ROPE_B = 32
V_B = 128
D_INNER = 512
SSM_HEAD = 64
H_C = D_INNER // SSM_HEAD
N_GROUPS = 2
D_STATE = 64
SSM_CONV = 3
CHUNK = 128
SSM_XBC = D_INNER + 2 * N_GROUPS * D_STATE
H_D = 8
KV_D = 2
DH_D = 64
MEM_LEN = 256
H_MEM = 4
DH_MEM = 128
D_FF = 2816
FFN_CONV = 3

SPLIT_SIZES = (
    H_A * 2 * DH_A, H_A * 2 * DH_A, H_A * 2 * DH_A,
    Q_LORA, KV_LORA, ROPE_B,
    D_INNER, SSM_XBC, H_C, H_C,
    H_D * DH_D, KV_D * DH_D, KV_D * DH_D,
    N_BRANCH * D_MODEL,
)
PROJ_COLS = sum(SPLIT_SIZES)

kernel_name = "hybrid_bidir_gated_parallel_encoder"


def _rms_norm(x, g):
    xf = x.astype(jnp.float32)
    y = xf * lax.rsqrt(jnp.mean(xf * xf, axis=-1, keepdims=True) + EPS)
    return (y * g.astype(jnp.float32)).astype(x.dtype)


def _split_cols(u, sizes):
    parts, start = [], 0
    for s in sizes:
        parts.append(u[..., start:start + s])
        start += s
    return parts


def _rope(x, pos):
    d = x.shape[-1]
    half = d // 2
    freqs = ROPE_THETA ** (-jnp.arange(half, dtype=jnp.float32) / half)
    ang = pos.astype(jnp.float32)[:, None] * freqs[None, :]
    shape = (1, x.shape[1]) + (1,) * (x.ndim - 3) + (half,)
    cos = jnp.cos(ang).reshape(shape)
    sin = jnp.sin(ang).reshape(shape)
    x1 = x[..., :half].astype(jnp.float32)
    x2 = x[..., half:].astype(jnp.float32)
    return jnp.concatenate([x1 * cos - x2 * sin, x2 * cos + x1 * sin], axis=-1).astype(x.dtype)


def _axial_rope(x, row_pos, col_pos):
    half = x.shape[-1] // 2
    return jnp.concatenate([_rope(x[..., :half], row_pos), _rope(x[..., half:], col_pos)], axis=-1)


def _alibi_slopes(n):
    return jnp.array([2.0 ** (-8.0 * (i + 1) / n) for i in range(n)], dtype=jnp.float32)


def _dwconv_centred(x, w, b):
    k_width = w.shape[0]
    pad = k_width // 2
    L = x.shape[1]
    xp = jnp.pad(x, ((0, 0), (pad, pad), (0, 0)))
    out = xp[:, 0:L] * w[0]
    for k in range(1, k_width):
        out = out + xp[:, k:k + L] * w[k]
    return out + b


def _sweep_query_blocks(fn, *qs):
    b, L = qs[0].shape[:2]
    nb = L // QB
    blocks = tuple(jnp.moveaxis(q.reshape((b, nb, QB) + q.shape[2:]), 1, 0) for q in qs)
    starts = jnp.arange(nb, dtype=jnp.int32) * QB
    out = lax.map(lambda args: fn(*args), (starts,) + blocks)
    return jnp.moveaxis(out, 0, 1).reshape((b, L) + out.shape[3:])


def _diff_attention(q, k, v, lam_params, subln_g, layer_idx):
    b, L, _ = q.shape
    q = q.reshape(b, L, H_A, 2, DH_A)
    k = k.reshape(b, L, H_A, 2, DH_A)
    v = v.reshape(b, L, H_A, 2 * DH_A)
    lam_init = 0.8 - 0.6 * math.exp(-0.3 * layer_idx)
    lp = lam_params.astype(jnp.float32)
    lam = jnp.exp(jnp.sum(lp[0] * lp[1])) - jnp.exp(jnp.sum(lp[2] * lp[3])) + lam_init
    slopes = _alibi_slopes(H_A)
    k_pos = jnp.arange(L, dtype=jnp.float32)
    scale = DH_A ** -0.5

    def block(q0, qb):
        q_pos = (q0 + jnp.arange(QB, dtype=jnp.int32)).astype(jnp.float32)
        dist = jnp.abs(q_pos[:, None] - k_pos[None, :])
        logits = jnp.einsum("bqhmd,bkhmd->bhmqk", qb, k).astype(jnp.float32) * scale
        logits = logits - slopes[:, None, None, None] * dist
        p = jax.nn.softmax(logits, axis=-1)
        w = p[:, :, 0] - lam * p[:, :, 1]
        return jnp.einsum("bhqk,bkhe->bqhe", w.astype(v.dtype), v)

    o = _sweep_query_blocks(block, q)
    o = _rms_norm(o, subln_g) * (1.0 - lam_init)
    return o.reshape(b, L, H_A * 2 * DH_A)


def _mla(cq, ckv, kr, q_norm, kv_norm, w_uq, w_ukv, pos):
    b, L, _ = cq.shape
    q = (_rms_norm(cq, q_norm) @ w_uq).reshape(b, L, H_B, NOPE_B + ROPE_B)
    kv = (_rms_norm(ckv, kv_norm) @ w_ukv).reshape(b, L, H_B, NOPE_B + V_B)
    q_nope = q[..., :NOPE_B]
    q_rope = _rope(q[..., NOPE_B:], pos)
    k_nope = kv[..., :NOPE_B]
    v = kv[..., NOPE_B:]
    k_rope = _rope(kr, pos)
    scale = (NOPE_B + ROPE_B) ** -0.5

    def block(q0, qn, qr):
        logits = (jnp.einsum("bqhd,bkhd->bhqk", qn, k_nope)
                  + jnp.einsum("bqhd,bkd->bhqk", qr, k_rope)).astype(jnp.float32) * scale
        p = jax.nn.softmax(logits, axis=-1)
        return jnp.einsum("bhqk,bkhd->bqhd", p.astype(v.dtype), v)

    o = _sweep_query_blocks(block, q_nope, q_rope)
    return o.reshape(b, L, H_B * V_B)


def _ssd_chunked(x, dt, A, B, C):
    b, L, h, p = x.shape
    g, n = B.shape[2], B.shape[3]
    r = h // g
    nc = L // CHUNK
    x = x.astype(jnp.float32).reshape(b, nc, CHUNK, g, r, p)
    dt = dt.astype(jnp.float32).reshape(b, nc, CHUNK, g, r)
    B = B.astype(jnp.float32).reshape(b, nc, CHUNK, g, n)
    C = C.astype(jnp.float32).reshape(b, nc, CHUNK, g, n)
    dA_cs = jnp.cumsum(dt * A.reshape(g, r), axis=2)
    xdt = x * dt[..., None]
    mask = jnp.tril(jnp.ones((CHUNK, CHUNK), dtype=bool))
    seg = jnp.where(mask[:, :, None, None], dA_cs[:, :, :, None] - dA_cs[:, :, None], -jnp.inf)
    decay = jnp.exp(seg)
    cb = jnp.einsum("bcign,bcjgn->bcijg", C, B)
    y_diag = jnp.einsum("bcijg,bcijgr,bcjgrp->bcigrp", cb, decay, xdt)
    decay_to_end = jnp.exp(dA_cs[:, :, -1:] - dA_cs)
    states = jnp.einsum("bcjgn,bcjgr,bcjgrp->bcgrpn", B, decay_to_end, xdt)
    chunk_decay = jnp.exp(dA_cs[:, :, -1])

    def step(s, inp):
        st, dec = inp
        return s * dec[..., None, None] + st, s

    init = jnp.zeros((b, g, r, p, n), jnp.float32)
    _, prev = lax.scan(step, init, (jnp.moveaxis(states, 1, 0), jnp.moveaxis(chunk_decay, 1, 0)))
    prev = jnp.moveaxis(prev, 0, 1)
    y_off = jnp.einsum("bcign,bcigr,bcgrpn->bcigrp", C, jnp.exp(dA_cs), prev)
    return (y_diag + y_off).reshape(b, L, h, p)


def _bidir_ssd_mixer(z, xbc, dt_f, dt_b, conv_w, conv_b, a_log, dt_bias, d_skip, norm_g):
    b, L, _ = z.shape
    xbc = jax.nn.silu(_dwconv_centred(xbc, conv_w, conv_b))
    xs = xbc[..., :D_INNER].reshape(b, L, H_C, SSM_HEAD)
    Bm = xbc[..., D_INNER:D_INNER + N_GROUPS * D_STATE].reshape(b, L, N_GROUPS, D_STATE)
    Cm = xbc[..., D_INNER + N_GROUPS * D_STATE:].reshape(b, L, N_GROUPS, D_STATE)
    A = -jnp.exp(a_log.astype(jnp.float32))
    dtb32 = dt_bias.astype(jnp.float32)
    dtf = jax.nn.softplus(dt_f.astype(jnp.float32) + dtb32[0])
    dtb = jax.nn.softplus(dt_b.astype(jnp.float32) + dtb32[1])
    flip = lambda a: jnp.flip(a, axis=1)
    y_fwd = _ssd_chunked(xs, dtf, A[0], Bm, Cm)
    y_bwd = flip(_ssd_chunked(flip(xs), flip(dtb), A[1], flip(Bm), flip(Cm)))
    y = y_fwd + y_bwd + d_skip.astype(jnp.float32)[:, None] * xs.astype(jnp.float32)
    y = y.reshape(b, L, D_INNER).astype(z.dtype)
    return _rms_norm(y * jax.nn.silu(z), norm_g)


def _axial_gqa(q, k, v, q_norm, k_norm, row_pos, col_pos):
    b, L, _ = q.shape
    q = _rms_norm(q.reshape(b, L, H_D, DH_D), q_norm)
    k = _rms_norm(k.reshape(b, L, KV_D, DH_D), k_norm)
    v = v.reshape(b, L, KV_D, DH_D)
    q = _axial_rope(q, row_pos, col_pos).reshape(b, L, KV_D, H_D // KV_D, DH_D)
    k = _axial_rope(k, row_pos, col_pos)
    scale = DH_D ** -0.5

    def block(q0, qb):
        logits = jnp.einsum("bqgrd,bkgd->bgrqk", qb, k).astype(jnp.float32) * scale
        p = jax.nn.softmax(logits, axis=-1)
        return jnp.einsum("bgrqk,bkgd->bqgrd", p.astype(v.dtype), v)

    o = _sweep_query_blocks(block, q)
    return o.reshape(b, L, H_D * DH_D)


def _parallel_mixer(h, lp, layer_idx):
    b, L, _ = h.shape
    rows = L // GRID_W
    pos = jnp.arange(L, dtype=jnp.int32)
    row_pos = jnp.repeat(jnp.arange(rows, dtype=jnp.int32), GRID_W)
    col_pos = jnp.tile(jnp.arange(GRID_W, dtype=jnp.int32), rows)
    u = h @ lp["w_in"]
    (a_q, a_k, a_v, b_cq, b_ckv, b_kr, c_z, c_xbc, c_dtf, c_dtb,
     d_q, d_k, d_v, g) = _split_cols(u, SPLIT_SIZES)
    y_a = _diff_attention(a_q, a_k, a_v, lp["diff_lambda"], lp["diff_norm"], layer_idx)
    y_b = _mla(b_cq, b_ckv, b_kr, lp["mla_q_norm"], lp["mla_kv_norm"], lp["w_mla_uq"], lp["w_mla_ukv"], pos)
    y_c = _bidir_ssd_mixer(c_z, c_xbc, c_dtf, c_dtb, lp["ssm_conv_w"], lp["ssm_conv_b"], lp["ssm_a_log"],
                           lp["ssm_dt_bias"], lp["ssm_d"], lp["ssm_norm"])
    y_d = _axial_gqa(d_q, d_k, d_v, lp["gqa_q_norm"], lp["gqa_k_norm"], row_pos, col_pos)
    ys = jnp.stack([y_a, y_b, y_c, y_d], axis=2)
    proj = jnp.einsum("blnc,ncd->blnd", ys, lp["w_branch"])
    gates = jax.nn.sigmoid(g.reshape(b, L, N_BRANCH, D_MODEL))
    merged = jnp.sum(proj * gates, axis=2)
    return merged @ lp["w_out"]


def _mem_attention(h, mem, mem_norm, w_q, w_kv, w_o):
    b, L, _ = h.shape
    m = _rms_norm(mem, mem_norm)
    q = (h @ w_q).reshape(b, L, H_MEM, DH_MEM)
    kv = (m @ w_kv).reshape(b, mem.shape[1], H_MEM, 2 * DH_MEM)
    k, v = kv[..., :DH_MEM], kv[..., DH_MEM:]
    logits = jnp.einsum("blhd,bmhd->bhlm", q, k).astype(jnp.float32) * (DH_MEM ** -0.5)
    p = jax.nn.softmax(logits, axis=-1)
    o = jnp.einsum("bhlm,bmhd->blhd", p.astype(v.dtype), v)
    return o.reshape(b, L, H_MEM * DH_MEM) @ w_o


def _conv_ffn(h, w_in, conv_w, conv_b, w_out):
    u = _dwconv_centred(h @ w_in, conv_w, conv_b)
    a, g = u[..., :D_FF], u[..., D_FF:]
    return (jax.nn.gelu(a, approximate=False) * g) @ w_out


def _layer(x, mem, lp, layer_idx):
    pre, post = lp["norm_pre"], lp["norm_post"]
    x = x + _rms_norm(_parallel_mixer(_rms_norm(x, pre[0]), lp, layer_idx), post[0])
    x = x + _rms_norm(_mem_attention(_rms_norm(x, pre[1]), mem, lp["mem_norm"], lp["w_mem_q"],
                                     lp["w_mem_kv"], lp["w_mem_o"]), post[1])
    x = x + _rms_norm(_conv_ffn(_rms_norm(x, pre[2]), lp["w_ffn_in"], lp["ffn_conv_w"],
                                lp["ffn_conv_b"], lp["w_ffn_out"]), post[2])
    return x


def _trunk(x, mem, params):
    for l in range(DEPTH):
        lp = {name: arr[l] for name, arr in params.items()}
        x = _layer(x, mem, lp, l)
    return x


def setup_inputs(seed: int = 0) -> dict:
    key = jax.random.key(seed)
    it = iter(jax.random.split(key, 40))
    f32 = jnp.float32

    def dense(shape, fan_in):
        return jax.random.normal(next(it), shape, f32) * (fan_in ** -0.5)

    def gain(shape):
        return 1.0 + 0.02 * jax.random.normal(next(it), shape, f32)

    def small(shape):
        return 0.02 * jax.random.normal(next(it), shape, f32)

    x_prompt = jax.random.normal(next(it), (BATCH, SEQ, D_MODEL), f32)
    x_sample = jax.random.normal(next(it), (DEC_BATCH, DEC_SEQ, D_MODEL), f32)
    mem_prompt = jax.random.normal(next(it), (BATCH, MEM_LEN, D_MODEL), f32)
    mem_sample = jax.random.normal(next(it), (DEC_BATCH, MEM_LEN, D_MODEL), f32)
    w_in = dense((DEPTH, D_MODEL, PROJ_COLS), D_MODEL)
    w_branch = dense((DEPTH, N_BRANCH, BRANCH_W, D_MODEL), BRANCH_W)
    w_out = dense((DEPTH, D_MODEL, D_MODEL), D_MODEL)
    diff_lambda = 0.1 * jax.random.normal(next(it), (DEPTH, 4, DH_A), f32)
    diff_norm = gain((DEPTH, 2 * DH_A))
    mla_q_norm = gain((DEPTH, Q_LORA))
    mla_kv_norm = gain((DEPTH, KV_LORA))
    w_mla_uq = dense((DEPTH, Q_LORA, H_B * (NOPE_B + ROPE_B)), Q_LORA)
    w_mla_ukv = dense((DEPTH, KV_LORA, H_B * (NOPE_B + V_B)), KV_LORA)
    ssm_conv_w = dense((DEPTH, SSM_CONV, SSM_XBC), SSM_CONV)
    ssm_conv_b = small((DEPTH, SSM_XBC))
    ssm_a_log = jnp.log(jax.random.uniform(next(it), (DEPTH, 2, H_C), f32, 1.0, 16.0))
    dt0 = jnp.exp(jax.random.uniform(next(it), (DEPTH, 2, H_C), f32, math.log(1e-3), math.log(1e-1)))
    ssm_dt_bias = dt0 + jnp.log(-jnp.expm1(-dt0))
    ssm_d = gain((DEPTH, H_C))
    ssm_norm = gain((DEPTH, D_INNER))
    gqa_q_norm = gain((DEPTH, DH_D))
    gqa_k_norm = gain((DEPTH, DH_D))
    mem_norm = gain((DEPTH, D_MODEL))
    w_mem_q = dense((DEPTH, D_MODEL, H_MEM * DH_MEM), D_MODEL)
    w_mem_kv = dense((DEPTH, D_MODEL, 2 * H_MEM * DH_MEM), D_MODEL)
    w_mem_o = dense((DEPTH, H_MEM * DH_MEM, D_MODEL), H_MEM * DH_MEM)
    w_ffn_in = dense((DEPTH, D_MODEL, 2 * D_FF), D_MODEL)
    ffn_conv_w = dense((DEPTH, FFN_CONV, 2 * D_FF), FFN_CONV)
    ffn_conv_b = small((DEPTH, 2 * D_FF))
    w_ffn_out = dense((DEPTH, D_FF, D_MODEL), D_FF)
    norm_pre = gain((DEPTH, 3, D_MODEL))
    norm_post = gain((DEPTH, 3, D_MODEL))
    return {
        "x_prompt": x_prompt, "x_sample": x_sample, "mem_prompt": mem_prompt, "mem_sample": mem_sample,
        "w_in": w_in, "w_branch": w_branch, "w_out": w_out,
        "diff_lambda": diff_lambda, "diff_norm": diff_norm,
        "mla_q_norm": mla_q_norm, "mla_kv_norm": mla_kv_norm, "w_mla_uq": w_mla_uq, "w_mla_ukv": w_mla_ukv,
        "ssm_conv_w": ssm_conv_w, "ssm_conv_b": ssm_conv_b, "ssm_a_log": ssm_a_log,
        "ssm_dt_bias": ssm_dt_bias, "ssm_d": ssm_d, "ssm_norm": ssm_norm,
        "gqa_q_norm": gqa_q_norm, "gqa_k_norm": gqa_k_norm,
        "mem_norm": mem_norm, "w_mem_q": w_mem_q, "w_mem_kv": w_mem_kv, "w_mem_o": w_mem_o,
        "w_ffn_in": w_ffn_in, "ffn_conv_w": ffn_conv_w, "ffn_conv_b": ffn_conv_b, "w_ffn_out": w_ffn_out,
        "norm_pre": norm_pre, "norm_post": norm_post,
    }


def reference(x_prompt, x_sample, mem_prompt, mem_sample, w_in, w_branch, w_out, diff_lambda, diff_norm,
              mla_q_norm, mla_kv_norm, w_mla_uq, w_mla_ukv, ssm_conv_w, ssm_conv_b, ssm_a_log,
              ssm_dt_bias, ssm_d, ssm_norm, gqa_q_norm, gqa_k_norm, mem_norm, w_mem_q, w_mem_kv, w_mem_o,
              w_ffn_in, ffn_conv_w, ffn_conv_b, w_ffn_out, norm_pre, norm_post):
    params = {
        "w_in": w_in, "w_branch": w_branch, "w_out": w_out,
        "diff_lambda": diff_lambda, "diff_norm": diff_norm,
        "mla_q_norm": mla_q_norm, "mla_kv_norm": mla_kv_norm, "w_mla_uq": w_mla_uq, "w_mla_ukv": w_mla_ukv,
        "ssm_conv_w": ssm_conv_w, "ssm_conv_b": ssm_conv_b, "ssm_a_log": ssm_a_log,
        "ssm_dt_bias": ssm_dt_bias, "ssm_d": ssm_d, "ssm_norm": ssm_norm,
        "gqa_q_norm": gqa_q_norm, "gqa_k_norm": gqa_k_norm,
        "mem_norm": mem_norm, "w_mem_q": w_mem_q, "w_mem_kv": w_mem_kv, "w_mem_o": w_mem_o,
        "w_ffn_in": w_ffn_in, "ffn_conv_w": ffn_conv_w, "ffn_conv_b": ffn_conv_b, "w_ffn_out": w_ffn_out,
        "norm_pre": norm_pre, "norm_post": norm_post,
    }
    y_prompt = _trunk(x_prompt, mem_prompt, params)
    y_sample = _trunk(x_sample, mem_sample, params)
    return (y_prompt, y_sample)
```

```python
import math
import numpy as np
import ml_dtypes
import concourse.bass as bass
import concourse.mybir as mybir
from concourse.bass_utils import run_bass_kernel_spmd
from contextlib import ExitStack

F32 = mybir.dt.float32
BF16 = mybir.dt.bfloat16
AF = mybir.ActivationFunctionType
ALU = mybir.AluOpType
NPBF = ml_dtypes.bfloat16

ENGS = ("pe", "act", "dve", "pool")
D_MODEL = 1024
EPS = 1e-6
MEM_LEN = 256
D_FF = 2816
NEG = -30000.0
NCM = 28


class Buf:
    __slots__ = ("h", "name", "w", "r", "ld", "st")

    def __init__(self, h, name):
        self.h = h
        self.name = name
        self.w = None
        self.r = {}
        self.ld = None
        self.st = None

    def __getitem__(self, idx):
        return View(self, self.h[idx])


class View:
    __slots__ = ("b", "ap")

    def __init__(self, b, ap):
        self.b = b
        self.ap = ap

    def __getitem__(self, idx):
        return View(self.b, self.ap[idx])

    def re(self, pat, **kw):
        return View(self.b, self.ap.rearrange(pat, **kw))


class Trk:
    def __init__(self, nc, es, n_dma_sems=80):
        self.nc = nc
        self.eng = {"pe": nc.tensor, "act": nc.scalar, "dve": nc.vector, "pool": nc.gpsimd,
                    "sp": nc.sync}
        self.sem = {e: es.enter_context(nc.semaphore("c_" + e)) for e in ENGS}
        self.cnt = {e: 0 for e in ENGS}
        self.dsem = [es.enter_context(nc.semaphore("d%d" % i)) for i in range(n_dma_sems)]
        self.dcnt = [0] * n_dma_sems
        self.dfree = list(range(n_dma_sems))
        self.issuers = list(ENGS) + ["sp"]
        self.known = {e: {} for e in self.issuers}
        self.nins = 0

    def _handle(self, key):
        return self.sem[key] if isinstance(key, str) else self.dsem[key]

    def _wait(self, issuer, dep):
        if dep is None:
            return
        key, val = dep
        if issuer == "pe" and key == "pe":
            return
        if self.known[issuer].get(key, 0) >= val:
            return
        self.eng[issuer].wait_ge(self._handle(key), val)
        self.known[issuer][key] = val
        self.nins += 1

    def _deps(self, issuer, reads, writes):
        for v in reads:
            self._wait(issuer, v.b.w)
        for v in writes:
            self._wait(issuer, v.b.w)
            for k, val in v.b.r.items():
                self._wait(issuer, (k, val))

    def op(self, e, fn, reads=(), writes=()):
        self._deps(e, reads, writes)
        ins = fn()
        self.cnt[e] += 1
        self.nins += 1
        ins.then_inc(self.sem[e], 1)
        c = self.cnt[e]
        for v in reads:
            v.b.r[e] = c
        for v in writes:
            v.b.w = (e, c)
            v.b.r = {}
        return ins

    def dma(self, out, in_, own="out", q="sp", extra_reads=(), extra_writes=(), **kw):
        self._deps(q, [in_] + list(extra_reads), [out] + list(extra_writes))
        if own == "out":
            if out.b.ld is None:
                out.b.ld = self.dfree.pop()
            slot = out.b.ld
        else:
            if in_.b.st is None:
                in_.b.st = self.dfree.pop()
            slot = in_.b.st
        ins = self.eng[q].dma_start(out=out.ap, in_=in_.ap, **kw)
        self.nins += 1
        self.dcnt[slot] += 16
        ins.then_inc(self.dsem[slot], 16)
        c = self.dcnt[slot]
        for v in [in_] + list(extra_reads):
            v.b.r[slot] = c
        for v in [out] + list(extra_writes):
            v.b.w = (slot, c)
            v.b.r = {}
        return ins

    def full_barrier(self, release=()):
        for issuer in self.issuers:
            for e in ENGS:
                if self.cnt[e]:
                    self._wait(issuer, (e, self.cnt[e]))
            for s in range(len(self.dsem)):
                if self.dcnt[s]:
                    self._wait(issuer, (s, self.dcnt[s]))
        for b in release:
            for s in (b.ld, b.st):
                if s is not None:
                    self.dfree.append(s)
            b.ld = b.st = None

    def finish(self):
        for s in range(len(self.dsem)):
            if self.dcnt[s]:
                self._wait("sp", (s, self.dcnt[s]))


A_Q, A_K, A_V = 0, 512, 1024
B_CQ, B_CKV, B_KR = 1536, 1920, 2176
C_Z, C_XBC, C_DTF, C_DTB = 2208, 2720, 3488, 3496
D_Q, D_K, D_V = 3504, 4016, 4144
G_OFF = 4272
PC_XBC = 2208
PC_DT = 2208 + 768
PC_DQ = PC_DT + 16
PC_DK = PC_DQ + 512
PC_DV = PC_DK + 128
NPC = PC_DV + 128

VC = {}
_o = 0
for _n, _w in [("pre0", 8), ("pre1", 8), ("pre2", 8), ("post0", 8), ("post1", 8), ("post2", 8),
               ("diffn", 1), ("mlaq", 3), ("mlakv", 2), ("cw0", 6), ("cw1", 6), ("cw2", 6), ("cb", 6),
               ("ssmn", 4), ("gq", 1), ("gk", 1), ("memn", 8), ("dtb", 1), ("ssmd", 4),
               ("fw0", 44), ("fw1", 44), ("fw2", 44), ("fb", 44)]:
    VC[_n] = _o
    _o += _w
NV = _o


def build_program(L, depth=2, dbg=(), cut_scale=60.0, stages=None):
    nc = bass.Bass("TRN2", target_bir_lowering=False)
    NT = L // 512
    NCH = L // 128
    glob = ExitStack()
    T = Trk(nc, glob)
    V, A, P, G = nc.vector, nc.scalar, nc.tensor, nc.gpsimd
    ENG = {"dve": V, "act": A, "pool": G}

    def dram(name, shape, dt, kind="Internal"):
        if name in dbg:
            kind = "ExternalOutput"
        return nc.dram_tensor(name, shape, dt, kind=kind).ap()

    class DT:
        def __init__(self, name, shape, dt, kind="Internal", tok_axis=-1):
            self.ap = dram(name, shape, dt, kind)
            self.name = name
            self.tiles = {}

        def t(self, i):
            if i not in self.tiles:
                self.tiles[i] = Buf(self.ap, "%s.%s" % (self.name, i))
            return self.tiles[i]

        def v(self, i, ap):
            return View(self.t(i), ap)

    def ext(name, shape, dt=F32):
        return Buf(nc.dram_tensor(name, shape, dt, kind="ExternalInput").ap(), name)

    xT_in = ext("xT", [D_MODEL, L])
    memT_in = ext("memT", [D_MODEL, MEM_LEN])
    tmask_in = ext("tmask", [128, L])
    kaugA_in = ext("kaugA", [4, 5, L], BF16)
    qaugA_in = ext("qaugA", [3, 5, L], BF16)
    kmask_in = ext("kmask", [1, L], BF16)
    onesrow_in = ext("onesrow", [1, L], BF16)
    bdiag_in = ext("bdiag", [4, 128, 2048])
    cosB_in = ext("cosB", [32, L]); sinB_in = ext("sinB", [32, L])
    cosD_in = ext("cosD", [128, L]); sinD_in = ext("sinD", [128, L])
    cmat_in = ext("cmat", [128, NCM, 128])
    wP_in = ext("wP", [depth, D_MODEL, NPC])
    wG_in = ext("wG", [depth, D_MODEL, 4096 + 512])
    wuq_in = ext("wuq", [depth, 384, 384])
    wukvK_in = ext("wukvK", [depth, 256, 256])
    wukvV_in = ext("wukvV", [depth, 256, 512])
    wbr_in = ext("wbr", [depth, 2048, D_MODEL])
    wout_in = ext("wout", [depth, D_MODEL, D_MODEL])
    wmq_in = ext("wmq", [depth, D_MODEL, 512])
    wmk_in = ext("wmk", [depth, D_MODEL, 512])
    wmv_in = ext("wmv", [depth, D_MODEL, 512])
    wmo_in = ext("wmo", [depth, 512, D_MODEL])
    wfi_in = ext("wfi", [depth, D_MODEL, 2 * D_FF])
    wfo_in = ext("wfo", [depth, D_FF, D_MODEL])
    vecs_in = ext("vecs", [depth, 128, NV])
    rows_in = ext("rows", [depth, 128, 256 + 16])
    yT_out = DT("yT", [D_MODEL, L], F32, kind="ExternalOutput")

    xres = DT("xres", [D_MODEL, L], F32)
    hT = DT("hT", [D_MODEL, L], BF16)
    QA = DT("QA", [8, 64, L], BF16); KA = DT("KA", [8, 64, L], BF16); VA = DT("VA", [4, L, 128], BF16)
    QB = DT("QB", [4, 96, L], BF16); KB = DT("KB", [4, 96, L], BF16); VB = DT("VB", [4, L, 128], BF16)
    QD = DT("QD", [8, 64, L], BF16); KD = DT("KD", [2, 64, L], BF16); VD = DT("VD", [2, L, 64], BF16)
    XBC = DT("XBC", [768, L], F32); DTR = DT("DTR", [16, L], F32)
    XBCP = DT("XBCP", [768, L], F32); DTP = DT("DTP", [16, L], F32)
    OA = DT("OA", [8, 128, L], F32); OB = DT("OB", [512, L], BF16); OD = DT("OD", [512, L], BF16)
    YF = DT("YF", [512, L], F32); YS = DT("YS", [512, L], F32)
    H2 = DT("H2", [D_MODEL, L], BF16)

    def mm(out, lhsT, rhs, start=True, stop=True):
        T.op("pe", lambda: P.matmul(out.ap, lhsT=lhsT.ap, rhs=rhs.ap, start=start, stop=stop),
             reads=[lhsT, rhs], writes=[out])

    def trp(out, in_, ident):
        T.op("pe", lambda: P.matmul(out.ap, lhsT=in_.ap, rhs=ident.ap, start=True, stop=True), reads=[in_, ident], writes=[out])

    def act(out, in_, func, bias=None, scale=1.0):
        rd = [in_]
        kw = {}
        if isinstance(bias, View):
            rd.append(bias); kw["bias"] = bias.ap
        elif bias is not None:
            kw["bias"] = bias
        T.op("act", lambda: A.activation(out=out.ap, in_=in_.ap, func=func, scale=scale, **kw),
             reads=rd, writes=[out])

    def _s(x, rd):
        if isinstance(x, View):
            rd.append(x)
            return x.ap
        return x

    def ts(e, out, in0, s1, op0, s2=None, op1=None):
        rd = [in0]
        a1 = _s(s1, rd); a2 = _s(s2, rd)
        kw = {} if op1 is None else {"op1": op1}
        T.op(e, lambda: ENG[e].tensor_scalar(out=out.ap, in0=in0.ap, scalar1=a1, scalar2=a2, op0=op0, **kw),
             reads=rd, writes=[out])

    def tt(e, out, in0, in1, op):
        T.op(e, lambda: ENG[e].tensor_tensor(out=out.ap, in0=in0.ap, in1=in1.ap, op=op),
             reads=[in0, in1], writes=[out])

    def stt(e, out, in0, scalar, in1, op0, op1):
        e = "dve"
        rd = [in0, in1]
        a = _s(scalar, rd)
        T.op(e, lambda: ENG[e].scalar_tensor_tensor(out=out.ap, in0=in0.ap, scalar=a, in1=in1.ap, op0=op0, op1=op1),
             reads=rd, writes=[out])

    def cp(e, out, in_):
        if e == "act":
            T.op("act", lambda: A.copy(out=out.ap, in_=in_.ap), reads=[in_], writes=[out])
        else:
            T.op(e, lambda: ENG[e].tensor_copy(out=out.ap, in_=in_.ap), reads=[in_], writes=[out])

    def mset(e, out, val):
        T.op(e, lambda: ENG[e].memset(out.ap, val), writes=[out])

    def recip(out, in_):
        T.op("dve", lambda: V.reciprocal(out=out.ap, in_=in_.ap), reads=[in_], writes=[out])

    class Stage:
        def __init__(self, name):
            self.es = ExitStack()
            self.bufs = []
            self.name = name
            self.n = 0

        def sb(self, shape, dt, name=None):
            self.n += 1
            nm = "%s_%d_%s" % (self.name, self.n, name or "t")
            b = Buf(self.es.enter_context(nc.sbuf_tensor(nm, list(shape), dt)), nm)
            self.bufs.append(b)
            return b

        def ps(self, shape, dt=F32, name=None):
            self.n += 1
            nm = "%s_%d_%s" % (self.name, self.n, name or "p")
            b = Buf(self.es.enter_context(nc.psum_tensor(nm, list(shape), dt)), nm)
            self.bufs.append(b)
            return b

        def pool(self, n, shape, dt, name="pl"):
            return Ring([self.sb(shape, dt, name) for _ in range(n)])

        def pspool(self, n, shape, name="pp"):
            return Ring([self.ps(shape, F32, name) for _ in range(n)])

        def close(self):
            T.full_barrier(release=self.bufs)
            self.es.close()

    class Ring:
        def __init__(self, items):
            self.items = items
            self.i = 0

        def get(self):
            b = self.items[self.i % len(self.items)]
            self.i += 1
            return b

    def alloc_w(S, K, N, name):
        return S.sb([128, (K + 127) // 128, N], BF16, name)

    def load_w_bf16(S, src_buf, src_ap, K, N, stg_ring, name, engs=("dve", "pool"), w=None):
        kc_n = (K + 127) // 128
        if w is None:
            w = S.sb([128, kc_n, N], BF16, name)
        step = stg_ring.items[0].h.shape[1]
        i = 0
        for kc in range(kc_n):
            rows = min(128, K - kc * 128)
            for c0 in range(0, N, step):
                cw = min(step, N - c0)
                st = stg_ring.get()
                T.dma(st[0:rows, 0:cw], View(src_buf, src_ap[kc * 128:kc * 128 + rows, c0:c0 + cw]))
                cp(engs[i % len(engs)], w[0:rows, kc, c0:c0 + cw], st[0:rows, 0:cw])
                i += 1
        return w

    cm = Buf(glob.enter_context(nc.sbuf_tensor("cmat_sb", [128, NCM, 128], F32)), "cmat_sb")
    T.dma(cm[:], cmat_in[:, :, :])
    CM_ID, CM_ONES, CM_U, CM_LO, CM_NEGF, CM_NEGB, CM_BD64, CM_R32, CM_RD, CM_SEL127, CM_SEL64, CM_SEL0, CM_SELH = range(13)
    ident = cm[:, CM_ID, :]

    def cmat(i, rows=128, cols=128):
        return cm[0:rows, i, 0:cols]

    def rms_stats(S, pp, sq_chunks, n_feat, out_rstd, tmp, blockmat=None, extra_scale=None):
        pm = pp.get()
        n = len(sq_chunks)
        for i, sq in enumerate(sq_chunks):
            rows = sq.ap.shape[0]
            lhs = (blockmat if blockmat is not None else cmat(CM_ONES, rows, 128))
            mm(pm[:, 0:512], lhs, sq, start=(i == 0), stop=(i == n - 1))
        act(tmp, pm[:, 0:512], AF.Sqrt, bias=epsb[:, 0:1], scale=1.0 / n_feat)
        recip(out_rstd, tmp)
        if extra_scale is not None:
            ts("dve", out_rstd, out_rstd, extra_scale, ALU.mult)

    epsb = Buf(glob.enter_context(nc.sbuf_tensor("epsb", [128, 1], F32)), "epsb")
    mset("dve", epsb[:], EPS)

    def want(s):
        return stages is None or s in stages

    for l in range(depth):
        lam_init = 0.8 - 0.6 * math.exp(-0.3 * l)
        xsrc = None

        def xin_view(t):
            if l == 0:
                return View(xT_in, xT_in.h.rearrange("(kc p) n -> p kc n", p=128)[:, :, t * 512:(t + 1) * 512])
            return xres.v(t, xres.ap.rearrange("(kc p) n -> p kc n", p=128)[:, :, t * 512:(t + 1) * 512])

        if want("P"):
            S = Stage("P%d" % l)
            stg = S.pool(2, [128, 1880], F32, "stg")
            wP = load_w_bf16(S, wP_in, wP_in.h[l], D_MODEL, NPC, stg, "wP")
            wuq = load_w_bf16(S, wuq_in, wuq_in.h[l], 384, 384, stg, "wuq")
            wkK = load_w_bf16(S, wukvK_in, wukvK_in.h[l], 256, 256, stg, "wkK")
            wkV = load_w_bf16(S, wukvV_in, wukvV_in.h[l], 256, 512, stg, "wkV")
            vec = S.sb([128, NV], F32, "vec")
            T.dma(vec[:], vecs_in[l])
            xts = S.pool(2, [128, 8, 512], F32, "xt")
            hbs = S.pool(2, [128, 8, 512], BF16, "hb")
            f32p = S.pool(6, [128, 512], F32, "f")
            bfp = S.pool(6, [128, 512], BF16, "b")
            cqf = S.sb([128, 3, 512], F32, "cqf"); cqn = S.sb([128, 3, 512], BF16, "cqn")
            ckf = S.sb([128, 2, 512], F32, "ckf"); ckn = S.sb([128, 2, 512], BF16, "ckn")
            tabs = S.pool(2, [128, 4, 512], F32, "tab")
            krope = S.sb([128, 512], BF16, "krope")
            pp = S.pspool(6, [128, 512], "pp")

            def load_x(t):
                xt = xts.get()
                T.dma(xt[:], xin_view(t))
                return xt

            nxt = load_x(0)
            for t in range(NT):
                xt = nxt
                tb = tabs.get()
                sl = slice(t * 512, (t + 1) * 512)
                T.dma(tb[64:96, 0, :], cosB_in[:, sl]); T.dma(tb[64:96, 1, :], sinB_in[:, sl])
                T.dma(tb[:, 2, :], cosD_in[:, sl]); T.dma(tb[:, 3, :], sinD_in[:, sl])
                if t + 1 < NT:
                    nxt = load_x(t + 1)
                sqs = []
                pm = pp.get()
                for kc in range(8):
                    sq = f32p.get()
                    if kc % 2 == 0:
                        act(sq[:], xt[:, kc, :], AF.Square)
                    else:
                        tt("pool", sq[:], xt[:, kc, :], xt[:, kc, :], ALU.mult)
                    mm(pm[:], cmat(CM_ONES), sq[:], start=(kc == 0), stop=(kc == 7))
                tmp = f32p.get(); rstd = f32p.get()
                act(tmp[:], pm[:], AF.Sqrt, bias=epsb[:, 0:1], scale=1.0 / D_MODEL)
                recip(rstd[:], tmp[:])
                hb = hbs.get()
                for kc in range(8):
                    stt("dve" if kc % 2 == 0 else "pool", hb[:, kc, :], xt[:, kc, :],
                        vec[:, VC["pre0"] + kc:VC["pre0"] + kc + 1], rstd[:], ALU.mult, ALU.mult)
                T.dma(hT.v(t, hT.ap.rearrange("(kc p) n -> p kc n", p=128)[:, :, sl]), hb[:], own="in")
                if l == 0:
                    T.dma(xres.v(t, xres.ap.rearrange("(kc p) n -> p kc n", p=128)[:, :, sl]), xt[:], own="in")

                def proj(c0, m, n=512):
                    pm_ = pp.get()
                    for kc in range(8):
                        mm(pm_[0:m, 0:n], wP[:, kc, c0:c0 + m], hb[:, kc, 0:n], start=(kc == 0), stop=(kc == 7))
                    return pm_

                for which, c0, dst in (("q", A_Q, QA), ("k", A_K, KA)):
                    for i in range(4):
                        pm_ = proj(c0 + i * 128, 128)
                        ob = bfp.get()
                        cp("act" if i % 2 == 0 else "dve", ob[:], pm_[:])
                        T.dma(dst.v(t, dst.ap[2 * i:2 * i + 2, :, sl].rearrange("a p n -> (a p) n")), ob[:], own="in")
                for tk in range(4):
                    pm_ = pp.get()
                    for kc in range(8):
                        mm(pm_[:], hb[:, kc, tk * 128:(tk + 1) * 128], wP[:, kc, A_V:A_V + 512], start=(kc == 0), stop=(kc == 7))
                    ob = bfp.get()
                    cp("act" if tk % 2 == 0 else "dve", ob[:], pm_[:])
                    r0 = t * 512 + tk * 128
                    T.dma(VA.v(t, VA.ap[:, r0:r0 + 128, :].rearrange("h p d -> p h d")), ob[:].re("p (h d) -> p h d", h=4), own="in")
                sqs = []
                for i in range(3):
                    pm_ = proj(B_CQ + i * 128, 128)
                    cp("act", cqf[:, i, :], pm_[:])
                    sq = f32p.get()
                    tt("dve", sq[:], cqf[:, i, :], cqf[:, i, :], ALU.mult)
                    sqs.append(sq[:])
                rs = f32p.get(); tmp = f32p.get()
                rms_stats(S, pp, sqs, 384, rs[:], tmp[:])
                for i in range(3):
                    stt("dve", cqn[:, i, :], cqf[:, i, :], vec[:, VC["mlaq"] + i:VC["mlaq"] + i + 1], rs[:], ALU.mult, ALU.mult)
                sqs = []
                for i in range(2):
                    pm_ = proj(B_CKV + i * 128, 128)
                    cp("act", ckf[:, i, :], pm_[:])
                    sq = f32p.get()
                    tt("pool", sq[:], ckf[:, i, :], ckf[:, i, :], ALU.mult)
                    sqs.append(sq[:])
                rs = f32p.get(); tmp = f32p.get()
                rms_stats(S, pp, sqs, 256, rs[:], tmp[:])
                for i in range(2):
                    stt("dve", ckn[:, i, :], ckf[:, i, :], vec[:, VC["mlakv"] + i:VC["mlakv"] + i + 1], rs[:], ALU.mult, ALU.mult)

                def rope32(src_ps, dst_bf):
                    xs_ = f32p.get()
                    cp("act", xs_[0:96, :], View(src_ps.b, src_ps.b.h[0:96, :]))
                    pr = pp.get()
                    mm(pr[0:96, :], cm[0:96, CM_R32, 0:96], xs_[0:96, :])
                    t1 = f32p.get(); t2 = f32p.get()
                    tt("dve", t1[64:96, :], xs_[64:96, :], tb[64:96, 0, :], ALU.mult)
                    tt("dve", t2[64:96, :], pr[64:96, :], tb[64:96, 1, :], ALU.mult)
                    tt("pool", dst_bf, t1[64:96, :], t2[64:96, :], ALU.add)

                pm_ = proj(B_KR - 64, 96)
                rope32(pm_[64:96, :], krope[64:96, :])
                for h in range(4):
                    pq = pp.get()
                    for i in range(3):
                        mm(pq[0:96, :], wuq[:, i, h * 96:(h + 1) * 96], cqn[:, i, :], start=(i == 0), stop=(i == 2))
                    ob = bfp.get()
                    rope32(pq[64:96, :], ob[64:96, :])
                    cp("act", ob[0:64, :], pq[0:64, :])
                    T.dma(QB.v(t, QB.ap[h, :, sl]), ob[0:96, :], own="in")
                    pk = pp.get()
                    for i in range(2):
                        mm(pk[0:64, :], wkK[:, i, h * 64:(h + 1) * 64], ckn[:, i, :], start=(i == 0), stop=(i == 1))
                    ob = bfp.get()
                    cp("dve", ob[64:96, :], krope[64:96, :])
                    cp("act", ob[0:64, :], pk[0:64, :])
                    T.dma(KB.v(t, KB.ap[h, :, sl]), ob[0:96, :], own="in")
                for tk in range(4):
                    pm_ = pp.get()
                    for i in range(2):
                        mm(pm_[:], ckn[:, i, tk * 128:(tk + 1) * 128], wkV[:, i, :], start=(i == 0), stop=(i == 1))
                    ob = bfp.get()
                    cp("act" if tk % 2 == 0 else "dve", ob[:], pm_[:])
                    r0 = t * 512 + tk * 128
                    T.dma(VB.v(t, VB.ap[:, r0:r0 + 128, :].rearrange("h p d -> p h d")), ob[:].re("p (h d) -> p h d", h=4), own="in")
                for i in range(5):
                    isq = i < 4
                    pm_ = proj((PC_DQ + i * 128) if isq else PC_DK, 128)
                    xf = f32p.get()
                    cp("act", xf[:], pm_[:])
                    sq = f32p.get()
                    tt("pool", sq[:], xf[:], xf[:], ALU.mult)
                    rs = f32p.get(); tmp = f32p.get()
                    rms_stats(S, pp, [sq[:]], 64, rs[:], tmp[:], blockmat=cmat(CM_BD64))
                    gcol = VC["gq"] if isq else VC["gk"]
                    xn = f32p.get()
                    stt("dve", xn[:], xf[:], vec[:, gcol:gcol + 1], rs[:], ALU.mult, ALU.mult)
                    pr = pp.get()
                    mm(pr[:], cmat(CM_RD), xn[:])
                    t1 = f32p.get(); t2 = f32p.get()
                    tt("dve", t1[:], xn[:], tb[:, 2, :], ALU.mult)
                    tt("dve", t2[:], pr[:], tb[:, 3, :], ALU.mult)
                    ob = bfp.get()
                    tt("pool", ob[:], t1[:], t2[:], ALU.add)
                    dst = QD if isq else KD
                    a0 = 2 * i if isq else 0
                    T.dma(dst.v(t, dst.ap[a0:a0 + 2, :, sl].rearrange("a p n -> (a p) n")), ob[:], own="in")
                for tk in range(4):
                    pm_ = pp.get()
                    for kc in range(8):
                        mm(pm_[:, 0:128], hb[:, kc, tk * 128:(tk + 1) * 128], wP[:, kc, PC_DV:PC_DV + 128], start=(kc == 0), stop=(kc == 7))
                    ob = bfp.get()
                    cp("act" if tk % 2 == 0 else "dve", ob[:, 0:128], pm_[:, 0:128])
                    r0 = t * 512 + tk * 128
                    T.dma(VD.v(t, VD.ap[:, r0:r0 + 128, :].rearrange("h p d -> p h d")), ob[:, 0:128].re("p (h d) -> p h d", h=2), own="in")
                for i in range(6):
                    pm_ = proj(PC_XBC + i * 128, 128)
                    of = f32p.get()
                    cp("act" if i % 2 == 0 else "dve", of[:], pm_[:])
                    T.dma(XBC.v(t, XBC.ap[i * 128:(i + 1) * 128, sl]), of[:], own="in")
                pm_ = proj(PC_DT, 16)
                of = f32p.get()
                cp("act", of[0:16, :], pm_[0:16, :])
                T.dma(DTR.v(t, DTR.ap[:, sl]), of[0:16, :], own="in")
            S.close()

        if want("ATT"):
            S = Stage("AT%d" % l)
            Ktr = S.pool(2, [128, L], BF16, "Kt")
            Vtr = S.pool(2, [128, NCH, 128], BF16, "Vt")
            Qp = S.pool(6, [128, 512], BF16, "Q")
            PTp = S.pool(3, [128, 1024], BF16, "PT")
            accDp = S.pool(2, [128, 512], F32, "accD")
            accPp = S.pool(2, [128, 512], F32, "accP")
            tmpSp = S.pool(2, [128, 1024], F32, "tmpS")
            bd = S.sb([128, 2048], F32, "bd")
            osbp = S.pool(2, [128, 512], F32, "osb")
            obfp = S.pool(2, [128, 512], BF16, "obf")
            rdp = S.pool(2, [128, 512], F32, "rd")
            psS = S.pspool(2, [128, 1024], "psS")
            psO = S.pspool(2, [128, 512], "psO")
            psX = S.pspool(2, [128, 512], "psX")

            class Cache:
                def __init__(self, ring):
                    self.ring = ring
                    self.map = {}

                def get(self, key, loader):
                    if key in self.map:
                        return self.map[key]
                    b = self.ring.get()
                    for k_ in [k_ for k_, v_ in self.map.items() if v_ is b]:
                        del self.map[k_]
                    loader(b)
                    self.map[key] = b
                    return b

            kc_ = Cache(Ktr)
            vc_ = Cache(Vtr)
            bd_state = {"h": None}

            def all_tiles(dt_):
                return [dt_.t(i) for i in range(NT)]

            def dep_views(dt_):
                return [View(b_, dt_.ap) for b_ in all_tiles(dt_)]

            units = []
            for h in range(4):
                for m in range(2):
                    units.append(("A", h, m))
            for h in range(4):
                units.append(("B", h, 0))
            for g in range(2):
                for r_ in range(4):
                    units.append(("D", g, r_))

            def load_kv(u):
                kind, a, b2 = u
                if kind == "A":
                    hm = 2 * a + b2

                    def lk(kt):
                        T.dma(kt[0:64, :], View(KA.t(0), KA.ap[hm]), extra_reads=dep_views(KA)[1:])
                        T.dma(kt[64:69, :], kaugA_in[a])
                    def lv(vt):
                        for c0 in range(0, NCH, 16):
                            c1 = min(c0 + 16, NCH)
                            T.dma(vt[:, c0:c1, :], View(VA.t(0), VA.ap[a].rearrange("(c p) d -> p c d", p=128)[:, c0:c1, :]), extra_reads=dep_views(VA)[1:])
                    return kc_.get(("A", hm), lk), vc_.get(("A", a), lv)
                if kind == "B":
                    def lk(kt):
                        T.dma(kt[0:96, :], View(KB.t(0), KB.ap[a]), extra_reads=dep_views(KB)[1:])
                        T.dma(kt[96:97, :], kmask_in[:, :])
                    def lv(vt):
                        for c0 in range(0, NCH, 16):
                            c1 = min(c0 + 16, NCH)
                            T.dma(vt[:, c0:c1, :], View(VB.t(0), VB.ap[a].rearrange("(c p) d -> p c d", p=128)[:, c0:c1, :]), extra_reads=dep_views(VB)[1:])
                    return kc_.get(("B", a), lk), vc_.get(("B", a), lv)

                def lk(kt):
                    T.dma(kt[0:64, :], View(KD.t(0), KD.ap[a]), extra_reads=dep_views(KD)[1:])
                    T.dma(kt[64:65, :], kmask_in[:, :])
                def lv(vt):
                    for c0 in range(0, NCH, 16):
                        c1 = min(c0 + 16, NCH)
                        T.dma(vt[:, c0:c1, 0:64], View(VD.t(0), VD.ap[a].rearrange("(c p) d -> p c d", p=128)[:, c0:c1, :]), extra_reads=dep_views(VD)[1:])
                    mset("pool", vt[:, :, 64:65], 1.0)
                return kc_.get(("D", a), lk), vc_.get(("D", a), lv)

            def load_q(u, qt):
                kind, a, b2 = u
                sl = slice(qt * 512, (qt + 1) * 512)
                if kind == "A":
                    hm = 2 * a + b2
                    qs = []
                    for ver in range(3):
                        q = Qp.get()
                        T.dma(q[0:64, :], QA.v(qt, QA.ap[hm, :, sl]))
                        T.dma(q[64:69, :], qaugA_in[ver, :, sl])
                        qs.append(q[0:69, :])
                    return qs
                q = Qp.get()
                if kind == "B":
                    T.dma(q[0:96, :], QB.v(qt, QB.ap[a, :, sl]))
                    T.dma(q[96:97, :], onesrow_in[:, sl])
                    return [q[0:97, :]]
                hq = 4 * a + b2
                T.dma(q[0:64, :], QD.v(qt, QD.ap[hq, :, sl]))
                T.dma(q[64:65, :], onesrow_in[:, sl])
                return [q[0:65, :]]

            def pairs_for(u, qt):
                kind, a, _ = u
                out_ = []
                for j in range(NCH // 2):
                    c0, c1 = 2 * j, 2 * j + 1
                    if kind != "A":
                        out_.append((j, 0, None))
                        continue
                    m_ = 2.0 ** (-2.0 * (a + 1))
                    if c1 < 4 * qt:
                        mind = 512 * qt - (128 * c1 + 127)
                        if mind * m_ >= cut_scale:
                            continue
                        out_.append((j, 0, None))
                    elif c0 > 4 * qt + 3:
                        mind = 128 * c0 - (512 * qt + 511)
                        if mind * m_ >= cut_scale:
                            continue
                        out_.append((j, 1, None))
                    else:
                        out_.append((j, 2, j - 2 * qt))
                return out_

            def compute(u, kt, vt):
                kind, a, b2 = u
                DK = {"A": 69, "B": 97, "D": 65}[kind]
                DVa = {"A": 128, "B": 128, "D": 65}[kind]
                scale = {"A": 0.125, "B": 96 ** -0.5, "D": 0.125}[kind]
                if kind == "A" and bd_state["h"] != a:
                    T.dma(bd[:, :], bdiag_in[a])
                    bd_state["h"] = a
                nq = load_q(u, 0)
                for qt in range(NT):
                    qs = nq
                    if qt + 1 < NT:
                        nq = load_q(u, qt + 1)
                    sl = slice(qt * 512, (qt + 1) * 512)
                    prs = pairs_for(u, qt)
                    po = psO.get()
                    if kind != "D":
                        accD = accDp.get(); accP = accPp.get()
                        mset("dve", accD[:], 0.0)
                        mset("pool", accP[:], 0.0)
                    n = len(prs)

                    def emit_S(j, ver):
                        ps = psS.get()
                        q = qs[ver] if kind == "A" else qs[0]
                        mm(ps[:, 0:512], kt[0:DK, (2 * j) * 128:(2 * j + 1) * 128], q)
                        mm(ps[:, 512:1024], kt[0:DK, (2 * j + 1) * 128:(2 * j + 2) * 128], q)
                        return ps

                    def emit_rest(ps, j, ver, dp, idx):
                        src = ps
                        if ver == 2:
                            tmp = tmpSp.get()
                            tt("dve", tmp[:, 0:512], ps[:, 0:512], bd[:, dp * 1024:dp * 1024 + 512], ALU.add)
                            tt("dve", tmp[:, 512:1024], ps[:, 512:1024], bd[:, dp * 1024 + 512:(dp + 1) * 1024], ALU.add)
                            src = tmp
                        pt = PTp.get()
                        act(pt[:], src[:], AF.Exp, scale=scale)
                        mm(po[0:DVa, :], vt[:, 2 * j, 0:DVa], pt[:, 0:512], start=(idx == 0), stop=False)
                        mm(po[0:DVa, :], vt[:, 2 * j + 1, 0:DVa], pt[:, 512:1024], start=False, stop=(idx == n - 1))
                        if kind != "D":
                            tt("dve", accD[:], accD[:], pt[:, 0:512], ALU.add)
                            tt("pool", accP[:], accP[:], pt[:, 512:1024], ALU.add)

                    prev = None
                    for idx, (j, ver, dp) in enumerate(prs):
                        ps = emit_S(j, ver)
                        if prev is not None:
                            emit_rest(*prev)
                        prev = (ps, j, ver, dp, idx)
                    emit_rest(*prev)
                    px = psX.get()
                    rd = rdp.get()
                    if kind != "D":
                        mm(px[:], cmat(CM_ONES), accD[:], start=True, stop=False)
                        mm(px[:], cmat(CM_ONES), accP[:], start=False, stop=True)
                        ts("dve", rd[:], px[:], 1e-30, ALU.max)
                        recip(rd[:], rd[:])
                        if kind == "A":
                            o = osbp.get()
                            tt("dve", o[:], po[:], rd[:], ALU.mult)
                            T.dma(OA.v(qt, OA.ap[2 * a + b2, :, sl]), o[:], own="in")
                        else:
                            o = obfp.get()
                            tt("dve", o[:], po[:], rd[:], ALU.mult)
                            T.dma(OB.v(qt, OB.ap[a * 128:(a + 1) * 128, sl]), o[:], own="in")
                    else:
                        o65 = osbp.get()
                        cp("act", o65[0:65, :], po[0:65, :])
                        mm(px[0:64, :], cm[0:65, CM_SEL64, 0:64], o65[0:65, :])
                        ts("dve", rd[0:64, :], px[0:64, :], 1e-30, ALU.max)
                        recip(rd[0:64, :], rd[0:64, :])
                        o = obfp.get()
                        tt("dve", o[0:64, :], o65[0:64, :], rd[0:64, :], ALU.mult)
                        hq = 4 * a + b2
                        T.dma(OD.v(qt, OD.ap[hq * 64:(hq + 1) * 64, sl]), o[0:64, :], own="in")

            import os as _os
            _kinds = _os.environ.get("ATT_KINDS", "ABD")
            units = [u for u in units if u[0] in _kinds]
            cur = load_kv(units[0])
            for ui, u in enumerate(units):
                kt, vt = cur
                if ui + 1 < len(units):
                    cur = load_kv(units[ui + 1])
                compute(u, kt, vt)
            S.close()

        if want("SSD"):
            S = Stage("SD%d" % l)
            vec = S.sb([128, NV], F32, "vec")
            T.dma(vec[:], vecs_in[l])
            rws = S.sb([128, 272], F32, "rows")
            T.dma(rws[:], rows_in[l])
            aneg = S.sb([128, 16], F32, "aneg")
            act(aneg[:], rws[:, 256:272], AF.Exp)
            ts("dve", aneg[:], aneg[:], -1.0, ALU.mult)
            xins = S.pool(2, [128, 6, 514], F32, "xin")
            xps = S.pool(2, [128, 6, 512], F32, "xp")
            f32p = S.pool(4, [128, 512], F32, "f")
            dts = S.pool(2, [16, 512], F32, "dt")
            tms = S.pool(2, [16, 512], F32, "tm")
            XBCv = XBC.ap.rearrange("(c p) n -> p c n", p=128)
            XBCPv = XBCP.ap.rearrange("(c p) n -> p c n", p=128)
            for t in range(NT):
                sl = slice(t * 512, (t + 1) * 512)
                xin = xins.get()
                T.dma(xin[:, :, 1:513], XBC.v(t, XBCv[:, :, sl]))
                if t > 0:
                    T.dma(xin[:, :, 0:1], XBC.v(t - 1, XBCv[:, :, t * 512 - 1:t * 512]), allow_slow_non_contiguous=True)
                else:
                    mset("pool", xin[:, :, 0:1], 0.0)
                if t < NT - 1:
                    T.dma(xin[:, :, 513:514], XBC.v(t + 1, XBCv[:, :, (t + 1) * 512:(t + 1) * 512 + 1]), allow_slow_non_contiguous=True)
                else:
                    mset("pool", xin[:, :, 513:514], 0.0)
                xp = xps.get()
                for c in range(6):
                    a0 = f32p.get(); a1 = f32p.get()
                    ts("dve" if c % 2 == 0 else "pool", a0[:], xin[:, c, 0:512], vec[:, VC["cw0"] + c:VC["cw0"] + c + 1], ALU.mult,
                       vec[:, VC["cb"] + c:VC["cb"] + c + 1], ALU.add)
                    stt("dve", a1[:], xin[:, c, 1:513], vec[:, VC["cw1"] + c:VC["cw1"] + c + 1], a0[:], ALU.mult, ALU.add)
                    stt("dve", a0[:], xin[:, c, 2:514], vec[:, VC["cw2"] + c:VC["cw2"] + c + 1], a1[:], ALU.mult, ALU.add)
                    act(xp[:, c, :], a0[:], AF.Silu)
                T.dma(XBCP.v(t, XBCPv[:, :, sl]), xp[:], own="in")
                dtt = dts.get(); tm = tms.get()
                T.dma(dtt[:], DTR.v(t, DTR.ap[:, sl]))
                T.dma(tm[:], tmask_in[0:16, sl])
                e1 = f32p.get()
                act(e1[0:16, :], dtt[:], AF.Exp, bias=vec[0:16, VC["dtb"]:VC["dtb"] + 1])
                act(e1[0:16, :], e1[0:16, :], AF.Ln, bias=1.0)
                tt("dve", dtt[:], e1[0:16, :], tm[:], ALU.mult)
                T.dma(DTP.v(t, DTP.ap[:, sl]), dtt[:], own="in")

            import os as _os
            _ndir = int(_os.environ.get("SSD_NDIR", "2"))
            _cut = int(_os.environ.get("SSD_CUT", "99"))
            st32 = S.sb([128, 4, 64], F32, "st32")
            stbf = S.sb([128, 4, 64], BF16, "stbf")
            xtoks = S.pool(2, [128, 512], F32, "xtok")
            xdts = S.pool(2, [128, 512], BF16, "xdt")
            smalls = S.pool(2, [128, 512], F32, "small")
            csrows = S.pool(2, [8, 128], F32, "csrow")
            bbfs = S.pool(2, [128, 128], BF16, "bbf")
            cbfz = [S.pool(2, [128, 128], BF16, "cbfz%d" % g_) for g_ in range(2)]
            cdecz = [S.pool(3, [128, 128], BF16, "cdecz%d" % g_) for g_ in range(2)]
            for g_ in range(2):
                for b_ in cbfz[g_].items + cdecz[g_].items:
                    mset("pool", b_[:], 0.0)
            cbsbs = S.pool(2, [128, 256], F32, "cbsb")
            args = S.pool(3, [128, 128], F32, "arg")
            decs = S.pool(3, [128, 128], F32, "dec")
            mts = S.pool(3, [128, 128], BF16, "mt")
            ecss = S.pool(3, [128, 128], F32, "ecs")
            bws = S.pool(3, [128, 128], BF16, "bw")
            yfs = S.pool(2, [128, 4, 512], F32, "yf")
            yos = S.pool(2, [128, 4, 512], F32, "yo")
            dtps = S.pool(2, [16, 512], F32, "dtp")
            pT = S.ps([128, 512], F32, "pT")
            pmisc = S.pspool(2, [128, 512], "pmisc")
            pcb = S.ps([128, 512], F32, "pcb")
            pbcs = S.pspool(2, [128, 512], "pbc")
            pys = S.pspool(2, [128, 512], "py")
            YFv = YF.ap.rearrange("(c p) n -> p c n", p=128)
            YSv = YS.ap.rearrange("(c p) n -> p c n", p=128)
            for d in range(_ndir):
                mset("dve", st32[:], 0.0)
                mset("pool", stbf[:], 0.0)
                tri = CM_U if d == 0 else CM_LO
                neg = CM_NEGF if d == 0 else CM_NEGB
                sel = CM_SEL127 if d == 0 else CM_SEL0
                torder = list(range(NT)) if d == 0 else list(range(NT - 1, -1, -1))
                corder = list(range(4)) if d == 0 else [3, 2, 1, 0]

                def load_tile(t):
                    sl_ = slice(t * 512, (t + 1) * 512)
                    xp_ = xps.get(); dtp_ = dtps.get()
                    T.dma(xp_[:], XBCP.v(t, XBCPv[:, :, sl_]))
                    T.dma(dtp_[:], DTP.v(t, DTP.ap[:, sl_]))
                    yf_ = None
                    if d == 1:
                        yf_ = yfs.get()
                        T.dma(yf_[:], YF.v(t, YFv[:, :, sl_]))
                    return xp_, dtp_, yf_

                nxt = load_tile(torder[0])
                for ti, t in enumerate(torder):
                    xp, dtp, yf = nxt
                    if ti + 1 < NT:
                        nxt = load_tile(torder[ti + 1])
                    sl = slice(t * 512, (t + 1) * 512)
                    yo = yos.get()
                    for c in corder:
                        cs = slice(c * 128, (c + 1) * 128)
                        if _cut <= 0:
                            continue
                        for i in range(4):
                            trp(pT[:, i * 128:(i + 1) * 128], xp[:, i, cs], ident)
                        xtok = xtoks.get()
                        cp("act", xtok[:], pT[:])
                        pm = pmisc.get()
                        sm = smalls.get()
                        trp(pm[:, 0:128], xp[:, 4, cs], ident)
                        trp(pm[:, 128:144], dtp[0:16, cs], cm[0:16, CM_ID, 0:16])
                        cp("dve", sm[:, 0:144], pm[:, 0:144])
                        btok = sm[:, 0:128]
                        if _cut <= 1:
                            continue
                        dttok = sm[:, 128 + d * 8:128 + d * 8 + 8]
                        dA = sm[:, 144:152]
                        tt("dve", dA, dttok, aneg[:, d * 8:d * 8 + 8], ALU.mult)
                        pm2 = pmisc.get()
                        mm(pm2[:, 0:8], cmat(tri), dA)
                        cscol = sm[:, 152:160]
                        cp("dve", cscol, pm2[:, 0:8])
                        mm(pm2[:, 8:16], cmat(sel), cscol)
                        trp(pm2[0:8, 128:256], cscol, ident)
                        csrow = csrows.get()
                        cp("act", csrow[:], pm2[0:8, 128:256])
                        cslast = sm[:, 160:168]
                        cp("dve", cslast, pm2[:, 8:16])
                        warg = sm[:, 168:176]
                        tt("dve", warg, cslast, cscol, ALU.subtract)
                        wall = sm[:, 176:184]
                        act(wall, warg, AF.Exp)
                        cdall = sm[:, 184:192]
                        act(cdall, cslast, AF.Exp)
                        if _cut <= 2:
                            continue
                        bbf = bbfs.get()
                        cp("pool", bbf[:], xp[:, 4, cs])
                        for g in range(2):
                            gs = slice(g * 64, (g + 1) * 64)
                            cbf = cbfz[g].get()
                            cp("pool", cbf[gs, :], xp[gs, 5, cs])
                            mm(pcb[:, g * 128:(g + 1) * 128], bbf[:, :], cbf[:, :])
                        cbsb = cbsbs.get()
                        cp("act", cbsb[:], pcb[:, 0:256])
                        xdt = xdts.get()
                        for h in range(8):
                            ts("pool" if h % 2 == 0 else "dve", xdt[:, h * 64:(h + 1) * 64], xtok[:, h * 64:(h + 1) * 64], dttok[:, h:h + 1], ALU.mult)
                        for h in range(8):
                            if _cut <= 3:
                                continue
                            g = h // 4
                            gs = slice(g * 64, (g + 1) * 64)
                            hs = slice((h % 2) * 64, (h % 2) * 64 + 64)
                            pr = (h % 4) // 2
                            pbc = pbcs.get()
                            mm(pbc[:, 0:128], cm[0:8, CM_SELH + h, :], csrow[:])
                            arg = args.get()
                            stt("dve", arg[:], pbc[:, 0:128], cscol[:, h:h + 1], cmat(neg), ALU.subtract, ALU.add)
                            dec = decs.get()
                            act(dec[:], arg[:], AF.Exp)
                            mt = mts.get()
                            tt("dve", mt[:], cbsb[:, g * 128:(g + 1) * 128], dec[:], ALU.mult)
                            if _cut <= 4:
                                continue
                            ecs = ecss.get()
                            act(ecs[gs, :], pbc[gs, 0:128], AF.Exp)
                            cdec = cdecz[g].get()
                            tt("pool", cdec[gs, :], xp[gs, 5, cs], ecs[gs, :], ALU.mult)
                            py = pys.get()
                            pair_cols = slice((h // 2) * 128, (h // 2) * 128 + 128)
                            mm(py[:, 0:128], xdt[:, pair_cols], mt[:], start=True, stop=False)
                            mm(py[:, 0:128], stbf[:, pr * 2:pr * 2 + 2, :].re("p a b -> p (a b)"), cdec[:, :], start=False, stop=True)
                            if _cut <= 5:
                                continue
                            bw = bws.get()
                            ts("pool", bw[:], btok, wall[:, h:h + 1], ALU.mult)
                            mm(py[:, 128:192], bw[:], xdt[:, h * 64:(h + 1) * 64])
                            if _cut <= 6:
                                continue
                            if d == 0:
                                stt("dve", yo[hs, h // 2, cs], xp[hs, h // 2, cs], vec[hs, VC["ssmd"] + h // 2:VC["ssmd"] + h // 2 + 1], py[hs, 0:128], ALU.mult, ALU.add)
                            else:
                                tt("dve", yo[hs, h // 2, cs], py[hs, 0:128], yf[hs, h // 2, cs], ALU.add)
                            stt("dve", st32[gs, h % 4, :], st32[gs, h % 4, :], cdall[gs, h:h + 1], py[gs, 128:192], ALU.mult, ALU.add)
                        cp("act", stbf[:], st32[:])
                    if d == 0:
                        T.dma(YF.v(t, YFv[:, :, sl]), yo[:], own="in")
                    else:
                        T.dma(YS.v(t, YSv[:, :, sl]), yo[:], own="in")
            S.close()

        TM = 256
        NTM = L // TM
        if want("MERGE"):
            S = Stage("MG%d" % l)
            wG = alloc_w(S, D_MODEL, 4608, "wG"); wbr = alloc_w(S, 2048, D_MODEL, "wbr"); wo = alloc_w(S, D_MODEL, D_MODEL, "wo")
            vec = S.sb([128, NV], F32, "vec")
            rws = S.sb([128, 272], F32, "rows")
            lam = S.sb([128, 8], F32, "lam")
            ltmp = S.sb([128, 128], F32, "ltmp")
            W = Stage("MGw%d" % l)
            stg = W.pool(2, [128, 2048], F32, "stg")
            load_w_bf16(S, wG_in, wG_in.h[l], D_MODEL, 4608, stg, "wG", w=wG)
            load_w_bf16(S, wbr_in, wbr_in.h[l], 2048, D_MODEL, stg, "wbr", w=wbr)
            load_w_bf16(S, wout_in, wout_in.h[l], D_MODEL, D_MODEL, stg, "wo", w=wo)
            W.close()
            T.dma(vec[:], vecs_in[l])
            T.dma(rws[:], rows_in[l])
            tt("dve", ltmp[:, 0:64], rws[:, 0:64], rws[:, 64:128], ALU.mult)
            tt("dve", ltmp[:, 64:128], rws[:, 128:192], rws[:, 192:256], ALU.mult)
            T.op("dve", lambda: V.reduce_sum(out=lam.h[:, 0:1], in_=ltmp.h[:, 0:64], axis=mybir.AxisListType.X), reads=[ltmp[:]], writes=[lam[:]])
            T.op("dve", lambda: V.reduce_sum(out=lam.h[:, 1:2], in_=ltmp.h[:, 64:128], axis=mybir.AxisListType.X), reads=[ltmp[:]], writes=[lam[:]])
            act(lam[:, 2:4], lam[:, 0:2], AF.Exp)
            tt("dve", lam[:, 4:5], lam[:, 3:4], lam[:, 2:3], ALU.subtract)
            ts("dve", lam[:, 5:6], lam[:, 4:5], -lam_init, ALU.add)
            neglam = lam[:, 5:6]
            hb = S.sb([128, 8, TM], BF16, "hb"); xt = S.sb([128, 8, TM], F32, "xt")
            oa = S.sb([128, 8, TM], F32, "oa"); ob = S.sb([128, 4, TM], BF16, "ob"); od = S.sb([128, 4, TM], BF16, "od")
            ys = S.sb([128, 4, TM], F32, "ys"); tm = S.sb([128, TM], F32, "tm")
            ya = S.sb([128, 4, TM], BF16, "ya"); yc = S.sb([128, 4, TM], BF16, "yc"); yz = S.sb([128, 4, TM], F32, "yz")
            mrg = S.sb([128, 8, TM], BF16, "mrg"); outf = S.sb([128, 8, TM], F32, "outf")
            f32p = S.pool(6, [128, TM], F32, "f")
            sgp = S.pool(3, [128, TM], F32, "sg")
            rsb = S.sb([128, TM], F32, "rsb")
            pp = S.pspool(6, [128, 512], "pp")
            pst = S.pspool(2, [128, 512], "pst")
            kcv = lambda a: a.rearrange("(kc p) n -> p kc n", p=128)
            for i in range(NTM):
                sl = slice(i * TM, (i + 1) * TM)
                k_ = ("m", i)
                T.dma(hb[:], hT.v(k_, kcv(hT.ap)[:, :, sl]))
                T.dma(xt[:], xres.v(k_, kcv(xres.ap)[:, :, sl]))
                T.dma(oa[:], OA.v(k_, OA.ap[:, :, sl].rearrange("a p n -> p a n")))
                T.dma(ob[:], OB.v(k_, kcv(OB.ap)[:, :, sl]))
                T.dma(od[:], OD.v(k_, kcv(OD.ap)[:, :, sl]))
                T.dma(ys[:], YS.v(k_, kcv(YS.ap)[:, :, sl]))
                T.dma(tm[:], tmask_in[:, sl])
                for h in range(4):
                    o = f32p.get(); sq = f32p.get(); rs = f32p.get(); tmp = f32p.get()
                    stt("dve", o[:], oa[:, 2 * h + 1, :], neglam, oa[:, 2 * h, :], ALU.mult, ALU.add)
                    tt("pool", sq[:], o[:], o[:], ALU.mult)
                    pm = pp.get()
                    mm(pm[:, 0:TM], cmat(CM_ONES), sq[:])
                    act(tmp[:], pm[:, 0:TM], AF.Sqrt, bias=epsb[:, 0:1], scale=1.0 / 128)
                    recip(rs[:], tmp[:])
                    ts("dve", rs[:], rs[:], 1.0 - lam_init, ALU.mult)
                    stt("dve", ya[:, h, :], o[:], vec[:, VC["diffn"]:VC["diffn"] + 1], rs[:], ALU.mult, ALU.mult)
                pm = pst.get()
                for zc in range(4):
                    pz = pp.get()
                    for kc in range(8):
                        mm(pz[:, 0:TM], wG[:, kc, 4096 + zc * 128:4096 + (zc + 1) * 128], hb[:, kc, :], start=(kc == 0), stop=(kc == 7))
                    zs = f32p.get()
                    act(zs[:], pz[:, 0:TM], AF.Silu)
                    tt("dve", yz[:, zc, :], zs[:], ys[:, zc, :], ALU.mult)
                    sq = f32p.get()
                    tt("pool", sq[:], yz[:, zc, :], yz[:, zc, :], ALU.mult)
                    mm(pm[:, 0:TM], cmat(CM_ONES), sq[:], start=(zc == 0), stop=(zc == 3))
                tmp = f32p.get(); rs = f32p.get()
                act(tmp[:], pm[:, 0:TM], AF.Sqrt, bias=epsb[:, 0:1], scale=1.0 / 512)
                recip(rs[:], tmp[:])
                for zc in range(4):
                    stt("dve", yc[:, zc, :], yz[:, zc, :], vec[:, VC["ssmn"] + zc:VC["ssmn"] + zc + 1], rs[:], ALU.mult, ALU.mult)
                Ys = [ya, ob, yc, od]
                for oc in range(8):
                    macc = f32p.get()
                    for n_ in range(4):
                        pg = pp.get()
                        for kc in range(8):
                            mm(pg[:, 0:TM], wG[:, kc, n_ * 1024 + oc * 128:n_ * 1024 + (oc + 1) * 128], hb[:, kc, :], start=(kc == 0), stop=(kc == 7))
                        sg = sgp.get()
                        act(sg[:], pg[:, 0:TM], AF.Sigmoid)
                        pb = pp.get()
                        for kc in range(4):
                            mm(pb[:, 0:TM], wbr[:, n_ * 4 + kc, oc * 128:(oc + 1) * 128], Ys[n_][:, kc, :], start=(kc == 0), stop=(kc == 3))
                        if n_ == 0:
                            tt("dve", macc[:], pb[:, 0:TM], sg[:], ALU.mult)
                        else:
                            t2 = f32p.get()
                            tt("dve", t2[:], pb[:, 0:TM], sg[:], ALU.mult)
                            if n_ < 3:
                                tt("pool", macc[:], macc[:], t2[:], ALU.add)
                            else:
                                tt("pool", mrg[:, oc, :], macc[:], t2[:], ALU.add)
                pm = pst.get()
                for oc in range(8):
                    po = pp.get()
                    for kc in range(8):
                        mm(po[:, 0:TM], wo[:, kc, oc * 128:(oc + 1) * 128], mrg[:, kc, :], start=(kc == 0), stop=(kc == 7))
                    cp("act", outf[:, oc, :], po[:, 0:TM])
                    sq = f32p.get()
                    tt("pool", sq[:], outf[:, oc, :], outf[:, oc, :], ALU.mult)
                    mm(pm[:, 0:TM], cmat(CM_ONES), sq[:], start=(oc == 0), stop=(oc == 7))
                tmp = f32p.get(); rs = rsb
                act(tmp[:], pm[:, 0:TM], AF.Sqrt, bias=epsb[:, 0:1], scale=1.0 / D_MODEL)
                recip(rs[:], tmp[:])
                tt("dve", rs[:], rs[:], tm[:], ALU.mult)
                for oc in range(8):
                    t2 = f32p.get()
                    stt("dve", t2[:], outf[:, oc, :], vec[:, VC["post0"] + oc:VC["post0"] + oc + 1], rs[:], ALU.mult, ALU.mult)
                    tt("pool", xt[:, oc, :], xt[:, oc, :], t2[:], ALU.add)
                T.dma(xres.v(k_, kcv(xres.ap)[:, :, sl]), xt[:], own="in")
            S.close()

        if want("MEM"):
            S = Stage("MM%d" % l)
            wq = alloc_w(S, D_MODEL, 512, "wq"); wmo = alloc_w(S, 512, D_MODEL, "wmo")
            vec = S.sb([128, NV], F32, "vec")
            onesb = S.sb([128, 128], BF16, "onesb")
            Kmem = S.sb([128, 4, 256], BF16, "Kmem"); Vmem = S.sb([128, 2, 512], BF16, "Vmem")
            pp = S.pspool(3, [128, 512], "pp")
            pst = S.pspool(1, [128, 512], "pst")
            pS2 = S.pspool(2, [128, 1024], "pS2")
            f32p = S.pool(6, [128, 512], F32, "f")
            W = Stage("MMw%d" % l)
            stg = W.pool(2, [128, 2048], F32, "stg")
            load_w_bf16(S, wmq_in, wmq_in.h[l], D_MODEL, 512, stg, "wq", w=wq)
            load_w_bf16(S, wmo_in, wmo_in.h[l], 512, D_MODEL, stg, "wmo", w=wmo)
            wk = load_w_bf16(W, wmk_in, wmk_in.h[l], D_MODEL, 512, stg, "wk")
            wv = load_w_bf16(W, wmv_in, wmv_in.h[l], D_MODEL, 512, stg, "wv")
            T.dma(vec[:], vecs_in[l])
            cp("dve", onesb[:], cmat(CM_ONES))
            memf = W.sb([128, 8, 256], F32, "memf"); mn = W.sb([128, 8, 256], BF16, "mn")
            T.dma(memf[:], View(memT_in, memT_in.h.rearrange("(kc p) n -> p kc n", p=128)))
            pm = pst.get()
            for kc in range(8):
                sq = f32p.get()
                tt("dve", sq[:, 0:256], memf[:, kc, :], memf[:, kc, :], ALU.mult)
                mm(pm[:, 0:256], cmat(CM_ONES), sq[:, 0:256], start=(kc == 0), stop=(kc == 7))
            tmp = f32p.get(); rs = f32p.get()
            act(tmp[:, 0:256], pm[:, 0:256], AF.Sqrt, bias=epsb[:, 0:1], scale=1.0 / D_MODEL)
            recip(rs[:, 0:256], tmp[:, 0:256])
            for kc in range(8):
                stt("dve", mn[:, kc, :], memf[:, kc, :], vec[:, VC["memn"] + kc:VC["memn"] + kc + 1], rs[:, 0:256], ALU.mult, ALU.mult)
            for h in range(4):
                pk = pp.get()
                for kc in range(8):
                    mm(pk[:, 0:256], wk[:, kc, h * 128:(h + 1) * 128], mn[:, kc, :], start=(kc == 0), stop=(kc == 7))
                cp("act", Kmem[:, h, :], pk[:, 0:256])
            for mb in range(2):
                pv = pp.get()
                for kc in range(8):
                    mm(pv[:, :], mn[:, kc, mb * 128:(mb + 1) * 128], wv[:, kc, :], start=(kc == 0), stop=(kc == 7))
                cp("act", Vmem[:, mb, :], pv[:, :])
            W.close()
            xts = S.pool(2, [128, 8, 512], F32, "xt")
            tms = S.pool(2, [128, 512], F32, "tm")
            h1 = S.sb([128, 8, 512], BF16, "h1")
            h2s = S.pool(2, [128, 8, 512], BF16, "h2")
            oh = S.sb([128, 4, 512], BF16, "oh")
            outf = S.sb([128, 8, 512], F32, "outf")
            qhs = S.pool(2, [128, 512], BF16, "qh")
            rsb = S.sb([128, 512], F32, "rsb")
            pts = S.pool(2, [128, 1024], BF16, "pt")
            kcv = lambda a: a.rearrange("(kc p) n -> p kc n", p=128)

            def norm_to(xt_, col0, dst):
                pm_ = pst.get()
                for kc in range(8):
                    sq = f32p.get()
                    if kc % 2 == 0:
                        act(sq[:], xt_[:, kc, :], AF.Square)
                    else:
                        tt("pool", sq[:], xt_[:, kc, :], xt_[:, kc, :], ALU.mult)
                    mm(pm_[:], cmat(CM_ONES), sq[:], start=(kc == 0), stop=(kc == 7))
                tmp_ = f32p.get(); rs_ = f32p.get()
                act(tmp_[:], pm_[:], AF.Sqrt, bias=epsb[:, 0:1], scale=1.0 / D_MODEL)
                recip(rs_[:], tmp_[:])
                for kc in range(8):
                    stt("dve", dst[:, kc, :], xt_[:, kc, :], vec[:, col0 + kc:col0 + kc + 1], rs_[:], ALU.mult, ALU.mult)

            def ld(t):
                sl_ = slice(t * 512, (t + 1) * 512)
                xt_ = xts.get(); tm_ = tms.get()
                T.dma(xt_[:], xres.v(("e", t), kcv(xres.ap)[:, :, sl_]))
                T.dma(tm_[:], tmask_in[:, sl_])
                return xt_, tm_

            nxt = ld(0)
            for t in range(NT):
                xt, tm = nxt
                if t + 1 < NT:
                    nxt = ld(t + 1)
                sl = slice(t * 512, (t + 1) * 512)
                norm_to(xt, VC["pre1"], h1)
                for h in range(4):
                    pq = pp.get()
                    for kc in range(8):
                        mm(pq[:], wq[:, kc, h * 128:(h + 1) * 128], h1[:, kc, :], start=(kc == 0), stop=(kc == 7))
                    qh = qhs.get()
                    cp("act", qh[:], pq[:])
                    ps2 = pS2.get()
                    for mb in range(2):
                        mm(ps2[:, mb * 512:(mb + 1) * 512], Kmem[:, h, mb * 128:(mb + 1) * 128], qh[:])
                    pt = pts.get()
                    act(pt[:], ps2[:], AF.Exp, scale=128 ** -0.5)
                    po = pp.get(); pd = pp.get()
                    for mb in range(2):
                        mm(po[:], Vmem[:, mb, h * 128:(h + 1) * 128], pt[:, mb * 512:(mb + 1) * 512], start=(mb == 0), stop=(mb == 1))
                    for mb in range(2):
                        mm(pd[:], onesb[:], pt[:, mb * 512:(mb + 1) * 512], start=(mb == 0), stop=(mb == 1))
                    rd = f32p.get()
                    recip(rd[:], pd[:])
                    tt("dve", oh[:, h, :], po[:], rd[:], ALU.mult)
                pm = pst.get()
                for oc in range(8):
                    po = pp.get()
                    for h in range(4):
                        mm(po[:], wmo[:, h, oc * 128:(oc + 1) * 128], oh[:, h, :], start=(h == 0), stop=(h == 3))
                    cp("act", outf[:, oc, :], po[:])
                    sq = f32p.get()
                    tt("pool", sq[:], outf[:, oc, :], outf[:, oc, :], ALU.mult)
                    mm(pm[:], cmat(CM_ONES), sq[:], start=(oc == 0), stop=(oc == 7))
                tmp = f32p.get(); rs = rsb
                act(tmp[:], pm[:], AF.Sqrt, bias=epsb[:, 0:1], scale=1.0 / D_MODEL)
                recip(rs[:], tmp[:])
                tt("dve", rs[:], rs[:], tm[:], ALU.mult)
                for oc in range(8):
                    t2 = f32p.get()
                    stt("dve", t2[:], outf[:, oc, :], vec[:, VC["post1"] + oc:VC["post1"] + oc + 1], rs[:], ALU.mult, ALU.mult)
                    tt("pool", xt[:, oc, :], xt[:, oc, :], t2[:], ALU.add)
                T.dma(xres.v(("e", t), kcv(xres.ap)[:, :, sl]), xt[:], own="in")
                h2 = h2s.get()
                norm_to(xt, VC["pre2"], h2)
                T.dma(H2.v(("e", t), kcv(H2.ap)[:, :, sl]), h2[:], own="in")
            S.close()

        if want("FFN"):
            S = Stage("FF%d" % l)
            wfi = alloc_w(S, D_MODEL, 2 * D_FF, "wfi"); wfo = alloc_w(S, D_FF, D_MODEL, "wfo")
            vec = S.sb([128, NV], F32, "vec")
            W = Stage("FFw%d" % l)
            stg = W.pool(2, [128, 2048], F32, "stg")
            load_w_bf16(S, wfi_in, wfi_in.h[l], D_MODEL, 2 * D_FF, stg, "wfi", w=wfi)
            load_w_bf16(S, wfo_in, wfo_in.h[l], D_FF, D_MODEL, stg, "wfo", w=wfo)
            W.close()
            T.dma(vec[:], vecs_in[l])
            h2t = S.pool(2, [128, 8, TM + 2], BF16, "h2t")
            xt = S.sb([128, 8, TM], F32, "xt"); tm = S.sb([128, TM], F32, "tm")
            actT = S.sb([128, 22, TM], BF16, "actT")
            outf = S.sb([128, 8, TM], F32, "outf")
            f32p = S.pool(8, [128, TM], F32, "f")
            rsb = S.sb([128, TM], F32, "rsb")
            pp = S.pspool(7, [128, 512], "pp")
            pst = S.pspool(1, [128, 512], "pst")
            kcv = lambda a: a.rearrange("(kc p) n -> p kc n", p=128)
            H2v = kcv(H2.ap)
            last = (l == depth - 1)

            def ldh(i):
                c0 = i * TM
                hh = h2t.get()
                lo = max(c0 - 1, 0); hi = min(c0 + TM + 1, L)
                T.dma(hh[:, :, (lo - (c0 - 1)):(hi - (c0 - 1))], H2.v(("f", 0), H2v[:, :, lo:hi]))
                if c0 == 0:
                    mset("pool", hh[:, :, 0:1], 0.0)
                if c0 + TM == L:
                    mset("pool", hh[:, :, TM + 1:TM + 2], 0.0)
                return hh

            nh = ldh(0)
            for i in range(NTM):
                sl = slice(i * TM, (i + 1) * TM)
                hh = nh
                if i + 1 < NTM:
                    nh = ldh(i + 1)
                k_ = ("f", i + 1)
                T.dma(xt[:], xres.v(k_, kcv(xres.ap)[:, :, sl]))
                T.dma(tm[:], tmask_in[:, sl])
                for j in range(22):
                    cv = []
                    for half in range(2):
                        ch = j + 22 * half
                        pu = pp.get()
                        for kc in range(8):
                            mm(pu[:, 0:TM + 2], wfi[:, kc, ch * 128:(ch + 1) * 128], hh[:, kc, :], start=(kc == 0), stop=(kc == 7))
                        a0 = f32p.get(); a1 = f32p.get()
                        ts("dve", a0[:], pu[:, 0:TM], vec[:, VC["fw0"] + ch:VC["fw0"] + ch + 1], ALU.mult, vec[:, VC["fb"] + ch:VC["fb"] + ch + 1], ALU.add)
                        stt("dve", a1[:], pu[:, 1:TM + 1], vec[:, VC["fw1"] + ch:VC["fw1"] + ch + 1], a0[:], ALU.mult, ALU.add)
                        stt("dve", a0[:], pu[:, 2:TM + 2], vec[:, VC["fw2"] + ch:VC["fw2"] + ch + 1], a1[:], ALU.mult, ALU.add)
                        cv.append(a0)
                    ga = f32p.get()
                    act(ga[:], cv[0][:], AF.Gelu)
                    tt("pool", actT[:, j, :], ga[:], cv[1][:], ALU.mult)
                pm = pst.get()
                for oc in range(8):
                    po = pp.get()
                    for j in range(22):
                        mm(po[:, 0:TM], wfo[:, j, oc * 128:(oc + 1) * 128], actT[:, j, :], start=(j == 0), stop=(j == 21))
                    cp("act", outf[:, oc, :], po[:, 0:TM])
                    sq = f32p.get()
                    tt("pool", sq[:], outf[:, oc, :], outf[:, oc, :], ALU.mult)
                    mm(pm[:, 0:TM], cmat(CM_ONES), sq[:], start=(oc == 0), stop=(oc == 7))
                tmp = f32p.get(); rs = rsb
                act(tmp[:], pm[:, 0:TM], AF.Sqrt, bias=epsb[:, 0:1], scale=1.0 / D_MODEL)
                recip(rs[:], tmp[:])
                tt("dve", rs[:], rs[:], tm[:], ALU.mult)
                for oc in range(8):
                    t2 = f32p.get()
                    stt("dve", t2[:], outf[:, oc, :], vec[:, VC["post2"] + oc:VC["post2"] + oc + 1], rs[:], ALU.mult, ALU.mult)
                    tt("pool", xt[:, oc, :], xt[:, oc, :], t2[:], ALU.add)
                dst = yT_out if last else xres
                T.dma(dst.v(k_, kcv(dst.ap)[:, :, sl]), xt[:], own="in")
            S.close()

    T.finish()
    glob.close()
    return nc


def _const_tables(L):
    c = {}
    cm = np.zeros((NCM, 128, 128), np.float32)
    k = np.arange(128)
    cm[0] = np.eye(128)
    cm[1] = 1.0
    cm[2] = (k[:, None] <= k[None, :])
    cm[3] = (k[:, None] >= k[None, :])
    cm[4] = np.where(k[:, None] <= k[None, :], 0.0, NEG)
    cm[5] = np.where(k[:, None] >= k[None, :], 0.0, NEG)
    cm[6] = (k[:, None] // 64 == k[None, :] // 64)
    for m in range(32):
        if m < 16:
            cm[7][64 + m + 16, 64 + m] = -1.0
        else:
            cm[7][64 + m - 16, 64 + m] = 1.0
    for blk in range(2):
        for sub in range(2):
            o = blk * 64 + sub * 32
            for m in range(32):
                if m < 16:
                    cm[8][o + m + 16, o + m] = -1.0
                else:
                    cm[8][o + m - 16, o + m] = 1.0
    cm[9][127, :] = 1.0
    cm[10][64, :] = 1.0
    cm[11][0, :] = 1.0
    for h_ in range(16):
        cm[12 + h_][h_, :] = 1.0
    c["cmat"] = np.ascontiguousarray(cm.transpose(1, 0, 2))
    pos = np.arange(L, dtype=np.float32)
    freqs = (np.float32(10000.0) ** (-(np.arange(16, dtype=np.float32)) / np.float32(16))).astype(np.float32)
    angB = (pos[None, :] * freqs[:, None]).astype(np.float32)
    c["cosB"] = np.concatenate([np.cos(angB), np.cos(angB)], 0).astype(np.float32)
    c["sinB"] = np.concatenate([np.sin(angB), np.sin(angB)], 0).astype(np.float32)
    rowp = (np.arange(L) // 64).astype(np.float32)
    colp = (np.arange(L) % 64).astype(np.float32)
    angR = (rowp[None, :] * freqs[:, None]).astype(np.float32)
    angC = (colp[None, :] * freqs[:, None]).astype(np.float32)
    cD = np.concatenate([np.cos(angR), np.cos(angR), np.cos(angC), np.cos(angC)], 0)
    sD = np.concatenate([np.sin(angR), np.sin(angR), np.sin(angC), np.sin(angC)], 0)
    c["cosD"] = np.concatenate([cD, cD], 0).astype(np.float32)
    c["sinD"] = np.concatenate([sD, sD], 0).astype(np.float32)
    return c


def _alibi_tables(L, Lreal):
    j = np.arange(L)
    jh = (j // 128).astype(np.float32)
    jl = (j % 128).astype(np.float32)
    kmask = np.where(j < Lreal, 0.0, NEG).astype(np.float32)
    kaug = np.zeros((4, 5, L), np.float32)
    bdiag = np.zeros((4, 128, 2048), np.float32)
    il = np.arange(512, dtype=np.float32)
    jl128 = np.arange(128, dtype=np.float32)
    for h in range(4):
        m = 2.0 ** (-2.0 * (h + 1))
        kaug[h, 0] = -8 * m * 128
        kaug[h, 1] = -8 * m
        kaug[h, 2] = 8 * m * 128 * jh
        kaug[h, 3] = 8 * m * jl
        kaug[h, 4] = kmask
        for dp in range(2):
            for hf in range(2):
                koff = 128 * (2 * dp + hf)
                c0 = dp * 1024 + hf * 512
                bdiag[h, :, c0:c0 + 512] = -8 * m * np.abs(il[None, :] - (koff + jl128[:, None]))
    qaug = np.zeros((3, 5, L), np.float32)
    qaug[0, 0] = jh; qaug[0, 1] = jl; qaug[0, 2] = 1; qaug[0, 3] = 1; qaug[0, 4] = 1
    qaug[1, 0] = -jh; qaug[1, 1] = -jl; qaug[1, 2] = -1; qaug[1, 3] = -1; qaug[1, 4] = 1
    qaug[2, 4] = 1
    return dict(kaugA=kaug.astype(NPBF), qaugA=qaug.astype(NPBF), bdiag=bdiag,
                kmask=kmask[None, :].astype(NPBF), onesrow=np.ones((1, L), NPBF))


def _weights_layout(p, depth):
    f = lambda a: np.ascontiguousarray(np.asarray(a, dtype=np.float32))
    w_in = f(p["w_in"])
    o = {}
    o["wP"] = f(np.concatenate([w_in[:, :, 0:2208], w_in[:, :, 2720:4272]], axis=2))
    o["wG"] = f(np.concatenate([w_in[:, :, G_OFF:G_OFF + 4096], w_in[:, :, C_Z:C_Z + 512]], axis=2))
    o["wuq"] = f(p["w_mla_uq"])
    wukv = f(p["w_mla_ukv"]).reshape(depth, 256, 4, 192)
    o["wukvK"] = f(wukv[..., 0:64].reshape(depth, 256, 256))
    o["wukvV"] = f(wukv[..., 64:192].reshape(depth, 256, 512))
    o["wbr"] = f(p["w_branch"]).reshape(depth, 2048, D_MODEL)
    o["wout"] = f(p["w_out"])
    o["wmq"] = f(p["w_mem_q"])
    wkv = f(p["w_mem_kv"]).reshape(depth, D_MODEL, 4, 256)
    o["wmk"] = f(wkv[..., 0:128].reshape(depth, D_MODEL, 512))
    o["wmv"] = f(wkv[..., 128:256].reshape(depth, D_MODEL, 512))
    o["wmo"] = f(p["w_mem_o"])
    o["wfi"] = f(p["w_ffn_in"])
    o["wfo"] = f(p["w_ffn_out"])
    vecs = np.zeros((depth, 128, NV), np.float32)

    def put(name, arr, width):
        a = f(arr).reshape(depth, width, 128).transpose(0, 2, 1)
        vecs[:, :, VC[name]:VC[name] + width] = a

    npre = f(p["norm_pre"]); npost = f(p["norm_post"])
    for i in range(3):
        put("pre%d" % i, npre[:, i], 8)
        put("post%d" % i, npost[:, i], 8)
    put("diffn", p["diff_norm"], 1)
    put("mlaq", p["mla_q_norm"], 3)
    put("mlakv", p["mla_kv_norm"], 2)
    cw = f(p["ssm_conv_w"])
    for k_ in range(3):
        put("cw%d" % k_, cw[:, k_], 6)
    put("cb", p["ssm_conv_b"], 6)
    put("ssmn", p["ssm_norm"], 4)
    put("gq", np.tile(f(p["gqa_q_norm"]), (1, 2)), 1)
    put("gk", np.tile(f(p["gqa_k_norm"]), (1, 2)), 1)
    put("memn", p["mem_norm"], 8)
    dtb = f(p["ssm_dt_bias"]).reshape(depth, 16)
    vecs[:, 0:16, VC["dtb"]] = dtb
    put("ssmd", np.repeat(f(p["ssm_d"]), 64, axis=1), 4)
    fw = f(p["ffn_conv_w"])
    for k_ in range(3):
        put("fw%d" % k_, fw[:, k_], 44)
    put("fb", p["ffn_conv_b"], 44)
    o["vecs"] = vecs
    rows = np.zeros((depth, 128, 272), np.float32)
    rows[:, :, 0:256] = f(p["diff_lambda"]).reshape(depth, 1, 256)
    rows[:, :, 256:272] = f(p["ssm_a_log"]).reshape(depth, 1, 16)
    o["rows"] = rows
    return o


def make_in_maps(seqs, mems, params, L, depth):
    consts = _const_tables(L)
    wl = _weights_layout(params, depth)
    maps = []
    cache = {}
    for x, mem in zip(seqs, mems):
        Lr = x.shape[0]
        if Lr not in cache:
            cache[Lr] = _alibi_tables(L, Lr)
        m = dict(consts)
        m.update(wl)
        m.update(cache[Lr])
        xT = np.zeros((D_MODEL, L), np.float32)
        xT[:, :Lr] = np.asarray(x, np.float32).T
        m["xT"] = xT
        m["memT"] = np.ascontiguousarray(np.asarray(mem, np.float32).T)
        tm = np.zeros((128, L), np.float32)
        tm[:, :Lr] = 1.0
        m["tmask"] = tm
        maps.append(m)
    return maps


PARAM_NAMES = ["w_in", "w_branch", "w_out", "diff_lambda", "diff_norm", "mla_q_norm", "mla_kv_norm",
               "w_mla_uq", "w_mla_ukv", "ssm_conv_w", "ssm_conv_b", "ssm_a_log", "ssm_dt_bias", "ssm_d",
               "ssm_norm", "gqa_q_norm", "gqa_k_norm", "mem_norm", "w_mem_q", "w_mem_kv", "w_mem_o",
               "w_ffn_in", "ffn_conv_w", "ffn_conv_b", "w_ffn_out", "norm_pre", "norm_post"]


def kernel(**inputs):
    xp = np.asarray(inputs["x_prompt"], np.float32)
    xs = np.asarray(inputs["x_sample"], np.float32)
    mp = np.asarray(inputs["mem_prompt"], np.float32)
    ms = np.asarray(inputs["mem_sample"], np.float32)
    params = {n: np.asarray(inputs[n], np.float32) for n in PARAM_NAMES}
    depth = params["w_in"].shape[0]
    L = xp.shape[1]
    seqs = [xp[0], xp[1], xs[0], xs[1], xs[2], xs[3], xs[0], xs[1]]
    mems = [mp[0], mp[1], ms[0], ms[1], ms[2], ms[3], ms[0], ms[1]]
    outs = run_trunk(seqs, mems, params, L, depth)
    y_prompt = np.stack([outs[0], outs[1]], 0).astype(np.float32)
    y_sample = np.stack([outs[2], outs[3], outs[4], outs[5]], 0).astype(np.float32)
    return (y_prompt, y_sample)


_NC_CACHE = {}


def run_trunk(seqs, mems, params, L, depth):
    key = (L, depth)
    if key not in _NC_CACHE:
        _NC_CACHE[key] = build_program(L, depth=depth)
    nc = _NC_CACHE[key]
    maps = make_in_maps(seqs, mems, params, L, depth)
    res = run_bass_kernel_spmd(nc, maps, core_ids=list(range(8)))
    outs = []
    for i, x in enumerate(seqs):
        yT = np.asarray(res.results[i]["yT"], np.float32)
        outs.append(np.ascontiguousarray(yT[:, :x.shape[0]].T))
    return outs
```

```python
import math
import numpy as np
import ml_dtypes
import concourse.bass as bass
import concourse.mybir as mybir
from concourse.bass_utils import run_bass_kernel_spmd
from contextlib import ExitStack

F32 = mybir.dt.float32
BF16 = mybir.dt.bfloat16
AF = mybir.ActivationFunctionType
ALU = mybir.AluOpType
NPBF = ml_dtypes.bfloat16

ENGS = ("pe", "act", "dve", "pool")
D_MODEL = 1024
EPS = 1e-6
MEM_LEN = 256
D_FF = 2816
NEG = -30000.0
NCM = 28


class Buf:
    __slots__ = ("h", "name", "w", "r", "ld", "st")

    def __init__(self, h, name):
        self.h = h
        self.name = name
        self.w = None
        self.r = {}
        self.ld = None
        self.st = None

    def __getitem__(self, idx):
        return View(self, self.h[idx])


class View:
    __slots__ = ("b", "ap")

    def __init__(self, b, ap):
        self.b = b
        self.ap = ap

    def __getitem__(self, idx):
        return View(self.b, self.ap[idx])

    def re(self, pat, **kw):
        return View(self.b, self.ap.rearrange(pat, **kw))


class Trk:
    def __init__(self, nc, es, n_dma_sems=80):
        self.nc = nc
        self.eng = {"pe": nc.tensor, "act": nc.scalar, "dve": nc.vector, "pool": nc.gpsimd,
                    "sp": nc.sync}
        self.sem = {e: es.enter_context(nc.semaphore("c_" + e)) for e in ENGS}
        self.cnt = {e: 0 for e in ENGS}
        self.dsem = [es.enter_context(nc.semaphore("d%d" % i)) for i in range(n_dma_sems)]
        self.dcnt = [0] * n_dma_sems
        self.dfree = list(range(n_dma_sems))
        self.issuers = list(ENGS) + ["sp"]
        self.known = {e: {} for e in self.issuers}
        self.nins = 0

    def _handle(self, key):
        return self.sem[key] if isinstance(key, str) else self.dsem[key]

    def _wait(self, issuer, dep):
        if dep is None:
            return
        key, val = dep
        if issuer == "pe" and key == "pe":
            return
        if self.known[issuer].get(key, 0) >= val:
            return
        self.eng[issuer].wait_ge(self._handle(key), val)
        self.known[issuer][key] = val
        self.nins += 1

    def _deps(self, issuer, reads, writes):
        for v in reads:
            self._wait(issuer, v.b.w)
        for v in writes:
            self._wait(issuer, v.b.w)
            for k, val in v.b.r.items():
                self._wait(issuer, (k, val))

    def op(self, e, fn, reads=(), writes=()):
        self._deps(e, reads, writes)
        ins = fn()
        self.cnt[e] += 1
        self.nins += 1
        ins.then_inc(self.sem[e], 1)
        c = self.cnt[e]
        for v in reads:
            v.b.r[e] = c
        for v in writes:
            v.b.w = (e, c)
            v.b.r = {}
        return ins

    def dma(self, out, in_, own="out", q="sp", extra_reads=(), extra_writes=(), **kw):
        self._deps(q, [in_] + list(extra_reads), [out] + list(extra_writes))
        if own == "out":
            if out.b.ld is None:
                out.b.ld = self.dfree.pop()
            slot = out.b.ld
        else:
            if in_.b.st is None:
                in_.b.st = self.dfree.pop()
            slot = in_.b.st
        ins = self.eng[q].dma_start(out=out.ap, in_=in_.ap, **kw)
        self.nins += 1
        self.dcnt[slot] += 16
        ins.then_inc(self.dsem[slot], 16)
        c = self.dcnt[slot]
        for v in [in_] + list(extra_reads):
            v.b.r[slot] = c
        for v in [out] + list(extra_writes):
            v.b.w = (slot, c)
            v.b.r = {}
        return ins

    def full_barrier(self, release=()):
        for issuer in self.issuers:
            for e in ENGS:
                if self.cnt[e]:
                    self._wait(issuer, (e, self.cnt[e]))
            for s in range(len(self.dsem)):
                if self.dcnt[s]:
                    self._wait(issuer, (s, self.dcnt[s]))
        for b in release:
            for s in (b.ld, b.st):
                if s is not None:
                    self.dfree.append(s)
            b.ld = b.st = None

    def finish(self):
        for s in range(len(self.dsem)):
            if self.dcnt[s]:
                self._wait("sp", (s, self.dcnt[s]))


A_Q, A_K, A_V = 0, 512, 1024
B_CQ, B_CKV, B_KR = 1536, 1920, 2176
C_Z, C_XBC, C_DTF, C_DTB = 2208, 2720, 3488, 3496
D_Q, D_K, D_V = 3504, 4016, 4144
G_OFF = 4272
PC_XBC = 2208
PC_DT = 2208 + 768
PC_DQ = PC_DT + 16
PC_DK = PC_DQ + 512
PC_DV = PC_DK + 128
NPC = PC_DV + 128

VC = {}
_o = 0
for _n, _w in [("pre0", 8), ("pre1", 8), ("pre2", 8), ("post0", 8), ("post1", 8), ("post2", 8),
               ("diffn", 1), ("mlaq", 3), ("mlakv", 2), ("cw0", 6), ("cw1", 6), ("cw2", 6), ("cb", 6),
               ("ssmn", 4), ("gq", 1), ("gk", 1), ("memn", 8), ("dtb", 1), ("ssmd", 4),
               ("fw0", 44), ("fw1", 44), ("fw2", 44), ("fb", 44)]:
    VC[_n] = _o
    _o += _w
NV = _o


def build_program(L, depth=2, dbg=(), cut_scale=60.0, stages=None):
    nc = bass.Bass("TRN2", target_bir_lowering=False)
    NT = L // 512
    NCH = L // 128
    glob = ExitStack()
    T = Trk(nc, glob)
    V, A, P, G = nc.vector, nc.scalar, nc.tensor, nc.gpsimd
    ENG = {"dve": V, "act": A, "pool": G}

    def dram(name, shape, dt, kind="Internal"):
        if name in dbg:
            kind = "ExternalOutput"
        return nc.dram_tensor(name, shape, dt, kind=kind).ap()

    class DT:
        def __init__(self, name, shape, dt, kind="Internal", tok_axis=-1):
            self.ap = dram(name, shape, dt, kind)
            self.name = name
            self.tiles = {}

        def t(self, i):
            if i not in self.tiles:
                self.tiles[i] = Buf(self.ap, "%s.%s" % (self.name, i))
            return self.tiles[i]

        def v(self, i, ap):
            return View(self.t(i), ap)

    def ext(name, shape, dt=F32):
        return Buf(nc.dram_tensor(name, shape, dt, kind="ExternalInput").ap(), name)

    xT_in = ext("xT", [D_MODEL, L])
    memT_in = ext("memT", [D_MODEL, MEM_LEN])
    tmask_in = ext("tmask", [128, L])
    kaugA_in = ext("kaugA", [4, 5, L], BF16)
    qaugA_in = ext("qaugA", [3, 5, L], BF16)
    kmask_in = ext("kmask", [1, L], BF16)
    onesrow_in = ext("onesrow", [1, L], BF16)
    bdiag_in = ext("bdiag", [4, 128, 2048])
    cosB_in = ext("cosB", [32, L]); sinB_in = ext("sinB", [32, L])
    cosD_in = ext("cosD", [128, L]); sinD_in = ext("sinD", [128, L])
    cmat_in = ext("cmat", [128, NCM, 128])
    wP_in = ext("wP", [depth, D_MODEL, NPC])
    wG_in = ext("wG", [depth, D_MODEL, 4096 + 512])
    wuq_in = ext("wuq", [depth, 384, 384])
    wukvK_in = ext("wukvK", [depth, 256, 256])
    wukvV_in = ext("wukvV", [depth, 256, 512])
    wbr_in = ext("wbr", [depth, 2048, D_MODEL])
    wout_in = ext("wout", [depth, D_MODEL, D_MODEL])
    wmq_in = ext("wmq", [depth, D_MODEL, 512])
    wmk_in = ext("wmk", [depth, D_MODEL, 512])
    wmv_in = ext("wmv", [depth, D_MODEL, 512])
    wmo_in = ext("wmo", [depth, 512, D_MODEL])
    wfi_in = ext("wfi", [depth, D_MODEL, 2 * D_FF])
    wfo_in = ext("wfo", [depth, D_FF, D_MODEL])
    vecs_in = ext("vecs", [depth, 128, NV])
    rows_in = ext("rows", [depth, 128, 256 + 16])
    yT_out = DT("yT", [D_MODEL, L], F32, kind="ExternalOutput")

    xres = DT("xres", [D_MODEL, L], F32)
    hT = DT("hT", [D_MODEL, L], BF16)
    QA = DT("QA", [8, 64, L], BF16); KA = DT("KA", [8, 64, L], BF16); VA = DT("VA", [4, L, 128], BF16)
    QB = DT("QB", [4, 96, L], BF16); KB = DT("KB", [4, 96, L], BF16); VB = DT("VB", [4, L, 128], BF16)
    QD = DT("QD", [8, 64, L], BF16); KD = DT("KD", [2, 64, L], BF16); VD = DT("VD", [2, L, 64], BF16)
    XBC = DT("XBC", [768, L], F32); DTR = DT("DTR", [16, L], F32)
    XBCP = DT("XBCP", [768, L], F32); DTP = DT("DTP", [16, L], F32)
    OA = DT("OA", [8, 128, L], F32); OB = DT("OB", [512, L], BF16); OD = DT("OD", [512, L], BF16)
    YF = DT("YF", [512, L], F32); YS = DT("YS", [512, L], F32)
    H2 = DT("H2", [D_MODEL, L], BF16)

    def mm(out, lhsT, rhs, start=True, stop=True):
        T.op("pe", lambda: P.matmul(out.ap, lhsT=lhsT.ap, rhs=rhs.ap, start=start, stop=stop),
             reads=[lhsT, rhs], writes=[out])

    def trp(out, in_, ident):
        T.op("pe", lambda: P.matmul(out.ap, lhsT=in_.ap, rhs=ident.ap, start=True, stop=True), reads=[in_, ident], writes=[out])

    def act(out, in_, func, bias=None, scale=1.0):
        rd = [in_]
        kw = {}
        if isinstance(bias, View):
            rd.append(bias); kw["bias"] = bias.ap
        elif bias is not None:
            kw["bias"] = bias
        T.op("act", lambda: A.activation(out=out.ap, in_=in_.ap, func=func, scale=scale, **kw),
             reads=rd, writes=[out])

    def _s(x, rd):
        if isinstance(x, View):
            rd.append(x)
            return x.ap
        return x

    def ts(e, out, in0, s1, op0, s2=None, op1=None):
        rd = [in0]
        a1 = _s(s1, rd); a2 = _s(s2, rd)
        kw = {} if op1 is None else {"op1": op1}
        T.op(e, lambda: ENG[e].tensor_scalar(out=out.ap, in0=in0.ap, scalar1=a1, scalar2=a2, op0=op0, **kw),
             reads=rd, writes=[out])

    def tt(e, out, in0, in1, op):
        T.op(e, lambda: ENG[e].tensor_tensor(out=out.ap, in0=in0.ap, in1=in1.ap, op=op),
             reads=[in0, in1], writes=[out])

    def stt(e, out, in0, scalar, in1, op0, op1):
        e = "dve"
        rd = [in0, in1]
        a = _s(scalar, rd)
        T.op(e, lambda: ENG[e].scalar_tensor_tensor(out=out.ap, in0=in0.ap, scalar=a, in1=in1.ap, op0=op0, op1=op1),
             reads=rd, writes=[out])

    def cp(e, out, in_):
        if e == "act":
            T.op("act", lambda: A.copy(out=out.ap, in_=in_.ap), reads=[in_], writes=[out])
        else:
            T.op(e, lambda: ENG[e].tensor_copy(out=out.ap, in_=in_.ap), reads=[in_], writes=[out])

    def mset(e, out, val):
        T.op(e, lambda: ENG[e].memset(out.ap, val), writes=[out])

    def recip(out, in_):
        T.op("dve", lambda: V.reciprocal(out=out.ap, in_=in_.ap), reads=[in_], writes=[out])

    class Stage:
        def __init__(self, name):
            self.es = ExitStack()
            self.bufs = []
            self.name = name
            self.n = 0

        def sb(self, shape, dt, name=None):
            self.n += 1
            nm = "%s_%d_%s" % (self.name, self.n, name or "t")
            b = Buf(self.es.enter_context(nc.sbuf_tensor(nm, list(shape), dt)), nm)
            self.bufs.append(b)
            return b

        def ps(self, shape, dt=F32, name=None):
            self.n += 1
            nm = "%s_%d_%s" % (self.name, self.n, name or "p")
            b = Buf(self.es.enter_context(nc.psum_tensor(nm, list(shape), dt)), nm)
            self.bufs.append(b)
            return b

        def pool(self, n, shape, dt, name="pl"):
            return Ring([self.sb(shape, dt, name) for _ in range(n)])

        def pspool(self, n, shape, name="pp"):
            return Ring([self.ps(shape, F32, name) for _ in range(n)])

        def close(self):
            T.full_barrier(release=self.bufs)
            self.es.close()

    class Ring:
        def __init__(self, items):
            self.items = items
            self.i = 0

        def get(self):
            b = self.items[self.i % len(self.items)]
            self.i += 1
            return b

    def alloc_w(S, K, N, name):
        return S.sb([128, (K + 127) // 128, N], BF16, name)

    def load_w_bf16(S, src_buf, src_ap, K, N, stg_ring, name, engs=("dve", "pool"), w=None):
        kc_n = (K + 127) // 128
        if w is None:
            w = S.sb([128, kc_n, N], BF16, name)
        step = stg_ring.items[0].h.shape[1]
        i = 0
        for kc in range(kc_n):
            rows = min(128, K - kc * 128)
            for c0 in range(0, N, step):
                cw = min(step, N - c0)
                st = stg_ring.get()
                T.dma(st[0:rows, 0:cw], View(src_buf, src_ap[kc * 128:kc * 128 + rows, c0:c0 + cw]))
                cp(engs[i % len(engs)], w[0:rows, kc, c0:c0 + cw], st[0:rows, 0:cw])
                i += 1
        return w

    cm = Buf(glob.enter_context(nc.sbuf_tensor("cmat_sb", [128, NCM, 128], F32)), "cmat_sb")
    T.dma(cm[:], cmat_in[:, :, :])
    CM_ID, CM_ONES, CM_U, CM_LO, CM_NEGF, CM_NEGB, CM_BD64, CM_R32, CM_RD, CM_SEL127, CM_SEL64, CM_SEL0, CM_SELH = range(13)
    ident = cm[:, CM_ID, :]

    def cmat(i, rows=128, cols=128):
        return cm[0:rows, i, 0:cols]

    def rms_stats(S, pp, sq_chunks, n_feat, out_rstd, tmp, blockmat=None, extra_scale=None):
        pm = pp.get()
        n = len(sq_chunks)
        for i, sq in enumerate(sq_chunks):
            rows = sq.ap.shape[0]
            lhs = (blockmat if blockmat is not None else cmat(CM_ONES, rows, 128))
            mm(pm[:, 0:512], lhs, sq, start=(i == 0), stop=(i == n - 1))
        act(tmp, pm[:, 0:512], AF.Sqrt, bias=epsb[:, 0:1], scale=1.0 / n_feat)
        recip(out_rstd, tmp)
        if extra_scale is not None:
            ts("dve", out_rstd, out_rstd, extra_scale, ALU.mult)

    epsb = Buf(glob.enter_context(nc.sbuf_tensor("epsb", [128, 1], F32)), "epsb")
    mset("dve", epsb[:], EPS)

    def want(s):
        return stages is None or s in stages

    for l in range(depth):
        lam_init = 0.8 - 0.6 * math.exp(-0.3 * l)
        xsrc = None

        def xin_view(t):
            if l == 0:
                return View(xT_in, xT_in.h.rearrange("(kc p) n -> p kc n", p=128)[:, :, t * 512:(t + 1) * 512])
            return xres.v(t, xres.ap.rearrange("(kc p) n -> p kc n", p=128)[:, :, t * 512:(t + 1) * 512])

        if want("P"):
            S = Stage("P%d" % l)
            stg = S.pool(2, [128, 1880], F32, "stg")
            wP = load_w_bf16(S, wP_in, wP_in.h[l], D_MODEL, NPC, stg, "wP")
            wuq = load_w_bf16(S, wuq_in, wuq_in.h[l], 384, 384, stg, "wuq")
            wkK = load_w_bf16(S, wukvK_in, wukvK_in.h[l], 256, 256, stg, "wkK")
            wkV = load_w_bf16(S, wukvV_in, wukvV_in.h[l], 256, 512, stg, "wkV")
            vec = S.sb([128, NV], F32, "vec")
            T.dma(vec[:], vecs_in[l])
            xts = S.pool(2, [128, 8, 512], F32, "xt")
            hbs = S.pool(2, [128, 8, 512], BF16, "hb")
            f32p = S.pool(6, [128, 512], F32, "f")
            bfp = S.pool(6, [128, 512], BF16, "b")
            cqf = S.sb([128, 3, 512], F32, "cqf"); cqn = S.sb([128, 3, 512], BF16, "cqn")
            ckf = S.sb([128, 2, 512], F32, "ckf"); ckn = S.sb([128, 2, 512], BF16, "ckn")
            tabs = S.pool(2, [128, 4, 512], F32, "tab")
            krope = S.sb([128, 512], BF16, "krope")
            pp = S.pspool(6, [128, 512], "pp")

            def load_x(t):
                xt = xts.get()
                T.dma(xt[:], xin_view(t))
                return xt

            nxt = load_x(0)
            for t in range(NT):
                xt = nxt
                tb = tabs.get()
                sl = slice(t * 512, (t + 1) * 512)
                T.dma(tb[64:96, 0, :], cosB_in[:, sl]); T.dma(tb[64:96, 1, :], sinB_in[:, sl])
                T.dma(tb[:, 2, :], cosD_in[:, sl]); T.dma(tb[:, 3, :], sinD_in[:, sl])
                if t + 1 < NT:
                    nxt = load_x(t + 1)
                sqs = []
                pm = pp.get()
                for kc in range(8):
                    sq = f32p.get()
                    if kc % 2 == 0:
                        act(sq[:], xt[:, kc, :], AF.Square)
                    else:
                        tt("pool", sq[:], xt[:, kc, :], xt[:, kc, :], ALU.mult)
                    mm(pm[:], cmat(CM_ONES), sq[:], start=(kc == 0), stop=(kc == 7))
                tmp = f32p.get(); rstd = f32p.get()
                act(tmp[:], pm[:], AF.Sqrt, bias=epsb[:, 0:1], scale=1.0 / D_MODEL)
                recip(rstd[:], tmp[:])
                hb = hbs.get()
                for kc in range(8):
                    stt("dve" if kc % 2 == 0 else "pool", hb[:, kc, :], xt[:, kc, :],
                        vec[:, VC["pre0"] + kc:VC["pre0"] + kc + 1], rstd[:], ALU.mult, ALU.mult)
                T.dma(hT.v(t, hT.ap.rearrange("(kc p) n -> p kc n", p=128)[:, :, sl]), hb[:], own="in")
                if l == 0:
                    T.dma(xres.v(t, xres.ap.rearrange("(kc p) n -> p kc n", p=128)[:, :, sl]), xt[:], own="in")

                def proj(c0, m, n=512):
                    pm_ = pp.get()
                    for kc in range(8):
                        mm(pm_[0:m, 0:n], wP[:, kc, c0:c0 + m], hb[:, kc, 0:n], start=(kc == 0), stop=(kc == 7))
                    return pm_

                for which, c0, dst in (("q", A_Q, QA), ("k", A_K, KA)):
                    for i in range(4):
                        pm_ = proj(c0 + i * 128, 128)
                        ob = bfp.get()
                        cp("act" if i % 2 == 0 else "dve", ob[:], pm_[:])
                        T.dma(dst.v(t, dst.ap[2 * i:2 * i + 2, :, sl].rearrange("a p n -> (a p) n")), ob[:], own="in")
                for tk in range(4):
                    pm_ = pp.get()
                    for kc in range(8):
                        mm(pm_[:], hb[:, kc, tk * 128:(tk + 1) * 128], wP[:, kc, A_V:A_V + 512], start=(kc == 0), stop=(kc == 7))
                    ob = bfp.get()
                    cp("act" if tk % 2 == 0 else "dve", ob[:], pm_[:])
                    r0 = t * 512 + tk * 128
                    T.dma(VA.v(t, VA.ap[:, r0:r0 + 128, :].rearrange("h p d -> p h d")), ob[:].re("p (h d) -> p h d", h=4), own="in")
                sqs = []
                for i in range(3):
                    pm_ = proj(B_CQ + i * 128, 128)
                    cp("act", cqf[:, i, :], pm_[:])
                    sq = f32p.get()
                    tt("dve", sq[:], cqf[:, i, :], cqf[:, i, :], ALU.mult)
                    sqs.append(sq[:])
                rs = f32p.get(); tmp = f32p.get()
                rms_stats(S, pp, sqs, 384, rs[:], tmp[:])
                for i in range(3):
                    stt("dve", cqn[:, i, :], cqf[:, i, :], vec[:, VC["mlaq"] + i:VC["mlaq"] + i + 1], rs[:], ALU.mult, ALU.mult)
                sqs = []
                for i in range(2):
                    pm_ = proj(B_CKV + i * 128, 128)
                    cp("act", ckf[:, i, :], pm_[:])
                    sq = f32p.get()
                    tt("pool", sq[:], ckf[:, i, :], ckf[:, i, :], ALU.mult)
                    sqs.append(sq[:])
                rs = f32p.get(); tmp = f32p.get()
                rms_stats(S, pp, sqs, 256, rs[:], tmp[:])
                for i in range(2):
                    stt("dve", ckn[:, i, :], ckf[:, i, :], vec[:, VC["mlakv"] + i:VC["mlakv"] + i + 1], rs[:], ALU.mult, ALU.mult)

                def rope32(src_ps, dst_bf):
                    xs_ = f32p.get()
                    cp("act", xs_[0:96, :], View(src_ps.b, src_ps.b.h[0:96, :]))
                    pr = pp.get()
                    mm(pr[0:96, :], cm[0:96, CM_R32, 0:96], xs_[0:96, :])
                    t1 = f32p.get(); t2 = f32p.get()
                    tt("dve", t1[64:96, :], xs_[64:96, :], tb[64:96, 0, :], ALU.mult)
                    tt("dve", t2[64:96, :], pr[64:96, :], tb[64:96, 1, :], ALU.mult)
                    tt("pool", dst_bf, t1[64:96, :], t2[64:96, :], ALU.add)

                pm_ = proj(B_KR - 64, 96)
                rope32(pm_[64:96, :], krope[64:96, :])
                for h in range(4):
                    pq = pp.get()
                    for i in range(3):
                        mm(pq[0:96, :], wuq[:, i, h * 96:(h + 1) * 96], cqn[:, i, :], start=(i == 0), stop=(i == 2))
                    ob = bfp.get()
                    rope32(pq[64:96, :], ob[64:96, :])
                    cp("act", ob[0:64, :], pq[0:64, :])
                    T.dma(QB.v(t, QB.ap[h, :, sl]), ob[0:96, :], own="in")
                    pk = pp.get()
                    for i in range(2):
                        mm(pk[0:64, :], wkK[:, i, h * 64:(h + 1) * 64], ckn[:, i, :], start=(i == 0), stop=(i == 1))
                    ob = bfp.get()
                    cp("dve", ob[64:96, :], krope[64:96, :])
                    cp("act", ob[0:64, :], pk[0:64, :])
                    T.dma(KB.v(t, KB.ap[h, :, sl]), ob[0:96, :], own="in")
                for tk in range(4):
                    pm_ = pp.get()
                    for i in range(2):
                        mm(pm_[:], ckn[:, i, tk * 128:(tk + 1) * 128], wkV[:, i, :], start=(i == 0), stop=(i == 1))
                    ob = bfp.get()
                    cp("act" if tk % 2 == 0 else "dve", ob[:], pm_[:])
                    r0 = t * 512 + tk * 128
                    T.dma(VB.v(t, VB.ap[:, r0:r0 + 128, :].rearrange("h p d -> p h d")), ob[:].re("p (h d) -> p h d", h=4), own="in")
                for i in range(5):
                    isq = i < 4
                    pm_ = proj((PC_DQ + i * 128) if isq else PC_DK, 128)
                    xf = f32p.get()
                    cp("act", xf[:], pm_[:])
                    sq = f32p.get()
                    tt("pool", sq[:], xf[:], xf[:], ALU.mult)
                    rs = f32p.get(); tmp = f32p.get()
                    rms_stats(S, pp, [sq[:]], 64, rs[:], tmp[:], blockmat=cmat(CM_BD64))
                    gcol = VC["gq"] if isq else VC["gk"]
                    xn = f32p.get()
                    stt("dve", xn[:], xf[:], vec[:, gcol:gcol + 1], rs[:], ALU.mult, ALU.mult)
                    pr = pp.get()
                    mm(pr[:], cmat(CM_RD), xn[:])
                    t1 = f32p.get(); t2 = f32p.get()
                    tt("dve", t1[:], xn[:], tb[:, 2, :], ALU.mult)
                    tt("dve", t2[:], pr[:], tb[:, 3, :], ALU.mult)
                    ob = bfp.get()
                    tt("pool", ob[:], t1[:], t2[:], ALU.add)
                    dst = QD if isq else KD
                    a0 = 2 * i if isq else 0
                    T.dma(dst.v(t, dst.ap[a0:a0 + 2, :, sl].rearrange("a p n -> (a p) n")), ob[:], own="in")
                for tk in range(4):
                    pm_ = pp.get()
                    for kc in range(8):
                        mm(pm_[:, 0:128], hb[:, kc, tk * 128:(tk + 1) * 128], wP[:, kc, PC_DV:PC_DV + 128], start=(kc == 0), stop=(kc == 7))
                    ob = bfp.get()
                    cp("act" if tk % 2 == 0 else "dve", ob[:, 0:128], pm_[:, 0:128])
                    r0 = t * 512 + tk * 128
                    T.dma(VD.v(t, VD.ap[:, r0:r0 + 128, :].rearrange("h p d -> p h d")), ob[:, 0:128].re("p (h d) -> p h d", h=2), own="in")
                for i in range(6):
                    pm_ = proj(PC_XBC + i * 128, 128)
                    of = f32p.get()
                    cp("act" if i % 2 == 0 else "dve", of[:], pm_[:])
                    T.dma(XBC.v(t, XBC.ap[i * 128:(i + 1) * 128, sl]), of[:], own="in")
                pm_ = proj(PC_DT, 16)
                of = f32p.get()
                cp("act", of[0:16, :], pm_[0:16, :])
                T.dma(DTR.v(t, DTR.ap[:, sl]), of[0:16, :], own="in")
            S.close()

        if want("ATT"):
            S = Stage("AT%d" % l)
            Ktr = S.pool(2, [128, L], BF16, "Kt")
            Vtr = S.pool(2, [128, NCH, 128], BF16, "Vt")
            Qp = S.pool(6, [128, 512], BF16, "Q")
            PTp = S.pool(3, [128, 1024], BF16, "PT")
            accDp = S.pool(2, [128, 512], F32, "accD")
            accPp = S.pool(2, [128, 512], F32, "accP")
            tmpSp = S.pool(2, [128, 1024], F32, "tmpS")
            bd = S.sb([128, 2048], F32, "bd")
            osbp = S.pool(2, [128, 512], F32, "osb")
            obfp = S.pool(2, [128, 512], BF16, "obf")
            rdp = S.pool(2, [128, 512], F32, "rd")
            psS = S.pspool(2, [128, 1024], "psS")
            psO = S.pspool(2, [128, 512], "psO")
            psX = S.pspool(2, [128, 512], "psX")
            onesb = S.sb([128, 128], BF16, "onesb")
            cp("dve", onesb[:], cmat(CM_ONES))

            class Cache:
                def __init__(self, ring):
                    self.ring = ring
                    self.map = {}

                def get(self, key, loader):
                    if key in self.map:
                        return self.map[key]
                    b = self.ring.get()
                    for k_ in [k_ for k_, v_ in self.map.items() if v_ is b]:
                        del self.map[k_]
                    loader(b)
                    self.map[key] = b
                    return b

            kc_ = Cache(Ktr)
            vc_ = Cache(Vtr)
            bd_state = {"h": None}

            def all_tiles(dt_):
                return [dt_.t(i) for i in range(NT)]

            def dep_views(dt_):
                return [View(b_, dt_.ap) for b_ in all_tiles(dt_)]

            units = []
            for h in range(4):
                for m in range(2):
                    units.append(("A", h, m))
            for h in range(4):
                units.append(("B", h, 0))
            for g in range(2):
                for r_ in range(4):
                    units.append(("D", g, r_))

            def load_kv(u):
                kind, a, b2 = u
                if kind == "A":
                    hm = 2 * a + b2

                    def lk(kt):
                        T.dma(kt[0:64, :], View(KA.t(0), KA.ap[hm]), extra_reads=dep_views(KA)[1:])
                        T.dma(kt[64:69, :], kaugA_in[a])
                    def lv(vt):
                        for c0 in range(0, NCH, 16):
                            c1 = min(c0 + 16, NCH)
                            T.dma(vt[:, c0:c1, :], View(VA.t(0), VA.ap[a].rearrange("(c p) d -> p c d", p=128)[:, c0:c1, :]), extra_reads=dep_views(VA)[1:])
                    return kc_.get(("A", hm), lk), vc_.get(("A", a), lv)
                if kind == "B":
                    def lk(kt):
                        T.dma(kt[0:96, :], View(KB.t(0), KB.ap[a]), extra_reads=dep_views(KB)[1:])
                        T.dma(kt[96:97, :], kmask_in[:, :])
                    def lv(vt):
                        for c0 in range(0, NCH, 16):
                            c1 = min(c0 + 16, NCH)
                            T.dma(vt[:, c0:c1, :], View(VB.t(0), VB.ap[a].rearrange("(c p) d -> p c d", p=128)[:, c0:c1, :]), extra_reads=dep_views(VB)[1:])
                    return kc_.get(("B", a), lk), vc_.get(("B", a), lv)

                def lk(kt):
                    T.dma(kt[0:64, :], View(KD.t(0), KD.ap[a]), extra_reads=dep_views(KD)[1:])
                    T.dma(kt[64:65, :], kmask_in[:, :])
                def lv(vt):
                    for c0 in range(0, NCH, 16):
                        c1 = min(c0 + 16, NCH)
                        T.dma(vt[:, c0:c1, 0:64], View(VD.t(0), VD.ap[a].rearrange("(c p) d -> p c d", p=128)[:, c0:c1, :]), extra_reads=dep_views(VD)[1:])
                    mset("pool", vt[:, :, 64:65], 1.0)
                return kc_.get(("D", a), lk), vc_.get(("D", a), lv)

            def load_q(u, qt):
                kind, a, b2 = u
                sl = slice(qt * 512, (qt + 1) * 512)
                if kind == "A":
                    hm = 2 * a + b2
                    qs = []
                    for ver in range(3):
                        q = Qp.get()
                        T.dma(q[0:64, :], QA.v(qt, QA.ap[hm, :, sl]))
                        T.dma(q[64:69, :], qaugA_in[ver, :, sl])
                        qs.append(q[0:69, :])
                    return qs
                q = Qp.get()
                if kind == "B":
                    T.dma(q[0:96, :], QB.v(qt, QB.ap[a, :, sl]))
                    T.dma(q[96:97, :], onesrow_in[:, sl])
                    return [q[0:97, :]]
                hq = 4 * a + b2
                T.dma(q[0:64, :], QD.v(qt, QD.ap[hq, :, sl]))
                T.dma(q[64:65, :], onesrow_in[:, sl])
                return [q[0:65, :]]

            def pairs_for(u, qt):
                kind, a, _ = u
                out_ = []
                for j in range(NCH // 2):
                    c0, c1 = 2 * j, 2 * j + 1
                    if kind != "A":
                        out_.append((j, 0, None))
                        continue
                    m_ = 2.0 ** (-2.0 * (a + 1))
                    if c1 < 4 * qt:
                        mind = 512 * qt - (128 * c1 + 127)
                        if mind * m_ >= cut_scale:
                            continue
                        out_.append((j, 0, None))
                    elif c0 > 4 * qt + 3:
                        mind = 128 * c0 - (512 * qt + 511)
                        if mind * m_ >= cut_scale:
                            continue
                        out_.append((j, 1, None))
                    else:
                        out_.append((j, 2, j - 2 * qt))
                return out_

            def compute(u, kt, vt):
                kind, a, b2 = u
                DK = {"A": 69, "B": 97, "D": 65}[kind]
                DVa = {"A": 128, "B": 128, "D": 65}[kind]
                scale = {"A": 0.125, "B": 96 ** -0.5, "D": 0.125}[kind]
                if kind == "A" and bd_state["h"] != a:
                    T.dma(bd[:, :], bdiag_in[a])
                    bd_state["h"] = a
                nq = load_q(u, 0)
                for qt in range(NT):
                    qs = nq
                    if qt + 1 < NT:
                        nq = load_q(u, qt + 1)
                    sl = slice(qt * 512, (qt + 1) * 512)
                    prs = pairs_for(u, qt)
                    po = psO.get()
                    px = psX.get()
                    if kind != "D":
                        accD = accDp.get(); accP = accPp.get()
                        mset("dve", accD[:], 0.0)
                        mset("pool", accP[:], 0.0)
                    n = len(prs)

                    def emit_S(j, ver):
                        ps = psS.get()
                        q = qs[ver] if kind == "A" else qs[0]
                        mm(ps[:, 0:512], kt[0:DK, (2 * j) * 128:(2 * j + 1) * 128], q)
                        mm(ps[:, 512:1024], kt[0:DK, (2 * j + 1) * 128:(2 * j + 2) * 128], q)
                        return ps

                    def emit_rest(ps, j, ver, dp, idx):
                        src = ps
                        if ver == 2:
                            tmp = tmpSp.get()
                            tt("dve", tmp[:, 0:512], ps[:, 0:512], bd[:, dp * 1024:dp * 1024 + 512], ALU.add)
                            tt("dve", tmp[:, 512:1024], ps[:, 512:1024], bd[:, dp * 1024 + 512:(dp + 1) * 1024], ALU.add)
                            src = tmp
                        pt = PTp.get()
                        act(pt[:], src[:], AF.Exp, scale=scale)
                        mm(po[0:DVa, :], vt[:, 2 * j, 0:DVa], pt[:, 0:512], start=(idx == 0), stop=False)
                        mm(po[0:DVa, :], vt[:, 2 * j + 1, 0:DVa], pt[:, 512:1024], start=False, stop=(idx == n - 1))
                        if kind != "D":
                            mm(px[:], onesb[:], pt[:, 0:512], start=(idx == 0), stop=False)
                            if idx % 3 != 2:
                                tt("dve", accD[:], accD[:], pt[:, 512:1024], ALU.add)
                            else:
                                tt("pool", accP[:], accP[:], pt[:, 512:1024], ALU.add)

                    prev = None
                    for idx, (j, ver, dp) in enumerate(prs):
                        ps = emit_S(j, ver)
                        if prev is not None:
                            emit_rest(*prev)
                        prev = (ps, j, ver, dp, idx)
                    emit_rest(*prev)
                    rd = rdp.get()
                    if kind != "D":
                        mm(px[:], cmat(CM_ONES), accD[:], start=False, stop=False)
                        mm(px[:], cmat(CM_ONES), accP[:], start=False, stop=True)
                        ts("dve", rd[:], px[:], 1e-30, ALU.max)
                        recip(rd[:], rd[:])
                        if kind == "A":
                            o = osbp.get()
                            tt("dve", o[:], po[:], rd[:], ALU.mult)
                            T.dma(OA.v(qt, OA.ap[2 * a + b2, :, sl]), o[:], own="in")
                        else:
                            o = obfp.get()
                            tt("dve", o[:], po[:], rd[:], ALU.mult)
                            T.dma(OB.v(qt, OB.ap[a * 128:(a + 1) * 128, sl]), o[:], own="in")
                    else:
                        o65 = osbp.get()
                        cp("act", o65[0:65, :], po[0:65, :])
                        mm(px[0:64, :], cm[0:65, CM_SEL64, 0:64], o65[0:65, :])
                        ts("dve", rd[0:64, :], px[0:64, :], 1e-30, ALU.max)
                        recip(rd[0:64, :], rd[0:64, :])
                        o = obfp.get()
                        tt("dve", o[0:64, :], o65[0:64, :], rd[0:64, :], ALU.mult)
                        hq = 4 * a + b2
                        T.dma(OD.v(qt, OD.ap[hq * 64:(hq + 1) * 64, sl]), o[0:64, :], own="in")

            import os as _os
            _kinds = _os.environ.get("ATT_KINDS", "ABD")
            units = [u for u in units if u[0] in _kinds]
            cur = load_kv(units[0])
            for ui, u in enumerate(units):
                kt, vt = cur
                if ui + 1 < len(units):
                    cur = load_kv(units[ui + 1])
                compute(u, kt, vt)
            S.close()

        if want("SSD"):
            S = Stage("SD%d" % l)
            vec = S.sb([128, NV], F32, "vec")
            T.dma(vec[:], vecs_in[l])
            rws = S.sb([128, 272], F32, "rows")
            T.dma(rws[:], rows_in[l])
            aneg = S.sb([128, 16], F32, "aneg")
            act(aneg[:], rws[:, 256:272], AF.Exp)
            ts("dve", aneg[:], aneg[:], -1.0, ALU.mult)
            xins = S.pool(2, [128, 6, 514], F32, "xin")
            xps = S.pool(2, [128, 6, 512], F32, "xp")
            f32p = S.pool(4, [128, 512], F32, "f")
            dts = S.pool(2, [16, 512], F32, "dt")
            tms = S.pool(2, [16, 512], F32, "tm")
            XBCv = XBC.ap.rearrange("(c p) n -> p c n", p=128)
            XBCPv = XBCP.ap.rearrange("(c p) n -> p c n", p=128)
            for t in range(NT):
                sl = slice(t * 512, (t + 1) * 512)
                xin = xins.get()
                T.dma(xin[:, :, 1:513], XBC.v(t, XBCv[:, :, sl]))
                if t > 0:
                    T.dma(xin[:, :, 0:1], XBC.v(t - 1, XBCv[:, :, t * 512 - 1:t * 512]), allow_slow_non_contiguous=True)
                else:
                    mset("pool", xin[:, :, 0:1], 0.0)
                if t < NT - 1:
                    T.dma(xin[:, :, 513:514], XBC.v(t + 1, XBCv[:, :, (t + 1) * 512:(t + 1) * 512 + 1]), allow_slow_non_contiguous=True)
                else:
                    mset("pool", xin[:, :, 513:514], 0.0)
                xp = xps.get()
                for c in range(6):
                    a0 = f32p.get(); a1 = f32p.get()
                    ts("dve" if c % 2 == 0 else "pool", a0[:], xin[:, c, 0:512], vec[:, VC["cw0"] + c:VC["cw0"] + c + 1], ALU.mult,
                       vec[:, VC["cb"] + c:VC["cb"] + c + 1], ALU.add)
                    stt("dve", a1[:], xin[:, c, 1:513], vec[:, VC["cw1"] + c:VC["cw1"] + c + 1], a0[:], ALU.mult, ALU.add)
                    stt("dve", a0[:], xin[:, c, 2:514], vec[:, VC["cw2"] + c:VC["cw2"] + c + 1], a1[:], ALU.mult, ALU.add)
                    act(xp[:, c, :], a0[:], AF.Silu)
                T.dma(XBCP.v(t, XBCPv[:, :, sl]), xp[:], own="in")
                dtt = dts.get(); tm = tms.get()
                T.dma(dtt[:], DTR.v(t, DTR.ap[:, sl]))
                T.dma(tm[:], tmask_in[0:16, sl])
                e1 = f32p.get()
                act(e1[0:16, :], dtt[:], AF.Exp, bias=vec[0:16, VC["dtb"]:VC["dtb"] + 1])
                act(e1[0:16, :], e1[0:16, :], AF.Ln, bias=1.0)
                tt("dve", dtt[:], e1[0:16, :], tm[:], ALU.mult)
                T.dma(DTP.v(t, DTP.ap[:, sl]), dtt[:], own="in")

            import os as _os
            _ndir = int(_os.environ.get("SSD_NDIR", "2"))
            _cut = int(_os.environ.get("SSD_CUT", "99"))
            st32 = S.sb([128, 4, 64], F32, "st32")
            stbf = S.sb([128, 4, 64], BF16, "stbf")
            xtoks = S.pool(2, [128, 512], F32, "xtok")
            xdts = S.pool(2, [128, 512], BF16, "xdt")
            smalls = S.pool(2, [128, 512], F32, "small")
            csrows = S.pool(2, [8, 128], F32, "csrow")
            bbfs = S.pool(2, [128, 128], BF16, "bbf")
            cbfz = [S.pool(2, [128, 128], BF16, "cbfz%d" % g_) for g_ in range(2)]
            cdecz = [S.pool(3, [128, 128], BF16, "cdecz%d" % g_) for g_ in range(2)]
            for g_ in range(2):
                for b_ in cbfz[g_].items + cdecz[g_].items:
                    mset("pool", b_[:], 0.0)
            cbsbs = S.pool(2, [128, 256], F32, "cbsb")
            args = S.pool(3, [128, 128], F32, "arg")
            decs = S.pool(3, [128, 128], F32, "dec")
            mts = S.pool(3, [128, 128], BF16, "mt")
            ecss = S.pool(3, [128, 128], F32, "ecs")
            bws = S.pool(3, [128, 128], BF16, "bw")
            yfs = S.pool(2, [128, 4, 512], F32, "yf")
            yos = S.pool(2, [128, 4, 512], F32, "yo")
            dtps = S.pool(2, [16, 512], F32, "dtp")
            pT = S.ps([128, 512], F32, "pT")
            pmisc = S.pspool(2, [128, 512], "pmisc")
            pcb = S.ps([128, 512], F32, "pcb")
            pbcs = S.pspool(2, [128, 512], "pbc")
            pys = S.pspool(2, [128, 512], "py")
            YFv = YF.ap.rearrange("(c p) n -> p c n", p=128)
            YSv = YS.ap.rearrange("(c p) n -> p c n", p=128)
            for d in range(_ndir):
                mset("dve", st32[:], 0.0)
                mset("pool", stbf[:], 0.0)
                tri = CM_U if d == 0 else CM_LO
                neg = CM_NEGF if d == 0 else CM_NEGB
                sel = CM_SEL127 if d == 0 else CM_SEL0
                torder = list(range(NT)) if d == 0 else list(range(NT - 1, -1, -1))
                corder = list(range(4)) if d == 0 else [3, 2, 1, 0]

                def load_tile(t):
                    sl_ = slice(t * 512, (t + 1) * 512)
                    xp_ = xps.get(); dtp_ = dtps.get()
                    T.dma(xp_[:], XBCP.v(t, XBCPv[:, :, sl_]))
                    T.dma(dtp_[:], DTP.v(t, DTP.ap[:, sl_]))
                    yf_ = None
                    if d == 1:
                        yf_ = yfs.get()
                        T.dma(yf_[:], YF.v(t, YFv[:, :, sl_]))
                    return xp_, dtp_, yf_

                items = [(ti, t, c) for ti, t in enumerate(torder) for c in corder]
                tiles = {}

                def get_tile(ti):
                    if ti not in tiles:
                        tiles[ti] = load_tile(torder[ti]) + (yos.get(),)
                    return tiles[ti]

                def preamble(item):
                    ti, t, c = item
                    xp, dtp, yf, yo = get_tile(ti)
                    if c == corder[1] and ti + 1 < NT:
                        get_tile(ti + 1)
                    cs = slice(c * 128, (c + 1) * 128)
                    for i in range(4):
                        trp(pT[:, i * 128:(i + 1) * 128], xp[:, i, cs], ident)
                    xtok = xtoks.get()
                    cp("act", xtok[:], pT[:])
                    pm = pmisc.get()
                    sm = smalls.get()
                    trp(pm[:, 0:128], xp[:, 4, cs], ident)
                    trp(pm[:, 128:144], dtp[0:16, cs], cm[0:16, CM_ID, 0:16])
                    cp("dve", sm[:, 0:144], pm[:, 0:144])
                    btok = sm[:, 0:128]
                    dttok = sm[:, 128 + d * 8:128 + d * 8 + 8]
                    dA = sm[:, 144:152]
                    tt("dve", dA, dttok, aneg[:, d * 8:d * 8 + 8], ALU.mult)
                    pm2 = pmisc.get()
                    mm(pm2[:, 0:8], cmat(tri), dA)
                    cscol = sm[:, 152:160]
                    cp("dve", cscol, pm2[:, 0:8])
                    mm(pm2[:, 8:16], cmat(sel), cscol)
                    trp(pm2[0:8, 128:256], cscol, ident)
                    csrow = csrows.get()
                    cp("act", csrow[:], pm2[0:8, 128:256])
                    cslast = sm[:, 160:168]
                    cp("dve", cslast, pm2[:, 8:16])
                    warg = sm[:, 168:176]
                    tt("dve", warg, cslast, cscol, ALU.subtract)
                    wall = sm[:, 176:184]
                    act(wall, warg, AF.Exp)
                    cdall = sm[:, 184:192]
                    act(cdall, cslast, AF.Exp)
                    bbf = bbfs.get()
                    cp("pool", bbf[:], xp[:, 4, cs])
                    for g in range(2):
                        gs = slice(g * 64, (g + 1) * 64)
                        cbf = cbfz[g].get()
                        cp("pool", cbf[gs, :], xp[gs, 5, cs])
                        mm(pcb[:, g * 128:(g + 1) * 128], bbf[:, :], cbf[:, :])
                    cbsb = cbsbs.get()
                    cp("act", cbsb[:], pcb[:, 0:256])
                    xdt = xdts.get()
                    for h in range(8):
                        ts("pool" if h % 2 == 0 else "dve", xdt[:, h * 64:(h + 1) * 64], xtok[:, h * 64:(h + 1) * 64], dttok[:, h:h + 1], ALU.mult)
                    return dict(xp=xp, yf=yf, yo=yo, cs=cs, btok=btok, cscol=cscol, csrow=csrow, wall=wall,
                                cdall=cdall, cbsb=cbsb, xdt=xdt)

                def heads(cx):
                    xp, yf, yo, cs = cx["xp"], cx["yf"], cx["yo"], cx["cs"]
                    btok, cscol, csrow, wall, cdall, cbsb, xdt = (cx[k_] for k_ in ("btok", "cscol", "csrow", "wall", "cdall", "cbsb", "xdt"))
                    for h in range(8):
                        g = h // 4
                        gs = slice(g * 64, (g + 1) * 64)
                        hs = slice((h % 2) * 64, (h % 2) * 64 + 64)
                        pr = (h % 4) // 2
                        pbc = pbcs.get()
                        mm(pbc[:, 0:128], cm[0:8, CM_SELH + h, :], csrow[:])
                        arg = args.get()
                        stt("dve", arg[:], pbc[:, 0:128], cscol[:, h:h + 1], cmat(neg), ALU.subtract, ALU.add)
                        dec = decs.get()
                        act(dec[:], arg[:], AF.Exp)
                        mt = mts.get()
                        tt("dve", mt[:], cbsb[:, g * 128:(g + 1) * 128], dec[:], ALU.mult)
                        ecs = ecss.get()
                        act(ecs[gs, :], pbc[gs, 0:128], AF.Exp)
                        cdec = cdecz[g].get()
                        tt("pool", cdec[gs, :], xp[gs, 5, cs], ecs[gs, :], ALU.mult)
                        py = pys.get()
                        pair_cols = slice((h // 2) * 128, (h // 2) * 128 + 128)
                        mm(py[:, 0:128], xdt[:, pair_cols], mt[:], start=True, stop=False)
                        mm(py[:, 0:128], stbf[:, pr * 2:pr * 2 + 2, :].re("p a b -> p (a b)"), cdec[:, :], start=False, stop=True)
                        bw = bws.get()
                        ts("pool", bw[:], btok, wall[:, h:h + 1], ALU.mult)
                        mm(py[:, 128:192], bw[:], xdt[:, h * 64:(h + 1) * 64])
                        if d == 0:
                            stt("dve", yo[hs, h // 2, cs], xp[hs, h // 2, cs], vec[hs, VC["ssmd"] + h // 2:VC["ssmd"] + h // 2 + 1], py[hs, 0:128], ALU.mult, ALU.add)
                        else:
                            tt("dve", yo[hs, h // 2, cs], py[hs, 0:128], yf[hs, h // 2, cs], ALU.add)
                        stt("dve", st32[gs, h % 4, :], st32[gs, h % 4, :], cdall[gs, h:h + 1], py[gs, 128:192], ALU.mult, ALU.add)
                    cp("act", stbf[:], st32[:])

                cxn = preamble(items[0])
                for ii, item in enumerate(items):
                    cx = cxn
                    if ii + 1 < len(items):
                        cxn = preamble(items[ii + 1])
                    heads(cx)
                    ti, t, c = item
                    if c == corder[-1]:
                        sl = slice(t * 512, (t + 1) * 512)
                        if d == 0:
                            T.dma(YF.v(t, YFv[:, :, sl]), cx["yo"][:], own="in")
                        else:
                            T.dma(YS.v(t, YSv[:, :, sl]), cx["yo"][:], own="in")
            S.close()

        TM = 256
        NTM = L // TM
        if want("MERGE"):
            S = Stage("MG%d" % l)
            wG = alloc_w(S, D_MODEL, 4608, "wG"); wbr = alloc_w(S, 2048, D_MODEL, "wbr"); wo = alloc_w(S, D_MODEL, D_MODEL, "wo")
            vec = S.sb([128, NV], F32, "vec")
            rws = S.sb([128, 272], F32, "rows")
            lam = S.sb([128, 8], F32, "lam")
            ltmp = S.sb([128, 128], F32, "ltmp")
            W = Stage("MGw%d" % l)
            stg = W.pool(2, [128, 2048], F32, "stg")
            load_w_bf16(S, wG_in, wG_in.h[l], D_MODEL, 4608, stg, "wG", w=wG)
            load_w_bf16(S, wbr_in, wbr_in.h[l], 2048, D_MODEL, stg, "wbr", w=wbr)
            load_w_bf16(S, wout_in, wout_in.h[l], D_MODEL, D_MODEL, stg, "wo", w=wo)
            W.close()
            T.dma(vec[:], vecs_in[l])
            T.dma(rws[:], rows_in[l])
            tt("dve", ltmp[:, 0:64], rws[:, 0:64], rws[:, 64:128], ALU.mult)
            tt("dve", ltmp[:, 64:128], rws[:, 128:192], rws[:, 192:256], ALU.mult)
            T.op("dve", lambda: V.reduce_sum(out=lam.h[:, 0:1], in_=ltmp.h[:, 0:64], axis=mybir.AxisListType.X), reads=[ltmp[:]], writes=[lam[:]])
            T.op("dve", lambda: V.reduce_sum(out=lam.h[:, 1:2], in_=ltmp.h[:, 64:128], axis=mybir.AxisListType.X), reads=[ltmp[:]], writes=[lam[:]])
            act(lam[:, 2:4], lam[:, 0:2], AF.Exp)
            tt("dve", lam[:, 4:5], lam[:, 3:4], lam[:, 2:3], ALU.subtract)
            ts("dve", lam[:, 5:6], lam[:, 4:5], -lam_init, ALU.add)
            neglam = lam[:, 5:6]
            hb = S.sb([128, 8, TM], BF16, "hb"); xt = S.sb([128, 8, TM], F32, "xt")
            oa = S.sb([128, 8, TM], F32, "oa"); ob = S.sb([128, 4, TM], BF16, "ob"); od = S.sb([128, 4, TM], BF16, "od")
            ys = S.sb([128, 4, TM], F32, "ys"); tm = S.sb([128, TM], F32, "tm")
            ya = S.sb([128, 4, TM], BF16, "ya"); yc = S.sb([128, 4, TM], BF16, "yc"); yz = S.sb([128, 4, TM], F32, "yz")
            mrg = S.sb([128, 8, TM], BF16, "mrg"); outf = S.sb([128, 8, TM], F32, "outf")
            f32p = S.pool(6, [128, TM], F32, "f")
            sgp = S.pool(3, [128, TM], F32, "sg")
            rsb = S.sb([128, TM], F32, "rsb")
            pp = S.pspool(6, [128, 512], "pp")
            pst = S.pspool(2, [128, 512], "pst")
            kcv = lambda a: a.rearrange("(kc p) n -> p kc n", p=128)
            for i in range(NTM):
                sl = slice(i * TM, (i + 1) * TM)
                k_ = ("m", i)
                T.dma(hb[:], hT.v(k_, kcv(hT.ap)[:, :, sl]))
                T.dma(xt[:], xres.v(k_, kcv(xres.ap)[:, :, sl]))
                T.dma(oa[:], OA.v(k_, OA.ap[:, :, sl].rearrange("a p n -> p a n")))
                T.dma(ob[:], OB.v(k_, kcv(OB.ap)[:, :, sl]))
                T.dma(od[:], OD.v(k_, kcv(OD.ap)[:, :, sl]))
                T.dma(ys[:], YS.v(k_, kcv(YS.ap)[:, :, sl]))
                T.dma(tm[:], tmask_in[:, sl])
                for h in range(4):
                    o = f32p.get(); sq = f32p.get(); rs = f32p.get(); tmp = f32p.get()
                    stt("dve", o[:], oa[:, 2 * h + 1, :], neglam, oa[:, 2 * h, :], ALU.mult, ALU.add)
                    tt("pool", sq[:], o[:], o[:], ALU.mult)
                    pm = pp.get()
                    mm(pm[:, 0:TM], cmat(CM_ONES), sq[:])
                    act(tmp[:], pm[:, 0:TM], AF.Sqrt, bias=epsb[:, 0:1], scale=1.0 / 128)
                    recip(rs[:], tmp[:])
                    ts("dve", rs[:], rs[:], 1.0 - lam_init, ALU.mult)
                    stt("dve", ya[:, h, :], o[:], vec[:, VC["diffn"]:VC["diffn"] + 1], rs[:], ALU.mult, ALU.mult)
                pm = pst.get()
                for zc in range(4):
                    pz = pp.get()
                    for kc in range(8):
                        mm(pz[:, 0:TM], wG[:, kc, 4096 + zc * 128:4096 + (zc + 1) * 128], hb[:, kc, :], start=(kc == 0), stop=(kc == 7))
                    zs = f32p.get()
                    act(zs[:], pz[:, 0:TM], AF.Silu)
                    tt("dve", yz[:, zc, :], zs[:], ys[:, zc, :], ALU.mult)
                    sq = f32p.get()
                    tt("pool", sq[:], yz[:, zc, :], yz[:, zc, :], ALU.mult)
                    mm(pm[:, 0:TM], cmat(CM_ONES), sq[:], start=(zc == 0), stop=(zc == 3))
                tmp = f32p.get(); rs = f32p.get()
                act(tmp[:], pm[:, 0:TM], AF.Sqrt, bias=epsb[:, 0:1], scale=1.0 / 512)
                recip(rs[:], tmp[:])
                for zc in range(4):
                    stt("dve", yc[:, zc, :], yz[:, zc, :], vec[:, VC["ssmn"] + zc:VC["ssmn"] + zc + 1], rs[:], ALU.mult, ALU.mult)
                Ys = [ya, ob, yc, od]
                for oc in range(8):
                    macc = f32p.get()
                    for n_ in range(4):
                        pg = pp.get()
                        for kc in range(8):
                            mm(pg[:, 0:TM], wG[:, kc, n_ * 1024 + oc * 128:n_ * 1024 + (oc + 1) * 128], hb[:, kc, :], start=(kc == 0), stop=(kc == 7))
                        sg = sgp.get()
                        act(sg[:], pg[:, 0:TM], AF.Sigmoid)
                        pb = pp.get()
                        for kc in range(4):
                            mm(pb[:, 0:TM], wbr[:, n_ * 4 + kc, oc * 128:(oc + 1) * 128], Ys[n_][:, kc, :], start=(kc == 0), stop=(kc == 3))
                        if n_ == 0:
                            tt("dve", macc[:], pb[:, 0:TM], sg[:], ALU.mult)
                        else:
                            t2 = f32p.get()
                            tt("dve", t2[:], pb[:, 0:TM], sg[:], ALU.mult)
                            if n_ < 3:
                                tt("pool", macc[:], macc[:], t2[:], ALU.add)
                            else:
                                tt("pool", mrg[:, oc, :], macc[:], t2[:], ALU.add)
                pm = pst.get()
                for oc in range(8):
                    po = pp.get()
                    for kc in range(8):
                        mm(po[:, 0:TM], wo[:, kc, oc * 128:(oc + 1) * 128], mrg[:, kc, :], start=(kc == 0), stop=(kc == 7))
                    cp("act", outf[:, oc, :], po[:, 0:TM])
                    sq = f32p.get()
                    tt("pool", sq[:], outf[:, oc, :], outf[:, oc, :], ALU.mult)
                    mm(pm[:, 0:TM], cmat(CM_ONES), sq[:], start=(oc == 0), stop=(oc == 7))
                tmp = f32p.get(); rs = rsb
                act(tmp[:], pm[:, 0:TM], AF.Sqrt, bias=epsb[:, 0:1], scale=1.0 / D_MODEL)
                recip(rs[:], tmp[:])
                tt("dve", rs[:], rs[:], tm[:], ALU.mult)
                for oc in range(8):
                    t2 = f32p.get()
                    stt("dve", t2[:], outf[:, oc, :], vec[:, VC["post0"] + oc:VC["post0"] + oc + 1], rs[:], ALU.mult, ALU.mult)
                    tt("pool", xt[:, oc, :], xt[:, oc, :], t2[:], ALU.add)
                T.dma(xres.v(k_, kcv(xres.ap)[:, :, sl]), xt[:], own="in")
            S.close()

        if want("MEM"):
            S = Stage("MM%d" % l)
            wq = alloc_w(S, D_MODEL, 512, "wq"); wmo = alloc_w(S, 512, D_MODEL, "wmo")
            vec = S.sb([128, NV], F32, "vec")
            onesb = S.sb([128, 128], BF16, "onesb")
            Kmem = S.sb([128, 4, 256], BF16, "Kmem"); Vmem = S.sb([128, 2, 512], BF16, "Vmem")
            pp = S.pspool(3, [128, 512], "pp")
            pst = S.pspool(1, [128, 512], "pst")
            pS2 = S.pspool(2, [128, 1024], "pS2")
            f32p = S.pool(6, [128, 512], F32, "f")
            W = Stage("MMw%d" % l)
            stg = W.pool(2, [128, 2048], F32, "stg")
            load_w_bf16(S, wmq_in, wmq_in.h[l], D_MODEL, 512, stg, "wq", w=wq)
            load_w_bf16(S, wmo_in, wmo_in.h[l], 512, D_MODEL, stg, "wmo", w=wmo)
            wk = load_w_bf16(W, wmk_in, wmk_in.h[l], D_MODEL, 512, stg, "wk")
            wv = load_w_bf16(W, wmv_in, wmv_in.h[l], D_MODEL, 512, stg, "wv")
            T.dma(vec[:], vecs_in[l])
            cp("dve", onesb[:], cmat(CM_ONES))
            memf = W.sb([128, 8, 256], F32, "memf"); mn = W.sb([128, 8, 256], BF16, "mn")
            T.dma(memf[:], View(memT_in, memT_in.h.rearrange("(kc p) n -> p kc n", p=128)))
            pm = pst.get()
            for kc in range(8):
                sq = f32p.get()
                tt("dve", sq[:, 0:256], memf[:, kc, :], memf[:, kc, :], ALU.mult)
                mm(pm[:, 0:256], cmat(CM_ONES), sq[:, 0:256], start=(kc == 0), stop=(kc == 7))
            tmp = f32p.get(); rs = f32p.get()
            act(tmp[:, 0:256], pm[:, 0:256], AF.Sqrt, bias=epsb[:, 0:1], scale=1.0 / D_MODEL)
            recip(rs[:, 0:256], tmp[:, 0:256])
            for kc in range(8):
                stt("dve", mn[:, kc, :], memf[:, kc, :], vec[:, VC["memn"] + kc:VC["memn"] + kc + 1], rs[:, 0:256], ALU.mult, ALU.mult)
            for h in range(4):
                pk = pp.get()
                for kc in range(8):
                    mm(pk[:, 0:256], wk[:, kc, h * 128:(h + 1) * 128], mn[:, kc, :], start=(kc == 0), stop=(kc == 7))
                cp("act", Kmem[:, h, :], pk[:, 0:256])
            for mb in range(2):
                pv = pp.get()
                for kc in range(8):
                    mm(pv[:, :], mn[:, kc, mb * 128:(mb + 1) * 128], wv[:, kc, :], start=(kc == 0), stop=(kc == 7))
                cp("act", Vmem[:, mb, :], pv[:, :])
            W.close()
            xts = S.pool(2, [128, 8, 512], F32, "xt")
            tms = S.pool(2, [128, 512], F32, "tm")
            h1 = S.sb([128, 8, 512], BF16, "h1")
            h2s = S.pool(2, [128, 8, 512], BF16, "h2")
            oh = S.sb([128, 4, 512], BF16, "oh")
            outf = S.sb([128, 8, 512], F32, "outf")
            qhs = S.pool(2, [128, 512], BF16, "qh")
            rsb = S.sb([128, 512], F32, "rsb")
            pts = S.pool(2, [128, 1024], BF16, "pt")
            kcv = lambda a: a.rearrange("(kc p) n -> p kc n", p=128)

            def norm_to(xt_, col0, dst):
                pm_ = pst.get()
                for kc in range(8):
                    sq = f32p.get()
                    if kc % 2 == 0:
                        act(sq[:], xt_[:, kc, :], AF.Square)
                    else:
                        tt("pool", sq[:], xt_[:, kc, :], xt_[:, kc, :], ALU.mult)
                    mm(pm_[:], cmat(CM_ONES), sq[:], start=(kc == 0), stop=(kc == 7))
                tmp_ = f32p.get(); rs_ = f32p.get()
                act(tmp_[:], pm_[:], AF.Sqrt, bias=epsb[:, 0:1], scale=1.0 / D_MODEL)
                recip(rs_[:], tmp_[:])
                for kc in range(8):
                    stt("dve", dst[:, kc, :], xt_[:, kc, :], vec[:, col0 + kc:col0 + kc + 1], rs_[:], ALU.mult, ALU.mult)

            def ld(t):
                sl_ = slice(t * 512, (t + 1) * 512)
                xt_ = xts.get(); tm_ = tms.get()
                T.dma(xt_[:], xres.v(("e", t), kcv(xres.ap)[:, :, sl_]))
                T.dma(tm_[:], tmask_in[:, sl_])
                return xt_, tm_

            nxt = ld(0)
            for t in range(NT):
                xt, tm = nxt
                if t + 1 < NT:
                    nxt = ld(t + 1)
                sl = slice(t * 512, (t + 1) * 512)
                norm_to(xt, VC["pre1"], h1)
                for h in range(4):
                    pq = pp.get()
                    for kc in range(8):
                        mm(pq[:], wq[:, kc, h * 128:(h + 1) * 128], h1[:, kc, :], start=(kc == 0), stop=(kc == 7))
                    qh = qhs.get()
                    cp("act", qh[:], pq[:])
                    ps2 = pS2.get()
                    for mb in range(2):
                        mm(ps2[:, mb * 512:(mb + 1) * 512], Kmem[:, h, mb * 128:(mb + 1) * 128], qh[:])
                    pt = pts.get()
                    act(pt[:], ps2[:], AF.Exp, scale=128 ** -0.5)
                    po = pp.get(); pd = pp.get()
                    for mb in range(2):
                        mm(po[:], Vmem[:, mb, h * 128:(h + 1) * 128], pt[:, mb * 512:(mb + 1) * 512], start=(mb == 0), stop=(mb == 1))
                    for mb in range(2):
                        mm(pd[:], onesb[:], pt[:, mb * 512:(mb + 1) * 512], start=(mb == 0), stop=(mb == 1))
                    rd = f32p.get()
                    recip(rd[:], pd[:])
                    tt("dve", oh[:, h, :], po[:], rd[:], ALU.mult)
                pm = pst.get()
                for oc in range(8):
                    po = pp.get()
                    for h in range(4):
                        mm(po[:], wmo[:, h, oc * 128:(oc + 1) * 128], oh[:, h, :], start=(h == 0), stop=(h == 3))
                    cp("act", outf[:, oc, :], po[:])
                    sq = f32p.get()
                    tt("pool", sq[:], outf[:, oc, :], outf[:, oc, :], ALU.mult)
                    mm(pm[:], cmat(CM_ONES), sq[:], start=(oc == 0), stop=(oc == 7))
                tmp = f32p.get(); rs = rsb
                act(tmp[:], pm[:], AF.Sqrt, bias=epsb[:, 0:1], scale=1.0 / D_MODEL)
                recip(rs[:], tmp[:])
                tt("dve", rs[:], rs[:], tm[:], ALU.mult)
                for oc in range(8):
                    t2 = f32p.get()
                    stt("dve", t2[:], outf[:, oc, :], vec[:, VC["post1"] + oc:VC["post1"] + oc + 1], rs[:], ALU.mult, ALU.mult)
                    tt("pool", xt[:, oc, :], xt[:, oc, :], t2[:], ALU.add)
                T.dma(xres.v(("e", t), kcv(xres.ap)[:, :, sl]), xt[:], own="in")
                h2 = h2s.get()
                norm_to(xt, VC["pre2"], h2)
                T.dma(H2.v(("e", t), kcv(H2.ap)[:, :, sl]), h2[:], own="in")
            S.close()

        if want("FFN"):
            S = Stage("FF%d" % l)
            wfi = alloc_w(S, D_MODEL, 2 * D_FF, "wfi"); wfo = alloc_w(S, D_FF, D_MODEL, "wfo")
            vec = S.sb([128, NV], F32, "vec")
            W = Stage("FFw%d" % l)
            stg = W.pool(2, [128, 2048], F32, "stg")
            load_w_bf16(S, wfi_in, wfi_in.h[l], D_MODEL, 2 * D_FF, stg, "wfi", w=wfi)
            load_w_bf16(S, wfo_in, wfo_in.h[l], D_FF, D_MODEL, stg, "wfo", w=wfo)
            W.close()
            T.dma(vec[:], vecs_in[l])
            h2t = S.pool(2, [128, 8, TM + 2], BF16, "h2t")
            xt = S.sb([128, 8, TM], F32, "xt"); tm = S.sb([128, TM], F32, "tm")
            actT = S.sb([128, 22, TM], BF16, "actT")
            outf = S.sb([128, 8, TM], F32, "outf")
            f32p = S.pool(8, [128, TM], F32, "f")
            rsb = S.sb([128, TM], F32, "rsb")
            pp = S.pspool(7, [128, 512], "pp")
            pst = S.pspool(1, [128, 512], "pst")
            kcv = lambda a: a.rearrange("(kc p) n -> p kc n", p=128)
            H2v = kcv(H2.ap)
            last = (l == depth - 1)

            def ldh(i):
                c0 = i * TM
                hh = h2t.get()
                lo = max(c0 - 1, 0); hi = min(c0 + TM + 1, L)
                T.dma(hh[:, :, (lo - (c0 - 1)):(hi - (c0 - 1))], H2.v(("f", 0), H2v[:, :, lo:hi]))
                if c0 == 0:
                    mset("pool", hh[:, :, 0:1], 0.0)
                if c0 + TM == L:
                    mset("pool", hh[:, :, TM + 1:TM + 2], 0.0)
                return hh

            nh = ldh(0)
            for i in range(NTM):
                sl = slice(i * TM, (i + 1) * TM)
                hh = nh
                if i + 1 < NTM:
                    nh = ldh(i + 1)
                k_ = ("f", i + 1)
                T.dma(xt[:], xres.v(k_, kcv(xres.ap)[:, :, sl]))
                T.dma(tm[:], tmask_in[:, sl])
                for j in range(22):
                    cv = []
                    for half in range(2):
                        ch = j + 22 * half
                        pu = pp.get()
                        for kc in range(8):
                            mm(pu[:, 0:TM + 2], wfi[:, kc, ch * 128:(ch + 1) * 128], hh[:, kc, :], start=(kc == 0), stop=(kc == 7))
                        a0 = f32p.get(); a1 = f32p.get()
                        ts("dve", a0[:], pu[:, 0:TM], vec[:, VC["fw0"] + ch:VC["fw0"] + ch + 1], ALU.mult, vec[:, VC["fb"] + ch:VC["fb"] + ch + 1], ALU.add)
                        stt("dve", a1[:], pu[:, 1:TM + 1], vec[:, VC["fw1"] + ch:VC["fw1"] + ch + 1], a0[:], ALU.mult, ALU.add)
                        stt("dve", a0[:], pu[:, 2:TM + 2], vec[:, VC["fw2"] + ch:VC["fw2"] + ch + 1], a1[:], ALU.mult, ALU.add)
                        cv.append(a0)
                    ga = f32p.get()
                    act(ga[:], cv[0][:], AF.Gelu)
                    tt("pool", actT[:, j, :], ga[:], cv[1][:], ALU.mult)
                pm = pst.get()
                for oc in range(8):
                    po = pp.get()
                    for j in range(22):
                        mm(po[:, 0:TM], wfo[:, j, oc * 128:(oc + 1) * 128], actT[:, j, :], start=(j == 0), stop=(j == 21))
                    cp("act", outf[:, oc, :], po[:, 0:TM])
                    sq = f32p.get()
                    tt("pool", sq[:], outf[:, oc, :], outf[:, oc, :], ALU.mult)
                    mm(pm[:, 0:TM], cmat(CM_ONES), sq[:], start=(oc == 0), stop=(oc == 7))
                tmp = f32p.get(); rs = rsb
                act(tmp[:], pm[:, 0:TM], AF.Sqrt, bias=epsb[:, 0:1], scale=1.0 / D_MODEL)
                recip(rs[:], tmp[:])
                tt("dve", rs[:], rs[:], tm[:], ALU.mult)
                for oc in range(8):
                    t2 = f32p.get()
                    stt("dve", t2[:], outf[:, oc, :], vec[:, VC["post2"] + oc:VC["post2"] + oc + 1], rs[:], ALU.mult, ALU.mult)
                    tt("pool", xt[:, oc, :], xt[:, oc, :], t2[:], ALU.add)
                dst = yT_out if last else xres
                T.dma(dst.v(k_, kcv(dst.ap)[:, :, sl]), xt[:], own="in")
            S.close()

    T.finish()
    glob.close()
    return nc


def _const_tables(L):
    c = {}
    cm = np.zeros((NCM, 128, 128), np.float32)
    k = np.arange(128)
    cm[0] = np.eye(128)
    cm[1] = 1.0
    cm[2] = (k[:, None] <= k[None, :])
    cm[3] = (k[:, None] >= k[None, :])
    cm[4] = np.where(k[:, None] <= k[None, :], 0.0, NEG)
    cm[5] = np.where(k[:, None] >= k[None, :], 0.0, NEG)
    cm[6] = (k[:, None] // 64 == k[None, :] // 64)
    for m in range(32):
        if m < 16:
            cm[7][64 + m + 16, 64 + m] = -1.0
        else:
            cm[7][64 + m - 16, 64 + m] = 1.0
    for blk in range(2):
        for sub in range(2):
            o = blk * 64 + sub * 32
            for m in range(32):
                if m < 16:
                    cm[8][o + m + 16, o + m] = -1.0
                else:
                    cm[8][o + m - 16, o + m] = 1.0
    cm[9][127, :] = 1.0
    cm[10][64, :] = 1.0
    cm[11][0, :] = 1.0
    for h_ in range(16):
        cm[12 + h_][h_, :] = 1.0
    c["cmat"] = np.ascontiguousarray(cm.transpose(1, 0, 2))
    pos = np.arange(L, dtype=np.float32)
    freqs = (np.float32(10000.0) ** (-(np.arange(16, dtype=np.float32)) / np.float32(16))).astype(np.float32)
    angB = (pos[None, :] * freqs[:, None]).astype(np.float32)
    c["cosB"] = np.concatenate([np.cos(angB), np.cos(angB)], 0).astype(np.float32)
    c["sinB"] = np.concatenate([np.sin(angB), np.sin(angB)], 0).astype(np.float32)
    rowp = (np.arange(L) // 64).astype(np.float32)
    colp = (np.arange(L) % 64).astype(np.float32)
    angR = (rowp[None, :] * freqs[:, None]).astype(np.float32)
    angC = (colp[None, :] * freqs[:, None]).astype(np.float32)
    cD = np.concatenate([np.cos(angR), np.cos(angR), np.cos(angC), np.cos(angC)], 0)
    sD = np.concatenate([np.sin(angR), np.sin(angR), np.sin(angC), np.sin(angC)], 0)
    c["cosD"] = np.concatenate([cD, cD], 0).astype(np.float32)
    c["sinD"] = np.concatenate([sD, sD], 0).astype(np.float32)
    return c


def _alibi_tables(L, Lreal):
    j = np.arange(L)
    jh = (j // 128).astype(np.float32)
    jl = (j % 128).astype(np.float32)
    kmask = np.where(j < Lreal, 0.0, NEG).astype(np.float32)
    kaug = np.zeros((4, 5, L), np.float32)
    bdiag = np.zeros((4, 128, 2048), np.float32)
    il = np.arange(512, dtype=np.float32)
    jl128 = np.arange(128, dtype=np.float32)
    for h in range(4):
        m = 2.0 ** (-2.0 * (h + 1))
        kaug[h, 0] = -8 * m * 128
        kaug[h, 1] = -8 * m
        kaug[h, 2] = 8 * m * 128 * jh
        kaug[h, 3] = 8 * m * jl
        kaug[h, 4] = kmask
        for dp in range(2):
            for hf in range(2):
                koff = 128 * (2 * dp + hf)
                c0 = dp * 1024 + hf * 512
                bdiag[h, :, c0:c0 + 512] = -8 * m * np.abs(il[None, :] - (koff + jl128[:, None]))
    qaug = np.zeros((3, 5, L), np.float32)
    qaug[0, 0] = jh; qaug[0, 1] = jl; qaug[0, 2] = 1; qaug[0, 3] = 1; qaug[0, 4] = 1
    qaug[1, 0] = -jh; qaug[1, 1] = -jl; qaug[1, 2] = -1; qaug[1, 3] = -1; qaug[1, 4] = 1
    qaug[2, 4] = 1
    return dict(kaugA=kaug.astype(NPBF), qaugA=qaug.astype(NPBF), bdiag=bdiag,
                kmask=kmask[None, :].astype(NPBF), onesrow=np.ones((1, L), NPBF))


def _weights_layout(p, depth):
    f = lambda a: np.ascontiguousarray(np.asarray(a, dtype=np.float32))
    w_in = f(p["w_in"])
    o = {}
    o["wP"] = f(np.concatenate([w_in[:, :, 0:2208], w_in[:, :, 2720:4272]], axis=2))
    o["wG"] = f(np.concatenate([w_in[:, :, G_OFF:G_OFF + 4096], w_in[:, :, C_Z:C_Z + 512]], axis=2))
    o["wuq"] = f(p["w_mla_uq"])
    wukv = f(p["w_mla_ukv"]).reshape(depth, 256, 4, 192)
    o["wukvK"] = f(wukv[..., 0:64].reshape(depth, 256, 256))
    o["wukvV"] = f(wukv[..., 64:192].reshape(depth, 256, 512))
    o["wbr"] = f(p["w_branch"]).reshape(depth, 2048, D_MODEL)
    o["wout"] = f(p["w_out"])
    o["wmq"] = f(p["w_mem_q"])
    wkv = f(p["w_mem_kv"]).reshape(depth, D_MODEL, 4, 256)
    o["wmk"] = f(wkv[..., 0:128].reshape(depth, D_MODEL, 512))
    o["wmv"] = f(wkv[..., 128:256].reshape(depth, D_MODEL, 512))
    o["wmo"] = f(p["w_mem_o"])
    o["wfi"] = f(p["w_ffn_in"])
    o["wfo"] = f(p["w_ffn_out"])
    vecs = np.zeros((depth, 128, NV), np.float32)

    def put(name, arr, width):
        a = f(arr).reshape(depth, width, 128).transpose(0, 2, 1)
        vecs[:, :, VC[name]:VC[name] + width] = a

    npre = f(p["norm_pre"]); npost = f(p["norm_post"])
    for i in range(3):
        put("pre%d" % i, npre[:, i], 8)
        put("post%d" % i, npost[:, i], 8)
    put("diffn", p["diff_norm"], 1)
    put("mlaq", p["mla_q_norm"], 3)
    put("mlakv", p["mla_kv_norm"], 2)
    cw = f(p["ssm_conv_w"])
    for k_ in range(3):
        put("cw%d" % k_, cw[:, k_], 6)
    put("cb", p["ssm_conv_b"], 6)
    put("ssmn", p["ssm_norm"], 4)
    put("gq", np.tile(f(p["gqa_q_norm"]), (1, 2)), 1)
    put("gk", np.tile(f(p["gqa_k_norm"]), (1, 2)), 1)
    put("memn", p["mem_norm"], 8)
    dtb = f(p["ssm_dt_bias"]).reshape(depth, 16)
    vecs[:, 0:16, VC["dtb"]] = dtb
    put("ssmd", np.repeat(f(p["ssm_d"]), 64, axis=1), 4)
    fw = f(p["ffn_conv_w"])
    for k_ in range(3):
        put("fw%d" % k_, fw[:, k_], 44)
    put("fb", p["ffn_conv_b"], 44)
    o["vecs"] = vecs
    rows = np.zeros((depth, 128, 272), np.float32)
    rows[:, :, 0:256] = f(p["diff_lambda"]).reshape(depth, 1, 256)
    rows[:, :, 256:272] = f(p["ssm_a_log"]).reshape(depth, 1, 16)
    o["rows"] = rows
    return o


def make_in_maps(seqs, mems, params, L, depth):
    consts = _const_tables(L)
    wl = _weights_layout(params, depth)
    maps = []
    cache = {}
    for x, mem in zip(seqs, mems):
        Lr = x.shape[0]
        if Lr not in cache:
            cache[Lr] = _alibi_tables(L, Lr)
        m = dict(consts)
        m.update(wl)
        m.update(cache[Lr])
        xT = np.zeros((D_MODEL, L), np.float32)
        xT[:, :Lr] = np.asarray(x, np.float32).T
        m["xT"] = xT
        m["memT"] = np.ascontiguousarray(np.asarray(mem, np.float32).T)
        tm = np.zeros((128, L), np.float32)
        tm[:, :Lr] = 1.0
        m["tmask"] = tm
        maps.append(m)
    return maps


PARAM_NAMES = ["w_in", "w_branch", "w_out", "diff_lambda", "diff_norm", "mla_q_norm", "mla_kv_norm",
               "w_mla_uq", "w_mla_ukv", "ssm_conv_w", "ssm_conv_b", "ssm_a_log", "ssm_dt_bias", "ssm_d",
               "ssm_norm", "gqa_q_norm", "gqa_k_norm", "mem_norm", "w_mem_q", "w_mem_kv", "w_mem_o",
               "w_ffn_in", "ffn_conv_w", "ffn_conv_b", "w_ffn_out", "norm_pre", "norm_post"]


def kernel(**inputs):
    xp = np.asarray(inputs["x_prompt"], np.float32)
    xs = np.asarray(inputs["x_sample"], np.float32)
    mp = np.asarray(inputs["mem_prompt"], np.float32)
    ms = np.asarray(inputs["mem_sample"], np.float32)
    params = {n: np.asarray(inputs[n], np.float32) for n in PARAM_NAMES}
    depth = params["w_in"].shape[0]
    L = xp.shape[1]
    seqs = [xp[0], xp[1], xs[0], xs[1], xs[2], xs[3], xs[0], xs[1]]
    mems = [mp[0], mp[1], ms[0], ms[1], ms[2], ms[3], ms[0], ms[1]]
    outs = run_trunk(seqs, mems, params, L, depth)
    y_prompt = np.stack([outs[0], outs[1]], 0).astype(np.float32)
    y_sample = np.stack([outs[2], outs[3], outs[4], outs[5]], 0).astype(np.float32)
    return (y_prompt, y_sample)


_NC_CACHE = {}


def run_trunk(seqs, mems, params, L, depth):
    key = (L, depth)
    if key not in _NC_CACHE:
        _NC_CACHE[key] = build_program(L, depth=depth)
    nc = _NC_CACHE[key]
    maps = make_in_maps(seqs, mems, params, L, depth)
    res = run_bass_kernel_spmd(nc, maps, core_ids=list(range(8)))
    outs = []
    for i, x in enumerate(seqs):
        yT = np.asarray(res.results[i]["yT"], np.float32)
        outs.append(np.ascontiguousarray(yT[:, :x.shape[0]].T))
    return outs
```

```python
import math
import numpy as np
import ml_dtypes
import concourse.bass as bass
import concourse.mybir as mybir
from concourse.bass_utils import run_bass_kernel_spmd
from contextlib import ExitStack

F32 = mybir.dt.float32
BF16 = mybir.dt.bfloat16
AF = mybir.ActivationFunctionType
ALU = mybir.AluOpType
NPBF = ml_dtypes.bfloat16

ENGS = ("pe", "act", "dve", "pool")
D_MODEL = 1024
EPS = 1e-6
MEM_LEN = 256
D_FF = 2816
NEG = -30000.0
NCM = 28


class Buf:
    __slots__ = ("h", "name", "w", "r", "ld", "st")

    def __init__(self, h, name):
        self.h = h
        self.name = name
        self.w = None
        self.r = {}
        self.ld = None
        self.st = None

    def __getitem__(self, idx):
        return View(self, self.h[idx])


class View:
    __slots__ = ("b", "ap")

    def __init__(self, b, ap):
        self.b = b
        self.ap = ap

    def __getitem__(self, idx):
        return View(self.b, self.ap[idx])

    def re(self, pat, **kw):
        return View(self.b, self.ap.rearrange(pat, **kw))


class Trk:
    def __init__(self, nc, es, n_dma_sems=80):
        self.nc = nc
        self.eng = {"pe": nc.tensor, "act": nc.scalar, "dve": nc.vector, "pool": nc.gpsimd,
                    "sp": nc.sync}
        self.sem = {e: es.enter_context(nc.semaphore("c_" + e)) for e in ENGS}
        self.cnt = {e: 0 for e in ENGS}
        self.dsem = [es.enter_context(nc.semaphore("d%d" % i)) for i in range(n_dma_sems)]
        self.dcnt = [0] * n_dma_sems
        self.dfree = list(range(n_dma_sems))
        self.issuers = list(ENGS) + ["sp"]
        self.known = {e: {} for e in self.issuers}
        self.nins = 0

    def _handle(self, key):
        return self.sem[key] if isinstance(key, str) else self.dsem[key]

    def _wait(self, issuer, dep):
        if dep is None:
            return
        key, val = dep
        if issuer == "pe" and key == "pe":
            return
        if self.known[issuer].get(key, 0) >= val:
            return
        self.eng[issuer].wait_ge(self._handle(key), val)
        self.known[issuer][key] = val
        self.nins += 1

    def _deps(self, issuer, reads, writes):
        for v in reads:
            self._wait(issuer, v.b.w)
        for v in writes:
            self._wait(issuer, v.b.w)
            for k, val in v.b.r.items():
                self._wait(issuer, (k, val))

    def op(self, e, fn, reads=(), writes=()):
        self._deps(e, reads, writes)
        ins = fn()
        self.cnt[e] += 1
        self.nins += 1
        ins.then_inc(self.sem[e], 1)
        c = self.cnt[e]
        for v in reads:
            v.b.r[e] = c
        for v in writes:
            v.b.w = (e, c)
            v.b.r = {}
        return ins

    def dma(self, out, in_, own="out", q="sp", extra_reads=(), extra_writes=(), **kw):
        self._deps(q, [in_] + list(extra_reads), [out] + list(extra_writes))
        if own == "out":
            if out.b.ld is None:
                out.b.ld = self.dfree.pop()
            slot = out.b.ld
        else:
            if in_.b.st is None:
                in_.b.st = self.dfree.pop()
            slot = in_.b.st
        ins = self.eng[q].dma_start(out=out.ap, in_=in_.ap, **kw)
        self.nins += 1
        self.dcnt[slot] += 16
        ins.then_inc(self.dsem[slot], 16)
        c = self.dcnt[slot]
        for v in [in_] + list(extra_reads):
            v.b.r[slot] = c
        for v in [out] + list(extra_writes):
            v.b.w = (slot, c)
            v.b.r = {}
        return ins

    def full_barrier(self, release=()):
        for issuer in self.issuers:
            for e in ENGS:
                if self.cnt[e]:
                    self._wait(issuer, (e, self.cnt[e]))
            for s in range(len(self.dsem)):
                if self.dcnt[s]:
                    self._wait(issuer, (s, self.dcnt[s]))
        for b in release:
            for s in (b.ld, b.st):
                if s is not None:
                    self.dfree.append(s)
            b.ld = b.st = None

    def finish(self):
        for s in range(len(self.dsem)):
            if self.dcnt[s]:
                self._wait("sp", (s, self.dcnt[s]))


A_Q, A_K, A_V = 0, 512, 1024
B_CQ, B_CKV, B_KR = 1536, 1920, 2176
C_Z, C_XBC, C_DTF, C_DTB = 2208, 2720, 3488, 3496
D_Q, D_K, D_V = 3504, 4016, 4144
G_OFF = 4272
PC_XBC = 2208
PC_DT = 2208 + 768
PC_DQ = PC_DT + 16
PC_DK = PC_DQ + 512
PC_DV = PC_DK + 128
NPC = PC_DV + 128

VC = {}
_o = 0
for _n, _w in [("pre0", 8), ("pre1", 8), ("pre2", 8), ("post0", 8), ("post1", 8), ("post2", 8),
               ("diffn", 1), ("mlaq", 3), ("mlakv", 2), ("cw0", 6), ("cw1", 6), ("cw2", 6), ("cb", 6),
               ("ssmn", 4), ("gq", 1), ("gk", 1), ("memn", 8), ("dtb", 1), ("ssmd", 4),
               ("fw0", 44), ("fw1", 44), ("fw2", 44), ("fb", 44)]:
    VC[_n] = _o
    _o += _w
NV = _o


def build_program(L, depth=2, dbg=(), cut_scale=60.0, stages=None):
    nc = bass.Bass("TRN2", target_bir_lowering=False)
    NT = L // 512
    NCH = L // 128
    glob = ExitStack()
    T = Trk(nc, glob)
    V, A, P, G = nc.vector, nc.scalar, nc.tensor, nc.gpsimd
    ENG = {"dve": V, "act": A, "pool": G}

    def dram(name, shape, dt, kind="Internal"):
        if name in dbg:
            kind = "ExternalOutput"
        return nc.dram_tensor(name, shape, dt, kind=kind).ap()

    class DT:
        def __init__(self, name, shape, dt, kind="Internal", tok_axis=-1):
            self.ap = dram(name, shape, dt, kind)
            self.name = name
            self.tiles = {}

        def t(self, i):
            if i not in self.tiles:
                self.tiles[i] = Buf(self.ap, "%s.%s" % (self.name, i))
            return self.tiles[i]

        def v(self, i, ap):
            return View(self.t(i), ap)

    def ext(name, shape, dt=F32):
        return Buf(nc.dram_tensor(name, shape, dt, kind="ExternalInput").ap(), name)

    xT_in = ext("xT", [D_MODEL, L])
    memT_in = ext("memT", [D_MODEL, MEM_LEN])
    tmask_in = ext("tmask", [128, L])
    kaugA_in = ext("kaugA", [4, 5, L], BF16)
    qaugA_in = ext("qaugA", [3, 5, L], BF16)
    kmask_in = ext("kmask", [1, L], BF16)
    onesrow_in = ext("onesrow", [1, L], BF16)
    bdiag_in = ext("bdiag", [4, 128, 2048])
    cosB_in = ext("cosB", [32, L]); sinB_in = ext("sinB", [32, L])
    cosD_in = ext("cosD", [128, L]); sinD_in = ext("sinD", [128, L])
    cmat_in = ext("cmat", [128, NCM, 128])
    wP_in = ext("wP", [depth, D_MODEL, NPC])
    wG_in = ext("wG", [depth, D_MODEL, 4096 + 512])
    wuq_in = ext("wuq", [depth, 384, 384])
    wukvK_in = ext("wukvK", [depth, 256, 256])
    wukvV_in = ext("wukvV", [depth, 256, 512])
    wbr_in = ext("wbr", [depth, 2048, D_MODEL])
    wout_in = ext("wout", [depth, D_MODEL, D_MODEL])
    wmq_in = ext("wmq", [depth, D_MODEL, 512])
    wmk_in = ext("wmk", [depth, D_MODEL, 512])
    wmv_in = ext("wmv", [depth, D_MODEL, 512])
    wmo_in = ext("wmo", [depth, 512, D_MODEL])
    wfi_in = ext("wfi", [depth, D_MODEL, 2 * D_FF])
    wfo_in = ext("wfo", [depth, D_FF, D_MODEL])
    vecs_in = ext("vecs", [depth, 128, NV])
    rows_in = ext("rows", [depth, 128, 256 + 16])
    yT_out = DT("yT", [D_MODEL, L], F32, kind="ExternalOutput")

    xres = DT("xres", [D_MODEL, L], F32)
    hT = DT("hT", [D_MODEL, L], BF16)
    QA = DT("QA", [8, 64, L], BF16); KA = DT("KA", [8, 64, L], BF16); VA = DT("VA", [4, L, 128], BF16)
    QB = DT("QB", [4, 96, L], BF16); KB = DT("KB", [4, 96, L], BF16); VB = DT("VB", [4, L, 128], BF16)
    QD = DT("QD", [8, 64, L], BF16); KD = DT("KD", [2, 64, L], BF16); VD = DT("VD", [2, L, 64], BF16)
    XBC = DT("XBC", [768, L], F32); DTR = DT("DTR", [16, L], F32)
    XBCP = DT("XBCP", [768, L], F32); DTP = DT("DTP", [16, L], F32)
    OA = DT("OA", [8, 128, L], F32); OB = DT("OB", [512, L], BF16); OD = DT("OD", [512, L], BF16)
    YF = DT("YF", [512, L], F32); YS = DT("YS", [512, L], F32)
    H2 = DT("H2", [D_MODEL, L], BF16)

    def mm(out, lhsT, rhs, start=True, stop=True):
        T.op("pe", lambda: P.matmul(out.ap, lhsT=lhsT.ap, rhs=rhs.ap, start=start, stop=stop),
             reads=[lhsT, rhs], writes=[out])

    def trp(out, in_, ident):
        T.op("pe", lambda: P.matmul(out.ap, lhsT=in_.ap, rhs=ident.ap, start=True, stop=True), reads=[in_, ident], writes=[out])

    def act(out, in_, func, bias=None, scale=1.0):
        rd = [in_]
        kw = {}
        if isinstance(scale, View):
            rd.append(scale)
            scale = scale.ap
        if isinstance(bias, View):
            rd.append(bias); kw["bias"] = bias.ap
        elif bias is not None:
            kw["bias"] = bias
        T.op("act", lambda: A.activation(out=out.ap, in_=in_.ap, func=func, scale=scale, **kw),
             reads=rd, writes=[out])

    def _s(x, rd):
        if isinstance(x, View):
            rd.append(x)
            return x.ap
        return x

    def ts(e, out, in0, s1, op0, s2=None, op1=None):
        rd = [in0]
        a1 = _s(s1, rd); a2 = _s(s2, rd)
        kw = {} if op1 is None else {"op1": op1}
        T.op(e, lambda: ENG[e].tensor_scalar(out=out.ap, in0=in0.ap, scalar1=a1, scalar2=a2, op0=op0, **kw),
             reads=rd, writes=[out])

    def tt(e, out, in0, in1, op):
        T.op(e, lambda: ENG[e].tensor_tensor(out=out.ap, in0=in0.ap, in1=in1.ap, op=op),
             reads=[in0, in1], writes=[out])

    def stt(e, out, in0, scalar, in1, op0, op1):
        e = "dve"
        rd = [in0, in1]
        a = _s(scalar, rd)
        T.op(e, lambda: ENG[e].scalar_tensor_tensor(out=out.ap, in0=in0.ap, scalar=a, in1=in1.ap, op0=op0, op1=op1),
             reads=rd, writes=[out])

    def cp(e, out, in_):
        if e == "act":
            T.op("act", lambda: A.copy(out=out.ap, in_=in_.ap), reads=[in_], writes=[out])
        else:
            T.op(e, lambda: ENG[e].tensor_copy(out=out.ap, in_=in_.ap), reads=[in_], writes=[out])

    def mset(e, out, val):
        T.op(e, lambda: ENG[e].memset(out.ap, val), writes=[out])

    def recip(out, in_):
        T.op("dve", lambda: V.reciprocal(out=out.ap, in_=in_.ap), reads=[in_], writes=[out])

    class Stage:
        def __init__(self, name):
            self.es = ExitStack()
            self.bufs = []
            self.name = name
            self.n = 0

        def sb(self, shape, dt, name=None):
            self.n += 1
            nm = "%s_%d_%s" % (self.name, self.n, name or "t")
            b = Buf(self.es.enter_context(nc.sbuf_tensor(nm, list(shape), dt)), nm)
            self.bufs.append(b)
            return b

        def ps(self, shape, dt=F32, name=None):
            self.n += 1
            nm = "%s_%d_%s" % (self.name, self.n, name or "p")
            b = Buf(self.es.enter_context(nc.psum_tensor(nm, list(shape), dt)), nm)
            self.bufs.append(b)
            return b

        def pool(self, n, shape, dt, name="pl"):
            return Ring([self.sb(shape, dt, name) for _ in range(n)])

        def pspool(self, n, shape, name="pp"):
            return Ring([self.ps(shape, F32, name) for _ in range(n)])

        def close(self):
            T.full_barrier(release=self.bufs)
            self.es.close()

    class Ring:
        def __init__(self, items):
            self.items = items
            self.i = 0

        def get(self):
            b = self.items[self.i % len(self.items)]
            self.i += 1
            return b

    def alloc_w(S, K, N, name):
        return S.sb([128, (K + 127) // 128, N], BF16, name)

    def load_w_bf16(S, src_buf, src_ap, K, N, stg_ring, name, engs=("dve", "pool"), w=None):
        kc_n = (K + 127) // 128
        if w is None:
            w = S.sb([128, kc_n, N], BF16, name)
        step = stg_ring.items[0].h.shape[1]
        i = 0
        for kc in range(kc_n):
            rows = min(128, K - kc * 128)
            for c0 in range(0, N, step):
                cw = min(step, N - c0)
                st = stg_ring.get()
                T.dma(st[0:rows, 0:cw], View(src_buf, src_ap[kc * 128:kc * 128 + rows, c0:c0 + cw]))
                cp(engs[i % len(engs)], w[0:rows, kc, c0:c0 + cw], st[0:rows, 0:cw])
                i += 1
        return w

    cm = Buf(glob.enter_context(nc.sbuf_tensor("cmat_sb", [128, NCM, 128], F32)), "cmat_sb")
    T.dma(cm[:], cmat_in[:, :, :])
    CM_ID, CM_ONES, CM_U, CM_LO, CM_NEGF, CM_NEGB, CM_BD64, CM_R32, CM_RD, CM_SEL127, CM_SEL64, CM_SEL0, CM_SELH = range(13)
    ident = cm[:, CM_ID, :]

    def cmat(i, rows=128, cols=128):
        return cm[0:rows, i, 0:cols]

    def rms_stats(S, pp, sq_chunks, n_feat, out_rstd, tmp, blockmat=None, extra_scale=None):
        pm = pp.get()
        n = len(sq_chunks)
        for i, sq in enumerate(sq_chunks):
            rows = sq.ap.shape[0]
            lhs = (blockmat if blockmat is not None else cmat(CM_ONES, rows, 128))
            mm(pm[:, 0:512], lhs, sq, start=(i == 0), stop=(i == n - 1))
        act(tmp, pm[:, 0:512], AF.Ln, bias=epsb[:, 0:1], scale=1.0 / n_feat)
        act(out_rstd, tmp, AF.Exp, scale=-0.5)
        if extra_scale is not None:
            ts("dve", out_rstd, out_rstd, extra_scale, ALU.mult)

    epsb = Buf(glob.enter_context(nc.sbuf_tensor("epsb", [128, 1], F32)), "epsb")
    mset("dve", epsb[:], EPS)

    def want(s):
        return stages is None or s in stages

    for l in range(depth):
        lam_init = 0.8 - 0.6 * math.exp(-0.3 * l)
        xsrc = None

        def xin_view(t):
            if l == 0:
                return View(xT_in, xT_in.h.rearrange("(kc p) n -> p kc n", p=128)[:, :, t * 512:(t + 1) * 512])
            return xres.v(t, xres.ap.rearrange("(kc p) n -> p kc n", p=128)[:, :, t * 512:(t + 1) * 512])

        if want("P"):
            S = Stage("P%d" % l)
            stg = S.pool(2, [128, 1880], F32, "stg")
            wP = load_w_bf16(S, wP_in, wP_in.h[l], D_MODEL, NPC, stg, "wP")
            wuq = load_w_bf16(S, wuq_in, wuq_in.h[l], 384, 384, stg, "wuq")
            wkK = load_w_bf16(S, wukvK_in, wukvK_in.h[l], 256, 256, stg, "wkK")
            wkV = load_w_bf16(S, wukvV_in, wukvV_in.h[l], 256, 512, stg, "wkV")
            vec = S.sb([128, NV], F32, "vec")
            T.dma(vec[:], vecs_in[l])
            xts = S.pool(2, [128, 8, 512], F32, "xt")
            hbs = S.pool(2, [128, 8, 512], BF16, "hb")
            f32p = S.pool(6, [128, 512], F32, "f")
            bfp = S.pool(6, [128, 512], BF16, "b")
            cqf = S.sb([128, 3, 512], F32, "cqf"); cqn = S.sb([128, 3, 512], BF16, "cqn")
            ckf = S.sb([128, 2, 512], F32, "ckf"); ckn = S.sb([128, 2, 512], BF16, "ckn")
            tabs = S.pool(2, [128, 4, 512], F32, "tab")
            krope = S.sb([128, 512], BF16, "krope")
            pp = S.pspool(6, [128, 512], "pp")

            def load_x(t):
                xt = xts.get()
                T.dma(xt[:], xin_view(t))
                return xt

            nxt = load_x(0)
            for t in range(NT):
                xt = nxt
                tb = tabs.get()
                sl = slice(t * 512, (t + 1) * 512)
                T.dma(tb[64:96, 0, :], cosB_in[:, sl]); T.dma(tb[64:96, 1, :], sinB_in[:, sl])
                T.dma(tb[:, 2, :], cosD_in[:, sl]); T.dma(tb[:, 3, :], sinD_in[:, sl])
                if t + 1 < NT:
                    nxt = load_x(t + 1)
                sqs = []
                pm = pp.get()
                for kc in range(8):
                    sq = f32p.get()
                    if kc % 2 == 0:
                        act(sq[:], xt[:, kc, :], AF.Square)
                    else:
                        tt("pool", sq[:], xt[:, kc, :], xt[:, kc, :], ALU.mult)
                    mm(pm[:], cmat(CM_ONES), sq[:], start=(kc == 0), stop=(kc == 7))
                tmp = f32p.get(); rstd = f32p.get()
                act(tmp[:], pm[:], AF.Ln, bias=epsb[:, 0:1], scale=1.0 / D_MODEL)
                act(rstd[:], tmp[:], AF.Exp, scale=-0.5)
                hb = hbs.get()
                for kc in range(8):
                    stt("dve" if kc % 2 == 0 else "pool", hb[:, kc, :], xt[:, kc, :],
                        vec[:, VC["pre0"] + kc:VC["pre0"] + kc + 1], rstd[:], ALU.mult, ALU.mult)
                T.dma(hT.v(t, hT.ap.rearrange("(kc p) n -> p kc n", p=128)[:, :, sl]), hb[:], own="in")
                if l == 0:
                    T.dma(xres.v(t, xres.ap.rearrange("(kc p) n -> p kc n", p=128)[:, :, sl]), xt[:], own="in")

                def proj(c0, m, n=512):
                    pm_ = pp.get()
                    for kc in range(8):
                        mm(pm_[0:m, 0:n], wP[:, kc, c0:c0 + m], hb[:, kc, 0:n], start=(kc == 0), stop=(kc == 7))
                    return pm_

                for which, c0, dst in (("q", A_Q, QA), ("k", A_K, KA)):
                    for i in range(4):
                        pm_ = proj(c0 + i * 128, 128)
                        ob = bfp.get()
                        cp("act" if i % 2 == 0 else "dve", ob[:], pm_[:])
                        T.dma(dst.v(t, dst.ap[2 * i:2 * i + 2, :, sl].rearrange("a p n -> (a p) n")), ob[:], own="in")
                for tk in range(4):
                    pm_ = pp.get()
                    for kc in range(8):
                        mm(pm_[:], hb[:, kc, tk * 128:(tk + 1) * 128], wP[:, kc, A_V:A_V + 512], start=(kc == 0), stop=(kc == 7))
                    ob = bfp.get()
                    cp("act" if tk % 2 == 0 else "dve", ob[:], pm_[:])
                    r0 = t * 512 + tk * 128
                    T.dma(VA.v(t, VA.ap[:, r0:r0 + 128, :].rearrange("h p d -> p h d")), ob[:].re("p (h d) -> p h d", h=4), own="in")
                sqs = []
                for i in range(3):
                    pm_ = proj(B_CQ + i * 128, 128)
                    cp("act", cqf[:, i, :], pm_[:])
                    sq = f32p.get()
                    tt("dve", sq[:], cqf[:, i, :], cqf[:, i, :], ALU.mult)
                    sqs.append(sq[:])
                rs = f32p.get(); tmp = f32p.get()
                rms_stats(S, pp, sqs, 384, rs[:], tmp[:])
                for i in range(3):
                    stt("dve", cqn[:, i, :], cqf[:, i, :], vec[:, VC["mlaq"] + i:VC["mlaq"] + i + 1], rs[:], ALU.mult, ALU.mult)
                sqs = []
                for i in range(2):
                    pm_ = proj(B_CKV + i * 128, 128)
                    cp("act", ckf[:, i, :], pm_[:])
                    sq = f32p.get()
                    tt("pool", sq[:], ckf[:, i, :], ckf[:, i, :], ALU.mult)
                    sqs.append(sq[:])
                rs = f32p.get(); tmp = f32p.get()
                rms_stats(S, pp, sqs, 256, rs[:], tmp[:])
                for i in range(2):
                    stt("dve", ckn[:, i, :], ckf[:, i, :], vec[:, VC["mlakv"] + i:VC["mlakv"] + i + 1], rs[:], ALU.mult, ALU.mult)

                def rope32(src_ps, dst_bf):
                    xs_ = f32p.get()
                    cp("act", xs_[0:96, :], View(src_ps.b, src_ps.b.h[0:96, :]))
                    pr = pp.get()
                    mm(pr[0:96, :], cm[0:96, CM_R32, 0:96], xs_[0:96, :])
                    t1 = f32p.get(); t2 = f32p.get()
                    tt("dve", t1[64:96, :], xs_[64:96, :], tb[64:96, 0, :], ALU.mult)
                    tt("dve", t2[64:96, :], pr[64:96, :], tb[64:96, 1, :], ALU.mult)
                    tt("pool", dst_bf, t1[64:96, :], t2[64:96, :], ALU.add)

                pm_ = proj(B_KR - 64, 96)
                rope32(pm_[64:96, :], krope[64:96, :])
                for h in range(4):
                    pq = pp.get()
                    for i in range(3):
                        mm(pq[0:96, :], wuq[:, i, h * 96:(h + 1) * 96], cqn[:, i, :], start=(i == 0), stop=(i == 2))
                    ob = bfp.get()
                    rope32(pq[64:96, :], ob[64:96, :])
                    cp("act", ob[0:64, :], pq[0:64, :])
                    T.dma(QB.v(t, QB.ap[h, :, sl]), ob[0:96, :], own="in")
                    pk = pp.get()
                    for i in range(2):
                        mm(pk[0:64, :], wkK[:, i, h * 64:(h + 1) * 64], ckn[:, i, :], start=(i == 0), stop=(i == 1))
                    ob = bfp.get()
                    cp("dve", ob[64:96, :], krope[64:96, :])
                    cp("act", ob[0:64, :], pk[0:64, :])
                    T.dma(KB.v(t, KB.ap[h, :, sl]), ob[0:96, :], own="in")
                for tk in range(4):
                    pm_ = pp.get()
                    for i in range(2):
                        mm(pm_[:], ckn[:, i, tk * 128:(tk + 1) * 128], wkV[:, i, :], start=(i == 0), stop=(i == 1))
                    ob = bfp.get()
                    cp("act" if tk % 2 == 0 else "dve", ob[:], pm_[:])
                    r0 = t * 512 + tk * 128
                    T.dma(VB.v(t, VB.ap[:, r0:r0 + 128, :].rearrange("h p d -> p h d")), ob[:].re("p (h d) -> p h d", h=4), own="in")
                for i in range(5):
                    isq = i < 4
                    pm_ = proj((PC_DQ + i * 128) if isq else PC_DK, 128)
                    xf = f32p.get()
                    cp("act", xf[:], pm_[:])
                    sq = f32p.get()
                    tt("pool", sq[:], xf[:], xf[:], ALU.mult)
                    rs = f32p.get(); tmp = f32p.get()
                    rms_stats(S, pp, [sq[:]], 64, rs[:], tmp[:], blockmat=cmat(CM_BD64))
                    gcol = VC["gq"] if isq else VC["gk"]
                    xn = f32p.get()
                    stt("dve", xn[:], xf[:], vec[:, gcol:gcol + 1], rs[:], ALU.mult, ALU.mult)
                    pr = pp.get()
                    mm(pr[:], cmat(CM_RD), xn[:])
                    t1 = f32p.get(); t2 = f32p.get()
                    tt("dve", t1[:], xn[:], tb[:, 2, :], ALU.mult)
                    tt("dve", t2[:], pr[:], tb[:, 3, :], ALU.mult)
                    ob = bfp.get()
                    tt("pool", ob[:], t1[:], t2[:], ALU.add)
                    dst = QD if isq else KD
                    a0 = 2 * i if isq else 0
                    T.dma(dst.v(t, dst.ap[a0:a0 + 2, :, sl].rearrange("a p n -> (a p) n")), ob[:], own="in")
                for tk in range(4):
                    pm_ = pp.get()
                    for kc in range(8):
                        mm(pm_[:, 0:128], hb[:, kc, tk * 128:(tk + 1) * 128], wP[:, kc, PC_DV:PC_DV + 128], start=(kc == 0), stop=(kc == 7))
                    ob = bfp.get()
                    cp("act" if tk % 2 == 0 else "dve", ob[:, 0:128], pm_[:, 0:128])
                    r0 = t * 512 + tk * 128
                    T.dma(VD.v(t, VD.ap[:, r0:r0 + 128, :].rearrange("h p d -> p h d")), ob[:, 0:128].re("p (h d) -> p h d", h=2), own="in")
                for i in range(6):
                    pm_ = proj(PC_XBC + i * 128, 128)
                    of = f32p.get()
                    cp("act" if i % 2 == 0 else "dve", of[:], pm_[:])
                    T.dma(XBC.v(t, XBC.ap[i * 128:(i + 1) * 128, sl]), of[:], own="in")
                pm_ = proj(PC_DT, 16)
                of = f32p.get()
                cp("act", of[0:16, :], pm_[0:16, :])
                T.dma(DTR.v(t, DTR.ap[:, sl]), of[0:16, :], own="in")
            S.close()

        if want("ATT"):
            S = Stage("AT%d" % l)
            Ktr = S.pool(2, [128, L], BF16, "Kt")
            Vtr = S.pool(2, [128, NCH, 128], BF16, "Vt")
            Qp = S.pool(6, [128, 512], BF16, "Q")
            PTp = S.pool(3, [128, 1024], BF16, "PT")
            accDp = S.pool(2, [128, 512], F32, "accD")
            accPp = S.pool(2, [128, 512], F32, "accP")
            accD2p = S.pool(2, [128, 512], F32, "accD2")
            tmpSp = S.pool(2, [128, 1024], F32, "tmpS")
            bd = S.sb([128, 2048], F32, "bd")
            osbp = S.pool(2, [128, 512], F32, "osb")
            obfp = S.pool(2, [128, 512], BF16, "obf")
            rdp = S.pool(2, [128, 512], F32, "rd")
            psS = S.pspool(2, [128, 1024], "psS")
            psO = S.pspool(2, [128, 512], "psO")
            psX = S.pspool(2, [128, 512], "psX")
            onesb = S.sb([128, 128], BF16, "onesb")
            cp("dve", onesb[:], cmat(CM_ONES))

            class Cache:
                def __init__(self, ring):
                    self.ring = ring
                    self.map = {}

                def get(self, key, loader):
                    if key in self.map:
                        return self.map[key]
                    b = self.ring.get()
                    for k_ in [k_ for k_, v_ in self.map.items() if v_ is b]:
                        del self.map[k_]
                    loader(b)
                    self.map[key] = b
                    return b

            kc_ = Cache(Ktr)
            vc_ = Cache(Vtr)
            bd_state = {"h": None}

            def all_tiles(dt_):
                return [dt_.t(i) for i in range(NT)]

            def dep_views(dt_):
                return [View(b_, dt_.ap) for b_ in all_tiles(dt_)]

            units = []
            for h in range(4):
                for m in range(2):
                    units.append(("A", h, m))
            for h in range(4):
                units.append(("B", h, 0))
            for g in range(2):
                for r_ in range(4):
                    units.append(("D", g, r_))

            def load_kv(u):
                kind, a, b2 = u
                if kind == "A":
                    hm = 2 * a + b2

                    def lk(kt):
                        T.dma(kt[0:64, :], View(KA.t(0), KA.ap[hm]), extra_reads=dep_views(KA)[1:])
                        T.dma(kt[64:69, :], kaugA_in[a])
                    def lv(vt):
                        for c0 in range(0, NCH, 16):
                            c1 = min(c0 + 16, NCH)
                            T.dma(vt[:, c0:c1, :], View(VA.t(0), VA.ap[a].rearrange("(c p) d -> p c d", p=128)[:, c0:c1, :]), extra_reads=dep_views(VA)[1:])
                    return kc_.get(("A", hm), lk), vc_.get(("A", a), lv)
                if kind == "B":
                    def lk(kt):
                        T.dma(kt[0:96, :], View(KB.t(0), KB.ap[a]), extra_reads=dep_views(KB)[1:])
                        T.dma(kt[96:97, :], kmask_in[:, :])
                    def lv(vt):
                        for c0 in range(0, NCH, 16):
                            c1 = min(c0 + 16, NCH)
                            T.dma(vt[:, c0:c1, :], View(VB.t(0), VB.ap[a].rearrange("(c p) d -> p c d", p=128)[:, c0:c1, :]), extra_reads=dep_views(VB)[1:])
                    return kc_.get(("B", a), lk), vc_.get(("B", a), lv)

                def lk(kt):
                    T.dma(kt[0:64, :], View(KD.t(0), KD.ap[a]), extra_reads=dep_views(KD)[1:])
                    T.dma(kt[64:65, :], kmask_in[:, :])
                def lv(vt):
                    for c0 in range(0, NCH, 16):
                        c1 = min(c0 + 16, NCH)
                        T.dma(vt[:, c0:c1, 0:64], View(VD.t(0), VD.ap[a].rearrange("(c p) d -> p c d", p=128)[:, c0:c1, :]), extra_reads=dep_views(VD)[1:])
                    mset("pool", vt[:, :, 64:65], 1.0)
                return kc_.get(("D", a), lk), vc_.get(("D", a), lv)

            def load_q(u, qt):
                kind, a, b2 = u
                sl = slice(qt * 512, (qt + 1) * 512)
                if kind == "A":
                    hm = 2 * a + b2
                    qs = []
                    for ver in range(3):
                        q = Qp.get()
                        T.dma(q[0:64, :], QA.v(qt, QA.ap[hm, :, sl]))
                        T.dma(q[64:69, :], qaugA_in[ver, :, sl])
                        qs.append(q[0:69, :])
                    return qs
                q = Qp.get()
                if kind == "B":
                    T.dma(q[0:96, :], QB.v(qt, QB.ap[a, :, sl]))
                    T.dma(q[96:97, :], onesrow_in[:, sl])
                    return [q[0:97, :]]
                hq = 4 * a + b2
                T.dma(q[0:64, :], QD.v(qt, QD.ap[hq, :, sl]))
                T.dma(q[64:65, :], onesrow_in[:, sl])
                return [q[0:65, :]]

            def pairs_for(u, qt):
                kind, a, _ = u
                out_ = []
                for j in range(NCH // 2):
                    c0, c1 = 2 * j, 2 * j + 1
                    if kind != "A":
                        out_.append((j, 0, None))
                        continue
                    m_ = 2.0 ** (-2.0 * (a + 1))
                    if c1 < 4 * qt:
                        mind = 512 * qt - (128 * c1 + 127)
                        if mind * m_ >= cut_scale:
                            continue
                        out_.append((j, 0, None))
                    elif c0 > 4 * qt + 3:
                        mind = 128 * c0 - (512 * qt + 511)
                        if mind * m_ >= cut_scale:
                            continue
                        out_.append((j, 1, None))
                    else:
                        out_.append((j, 2, j - 2 * qt))
                return out_

            def compute(u, kt, vt):
                kind, a, b2 = u
                DK = {"A": 69, "B": 97, "D": 65}[kind]
                DVa = {"A": 128, "B": 128, "D": 65}[kind]
                scale = {"A": 0.125, "B": 96 ** -0.5, "D": 0.125}[kind]
                if kind == "A" and bd_state["h"] != a:
                    T.dma(bd[:, :], bdiag_in[a])
                    bd_state["h"] = a
                nq = load_q(u, 0)
                for qt in range(NT):
                    qs = nq
                    if qt + 1 < NT:
                        nq = load_q(u, qt + 1)
                    sl = slice(qt * 512, (qt + 1) * 512)
                    prs = pairs_for(u, qt)
                    po = psO.get()
                    px = psX.get()
                    if kind != "D":
                        accD = accDp.get(); accP = accPp.get(); accD2 = accD2p.get()
                        mset("dve", accD[:], 0.0)
                        mset("dve", accD2[:], 0.0)
                        mset("pool", accP[:], 0.0)
                    n = len(prs)

                    def emit_S(j, ver):
                        ps = psS.get()
                        q = qs[ver] if kind == "A" else qs[0]
                        mm(ps[:, 0:512], kt[0:DK, (2 * j) * 128:(2 * j + 1) * 128], q)
                        mm(ps[:, 512:1024], kt[0:DK, (2 * j + 1) * 128:(2 * j + 2) * 128], q)
                        return ps

                    def emit_rest(ps, j, ver, dp, idx):
                        src = ps
                        if ver == 2:
                            tmp = tmpSp.get()
                            tt("dve", tmp[:, 0:512], ps[:, 0:512], bd[:, dp * 1024:dp * 1024 + 512], ALU.add)
                            tt("dve", tmp[:, 512:1024], ps[:, 512:1024], bd[:, dp * 1024 + 512:(dp + 1) * 1024], ALU.add)
                            src = tmp
                        pt = PTp.get()
                        act(pt[:], src[:], AF.Exp, scale=scale)
                        mm(po[0:DVa, :], vt[:, 2 * j, 0:DVa], pt[:, 0:512], start=(idx == 0), stop=False)
                        mm(po[0:DVa, :], vt[:, 2 * j + 1, 0:DVa], pt[:, 512:1024], start=False, stop=(idx == n - 1))
                        if kind != "D":
                            tt("dve", accD[:], accD[:], pt[:, 0:512], ALU.add)
                            if idx % 2 == 0:
                                tt("dve", accD2[:], accD2[:], pt[:, 512:1024], ALU.add)
                            else:
                                tt("pool", accP[:], accP[:], pt[:, 512:1024], ALU.add)

                    prev = None
                    for idx, (j, ver, dp) in enumerate(prs):
                        ps = emit_S(j, ver)
                        if prev is not None:
                            emit_rest(*prev)
                        prev = (ps, j, ver, dp, idx)
                    emit_rest(*prev)
                    rd = rdp.get()
                    if kind != "D":
                        mm(px[:], cmat(CM_ONES), accD[:], start=True, stop=False)
                        mm(px[:], cmat(CM_ONES), accD2[:], start=False, stop=False)
                        mm(px[:], cmat(CM_ONES), accP[:], start=False, stop=True)
                        ts("dve", rd[:], px[:], 1e-30, ALU.max)
                        recip(rd[:], rd[:])
                        if kind == "A":
                            o = osbp.get()
                            tt("dve", o[:], po[:], rd[:], ALU.mult)
                            T.dma(OA.v(qt, OA.ap[2 * a + b2, :, sl]), o[:], own="in")
                        else:
                            o = obfp.get()
                            tt("dve", o[:], po[:], rd[:], ALU.mult)
                            T.dma(OB.v(qt, OB.ap[a * 128:(a + 1) * 128, sl]), o[:], own="in")
                    else:
                        o65 = osbp.get()
                        cp("act", o65[0:65, :], po[0:65, :])
                        mm(px[0:64, :], cm[0:65, CM_SEL64, 0:64], o65[0:65, :])
                        ts("dve", rd[0:64, :], px[0:64, :], 1e-30, ALU.max)
                        recip(rd[0:64, :], rd[0:64, :])
                        o = obfp.get()
                        tt("dve", o[0:64, :], o65[0:64, :], rd[0:64, :], ALU.mult)
                        hq = 4 * a + b2
                        T.dma(OD.v(qt, OD.ap[hq * 64:(hq + 1) * 64, sl]), o[0:64, :], own="in")

            import os as _os
            _kinds = _os.environ.get("ATT_KINDS", "ABD")
            units = [u for u in units if u[0] in _kinds]
            cur = load_kv(units[0])
            for ui, u in enumerate(units):
                kt, vt = cur
                if ui + 1 < len(units):
                    cur = load_kv(units[ui + 1])
                compute(u, kt, vt)
            S.close()

        if want("SSD"):
            S = Stage("SD%d" % l)
            vec = S.sb([128, NV], F32, "vec")
            T.dma(vec[:], vecs_in[l])
            rws = S.sb([128, 272], F32, "rows")
            T.dma(rws[:], rows_in[l])
            aneg = S.sb([128, 16], F32, "aneg")
            act(aneg[:], rws[:, 256:272], AF.Exp)
            ts("dve", aneg[:], aneg[:], -1.0, ALU.mult)
            xins = S.pool(2, [128, 6, 514], F32, "xin")
            xps = S.pool(2, [128, 6, 512], F32, "xp")
            f32p = S.pool(4, [128, 512], F32, "f")
            dts = S.pool(2, [16, 512], F32, "dt")
            tms = S.pool(2, [16, 512], F32, "tm")
            XBCv = XBC.ap.rearrange("(c p) n -> p c n", p=128)
            XBCPv = XBCP.ap.rearrange("(c p) n -> p c n", p=128)
            for t in range(NT):
                sl = slice(t * 512, (t + 1) * 512)
                xin = xins.get()
                T.dma(xin[:, :, 1:513], XBC.v(t, XBCv[:, :, sl]))
                if t > 0:
                    T.dma(xin[:, :, 0:1], XBC.v(t - 1, XBCv[:, :, t * 512 - 1:t * 512]), allow_slow_non_contiguous=True)
                else:
                    mset("pool", xin[:, :, 0:1], 0.0)
                if t < NT - 1:
                    T.dma(xin[:, :, 513:514], XBC.v(t + 1, XBCv[:, :, (t + 1) * 512:(t + 1) * 512 + 1]), allow_slow_non_contiguous=True)
                else:
                    mset("pool", xin[:, :, 513:514], 0.0)
                xp = xps.get()
                for c in range(6):
                    a0 = f32p.get(); a1 = f32p.get()
                    ts("dve" if c % 2 == 0 else "pool", a0[:], xin[:, c, 0:512], vec[:, VC["cw0"] + c:VC["cw0"] + c + 1], ALU.mult,
                       vec[:, VC["cb"] + c:VC["cb"] + c + 1], ALU.add)
                    stt("dve", a1[:], xin[:, c, 1:513], vec[:, VC["cw1"] + c:VC["cw1"] + c + 1], a0[:], ALU.mult, ALU.add)
                    stt("dve", a0[:], xin[:, c, 2:514], vec[:, VC["cw2"] + c:VC["cw2"] + c + 1], a1[:], ALU.mult, ALU.add)
                    act(xp[:, c, :], a0[:], AF.Silu)
                T.dma(XBCP.v(t, XBCPv[:, :, sl]), xp[:], own="in")
                dtt = dts.get(); tm = tms.get()
                T.dma(dtt[:], DTR.v(t, DTR.ap[:, sl]))
                T.dma(tm[:], tmask_in[0:16, sl])
                e1 = f32p.get()
                act(e1[0:16, :], dtt[:], AF.Exp, bias=vec[0:16, VC["dtb"]:VC["dtb"] + 1])
                act(e1[0:16, :], e1[0:16, :], AF.Ln, bias=1.0)
                tt("dve", dtt[:], e1[0:16, :], tm[:], ALU.mult)
                T.dma(DTP.v(t, DTP.ap[:, sl]), dtt[:], own="in")

            import os as _os
            _ndir = int(_os.environ.get("SSD_NDIR", "2"))
            _cut = int(_os.environ.get("SSD_CUT", "99"))
            st32 = S.sb([128, 4, 64], F32, "st32")
            stbf = S.sb([128, 4, 64], BF16, "stbf")
            xtoks = S.pool(2, [128, 512], F32, "xtok")
            xdts = S.pool(2, [128, 512], BF16, "xdt")
            smalls = S.pool(2, [128, 512], F32, "small")
            csrows = S.pool(2, [8, 128], F32, "csrow")
            bbfs = S.pool(2, [128, 128], BF16, "bbf")
            cbfz = [S.pool(2, [128, 128], BF16, "cbfz%d" % g_) for g_ in range(2)]
            cdecz = [S.pool(3, [128, 128], BF16, "cdecz%d" % g_) for g_ in range(2)]
            for g_ in range(2):
                for b_ in cbfz[g_].items + cdecz[g_].items:
                    mset("pool", b_[:], 0.0)
            cbsbs = S.pool(2, [128, 256], F32, "cbsb")
            args = S.pool(3, [128, 128], F32, "arg")
            decs = S.pool(3, [128, 128], F32, "dec")
            mts = S.pool(3, [128, 128], BF16, "mt")
            ecss = S.pool(3, [128, 128], F32, "ecs")
            bws = S.pool(3, [128, 128], BF16, "bw")
            yfs = S.pool(2, [128, 4, 512], F32, "yf")
            yos = S.pool(2, [128, 4, 512], F32, "yo")
            dtps = S.pool(2, [16, 512], F32, "dtp")
            pT = S.ps([128, 512], F32, "pT")
            pmisc = S.pspool(2, [128, 512], "pmisc")
            pcb = S.ps([128, 512], F32, "pcb")
            pbcs = S.pspool(2, [128, 512], "pbc")
            pys = S.pspool(2, [128, 512], "py")
            YFv = YF.ap.rearrange("(c p) n -> p c n", p=128)
            YSv = YS.ap.rearrange("(c p) n -> p c n", p=128)
            for d in range(_ndir):
                mset("dve", st32[:], 0.0)
                mset("pool", stbf[:], 0.0)
                tri = CM_U if d == 0 else CM_LO
                neg = CM_NEGF if d == 0 else CM_NEGB
                sel = CM_SEL127 if d == 0 else CM_SEL0
                torder = list(range(NT)) if d == 0 else list(range(NT - 1, -1, -1))
                corder = list(range(4)) if d == 0 else [3, 2, 1, 0]

                def load_tile(t):
                    sl_ = slice(t * 512, (t + 1) * 512)
                    xp_ = xps.get(); dtp_ = dtps.get()
                    T.dma(xp_[:], XBCP.v(t, XBCPv[:, :, sl_]))
                    T.dma(dtp_[:], DTP.v(t, DTP.ap[:, sl_]))
                    yf_ = None
                    if d == 1:
                        yf_ = yfs.get()
                        T.dma(yf_[:], YF.v(t, YFv[:, :, sl_]))
                    return xp_, dtp_, yf_

                items = [(ti, t, c) for ti, t in enumerate(torder) for c in corder]
                tiles = {}

                def get_tile(ti):
                    if ti not in tiles:
                        tiles[ti] = load_tile(torder[ti]) + (yos.get(),)
                    return tiles[ti]

                def preamble(item):
                    ti, t, c = item
                    xp, dtp, yf, yo = get_tile(ti)
                    if c == corder[1] and ti + 1 < NT:
                        get_tile(ti + 1)
                    cs = slice(c * 128, (c + 1) * 128)
                    for i in range(4):
                        trp(pT[:, i * 128:(i + 1) * 128], xp[:, i, cs], ident)
                    xtok = xtoks.get()
                    cp("act", xtok[:], pT[:])
                    pm = pmisc.get()
                    sm = smalls.get()
                    trp(pm[:, 0:128], xp[:, 4, cs], ident)
                    trp(pm[:, 128:144], dtp[0:16, cs], cm[0:16, CM_ID, 0:16])
                    cp("dve", sm[:, 0:144], pm[:, 0:144])
                    btok = sm[:, 0:128]
                    dttok = sm[:, 128 + d * 8:128 + d * 8 + 8]
                    dA = sm[:, 144:152]
                    tt("dve", dA, dttok, aneg[:, d * 8:d * 8 + 8], ALU.mult)
                    pm2 = pmisc.get()
                    mm(pm2[:, 0:8], cmat(tri), dA)
                    cscol = sm[:, 152:160]
                    cp("dve", cscol, pm2[:, 0:8])
                    mm(pm2[:, 8:16], cmat(sel), cscol)
                    trp(pm2[0:8, 128:256], cscol, ident)
                    csrow = csrows.get()
                    cp("act", csrow[:], pm2[0:8, 128:256])
                    cslast = sm[:, 160:168]
                    cp("dve", cslast, pm2[:, 8:16])
                    warg = sm[:, 168:176]
                    tt("dve", warg, cslast, cscol, ALU.subtract)
                    wall = sm[:, 176:184]
                    act(wall, warg, AF.Exp)
                    cdall = sm[:, 184:192]
                    act(cdall, cslast, AF.Exp)
                    bbf = bbfs.get()
                    cp("pool", bbf[:], xp[:, 4, cs])
                    for g in range(2):
                        gs = slice(g * 64, (g + 1) * 64)
                        cbf = cbfz[g].get()
                        cp("pool", cbf[gs, :], xp[gs, 5, cs])
                        mm(pcb[:, g * 128:(g + 1) * 128], bbf[:, :], cbf[:, :])
                    cbsb = cbsbs.get()
                    cp("act", cbsb[:], pcb[:, 0:256])
                    xdt = xdts.get()
                    for h in range(8):
                        if h % 2 == 0:
                            act(xdt[:, h * 64:(h + 1) * 64], xtok[:, h * 64:(h + 1) * 64], AF.Copy, scale=dttok[:, h:h + 1])
                        else:
                            ts("dve", xdt[:, h * 64:(h + 1) * 64], xtok[:, h * 64:(h + 1) * 64], dttok[:, h:h + 1], ALU.mult)
                    return dict(xp=xp, yf=yf, yo=yo, cs=cs, btok=btok, cscol=cscol, csrow=csrow, wall=wall,
                                cdall=cdall, cbsb=cbsb, xdt=xdt)

                def heads(cx):
                    xp, yf, yo, cs = cx["xp"], cx["yf"], cx["yo"], cx["cs"]
                    btok, cscol, csrow, wall, cdall, cbsb, xdt = (cx[k_] for k_ in ("btok", "cscol", "csrow", "wall", "cdall", "cbsb", "xdt"))
                    for h in range(8):
                        g = h // 4
                        gs = slice(g * 64, (g + 1) * 64)
                        hs = slice((h % 2) * 64, (h % 2) * 64 + 64)
                        pr = (h % 4) // 2
                        pbc = pbcs.get()
                        mm(pbc[:, 0:128], cm[0:8, CM_SELH + h, :], csrow[:])
                        arg = args.get()
                        stt("dve", arg[:], pbc[:, 0:128], cscol[:, h:h + 1], cmat(neg), ALU.subtract, ALU.add)
                        dec = decs.get()
                        act(dec[:], arg[:], AF.Exp)
                        mt = mts.get()
                        tt("dve", mt[:], cbsb[:, g * 128:(g + 1) * 128], dec[:], ALU.mult)
                        ecs = ecss.get()
                        act(ecs[gs, :], pbc[gs, 0:128], AF.Exp)
                        cdec = cdecz[g].get()
                        tt("pool", cdec[gs, :], xp[gs, 5, cs], ecs[gs, :], ALU.mult)
                        py = pys.get()
                        pair_cols = slice((h // 2) * 128, (h // 2) * 128 + 128)
                        mm(py[:, 0:128], xdt[:, pair_cols], mt[:], start=True, stop=False)
                        mm(py[:, 0:128], stbf[:, pr * 2:pr * 2 + 2, :].re("p a b -> p (a b)"), cdec[:, :], start=False, stop=True)
                        bw = bws.get()
                        act(bw[:], btok, AF.Copy, scale=wall[:, h:h + 1])
                        mm(py[:, 128:192], bw[:], xdt[:, h * 64:(h + 1) * 64])
                        if d == 0:
                            stt("dve", yo[hs, h // 2, cs], xp[hs, h // 2, cs], vec[hs, VC["ssmd"] + h // 2:VC["ssmd"] + h // 2 + 1], py[hs, 0:128], ALU.mult, ALU.add)
                        else:
                            tt("dve", yo[hs, h // 2, cs], py[hs, 0:128], yf[hs, h // 2, cs], ALU.add)
                        stt("dve", st32[gs, h % 4, :], st32[gs, h % 4, :], cdall[gs, h:h + 1], py[gs, 128:192], ALU.mult, ALU.add)
                    cp("act", stbf[:], st32[:])

                cxn = preamble(items[0])
                for ii, item in enumerate(items):
                    cx = cxn
                    if ii + 1 < len(items):
                        cxn = preamble(items[ii + 1])
                    heads(cx)
                    ti, t, c = item
                    if c == corder[-1]:
                        sl = slice(t * 512, (t + 1) * 512)
                        if d == 0:
                            T.dma(YF.v(t, YFv[:, :, sl]), cx["yo"][:], own="in")
                        else:
                            T.dma(YS.v(t, YSv[:, :, sl]), cx["yo"][:], own="in")
            S.close()

        TM = 256
        NTM = L // TM
        if want("MERGE"):
            S = Stage("MG%d" % l)
            wG = alloc_w(S, D_MODEL, 4608, "wG"); wbr = alloc_w(S, 2048, D_MODEL, "wbr"); wo = alloc_w(S, D_MODEL, D_MODEL, "wo")
            vec = S.sb([128, NV], F32, "vec")
            rws = S.sb([128, 272], F32, "rows")
            lam = S.sb([128, 8], F32, "lam")
            ltmp = S.sb([128, 128], F32, "ltmp")
            W = Stage("MGw%d" % l)
            stg = W.pool(2, [128, 2048], F32, "stg")
            load_w_bf16(S, wG_in, wG_in.h[l], D_MODEL, 4608, stg, "wG", w=wG)
            load_w_bf16(S, wbr_in, wbr_in.h[l], 2048, D_MODEL, stg, "wbr", w=wbr)
            load_w_bf16(S, wout_in, wout_in.h[l], D_MODEL, D_MODEL, stg, "wo", w=wo)
            W.close()
            T.dma(vec[:], vecs_in[l])
            T.dma(rws[:], rows_in[l])
            tt("dve", ltmp[:, 0:64], rws[:, 0:64], rws[:, 64:128], ALU.mult)
            tt("dve", ltmp[:, 64:128], rws[:, 128:192], rws[:, 192:256], ALU.mult)
            T.op("dve", lambda: V.reduce_sum(out=lam.h[:, 0:1], in_=ltmp.h[:, 0:64], axis=mybir.AxisListType.X), reads=[ltmp[:]], writes=[lam[:]])
            T.op("dve", lambda: V.reduce_sum(out=lam.h[:, 1:2], in_=ltmp.h[:, 64:128], axis=mybir.AxisListType.X), reads=[ltmp[:]], writes=[lam[:]])
            act(lam[:, 2:4], lam[:, 0:2], AF.Exp)
            tt("dve", lam[:, 4:5], lam[:, 3:4], lam[:, 2:3], ALU.subtract)
            ts("dve", lam[:, 5:6], lam[:, 4:5], -lam_init, ALU.add)
            neglam = lam[:, 5:6]
            hb = S.sb([128, 8, TM], BF16, "hb"); xt = S.sb([128, 8, TM], F32, "xt")
            oa = S.sb([128, 8, TM], F32, "oa"); ob = S.sb([128, 4, TM], BF16, "ob"); od = S.sb([128, 4, TM], BF16, "od")
            ys = S.sb([128, 4, TM], F32, "ys"); tm = S.sb([128, TM], F32, "tm")
            ya = S.sb([128, 4, TM], BF16, "ya"); yc = S.sb([128, 4, TM], BF16, "yc"); yz = S.sb([128, 4, TM], F32, "yz")
            mrg = S.sb([128, 8, TM], BF16, "mrg"); outf = S.sb([128, 8, TM], F32, "outf")
            f32p = S.pool(6, [128, TM], F32, "f")
            sgp = S.pool(3, [128, TM], F32, "sg")
            rsb = S.sb([128, TM], F32, "rsb")
            pp = S.pspool(6, [128, 512], "pp")
            pst = S.pspool(2, [128, 512], "pst")
            kcv = lambda a: a.rearrange("(kc p) n -> p kc n", p=128)
            for i in range(NTM):
                sl = slice(i * TM, (i + 1) * TM)
                k_ = ("m", i)
                T.dma(hb[:], hT.v(k_, kcv(hT.ap)[:, :, sl]))
                T.dma(xt[:], xres.v(k_, kcv(xres.ap)[:, :, sl]))
                T.dma(oa[:], OA.v(k_, OA.ap[:, :, sl].rearrange("a p n -> p a n")))
                T.dma(ob[:], OB.v(k_, kcv(OB.ap)[:, :, sl]))
                T.dma(od[:], OD.v(k_, kcv(OD.ap)[:, :, sl]))
                T.dma(ys[:], YS.v(k_, kcv(YS.ap)[:, :, sl]))
                T.dma(tm[:], tmask_in[:, sl])
                for h in range(4):
                    o = f32p.get(); sq = f32p.get(); rs = f32p.get(); tmp = f32p.get()
                    stt("dve", o[:], oa[:, 2 * h + 1, :], neglam, oa[:, 2 * h, :], ALU.mult, ALU.add)
                    tt("pool", sq[:], o[:], o[:], ALU.mult)
                    pm = pp.get()
                    mm(pm[:, 0:TM], cmat(CM_ONES), sq[:])
                    act(tmp[:], pm[:, 0:TM], AF.Ln, bias=epsb[:, 0:1], scale=1.0 / 128)
                    act(rs[:], tmp[:], AF.Exp, scale=-0.5)
                    ts("dve", rs[:], rs[:], 1.0 - lam_init, ALU.mult)
                    stt("dve", ya[:, h, :], o[:], vec[:, VC["diffn"]:VC["diffn"] + 1], rs[:], ALU.mult, ALU.mult)
                pm = pst.get()
                for zc in range(4):
                    pz = pp.get()
                    for kc in range(8):
                        mm(pz[:, 0:TM], wG[:, kc, 4096 + zc * 128:4096 + (zc + 1) * 128], hb[:, kc, :], start=(kc == 0), stop=(kc == 7))
                    zs = f32p.get()
                    act(zs[:], pz[:, 0:TM], AF.Silu)
                    tt("dve", yz[:, zc, :], zs[:], ys[:, zc, :], ALU.mult)
                    sq = f32p.get()
                    tt("pool", sq[:], yz[:, zc, :], yz[:, zc, :], ALU.mult)
                    mm(pm[:, 0:TM], cmat(CM_ONES), sq[:], start=(zc == 0), stop=(zc == 3))
                tmp = f32p.get(); rs = f32p.get()
                act(tmp[:], pm[:, 0:TM], AF.Ln, bias=epsb[:, 0:1], scale=1.0 / 512)
                act(rs[:], tmp[:], AF.Exp, scale=-0.5)
                for zc in range(4):
                    stt("dve", yc[:, zc, :], yz[:, zc, :], vec[:, VC["ssmn"] + zc:VC["ssmn"] + zc + 1], rs[:], ALU.mult, ALU.mult)
                Ys = [ya, ob, yc, od]
                for oc in range(8):
                    macc = f32p.get()
                    for n_ in range(4):
                        pg = pp.get()
                        for kc in range(8):
                            mm(pg[:, 0:TM], wG[:, kc, n_ * 1024 + oc * 128:n_ * 1024 + (oc + 1) * 128], hb[:, kc, :], start=(kc == 0), stop=(kc == 7))
                        sg = sgp.get()
                        act(sg[:], pg[:, 0:TM], AF.Sigmoid)
                        pb = pp.get()
                        for kc in range(4):
                            mm(pb[:, 0:TM], wbr[:, n_ * 4 + kc, oc * 128:(oc + 1) * 128], Ys[n_][:, kc, :], start=(kc == 0), stop=(kc == 3))
                        if n_ == 0:
                            tt("dve", macc[:], pb[:, 0:TM], sg[:], ALU.mult)
                        else:
                            t2 = f32p.get()
                            tt("dve", t2[:], pb[:, 0:TM], sg[:], ALU.mult)
                            if n_ < 3:
                                tt("pool", macc[:], macc[:], t2[:], ALU.add)
                            else:
                                tt("pool", mrg[:, oc, :], macc[:], t2[:], ALU.add)
                pm = pst.get()
                for oc in range(8):
                    po = pp.get()
                    for kc in range(8):
                        mm(po[:, 0:TM], wo[:, kc, oc * 128:(oc + 1) * 128], mrg[:, kc, :], start=(kc == 0), stop=(kc == 7))
                    cp("act", outf[:, oc, :], po[:, 0:TM])
                    sq = f32p.get()
                    tt("pool", sq[:], outf[:, oc, :], outf[:, oc, :], ALU.mult)
                    mm(pm[:, 0:TM], cmat(CM_ONES), sq[:], start=(oc == 0), stop=(oc == 7))
                tmp = f32p.get(); rs = rsb
                act(tmp[:], pm[:, 0:TM], AF.Ln, bias=epsb[:, 0:1], scale=1.0 / D_MODEL)
                act(rs[:], tmp[:], AF.Exp, scale=-0.5)
                tt("dve", rs[:], rs[:], tm[:], ALU.mult)
                for oc in range(8):
                    t2 = f32p.get()
                    stt("dve", t2[:], outf[:, oc, :], vec[:, VC["post0"] + oc:VC["post0"] + oc + 1], rs[:], ALU.mult, ALU.mult)
                    tt("pool", xt[:, oc, :], xt[:, oc, :], t2[:], ALU.add)
                T.dma(xres.v(k_, kcv(xres.ap)[:, :, sl]), xt[:], own="in")
            S.close()

        if want("MEM"):
            S = Stage("MM%d" % l)
            wq = alloc_w(S, D_MODEL, 512, "wq"); wmo = alloc_w(S, 512, D_MODEL, "wmo")
            vec = S.sb([128, NV], F32, "vec")
            onesb = S.sb([128, 128], BF16, "onesb")
            Kmem = S.sb([128, 4, 256], BF16, "Kmem"); Vmem = S.sb([128, 2, 512], BF16, "Vmem")
            pp = S.pspool(3, [128, 512], "pp")
            pst = S.pspool(1, [128, 512], "pst")
            pS2 = S.pspool(2, [128, 1024], "pS2")
            f32p = S.pool(6, [128, 512], F32, "f")
            W = Stage("MMw%d" % l)
            stg = W.pool(2, [128, 2048], F32, "stg")
            load_w_bf16(S, wmq_in, wmq_in.h[l], D_MODEL, 512, stg, "wq", w=wq)
            load_w_bf16(S, wmo_in, wmo_in.h[l], 512, D_MODEL, stg, "wmo", w=wmo)
            wk = load_w_bf16(W, wmk_in, wmk_in.h[l], D_MODEL, 512, stg, "wk")
            wv = load_w_bf16(W, wmv_in, wmv_in.h[l], D_MODEL, 512, stg, "wv")
            T.dma(vec[:], vecs_in[l])
            cp("dve", onesb[:], cmat(CM_ONES))
            memf = W.sb([128, 8, 256], F32, "memf"); mn = W.sb([128, 8, 256], BF16, "mn")
            T.dma(memf[:], View(memT_in, memT_in.h.rearrange("(kc p) n -> p kc n", p=128)))
            pm = pst.get()
            for kc in range(8):
                sq = f32p.get()
                tt("dve", sq[:, 0:256], memf[:, kc, :], memf[:, kc, :], ALU.mult)
                mm(pm[:, 0:256], cmat(CM_ONES), sq[:, 0:256], start=(kc == 0), stop=(kc == 7))
            tmp = f32p.get(); rs = f32p.get()
            act(tmp[:, 0:256], pm[:, 0:256], AF.Ln, bias=epsb[:, 0:1], scale=1.0 / D_MODEL)
            act(rs[:, 0:256], tmp[:, 0:256], AF.Exp, scale=-0.5)
            for kc in range(8):
                stt("dve", mn[:, kc, :], memf[:, kc, :], vec[:, VC["memn"] + kc:VC["memn"] + kc + 1], rs[:, 0:256], ALU.mult, ALU.mult)
            for h in range(4):
                pk = pp.get()
                for kc in range(8):
                    mm(pk[:, 0:256], wk[:, kc, h * 128:(h + 1) * 128], mn[:, kc, :], start=(kc == 0), stop=(kc == 7))
                cp("act", Kmem[:, h, :], pk[:, 0:256])
            for mb in range(2):
                pv = pp.get()
                for kc in range(8):
                    mm(pv[:, :], mn[:, kc, mb * 128:(mb + 1) * 128], wv[:, kc, :], start=(kc == 0), stop=(kc == 7))
                cp("act", Vmem[:, mb, :], pv[:, :])
            W.close()
            xts = S.pool(2, [128, 8, 512], F32, "xt")
            tms = S.pool(2, [128, 512], F32, "tm")
            h1 = S.sb([128, 8, 512], BF16, "h1")
            h2s = S.pool(2, [128, 8, 512], BF16, "h2")
            oh = S.sb([128, 4, 512], BF16, "oh")
            outf = S.sb([128, 8, 512], F32, "outf")
            qhs = S.pool(2, [128, 512], BF16, "qh")
            rsb = S.sb([128, 512], F32, "rsb")
            pts = S.pool(2, [128, 1024], BF16, "pt")
            kcv = lambda a: a.rearrange("(kc p) n -> p kc n", p=128)

            def norm_to(xt_, col0, dst):
                pm_ = pst.get()
                for kc in range(8):
                    sq = f32p.get()
                    if kc % 2 == 0:
                        act(sq[:], xt_[:, kc, :], AF.Square)
                    else:
                        tt("pool", sq[:], xt_[:, kc, :], xt_[:, kc, :], ALU.mult)
                    mm(pm_[:], cmat(CM_ONES), sq[:], start=(kc == 0), stop=(kc == 7))
                tmp_ = f32p.get(); rs_ = f32p.get()
                act(tmp_[:], pm_[:], AF.Ln, bias=epsb[:, 0:1], scale=1.0 / D_MODEL)
                act(rs_[:], tmp_[:], AF.Exp, scale=-0.5)
                for kc in range(8):
                    stt("dve", dst[:, kc, :], xt_[:, kc, :], vec[:, col0 + kc:col0 + kc + 1], rs_[:], ALU.mult, ALU.mult)

            def ld(t):
                sl_ = slice(t * 512, (t + 1) * 512)
                xt_ = xts.get(); tm_ = tms.get()
                T.dma(xt_[:], xres.v(("e", t), kcv(xres.ap)[:, :, sl_]))
                T.dma(tm_[:], tmask_in[:, sl_])
                return xt_, tm_

            nxt = ld(0)
            for t in range(NT):
                xt, tm = nxt
                if t + 1 < NT:
                    nxt = ld(t + 1)
                sl = slice(t * 512, (t + 1) * 512)
                norm_to(xt, VC["pre1"], h1)
                for h in range(4):
                    pq = pp.get()
                    for kc in range(8):
                        mm(pq[:], wq[:, kc, h * 128:(h + 1) * 128], h1[:, kc, :], start=(kc == 0), stop=(kc == 7))
                    qh = qhs.get()
                    cp("act", qh[:], pq[:])
                    ps2 = pS2.get()
                    for mb in range(2):
                        mm(ps2[:, mb * 512:(mb + 1) * 512], Kmem[:, h, mb * 128:(mb + 1) * 128], qh[:])
                    pt = pts.get()
                    act(pt[:], ps2[:], AF.Exp, scale=128 ** -0.5)
                    po = pp.get(); pd = pp.get()
                    for mb in range(2):
                        mm(po[:], Vmem[:, mb, h * 128:(h + 1) * 128], pt[:, mb * 512:(mb + 1) * 512], start=(mb == 0), stop=(mb == 1))
                    for mb in range(2):
                        mm(pd[:], onesb[:], pt[:, mb * 512:(mb + 1) * 512], start=(mb == 0), stop=(mb == 1))
                    rd = f32p.get()
                    recip(rd[:], pd[:])
                    tt("dve", oh[:, h, :], po[:], rd[:], ALU.mult)
                pm = pst.get()
                for oc in range(8):
                    po = pp.get()
                    for h in range(4):
                        mm(po[:], wmo[:, h, oc * 128:(oc + 1) * 128], oh[:, h, :], start=(h == 0), stop=(h == 3))
                    cp("act", outf[:, oc, :], po[:])
                    sq = f32p.get()
                    tt("pool", sq[:], outf[:, oc, :], outf[:, oc, :], ALU.mult)
                    mm(pm[:], cmat(CM_ONES), sq[:], start=(oc == 0), stop=(oc == 7))
                tmp = f32p.get(); rs = rsb
                act(tmp[:], pm[:], AF.Ln, bias=epsb[:, 0:1], scale=1.0 / D_MODEL)
                act(rs[:], tmp[:], AF.Exp, scale=-0.5)
                tt("dve", rs[:], rs[:], tm[:], ALU.mult)
                for oc in range(8):
                    t2 = f32p.get()
                    stt("dve", t2[:], outf[:, oc, :], vec[:, VC["post1"] + oc:VC["post1"] + oc + 1], rs[:], ALU.mult, ALU.mult)
                    tt("pool", xt[:, oc, :], xt[:, oc, :], t2[:], ALU.add)
                T.dma(xres.v(("e", t), kcv(xres.ap)[:, :, sl]), xt[:], own="in")
                h2 = h2s.get()
                norm_to(xt, VC["pre2"], h2)
                T.dma(H2.v(("e", t), kcv(H2.ap)[:, :, sl]), h2[:], own="in")
            S.close()

        if want("FFN"):
            S = Stage("FF%d" % l)
            wfi = alloc_w(S, D_MODEL, 2 * D_FF, "wfi"); wfo = alloc_w(S, D_FF, D_MODEL, "wfo")
            vec = S.sb([128, NV], F32, "vec")
            W = Stage("FFw%d" % l)
            stg = W.pool(2, [128, 2048], F32, "stg")
            load_w_bf16(S, wfi_in, wfi_in.h[l], D_MODEL, 2 * D_FF, stg, "wfi", w=wfi)
            load_w_bf16(S, wfo_in, wfo_in.h[l], D_FF, D_MODEL, stg, "wfo", w=wfo)
            W.close()
            T.dma(vec[:], vecs_in[l])
            h2t = S.pool(2, [128, 8, TM + 2], BF16, "h2t")
            xt = S.sb([128, 8, TM], F32, "xt"); tm = S.sb([128, TM], F32, "tm")
            actT = S.sb([128, 22, TM], BF16, "actT")
            outf = S.sb([128, 8, TM], F32, "outf")
            f32p = S.pool(8, [128, TM], F32, "f")
            rsb = S.sb([128, TM], F32, "rsb")
            pp = S.pspool(7, [128, 512], "pp")
            pst = S.pspool(1, [128, 512], "pst")
            kcv = lambda a: a.rearrange("(kc p) n -> p kc n", p=128)
            H2v = kcv(H2.ap)
            last = (l == depth - 1)

            def ldh(i):
                c0 = i * TM
                hh = h2t.get()
                lo = max(c0 - 1, 0); hi = min(c0 + TM + 1, L)
                T.dma(hh[:, :, (lo - (c0 - 1)):(hi - (c0 - 1))], H2.v(("f", 0), H2v[:, :, lo:hi]))
                if c0 == 0:
                    mset("pool", hh[:, :, 0:1], 0.0)
                if c0 + TM == L:
                    mset("pool", hh[:, :, TM + 1:TM + 2], 0.0)
                return hh

            nh = ldh(0)
            for i in range(NTM):
                sl = slice(i * TM, (i + 1) * TM)
                hh = nh
                if i + 1 < NTM:
                    nh = ldh(i + 1)
                k_ = ("f", i + 1)
                T.dma(xt[:], xres.v(k_, kcv(xres.ap)[:, :, sl]))
                T.dma(tm[:], tmask_in[:, sl])
                for j in range(22):
                    cv = []
                    for half in range(2):
                        ch = j + 22 * half
                        pu = pp.get()
                        for kc in range(8):
                            mm(pu[:, 0:TM + 2], wfi[:, kc, ch * 128:(ch + 1) * 128], hh[:, kc, :], start=(kc == 0), stop=(kc == 7))
                        a0 = f32p.get(); a1 = f32p.get()
                        ts("dve", a0[:], pu[:, 0:TM], vec[:, VC["fw0"] + ch:VC["fw0"] + ch + 1], ALU.mult, vec[:, VC["fb"] + ch:VC["fb"] + ch + 1], ALU.add)
                        stt("dve", a1[:], pu[:, 1:TM + 1], vec[:, VC["fw1"] + ch:VC["fw1"] + ch + 1], a0[:], ALU.mult, ALU.add)
                        stt("dve", a0[:], pu[:, 2:TM + 2], vec[:, VC["fw2"] + ch:VC["fw2"] + ch + 1], a1[:], ALU.mult, ALU.add)
                        cv.append(a0)
                    ga = f32p.get()
                    act(ga[:], cv[0][:], AF.Gelu)
                    tt("pool", actT[:, j, :], ga[:], cv[1][:], ALU.mult)
                pm = pst.get()
                for oc in range(8):
                    po = pp.get()
                    for j in range(22):
                        mm(po[:, 0:TM], wfo[:, j, oc * 128:(oc + 1) * 128], actT[:, j, :], start=(j == 0), stop=(j == 21))
                    cp("act", outf[:, oc, :], po[:, 0:TM])
                    sq = f32p.get()
                    tt("pool", sq[:], outf[:, oc, :], outf[:, oc, :], ALU.mult)
                    mm(pm[:, 0:TM], cmat(CM_ONES), sq[:], start=(oc == 0), stop=(oc == 7))
                tmp = f32p.get(); rs = rsb
                act(tmp[:], pm[:, 0:TM], AF.Ln, bias=epsb[:, 0:1], scale=1.0 / D_MODEL)
                act(rs[:], tmp[:], AF.Exp, scale=-0.5)
                tt("dve", rs[:], rs[:], tm[:], ALU.mult)
                for oc in range(8):
                    t2 = f32p.get()
                    stt("dve", t2[:], outf[:, oc, :], vec[:, VC["post2"] + oc:VC["post2"] + oc + 1], rs[:], ALU.mult, ALU.mult)
                    tt("pool", xt[:, oc, :], xt[:, oc, :], t2[:], ALU.add)
                dst = yT_out if last else xres
                T.dma(dst.v(k_, kcv(dst.ap)[:, :, sl]), xt[:], own="in")
            S.close()

    T.finish()
    glob.close()
    return nc


def _const_tables(L):
    c = {}
    cm = np.zeros((NCM, 128, 128), np.float32)
    k = np.arange(128)
    cm[0] = np.eye(128)
    cm[1] = 1.0
    cm[2] = (k[:, None] <= k[None, :])
    cm[3] = (k[:, None] >= k[None, :])
    cm[4] = np.where(k[:, None] <= k[None, :], 0.0, NEG)
    cm[5] = np.where(k[:, None] >= k[None, :], 0.0, NEG)
    cm[6] = (k[:, None] // 64 == k[None, :] // 64)
    for m in range(32):
        if m < 16:
            cm[7][64 + m + 16, 64 + m] = -1.0
        else:
            cm[7][64 + m - 16, 64 + m] = 1.0
    for blk in range(2):
        for sub in range(2):
            o = blk * 64 + sub * 32
            for m in range(32):
                if m < 16:
                    cm[8][o + m + 16, o + m] = -1.0
                else:
                    cm[8][o + m - 16, o + m] = 1.0
    cm[9][127, :] = 1.0
    cm[10][64, :] = 1.0
    cm[11][0, :] = 1.0
    for h_ in range(16):
        cm[12 + h_][h_, :] = 1.0
    c["cmat"] = np.ascontiguousarray(cm.transpose(1, 0, 2))
    pos = np.arange(L, dtype=np.float32)
    freqs = (np.float32(10000.0) ** (-(np.arange(16, dtype=np.float32)) / np.float32(16))).astype(np.float32)
    angB = (pos[None, :] * freqs[:, None]).astype(np.float32)
    c["cosB"] = np.concatenate([np.cos(angB), np.cos(angB)], 0).astype(np.float32)
    c["sinB"] = np.concatenate([np.sin(angB), np.sin(angB)], 0).astype(np.float32)
    rowp = (np.arange(L) // 64).astype(np.float32)
    colp = (np.arange(L) % 64).astype(np.float32)
    angR = (rowp[None, :] * freqs[:, None]).astype(np.float32)
    angC = (colp[None, :] * freqs[:, None]).astype(np.float32)
    cD = np.concatenate([np.cos(angR), np.cos(angR), np.cos(angC), np.cos(angC)], 0)
    sD = np.concatenate([np.sin(angR), np.sin(angR), np.sin(angC), np.sin(angC)], 0)
    c["cosD"] = np.concatenate([cD, cD], 0).astype(np.float32)
    c["sinD"] = np.concatenate([sD, sD], 0).astype(np.float32)
    return c


def _alibi_tables(L, Lreal):
    j = np.arange(L)
    jh = (j // 128).astype(np.float32)
    jl = (j % 128).astype(np.float32)
    kmask = np.where(j < Lreal, 0.0, NEG).astype(np.float32)
    kaug = np.zeros((4, 5, L), np.float32)
    bdiag = np.zeros((4, 128, 2048), np.float32)
    il = np.arange(512, dtype=np.float32)
    jl128 = np.arange(128, dtype=np.float32)
    for h in range(4):
        m = 2.0 ** (-2.0 * (h + 1))
        kaug[h, 0] = -8 * m * 128
        kaug[h, 1] = -8 * m
        kaug[h, 2] = 8 * m * 128 * jh
        kaug[h, 3] = 8 * m * jl
        kaug[h, 4] = kmask
        for dp in range(2):
            for hf in range(2):
                koff = 128 * (2 * dp + hf)
                c0 = dp * 1024 + hf * 512
                bdiag[h, :, c0:c0 + 512] = -8 * m * np.abs(il[None, :] - (koff + jl128[:, None]))
    qaug = np.zeros((3, 5, L), np.float32)
    qaug[0, 0] = jh; qaug[0, 1] = jl; qaug[0, 2] = 1; qaug[0, 3] = 1; qaug[0, 4] = 1
    qaug[1, 0] = -jh; qaug[1, 1] = -jl; qaug[1, 2] = -1; qaug[1, 3] = -1; qaug[1, 4] = 1
    qaug[2, 4] = 1
    return dict(kaugA=kaug.astype(NPBF), qaugA=qaug.astype(NPBF), bdiag=bdiag,
                kmask=kmask[None, :].astype(NPBF), onesrow=np.ones((1, L), NPBF))


def _weights_layout(p, depth):
    f = lambda a: np.ascontiguousarray(np.asarray(a, dtype=np.float32))
    w_in = f(p["w_in"])
    o = {}
    o["wP"] = f(np.concatenate([w_in[:, :, 0:2208], w_in[:, :, 2720:4272]], axis=2))
    o["wG"] = f(np.concatenate([w_in[:, :, G_OFF:G_OFF + 4096], w_in[:, :, C_Z:C_Z + 512]], axis=2))
    o["wuq"] = f(p["w_mla_uq"])
    wukv = f(p["w_mla_ukv"]).reshape(depth, 256, 4, 192)
    o["wukvK"] = f(wukv[..., 0:64].reshape(depth, 256, 256))
    o["wukvV"] = f(wukv[..., 64:192].reshape(depth, 256, 512))
    o["wbr"] = f(p["w_branch"]).reshape(depth, 2048, D_MODEL)
    o["wout"] = f(p["w_out"])
    o["wmq"] = f(p["w_mem_q"])
    wkv = f(p["w_mem_kv"]).reshape(depth, D_MODEL, 4, 256)
    o["wmk"] = f(wkv[..., 0:128].reshape(depth, D_MODEL, 512))
    o["wmv"] = f(wkv[..., 128:256].reshape(depth, D_MODEL, 512))
    o["wmo"] = f(p["w_mem_o"])
    o["wfi"] = f(p["w_ffn_in"])
    o["wfo"] = f(p["w_ffn_out"])
    vecs = np.zeros((depth, 128, NV), np.float32)

    def put(name, arr, width):
        a = f(arr).reshape(depth, width, 128).transpose(0, 2, 1)
        vecs[:, :, VC[name]:VC[name] + width] = a

    npre = f(p["norm_pre"]); npost = f(p["norm_post"])
    for i in range(3):
        put("pre%d" % i, npre[:, i], 8)
        put("post%d" % i, npost[:, i], 8)
    put("diffn", p["diff_norm"], 1)
    put("mlaq", p["mla_q_norm"], 3)
    put("mlakv", p["mla_kv_norm"], 2)
    cw = f(p["ssm_conv_w"])
    for k_ in range(3):
        put("cw%d" % k_, cw[:, k_], 6)
    put("cb", p["ssm_conv_b"], 6)
    put("ssmn", p["ssm_norm"], 4)
    put("gq", np.tile(f(p["gqa_q_norm"]), (1, 2)), 1)
    put("gk", np.tile(f(p["gqa_k_norm"]), (1, 2)), 1)
    put("memn", p["mem_norm"], 8)
    dtb = f(p["ssm_dt_bias"]).reshape(depth, 16)
    vecs[:, 0:16, VC["dtb"]] = dtb
    put("ssmd", np.repeat(f(p["ssm_d"]), 64, axis=1), 4)
    fw = f(p["ffn_conv_w"])
    for k_ in range(3):
        put("fw%d" % k_, fw[:, k_], 44)
    put("fb", p["ffn_conv_b"], 44)
    o["vecs"] = vecs
    rows = np.zeros((depth, 128, 272), np.float32)
    rows[:, :, 0:256] = f(p["diff_lambda"]).reshape(depth, 1, 256)
    rows[:, :, 256:272] = f(p["ssm_a_log"]).reshape(depth, 1, 16)
    o["rows"] = rows
    return o


def make_in_maps(seqs, mems, params, L, depth):
    consts = _const_tables(L)
    wl = _weights_layout(params, depth)
    maps = []
    cache = {}
    for x, mem in zip(seqs, mems):
        Lr = x.shape[0]
        if Lr not in cache:
            cache[Lr] = _alibi_tables(L, Lr)
        m = dict(consts)
        m.update(wl)
        m.update(cache[Lr])
        xT = np.zeros((D_MODEL, L), np.float32)
        xT[:, :Lr] = np.asarray(x, np.float32).T
        m["xT"] = xT
        m["memT"] = np.ascontiguousarray(np.asarray(mem, np.float32).T)
        tm = np.zeros((128, L), np.float32)
        tm[:, :Lr] = 1.0
        m["tmask"] = tm
        maps.append(m)
    return maps


PARAM_NAMES = ["w_in", "w_branch", "w_out", "diff_lambda", "diff_norm", "mla_q_norm", "mla_kv_norm",
               "w_mla_uq", "w_mla_ukv", "ssm_conv_w", "ssm_conv_b", "ssm_a_log", "ssm_dt_bias", "ssm_d",
               "ssm_norm", "gqa_q_norm", "gqa_k_norm", "mem_norm", "w_mem_q", "w_mem_kv", "w_mem_o",
               "w_ffn_in", "ffn_conv_w", "ffn_conv_b", "w_ffn_out", "norm_pre", "norm_post"]


def kernel(**inputs):
    xp = np.asarray(inputs["x_prompt"], np.float32)
    xs = np.asarray(inputs["x_sample"], np.float32)
    mp = np.asarray(inputs["mem_prompt"], np.float32)
    ms = np.asarray(inputs["mem_sample"], np.float32)
    params = {n: np.asarray(inputs[n], np.float32) for n in PARAM_NAMES}
    depth = params["w_in"].shape[0]
    L = xp.shape[1]
    seqs = [xp[0], xp[1], xs[0], xs[1], xs[2], xs[3], xs[0], xs[1]]
    mems = [mp[0], mp[1], ms[0], ms[1], ms[2], ms[3], ms[0], ms[1]]
    outs = run_trunk(seqs, mems, params, L, depth)
    y_prompt = np.stack([outs[0], outs[1]], 0).astype(np.float32)
    y_sample = np.stack([outs[2], outs[3], outs[4], outs[5]], 0).astype(np.float32)
    return (y_prompt, y_sample)


_NC_CACHE = {}


def run_trunk(seqs, mems, params, L, depth):
    key = (L, depth)
    if key not in _NC_CACHE:
        _NC_CACHE[key] = build_program(L, depth=depth)
    nc = _NC_CACHE[key]
    maps = make_in_maps(seqs, mems, params, L, depth)
    res = run_bass_kernel_spmd(nc, maps, core_ids=list(range(8)))
    outs = []
    for i, x in enumerate(seqs):
        yT = np.asarray(res.results[i]["yT"], np.float32)
        outs.append(np.ascontiguousarray(yT[:, :x.shape[0]].T))
    return outs
```

```python
import math
import numpy as np
import ml_dtypes
import concourse.bass as bass
import concourse.mybir as mybir
from concourse.bass_utils import run_bass_kernel_spmd
from contextlib import ExitStack

F32 = mybir.dt.float32
BF16 = mybir.dt.bfloat16
AF = mybir.ActivationFunctionType
ALU = mybir.AluOpType
NPBF = ml_dtypes.bfloat16

ENGS = ("pe", "act", "dve", "pool")
D_MODEL = 1024
EPS = 1e-6
MEM_LEN = 256
D_FF = 2816
NEG = -30000.0
NCM = 28


class Buf:
    __slots__ = ("h", "name", "w", "r", "ld", "st")

    def __init__(self, h, name):
        self.h = h
        self.name = name
        self.w = None
        self.r = {}
        self.ld = None
        self.st = None

    def __getitem__(self, idx):
        return View(self, self.h[idx])


class View:
    __slots__ = ("b", "ap")

    def __init__(self, b, ap):
        self.b = b
        self.ap = ap

    def __getitem__(self, idx):
        return View(self.b, self.ap[idx])

    def re(self, pat, **kw):
        return View(self.b, self.ap.rearrange(pat, **kw))


class Trk:
    def __init__(self, nc, es, n_dma_sems=80):
        self.nc = nc
        self.eng = {"pe": nc.tensor, "act": nc.scalar, "dve": nc.vector, "pool": nc.gpsimd,
                    "sp": nc.sync}
        self.sem = {e: es.enter_context(nc.semaphore("c_" + e)) for e in ENGS}
        self.cnt = {e: 0 for e in ENGS}
        self.dsem = [es.enter_context(nc.semaphore("d%d" % i)) for i in range(n_dma_sems)]
        self.dcnt = [0] * n_dma_sems
        self.dfree = list(range(n_dma_sems))
        self.issuers = list(ENGS) + ["sp"]
        self.known = {e: {} for e in self.issuers}
        self.nins = 0

    def _handle(self, key):
        return self.sem[key] if isinstance(key, str) else self.dsem[key]

    def _wait(self, issuer, dep):
        if dep is None:
            return
        key, val = dep
        if issuer == "pe" and key == "pe":
            return
        if self.known[issuer].get(key, 0) >= val:
            return
        self.eng[issuer].wait_ge(self._handle(key), val)
        self.known[issuer][key] = val
        self.nins += 1

    def _deps(self, issuer, reads, writes):
        for v in reads:
            self._wait(issuer, v.b.w)
        for v in writes:
            self._wait(issuer, v.b.w)
            for k, val in v.b.r.items():
                self._wait(issuer, (k, val))

    def op(self, e, fn, reads=(), writes=()):
        self._deps(e, reads, writes)
        ins = fn()
        self.cnt[e] += 1
        self.nins += 1
        ins.then_inc(self.sem[e], 1)
        c = self.cnt[e]
        for v in reads:
            v.b.r[e] = c
        for v in writes:
            v.b.w = (e, c)
            v.b.r = {}
        return ins

    def dma(self, out, in_, own="out", q="sp", extra_reads=(), extra_writes=(), **kw):
        self._deps(q, [in_] + list(extra_reads), [out] + list(extra_writes))
        if own == "out":
            if out.b.ld is None:
                out.b.ld = self.dfree.pop()
            slot = out.b.ld
        else:
            if in_.b.st is None:
                in_.b.st = self.dfree.pop()
            slot = in_.b.st
        ins = self.eng[q].dma_start(out=out.ap, in_=in_.ap, **kw)
        self.nins += 1
        self.dcnt[slot] += 16
        ins.then_inc(self.dsem[slot], 16)
        c = self.dcnt[slot]
        for v in [in_] + list(extra_reads):
            v.b.r[slot] = c
        for v in [out] + list(extra_writes):
            v.b.w = (slot, c)
            v.b.r = {}
        return ins

    def full_barrier(self, release=()):
        for issuer in self.issuers:
            for e in ENGS:
                if self.cnt[e]:
                    self._wait(issuer, (e, self.cnt[e]))
            for s in range(len(self.dsem)):
                if self.dcnt[s]:
                    self._wait(issuer, (s, self.dcnt[s]))
        for b in release:
            for s in (b.ld, b.st):
                if s is not None:
                    self.dfree.append(s)
            b.ld = b.st = None

    def finish(self):
        for s in range(len(self.dsem)):
            if self.dcnt[s]:
                self._wait("sp", (s, self.dcnt[s]))


A_Q, A_K, A_V = 0, 512, 1024
B_CQ, B_CKV, B_KR = 1536, 1920, 2176
C_Z, C_XBC, C_DTF, C_DTB = 2208, 2720, 3488, 3496
D_Q, D_K, D_V = 3504, 4016, 4144
G_OFF = 4272
PC_XBC = 2208
PC_DT = 2208 + 768
PC_DQ = PC_DT + 16
PC_DK = PC_DQ + 512
PC_DV = PC_DK + 128
NPC = PC_DV + 128

VC = {}
_o = 0
for _n, _w in [("pre0", 8), ("pre1", 8), ("pre2", 8), ("post0", 8), ("post1", 8), ("post2", 8),
               ("diffn", 1), ("mlaq", 3), ("mlakv", 2), ("cw0", 6), ("cw1", 6), ("cw2", 6), ("cb", 6),
               ("ssmn", 4), ("gq", 1), ("gk", 1), ("memn", 8), ("dtb", 1), ("ssmd", 4),
               ("fw0", 44), ("fw1", 44), ("fw2", 44), ("fb", 44)]:
    VC[_n] = _o
    _o += _w
NV = _o


def build_program(L, depth=2, dbg=(), cut_scale=60.0, stages=None):
    nc = bass.Bass("TRN2", target_bir_lowering=False)
    NT = L // 512
    NCH = L // 128
    glob = ExitStack()
    T = Trk(nc, glob)
    V, A, P, G = nc.vector, nc.scalar, nc.tensor, nc.gpsimd
    ENG = {"dve": V, "act": A, "pool": G}

    def dram(name, shape, dt, kind="Internal"):
        if name in dbg:
            kind = "ExternalOutput"
        return nc.dram_tensor(name, shape, dt, kind=kind).ap()

    class DT:
        def __init__(self, name, shape, dt, kind="Internal", tok_axis=-1):
            self.ap = dram(name, shape, dt, kind)
            self.name = name
            self.tiles = {}

        def t(self, i):
            if i not in self.tiles:
                self.tiles[i] = Buf(self.ap, "%s.%s" % (self.name, i))
            return self.tiles[i]

        def v(self, i, ap):
            return View(self.t(i), ap)

    def ext(name, shape, dt=F32):
        return Buf(nc.dram_tensor(name, shape, dt, kind="ExternalInput").ap(), name)

    xT_in = ext("xT", [D_MODEL, L])
    memT_in = ext("memT", [D_MODEL, MEM_LEN])
    tmask_in = ext("tmask", [128, L])
    kaugA_in = ext("kaugA", [4, 5, L], BF16)
    qaugA_in = ext("qaugA", [3, 5, L], BF16)
    kmask_in = ext("kmask", [1, L], BF16)
    onesrow_in = ext("onesrow", [1, L], BF16)
    bdiag_in = ext("bdiag", [4, 128, 2048])
    cosB_in = ext("cosB", [32, L]); sinB_in = ext("sinB", [32, L])
    cosD_in = ext("cosD", [128, L]); sinD_in = ext("sinD", [128, L])
    cmat_in = ext("cmat", [128, NCM, 128])
    wP_in = ext("wP", [depth, D_MODEL, NPC])
    wG_in = ext("wG", [depth, D_MODEL, 4096 + 512])
    wuq_in = ext("wuq", [depth, 384, 384])
    wukvK_in = ext("wukvK", [depth, 256, 256])
    wukvV_in = ext("wukvV", [depth, 256, 512])
    wbr_in = ext("wbr", [depth, 2048, D_MODEL])
    wout_in = ext("wout", [depth, D_MODEL, D_MODEL])
    wmq_in = ext("wmq", [depth, D_MODEL, 512])
    wmk_in = ext("wmk", [depth, D_MODEL, 512])
    wmv_in = ext("wmv", [depth, D_MODEL, 512])
    wmo_in = ext("wmo", [depth, 512, D_MODEL])
    wfi_in = ext("wfi", [depth, D_MODEL, 2 * D_FF])
    wfo_in = ext("wfo", [depth, D_FF, D_MODEL])
    vecs_in = ext("vecs", [depth, 128, NV])
    rows_in = ext("rows", [depth, 128, 256 + 16])
    yT_out = DT("yT", [D_MODEL, L], F32, kind="ExternalOutput")

    xres = DT("xres", [D_MODEL, L], F32)
    hT = DT("hT", [D_MODEL, L], BF16)
    QA = DT("QA", [8, 64, L], BF16); KA = DT("KA", [8, 64, L], BF16); VA = DT("VA", [4, L, 128], BF16)
    QB = DT("QB", [4, 96, L], BF16); KB = DT("KB", [4, 96, L], BF16); VB = DT("VB", [4, L, 128], BF16)
    QD = DT("QD", [8, 64, L], BF16); KD = DT("KD", [2, 64, L], BF16); VD = DT("VD", [2, L, 64], BF16)
    XBC = DT("XBC", [768, L], F32); DTR = DT("DTR", [16, L], F32)
    XBCP = DT("XBCP", [768, L], F32); DTP = DT("DTP", [16, L], F32)
    OA = DT("OA", [8, 128, L], F32); OB = DT("OB", [512, L], BF16); OD = DT("OD", [512, L], BF16)
    YF = DT("YF", [512, L], F32); YS = DT("YS", [512, L], F32)
    H2 = DT("H2", [D_MODEL, L], BF16)

    def mm(out, lhsT, rhs, start=True, stop=True):
        T.op("pe", lambda: P.matmul(out.ap, lhsT=lhsT.ap, rhs=rhs.ap, start=start, stop=stop),
             reads=[lhsT, rhs], writes=[out])

    def trp(out, in_, ident):
        T.op("pe", lambda: P.matmul(out.ap, lhsT=in_.ap, rhs=ident.ap, start=True, stop=True), reads=[in_, ident], writes=[out])

    def act(out, in_, func, bias=None, scale=1.0):
        rd = [in_]
        kw = {}
        if isinstance(scale, View):
            rd.append(scale)
            scale = scale.ap
        if isinstance(bias, View):
            rd.append(bias); kw["bias"] = bias.ap
        elif bias is not None:
            kw["bias"] = bias
        T.op("act", lambda: A.activation(out=out.ap, in_=in_.ap, func=func, scale=scale, **kw),
             reads=rd, writes=[out])

    def _s(x, rd):
        if isinstance(x, View):
            rd.append(x)
            return x.ap
        return x

    def ts(e, out, in0, s1, op0, s2=None, op1=None):
        rd = [in0]
        a1 = _s(s1, rd); a2 = _s(s2, rd)
        kw = {} if op1 is None else {"op1": op1}
        T.op(e, lambda: ENG[e].tensor_scalar(out=out.ap, in0=in0.ap, scalar1=a1, scalar2=a2, op0=op0, **kw),
             reads=rd, writes=[out])

    def tt(e, out, in0, in1, op):
        T.op(e, lambda: ENG[e].tensor_tensor(out=out.ap, in0=in0.ap, in1=in1.ap, op=op),
             reads=[in0, in1], writes=[out])

    def stt(e, out, in0, scalar, in1, op0, op1):
        e = "dve"
        rd = [in0, in1]
        a = _s(scalar, rd)
        T.op(e, lambda: ENG[e].scalar_tensor_tensor(out=out.ap, in0=in0.ap, scalar=a, in1=in1.ap, op0=op0, op1=op1),
             reads=rd, writes=[out])

    def cp(e, out, in_):
        if e == "act":
            T.op("act", lambda: A.copy(out=out.ap, in_=in_.ap), reads=[in_], writes=[out])
        else:
            T.op(e, lambda: ENG[e].tensor_copy(out=out.ap, in_=in_.ap), reads=[in_], writes=[out])

    def mset(e, out, val):
        T.op(e, lambda: ENG[e].memset(out.ap, val), writes=[out])

    def recip(out, in_):
        T.op("dve", lambda: V.reciprocal(out=out.ap, in_=in_.ap), reads=[in_], writes=[out])

    class Stage:
        def __init__(self, name):
            self.es = ExitStack()
            self.bufs = []
            self.name = name
            self.n = 0

        def sb(self, shape, dt, name=None):
            self.n += 1
            nm = "%s_%d_%s" % (self.name, self.n, name or "t")
            b = Buf(self.es.enter_context(nc.sbuf_tensor(nm, list(shape), dt)), nm)
            self.bufs.append(b)
            return b

        def ps(self, shape, dt=F32, name=None):
            self.n += 1
            nm = "%s_%d_%s" % (self.name, self.n, name or "p")
            b = Buf(self.es.enter_context(nc.psum_tensor(nm, list(shape), dt)), nm)
            self.bufs.append(b)
            return b

        def pool(self, n, shape, dt, name="pl"):
            return Ring([self.sb(shape, dt, name) for _ in range(n)])

        def pspool(self, n, shape, name="pp"):
            return Ring([self.ps(shape, F32, name) for _ in range(n)])

        def close(self):
            T.full_barrier(release=self.bufs)
            self.es.close()

    class Ring:
        def __init__(self, items):
            self.items = items
            self.i = 0

        def get(self):
            b = self.items[self.i % len(self.items)]
            self.i += 1
            return b

    def alloc_w(S, K, N, name):
        return S.sb([128, (K + 127) // 128, N], BF16, name)

    def load_w_bf16(S, src_buf, src_ap, K, N, stg_ring, name, engs=("dve", "pool"), w=None):
        kc_n = (K + 127) // 128
        if w is None:
            w = S.sb([128, kc_n, N], BF16, name)
        step = stg_ring.items[0].h.shape[1]
        i = 0
        for kc in range(kc_n):
            rows = min(128, K - kc * 128)
            for c0 in range(0, N, step):
                cw = min(step, N - c0)
                st = stg_ring.get()
                T.dma(st[0:rows, 0:cw], View(src_buf, src_ap[kc * 128:kc * 128 + rows, c0:c0 + cw]))
                cp(engs[i % len(engs)], w[0:rows, kc, c0:c0 + cw], st[0:rows, 0:cw])
                i += 1
        return w

    cm = Buf(glob.enter_context(nc.sbuf_tensor("cmat_sb", [128, NCM, 128], F32)), "cmat_sb")
    T.dma(cm[:], cmat_in[:, :, :])
    CM_ID, CM_ONES, CM_U, CM_LO, CM_NEGF, CM_NEGB, CM_BD64, CM_R32, CM_RD, CM_SEL127, CM_SEL64, CM_SEL0, CM_SELH = range(13)
    ident = cm[:, CM_ID, :]

    def cmat(i, rows=128, cols=128):
        return cm[0:rows, i, 0:cols]

    def rms_stats(S, pp, sq_chunks, n_feat, out_rstd, tmp, blockmat=None, extra_scale=None):
        pm = pp.get()
        n = len(sq_chunks)
        for i, sq in enumerate(sq_chunks):
            rows = sq.ap.shape[0]
            lhs = (blockmat if blockmat is not None else cmat(CM_ONES, rows, 128))
            mm(pm[:, 0:512], lhs, sq, start=(i == 0), stop=(i == n - 1))
        act(tmp, pm[:, 0:512], AF.Ln, bias=epsb[:, 0:1], scale=1.0 / n_feat)
        act(out_rstd, tmp, AF.Exp, scale=-0.5)
        if extra_scale is not None:
            ts("dve", out_rstd, out_rstd, extra_scale, ALU.mult)

    epsb = Buf(glob.enter_context(nc.sbuf_tensor("epsb", [128, 1], F32)), "epsb")
    mset("dve", epsb[:], EPS)

    def want(s):
        return stages is None or s in stages

    for l in range(depth):
        lam_init = 0.8 - 0.6 * math.exp(-0.3 * l)
        xsrc = None

        def xin_view(t):
            if l == 0:
                return View(xT_in, xT_in.h.rearrange("(kc p) n -> p kc n", p=128)[:, :, t * 512:(t + 1) * 512])
            return xres.v(t, xres.ap.rearrange("(kc p) n -> p kc n", p=128)[:, :, t * 512:(t + 1) * 512])

        if want("P"):
            S = Stage("P%d" % l)
            stg = S.pool(2, [128, 1880], F32, "stg")
            wP = load_w_bf16(S, wP_in, wP_in.h[l], D_MODEL, NPC, stg, "wP")
            wuq = load_w_bf16(S, wuq_in, wuq_in.h[l], 384, 384, stg, "wuq")
            wkK = load_w_bf16(S, wukvK_in, wukvK_in.h[l], 256, 256, stg, "wkK")
            wkV = load_w_bf16(S, wukvV_in, wukvV_in.h[l], 256, 512, stg, "wkV")
            vec = S.sb([128, NV], F32, "vec")
            T.dma(vec[:], vecs_in[l])
            xts = S.pool(2, [128, 8, 512], F32, "xt")
            hbs = S.pool(2, [128, 8, 512], BF16, "hb")
            f32p = S.pool(6, [128, 512], F32, "f")
            bfp = S.pool(6, [128, 512], BF16, "b")
            cqf = S.sb([128, 3, 512], F32, "cqf"); cqn = S.sb([128, 3, 512], BF16, "cqn")
            ckf = S.sb([128, 2, 512], F32, "ckf"); ckn = S.sb([128, 2, 512], BF16, "ckn")
            tabs = S.pool(2, [128, 4, 512], F32, "tab")
            krope = S.sb([128, 512], BF16, "krope")
            pp = S.pspool(6, [128, 512], "pp")

            def load_x(t):
                xt = xts.get()
                T.dma(xt[:], xin_view(t))
                return xt

            nxt = load_x(0)
            for t in range(NT):
                xt = nxt
                tb = tabs.get()
                sl = slice(t * 512, (t + 1) * 512)
                T.dma(tb[64:96, 0, :], cosB_in[:, sl]); T.dma(tb[64:96, 1, :], sinB_in[:, sl])
                T.dma(tb[:, 2, :], cosD_in[:, sl]); T.dma(tb[:, 3, :], sinD_in[:, sl])
                if t + 1 < NT:
                    nxt = load_x(t + 1)
                sqs = []
                pm = pp.get()
                for kc in range(8):
                    sq = f32p.get()
                    if kc % 2 == 0:
                        act(sq[:], xt[:, kc, :], AF.Square)
                    else:
                        tt("pool", sq[:], xt[:, kc, :], xt[:, kc, :], ALU.mult)
                    mm(pm[:], cmat(CM_ONES), sq[:], start=(kc == 0), stop=(kc == 7))
                tmp = f32p.get(); rstd = f32p.get()
                act(tmp[:], pm[:], AF.Ln, bias=epsb[:, 0:1], scale=1.0 / D_MODEL)
                act(rstd[:], tmp[:], AF.Exp, scale=-0.5)
                hb = hbs.get()
                for kc in range(8):
                    stt("dve" if kc % 2 == 0 else "pool", hb[:, kc, :], xt[:, kc, :],
                        vec[:, VC["pre0"] + kc:VC["pre0"] + kc + 1], rstd[:], ALU.mult, ALU.mult)
                T.dma(hT.v(t, hT.ap.rearrange("(kc p) n -> p kc n", p=128)[:, :, sl]), hb[:], own="in")
                if l == 0:
                    T.dma(xres.v(t, xres.ap.rearrange("(kc p) n -> p kc n", p=128)[:, :, sl]), xt[:], own="in")

                def proj(c0, m, n=512):
                    pm_ = pp.get()
                    for kc in range(8):
                        mm(pm_[0:m, 0:n], wP[:, kc, c0:c0 + m], hb[:, kc, 0:n], start=(kc == 0), stop=(kc == 7))
                    return pm_

                for which, c0, dst in (("q", A_Q, QA), ("k", A_K, KA)):
                    for i in range(4):
                        pm_ = proj(c0 + i * 128, 128)
                        ob = bfp.get()
                        cp("act" if i % 2 == 0 else "dve", ob[:], pm_[:])
                        T.dma(dst.v(t, dst.ap[2 * i:2 * i + 2, :, sl].rearrange("a p n -> (a p) n")), ob[:], own="in")
                for tk in range(4):
                    pm_ = pp.get()
                    for kc in range(8):
                        mm(pm_[:], hb[:, kc, tk * 128:(tk + 1) * 128], wP[:, kc, A_V:A_V + 512], start=(kc == 0), stop=(kc == 7))
                    ob = bfp.get()
                    cp("act" if tk % 2 == 0 else "dve", ob[:], pm_[:])
                    r0 = t * 512 + tk * 128
                    T.dma(VA.v(t, VA.ap[:, r0:r0 + 128, :].rearrange("h p d -> p h d")), ob[:].re("p (h d) -> p h d", h=4), own="in")
                sqs = []
                for i in range(3):
                    pm_ = proj(B_CQ + i * 128, 128)
                    cp("act", cqf[:, i, :], pm_[:])
                    sq = f32p.get()
                    tt("dve", sq[:], cqf[:, i, :], cqf[:, i, :], ALU.mult)
                    sqs.append(sq[:])
                rs = f32p.get(); tmp = f32p.get()
                rms_stats(S, pp, sqs, 384, rs[:], tmp[:])
                for i in range(3):
                    stt("dve", cqn[:, i, :], cqf[:, i, :], vec[:, VC["mlaq"] + i:VC["mlaq"] + i + 1], rs[:], ALU.mult, ALU.mult)
                sqs = []
                for i in range(2):
                    pm_ = proj(B_CKV + i * 128, 128)
                    cp("act", ckf[:, i, :], pm_[:])
                    sq = f32p.get()
                    tt("pool", sq[:], ckf[:, i, :], ckf[:, i, :], ALU.mult)
                    sqs.append(sq[:])
                rs = f32p.get(); tmp = f32p.get()
                rms_stats(S, pp, sqs, 256, rs[:], tmp[:])
                for i in range(2):
                    stt("dve", ckn[:, i, :], ckf[:, i, :], vec[:, VC["mlakv"] + i:VC["mlakv"] + i + 1], rs[:], ALU.mult, ALU.mult)

                def rope32(src_ps, dst_bf):
                    xs_ = f32p.get()
                    cp("act", xs_[0:96, :], View(src_ps.b, src_ps.b.h[0:96, :]))
                    pr = pp.get()
                    mm(pr[0:96, :], cm[0:96, CM_R32, 0:96], xs_[0:96, :])
                    t1 = f32p.get(); t2 = f32p.get()
                    tt("dve", t1[64:96, :], xs_[64:96, :], tb[64:96, 0, :], ALU.mult)
                    tt("dve", t2[64:96, :], pr[64:96, :], tb[64:96, 1, :], ALU.mult)
                    tt("pool", dst_bf, t1[64:96, :], t2[64:96, :], ALU.add)

                pm_ = proj(B_KR - 64, 96)
                rope32(pm_[64:96, :], krope[64:96, :])
                for h in range(4):
                    pq = pp.get()
                    for i in range(3):
                        mm(pq[0:96, :], wuq[:, i, h * 96:(h + 1) * 96], cqn[:, i, :], start=(i == 0), stop=(i == 2))
                    ob = bfp.get()
                    rope32(pq[64:96, :], ob[64:96, :])
                    cp("act", ob[0:64, :], pq[0:64, :])
                    T.dma(QB.v(t, QB.ap[h, :, sl]), ob[0:96, :], own="in")
                    pk = pp.get()
                    for i in range(2):
                        mm(pk[0:64, :], wkK[:, i, h * 64:(h + 1) * 64], ckn[:, i, :], start=(i == 0), stop=(i == 1))
                    ob = bfp.get()
                    cp("dve", ob[64:96, :], krope[64:96, :])
                    cp("act", ob[0:64, :], pk[0:64, :])
                    T.dma(KB.v(t, KB.ap[h, :, sl]), ob[0:96, :], own="in")
                for tk in range(4):
                    pm_ = pp.get()
                    for i in range(2):
                        mm(pm_[:], ckn[:, i, tk * 128:(tk + 1) * 128], wkV[:, i, :], start=(i == 0), stop=(i == 1))
                    ob = bfp.get()
                    cp("act" if tk % 2 == 0 else "dve", ob[:], pm_[:])
                    r0 = t * 512 + tk * 128
                    T.dma(VB.v(t, VB.ap[:, r0:r0 + 128, :].rearrange("h p d -> p h d")), ob[:].re("p (h d) -> p h d", h=4), own="in")
                for i in range(5):
                    isq = i < 4
                    pm_ = proj((PC_DQ + i * 128) if isq else PC_DK, 128)
                    xf = f32p.get()
                    cp("act", xf[:], pm_[:])
                    sq = f32p.get()
                    tt("pool", sq[:], xf[:], xf[:], ALU.mult)
                    rs = f32p.get(); tmp = f32p.get()
                    rms_stats(S, pp, [sq[:]], 64, rs[:], tmp[:], blockmat=cmat(CM_BD64))
                    gcol = VC["gq"] if isq else VC["gk"]
                    xn = f32p.get()
                    stt("dve", xn[:], xf[:], vec[:, gcol:gcol + 1], rs[:], ALU.mult, ALU.mult)
                    pr = pp.get()
                    mm(pr[:], cmat(CM_RD), xn[:])
                    t1 = f32p.get(); t2 = f32p.get()
                    tt("dve", t1[:], xn[:], tb[:, 2, :], ALU.mult)
                    tt("dve", t2[:], pr[:], tb[:, 3, :], ALU.mult)
                    ob = bfp.get()
                    tt("pool", ob[:], t1[:], t2[:], ALU.add)
                    dst = QD if isq else KD
                    a0 = 2 * i if isq else 0
                    T.dma(dst.v(t, dst.ap[a0:a0 + 2, :, sl].rearrange("a p n -> (a p) n")), ob[:], own="in")
                for tk in range(4):
                    pm_ = pp.get()
                    for kc in range(8):
                        mm(pm_[:, 0:128], hb[:, kc, tk * 128:(tk + 1) * 128], wP[:, kc, PC_DV:PC_DV + 128], start=(kc == 0), stop=(kc == 7))
                    ob = bfp.get()
                    cp("act" if tk % 2 == 0 else "dve", ob[:, 0:128], pm_[:, 0:128])
                    r0 = t * 512 + tk * 128
                    T.dma(VD.v(t, VD.ap[:, r0:r0 + 128, :].rearrange("h p d -> p h d")), ob[:, 0:128].re("p (h d) -> p h d", h=2), own="in")
                for i in range(6):
                    pm_ = proj(PC_XBC + i * 128, 128)
                    of = f32p.get()
                    cp("act" if i % 2 == 0 else "dve", of[:], pm_[:])
                    T.dma(XBC.v(t, XBC.ap[i * 128:(i + 1) * 128, sl]), of[:], own="in")
                pm_ = proj(PC_DT, 16)
                of = f32p.get()
                cp("act", of[0:16, :], pm_[0:16, :])
                T.dma(DTR.v(t, DTR.ap[:, sl]), of[0:16, :], own="in")
            S.close()

        if want("ATT"):
            S = Stage("AT%d" % l)
            Ktr = S.pool(2, [128, L], BF16, "Kt")
            Vtr = S.pool(2, [128, NCH, 128], BF16, "Vt")
            Qp = S.pool(6, [128, 512], BF16, "Q")
            PTp = S.pool(3, [128, 1024], BF16, "PT")
            accDp = S.pool(2, [128, 512], F32, "accD")
            accPp = S.pool(2, [128, 512], F32, "accP")
            accD2p = S.pool(2, [128, 512], F32, "accD2")
            tmpSp = S.pool(2, [128, 1024], F32, "tmpS")
            bd = S.sb([128, 2048], F32, "bd")
            osbp = S.pool(2, [128, 512], F32, "osb")
            obfp = S.pool(2, [128, 512], BF16, "obf")
            rdp = S.pool(2, [128, 512], F32, "rd")
            psS = S.pspool(2, [128, 1024], "psS")
            psO = S.pspool(2, [128, 512], "psO")
            psX = S.pspool(2, [128, 512], "psX")
            onesb = S.sb([128, 128], BF16, "onesb")
            cp("dve", onesb[:], cmat(CM_ONES))

            class Cache:
                def __init__(self, ring):
                    self.ring = ring
                    self.map = {}

                def get(self, key, loader):
                    if key in self.map:
                        return self.map[key]
                    b = self.ring.get()
                    for k_ in [k_ for k_, v_ in self.map.items() if v_ is b]:
                        del self.map[k_]
                    loader(b)
                    self.map[key] = b
                    return b

            kc_ = Cache(Ktr)
            vc_ = Cache(Vtr)
            bd_state = {"h": None}

            def all_tiles(dt_):
                return [dt_.t(i) for i in range(NT)]

            def dep_views(dt_):
                return [View(b_, dt_.ap) for b_ in all_tiles(dt_)]

            units = []
            for h in range(4):
                for m in range(2):
                    units.append(("A", h, m))
            for h in range(4):
                units.append(("B", h, 0))
            for g in range(2):
                for r_ in range(4):
                    units.append(("D", g, r_))

            def load_kv(u):
                kind, a, b2 = u
                if kind == "A":
                    hm = 2 * a + b2

                    def lk(kt):
                        T.dma(kt[0:64, :], View(KA.t(0), KA.ap[hm]), extra_reads=dep_views(KA)[1:])
                        T.dma(kt[64:69, :], kaugA_in[a])
                    def lv(vt):
                        for c0 in range(0, NCH, 16):
                            c1 = min(c0 + 16, NCH)
                            T.dma(vt[:, c0:c1, :], View(VA.t(0), VA.ap[a].rearrange("(c p) d -> p c d", p=128)[:, c0:c1, :]), extra_reads=dep_views(VA)[1:])
                    return kc_.get(("A", hm), lk), vc_.get(("A", a), lv)
                if kind == "B":
                    def lk(kt):
                        T.dma(kt[0:96, :], View(KB.t(0), KB.ap[a]), extra_reads=dep_views(KB)[1:])
                        T.dma(kt[96:97, :], kmask_in[:, :])
                    def lv(vt):
                        for c0 in range(0, NCH, 16):
                            c1 = min(c0 + 16, NCH)
                            T.dma(vt[:, c0:c1, :], View(VB.t(0), VB.ap[a].rearrange("(c p) d -> p c d", p=128)[:, c0:c1, :]), extra_reads=dep_views(VB)[1:])
                    return kc_.get(("B", a), lk), vc_.get(("B", a), lv)

                def lk(kt):
                    T.dma(kt[0:64, :], View(KD.t(0), KD.ap[a]), extra_reads=dep_views(KD)[1:])
                    T.dma(kt[64:65, :], kmask_in[:, :])
                def lv(vt):
                    for c0 in range(0, NCH, 16):
                        c1 = min(c0 + 16, NCH)
                        T.dma(vt[:, c0:c1, 0:64], View(VD.t(0), VD.ap[a].rearrange("(c p) d -> p c d", p=128)[:, c0:c1, :]), extra_reads=dep_views(VD)[1:])
                    mset("pool", vt[:, :, 64:65], 1.0)
                return kc_.get(("D", a), lk), vc_.get(("D", a), lv)

            def load_q(u, qt):
                kind, a, b2 = u
                sl = slice(qt * 512, (qt + 1) * 512)
                if kind == "A":
                    hm = 2 * a + b2
                    qs = []
                    for ver in range(3):
                        q = Qp.get()
                        T.dma(q[0:64, :], QA.v(qt, QA.ap[hm, :, sl]))
                        T.dma(q[64:69, :], qaugA_in[ver, :, sl])
                        qs.append(q[0:69, :])
                    return qs
                q = Qp.get()
                if kind == "B":
                    T.dma(q[0:96, :], QB.v(qt, QB.ap[a, :, sl]))
                    T.dma(q[96:97, :], onesrow_in[:, sl])
                    return [q[0:97, :]]
                hq = 4 * a + b2
                T.dma(q[0:64, :], QD.v(qt, QD.ap[hq, :, sl]))
                T.dma(q[64:65, :], onesrow_in[:, sl])
                return [q[0:65, :]]

            def pairs_for(u, qt):
                kind, a, _ = u
                out_ = []
                for j in range(NCH // 2):
                    c0, c1 = 2 * j, 2 * j + 1
                    if kind != "A":
                        out_.append((j, 0, None))
                        continue
                    m_ = 2.0 ** (-2.0 * (a + 1))
                    if c1 < 4 * qt:
                        mind = 512 * qt - (128 * c1 + 127)
                        if mind * m_ >= cut_scale:
                            continue
                        out_.append((j, 0, None))
                    elif c0 > 4 * qt + 3:
                        mind = 128 * c0 - (512 * qt + 511)
                        if mind * m_ >= cut_scale:
                            continue
                        out_.append((j, 1, None))
                    else:
                        out_.append((j, 2, j - 2 * qt))
                return out_

            def compute(u, kt, vt):
                kind, a, b2 = u
                DK = {"A": 69, "B": 97, "D": 65}[kind]
                DVa = {"A": 128, "B": 128, "D": 65}[kind]
                scale = {"A": 0.125, "B": 96 ** -0.5, "D": 0.125}[kind]
                if kind == "A" and bd_state["h"] != a:
                    T.dma(bd[:, :], bdiag_in[a])
                    bd_state["h"] = a
                nq = load_q(u, 0)
                for qt in range(NT):
                    qs = nq
                    if qt + 1 < NT:
                        nq = load_q(u, qt + 1)
                    sl = slice(qt * 512, (qt + 1) * 512)
                    prs = pairs_for(u, qt)
                    po = psO.get()
                    px = psX.get()
                    if kind != "D":
                        accD = accDp.get(); accP = accPp.get(); accD2 = accD2p.get()
                        mset("dve", accD[:], 0.0)
                        mset("dve", accD2[:], 0.0)
                        mset("pool", accP[:], 0.0)
                    n = len(prs)

                    def emit_S(j, ver):
                        ps = psS.get()
                        q = qs[ver] if kind == "A" else qs[0]
                        mm(ps[:, 0:512], kt[0:DK, (2 * j) * 128:(2 * j + 1) * 128], q)
                        mm(ps[:, 512:1024], kt[0:DK, (2 * j + 1) * 128:(2 * j + 2) * 128], q)
                        return ps

                    def emit_rest(ps, j, ver, dp, idx):
                        src = ps
                        if ver == 2:
                            tmp = tmpSp.get()
                            tt("dve", tmp[:, 0:512], ps[:, 0:512], bd[:, dp * 1024:dp * 1024 + 512], ALU.add)
                            tt("dve", tmp[:, 512:1024], ps[:, 512:1024], bd[:, dp * 1024 + 512:(dp + 1) * 1024], ALU.add)
                            src = tmp
                        pt = PTp.get()
                        act(pt[:], src[:], AF.Exp, scale=scale)
                        mm(po[0:DVa, :], vt[:, 2 * j, 0:DVa], pt[:, 0:512], start=(idx == 0), stop=False)
                        mm(po[0:DVa, :], vt[:, 2 * j + 1, 0:DVa], pt[:, 512:1024], start=False, stop=(idx == n - 1))
                        if kind != "D":
                            if idx % 2 == 0:
                                mm(px[:], onesb[:], pt[:, 0:512], start=(idx == 0), stop=False)
                                tt("dve", accD2[:], accD2[:], pt[:, 512:1024], ALU.add)
                            else:
                                tt("dve", accD[:], accD[:], pt[:, 0:512], ALU.add)
                                tt("pool", accP[:], accP[:], pt[:, 512:1024], ALU.add)

                    prev = None
                    for idx, (j, ver, dp) in enumerate(prs):
                        ps = emit_S(j, ver)
                        if prev is not None:
                            emit_rest(*prev)
                        prev = (ps, j, ver, dp, idx)
                    emit_rest(*prev)
                    rd = rdp.get()
                    if kind != "D":
                        mm(px[:], cmat(CM_ONES), accD[:], start=False, stop=False)
                        mm(px[:], cmat(CM_ONES), accD2[:], start=False, stop=False)
                        mm(px[:], cmat(CM_ONES), accP[:], start=False, stop=True)
                        ts("dve", rd[:], px[:], 1e-30, ALU.max)
                        recip(rd[:], rd[:])
                        if kind == "A":
                            o = osbp.get()
                            tt("dve", o[:], po[:], rd[:], ALU.mult)
                            T.dma(OA.v(qt, OA.ap[2 * a + b2, :, sl]), o[:], own="in")
                        else:
                            o = obfp.get()
                            tt("dve", o[:], po[:], rd[:], ALU.mult)
                            T.dma(OB.v(qt, OB.ap[a * 128:(a + 1) * 128, sl]), o[:], own="in")
                    else:
                        o65 = osbp.get()
                        cp("act", o65[0:65, :], po[0:65, :])
                        mm(px[0:64, :], cm[0:65, CM_SEL64, 0:64], o65[0:65, :])
                        ts("dve", rd[0:64, :], px[0:64, :], 1e-30, ALU.max)
                        recip(rd[0:64, :], rd[0:64, :])
                        o = obfp.get()
                        tt("dve", o[0:64, :], o65[0:64, :], rd[0:64, :], ALU.mult)
                        hq = 4 * a + b2
                        T.dma(OD.v(qt, OD.ap[hq * 64:(hq + 1) * 64, sl]), o[0:64, :], own="in")

            import os as _os
            _kinds = _os.environ.get("ATT_KINDS", "ABD")
            units = [u for u in units if u[0] in _kinds]
            cur = load_kv(units[0])
            for ui, u in enumerate(units):
                kt, vt = cur
                if ui + 1 < len(units):
                    cur = load_kv(units[ui + 1])
                compute(u, kt, vt)
            S.close()

        if want("SSD"):
            S = Stage("SD%d" % l)
            vec = S.sb([128, NV], F32, "vec")
            T.dma(vec[:], vecs_in[l])
            rws = S.sb([128, 272], F32, "rows")
            T.dma(rws[:], rows_in[l])
            aneg = S.sb([128, 16], F32, "aneg")
            act(aneg[:], rws[:, 256:272], AF.Exp)
            ts("dve", aneg[:], aneg[:], -1.0, ALU.mult)
            xins = S.pool(2, [128, 6, 514], F32, "xin")
            xps = S.pool(2, [128, 6, 512], F32, "xp")
            f32p = S.pool(4, [128, 512], F32, "f")
            dts = S.pool(2, [16, 512], F32, "dt")
            tms = S.pool(2, [16, 512], F32, "tm")
            XBCv = XBC.ap.rearrange("(c p) n -> p c n", p=128)
            XBCPv = XBCP.ap.rearrange("(c p) n -> p c n", p=128)
            for t in range(NT):
                sl = slice(t * 512, (t + 1) * 512)
                xin = xins.get()
                T.dma(xin[:, :, 1:513], XBC.v(t, XBCv[:, :, sl]))
                if t > 0:
                    T.dma(xin[:, :, 0:1], XBC.v(t - 1, XBCv[:, :, t * 512 - 1:t * 512]), allow_slow_non_contiguous=True)
                else:
                    mset("pool", xin[:, :, 0:1], 0.0)
                if t < NT - 1:
                    T.dma(xin[:, :, 513:514], XBC.v(t + 1, XBCv[:, :, (t + 1) * 512:(t + 1) * 512 + 1]), allow_slow_non_contiguous=True)
                else:
                    mset("pool", xin[:, :, 513:514], 0.0)
                xp = xps.get()
                for c in range(6):
                    a0 = f32p.get(); a1 = f32p.get()
                    ts("dve" if c % 2 == 0 else "pool", a0[:], xin[:, c, 0:512], vec[:, VC["cw0"] + c:VC["cw0"] + c + 1], ALU.mult,
                       vec[:, VC["cb"] + c:VC["cb"] + c + 1], ALU.add)
                    stt("dve", a1[:], xin[:, c, 1:513], vec[:, VC["cw1"] + c:VC["cw1"] + c + 1], a0[:], ALU.mult, ALU.add)
                    stt("dve", a0[:], xin[:, c, 2:514], vec[:, VC["cw2"] + c:VC["cw2"] + c + 1], a1[:], ALU.mult, ALU.add)
                    act(xp[:, c, :], a0[:], AF.Silu)
                T.dma(XBCP.v(t, XBCPv[:, :, sl]), xp[:], own="in")
                dtt = dts.get(); tm = tms.get()
                T.dma(dtt[:], DTR.v(t, DTR.ap[:, sl]))
                T.dma(tm[:], tmask_in[0:16, sl])
                e1 = f32p.get()
                act(e1[0:16, :], dtt[:], AF.Exp, bias=vec[0:16, VC["dtb"]:VC["dtb"] + 1])
                act(e1[0:16, :], e1[0:16, :], AF.Ln, bias=1.0)
                tt("dve", dtt[:], e1[0:16, :], tm[:], ALU.mult)
                T.dma(DTP.v(t, DTP.ap[:, sl]), dtt[:], own="in")

            import os as _os
            _ndir = int(_os.environ.get("SSD_NDIR", "2"))
            _cut = int(_os.environ.get("SSD_CUT", "99"))
            st32 = S.sb([128, 4, 64], F32, "st32")
            stbf = S.sb([128, 4, 64], BF16, "stbf")
            xtoks = S.pool(2, [128, 512], F32, "xtok")
            xdts = S.pool(2, [128, 512], BF16, "xdt")
            smalls = S.pool(2, [128, 512], F32, "small")
            csrows = S.pool(2, [8, 128], F32, "csrow")
            bbfs = S.pool(2, [128, 128], BF16, "bbf")
            cbfz = [S.pool(2, [128, 128], BF16, "cbfz%d" % g_) for g_ in range(2)]
            cdecz = [S.pool(3, [128, 128], BF16, "cdecz%d" % g_) for g_ in range(2)]
            for g_ in range(2):
                for b_ in cbfz[g_].items + cdecz[g_].items:
                    mset("pool", b_[:], 0.0)
            cbsbs = S.pool(2, [128, 256], F32, "cbsb")
            args = S.pool(3, [128, 128], F32, "arg")
            decs = S.pool(3, [128, 128], F32, "dec")
            mts = S.pool(3, [128, 128], BF16, "mt")
            ecss = S.pool(3, [128, 128], F32, "ecs")
            bws = S.pool(3, [128, 128], BF16, "bw")
            yfs = S.pool(2, [128, 4, 512], F32, "yf")
            yos = S.pool(2, [128, 4, 512], F32, "yo")
            dtps = S.pool(2, [16, 512], F32, "dtp")
            pT = S.ps([128, 512], F32, "pT")
            pmisc = S.pspool(2, [128, 512], "pmisc")
            pcb = S.ps([128, 512], F32, "pcb")
            pbcs = S.pspool(2, [128, 512], "pbc")
            pys = S.pspool(2, [128, 512], "py")
            YFv = YF.ap.rearrange("(c p) n -> p c n", p=128)
            YSv = YS.ap.rearrange("(c p) n -> p c n", p=128)
            for d in range(_ndir):
                mset("dve", st32[:], 0.0)
                mset("pool", stbf[:], 0.0)
                tri = CM_U if d == 0 else CM_LO
                neg = CM_NEGF if d == 0 else CM_NEGB
                sel = CM_SEL127 if d == 0 else CM_SEL0
                torder = list(range(NT)) if d == 0 else list(range(NT - 1, -1, -1))
                corder = list(range(4)) if d == 0 else [3, 2, 1, 0]

                def load_tile(t):
                    sl_ = slice(t * 512, (t + 1) * 512)
                    xp_ = xps.get(); dtp_ = dtps.get()
                    T.dma(xp_[:], XBCP.v(t, XBCPv[:, :, sl_]))
                    T.dma(dtp_[:], DTP.v(t, DTP.ap[:, sl_]))
                    yf_ = None
                    if d == 1:
                        yf_ = yfs.get()
                        T.dma(yf_[:], YF.v(t, YFv[:, :, sl_]))
                    return xp_, dtp_, yf_

                items = [(ti, t, c) for ti, t in enumerate(torder) for c in corder]
                tiles = {}

                def get_tile(ti):
                    if ti not in tiles:
                        tiles[ti] = load_tile(torder[ti]) + (yos.get(),)
                    return tiles[ti]

                def preamble(item):
                    ti, t, c = item
                    xp, dtp, yf, yo = get_tile(ti)
                    if c == corder[1] and ti + 1 < NT:
                        get_tile(ti + 1)
                    cs = slice(c * 128, (c + 1) * 128)
                    for i in range(4):
                        trp(pT[:, i * 128:(i + 1) * 128], xp[:, i, cs], ident)
                    xtok = xtoks.get()
                    cp("act", xtok[:], pT[:])
                    pm = pmisc.get()
                    sm = smalls.get()
                    trp(pm[:, 0:128], xp[:, 4, cs], ident)
                    trp(pm[:, 128:144], dtp[0:16, cs], cm[0:16, CM_ID, 0:16])
                    cp("dve", sm[:, 0:144], pm[:, 0:144])
                    btok = sm[:, 0:128]
                    dttok = sm[:, 128 + d * 8:128 + d * 8 + 8]
                    dA = sm[:, 144:152]
                    tt("dve", dA, dttok, aneg[:, d * 8:d * 8 + 8], ALU.mult)
                    pm2 = pmisc.get()
                    mm(pm2[:, 0:8], cmat(tri), dA)
                    cscol = sm[:, 152:160]
                    cp("dve", cscol, pm2[:, 0:8])
                    mm(pm2[:, 8:16], cmat(sel), cscol)
                    trp(pm2[0:8, 128:256], cscol, ident)
                    csrow = csrows.get()
                    cp("act", csrow[:], pm2[0:8, 128:256])
                    cslast = sm[:, 160:168]
                    cp("dve", cslast, pm2[:, 8:16])
                    warg = sm[:, 168:176]
                    tt("dve", warg, cslast, cscol, ALU.subtract)
                    wall = sm[:, 176:184]
                    act(wall, warg, AF.Exp)
                    cdall = sm[:, 184:192]
                    act(cdall, cslast, AF.Exp)
                    bbf = bbfs.get()
                    cp("pool", bbf[:], xp[:, 4, cs])
                    for g in range(2):
                        gs = slice(g * 64, (g + 1) * 64)
                        cbf = cbfz[g].get()
                        cp("pool", cbf[gs, :], xp[gs, 5, cs])
                        mm(pcb[:, g * 128:(g + 1) * 128], bbf[:, :], cbf[:, :])
                    cbsb = cbsbs.get()
                    cp("act", cbsb[:], pcb[:, 0:256])
                    xdt = xdts.get()
                    for h in range(8):
                        if h % 2 == 0:
                            act(xdt[:, h * 64:(h + 1) * 64], xtok[:, h * 64:(h + 1) * 64], AF.Copy, scale=dttok[:, h:h + 1])
                        else:
                            ts("dve", xdt[:, h * 64:(h + 1) * 64], xtok[:, h * 64:(h + 1) * 64], dttok[:, h:h + 1], ALU.mult)
                    return dict(xp=xp, yf=yf, yo=yo, cs=cs, btok=btok, cscol=cscol, csrow=csrow, wall=wall,
                                cdall=cdall, cbsb=cbsb, xdt=xdt)

                def heads(cx):
                    xp, yf, yo, cs = cx["xp"], cx["yf"], cx["yo"], cx["cs"]
                    btok, cscol, csrow, wall, cdall, cbsb, xdt = (cx[k_] for k_ in ("btok", "cscol", "csrow", "wall", "cdall", "cbsb", "xdt"))
                    for h in range(8):
                        g = h // 4
                        gs = slice(g * 64, (g + 1) * 64)
                        hs = slice((h % 2) * 64, (h % 2) * 64 + 64)
                        pr = (h % 4) // 2
                        pbc = pbcs.get()
                        mm(pbc[:, 0:128], cm[0:8, CM_SELH + h, :], csrow[:])
                        arg = args.get()
                        stt("dve", arg[:], pbc[:, 0:128], cscol[:, h:h + 1], cmat(neg), ALU.subtract, ALU.add)
                        dec = decs.get()
                        act(dec[:], arg[:], AF.Exp)
                        mt = mts.get()
                        tt("dve", mt[:], cbsb[:, g * 128:(g + 1) * 128], dec[:], ALU.mult)
                        ecs = ecss.get()
                        act(ecs[gs, :], pbc[gs, 0:128], AF.Exp)
                        cdec = cdecz[g].get()
                        tt("pool", cdec[gs, :], xp[gs, 5, cs], ecs[gs, :], ALU.mult)
                        py = pys.get()
                        pair_cols = slice((h // 2) * 128, (h // 2) * 128 + 128)
                        mm(py[:, 0:128], xdt[:, pair_cols], mt[:], start=True, stop=False)
                        mm(py[:, 0:128], stbf[:, pr * 2:pr * 2 + 2, :].re("p a b -> p (a b)"), cdec[:, :], start=False, stop=True)
                        bw = bws.get()
                        act(bw[:], btok, AF.Copy, scale=wall[:, h:h + 1])
                        mm(py[:, 128:192], bw[:], xdt[:, h * 64:(h + 1) * 64])
                        if d == 0:
                            stt("dve", yo[hs, h // 2, cs], xp[hs, h // 2, cs], vec[hs, VC["ssmd"] + h // 2:VC["ssmd"] + h // 2 + 1], py[hs, 0:128], ALU.mult, ALU.add)
                        else:
                            tt("dve", yo[hs, h // 2, cs], py[hs, 0:128], yf[hs, h // 2, cs], ALU.add)
                        stt("dve", st32[gs, h % 4, :], st32[gs, h % 4, :], cdall[gs, h:h + 1], py[gs, 128:192], ALU.mult, ALU.add)
                    cp("act", stbf[:], st32[:])

                cxn = preamble(items[0])
                for ii, item in enumerate(items):
                    cx = cxn
                    if ii + 1 < len(items):
                        cxn = preamble(items[ii + 1])
                    heads(cx)
                    ti, t, c = item
                    if c == corder[-1]:
                        sl = slice(t * 512, (t + 1) * 512)
                        if d == 0:
                            T.dma(YF.v(t, YFv[:, :, sl]), cx["yo"][:], own="in")
                        else:
                            T.dma(YS.v(t, YSv[:, :, sl]), cx["yo"][:], own="in")
            S.close()

        TM = 256
        NTM = L // TM
        if want("MERGE"):
            S = Stage("MG%d" % l)
            wG = alloc_w(S, D_MODEL, 4608, "wG"); wbr = alloc_w(S, 2048, D_MODEL, "wbr"); wo = alloc_w(S, D_MODEL, D_MODEL, "wo")
            vec = S.sb([128, NV], F32, "vec")
            rws = S.sb([128, 272], F32, "rows")
            lam = S.sb([128, 8], F32, "lam")
            ltmp = S.sb([128, 128], F32, "ltmp")
            W = Stage("MGw%d" % l)
            stg = W.pool(2, [128, 2048], F32, "stg")
            load_w_bf16(S, wG_in, wG_in.h[l], D_MODEL, 4608, stg, "wG", w=wG)
            load_w_bf16(S, wbr_in, wbr_in.h[l], 2048, D_MODEL, stg, "wbr", w=wbr)
            load_w_bf16(S, wout_in, wout_in.h[l], D_MODEL, D_MODEL, stg, "wo", w=wo)
            W.close()
            T.dma(vec[:], vecs_in[l])
            T.dma(rws[:], rows_in[l])
            tt("dve", ltmp[:, 0:64], rws[:, 0:64], rws[:, 64:128], ALU.mult)
            tt("dve", ltmp[:, 64:128], rws[:, 128:192], rws[:, 192:256], ALU.mult)
            T.op("dve", lambda: V.reduce_sum(out=lam.h[:, 0:1], in_=ltmp.h[:, 0:64], axis=mybir.AxisListType.X), reads=[ltmp[:]], writes=[lam[:]])
            T.op("dve", lambda: V.reduce_sum(out=lam.h[:, 1:2], in_=ltmp.h[:, 64:128], axis=mybir.AxisListType.X), reads=[ltmp[:]], writes=[lam[:]])
            act(lam[:, 2:4], lam[:, 0:2], AF.Exp)
            tt("dve", lam[:, 4:5], lam[:, 3:4], lam[:, 2:3], ALU.subtract)
            ts("dve", lam[:, 5:6], lam[:, 4:5], -lam_init, ALU.add)
            neglam = lam[:, 5:6]
            hb = S.sb([128, 8, TM], BF16, "hb"); xt = S.sb([128, 8, TM], F32, "xt")
            oa = S.sb([128, 8, TM], F32, "oa"); ob = S.sb([128, 4, TM], BF16, "ob"); od = S.sb([128, 4, TM], BF16, "od")
            ys = S.sb([128, 4, TM], F32, "ys"); tm = S.sb([128, TM], F32, "tm")
            ya = S.sb([128, 4, TM], BF16, "ya"); yc = S.sb([128, 4, TM], BF16, "yc"); yz = S.sb([128, 4, TM], F32, "yz")
            mrg = S.sb([128, 8, TM], BF16, "mrg"); outf = S.sb([128, 8, TM], F32, "outf")
            f32p = S.pool(6, [128, TM], F32, "f")
            sgp = S.pool(3, [128, TM], F32, "sg")
            rsb = S.sb([128, TM], F32, "rsb")
            pp = S.pspool(6, [128, 512], "pp")
            pst = S.pspool(2, [128, 512], "pst")
            kcv = lambda a: a.rearrange("(kc p) n -> p kc n", p=128)
            for i in range(NTM):
                sl = slice(i * TM, (i + 1) * TM)
                k_ = ("m", i)
                T.dma(hb[:], hT.v(k_, kcv(hT.ap)[:, :, sl]))
                T.dma(xt[:], xres.v(k_, kcv(xres.ap)[:, :, sl]))
                T.dma(oa[:], OA.v(k_, OA.ap[:, :, sl].rearrange("a p n -> p a n")))
                T.dma(ob[:], OB.v(k_, kcv(OB.ap)[:, :, sl]))
                T.dma(od[:], OD.v(k_, kcv(OD.ap)[:, :, sl]))
                T.dma(ys[:], YS.v(k_, kcv(YS.ap)[:, :, sl]))
                T.dma(tm[:], tmask_in[:, sl])
                for h in range(4):
                    o = f32p.get(); sq = f32p.get(); rs = f32p.get(); tmp = f32p.get()
                    stt("dve", o[:], oa[:, 2 * h + 1, :], neglam, oa[:, 2 * h, :], ALU.mult, ALU.add)
                    tt("pool", sq[:], o[:], o[:], ALU.mult)
                    pm = pp.get()
                    mm(pm[:, 0:TM], cmat(CM_ONES), sq[:])
                    act(tmp[:], pm[:, 0:TM], AF.Ln, bias=epsb[:, 0:1], scale=1.0 / 128)
                    act(rs[:], tmp[:], AF.Exp, scale=-0.5)
                    ts("dve", rs[:], rs[:], 1.0 - lam_init, ALU.mult)
                    stt("dve", ya[:, h, :], o[:], vec[:, VC["diffn"]:VC["diffn"] + 1], rs[:], ALU.mult, ALU.mult)
                pm = pst.get()
                for zc in range(4):
                    pz = pp.get()
                    for kc in range(8):
                        mm(pz[:, 0:TM], wG[:, kc, 4096 + zc * 128:4096 + (zc + 1) * 128], hb[:, kc, :], start=(kc == 0), stop=(kc == 7))
                    zs = f32p.get()
                    act(zs[:], pz[:, 0:TM], AF.Silu)
                    tt("dve", yz[:, zc, :], zs[:], ys[:, zc, :], ALU.mult)
                    sq = f32p.get()
                    tt("pool", sq[:], yz[:, zc, :], yz[:, zc, :], ALU.mult)
                    mm(pm[:, 0:TM], cmat(CM_ONES), sq[:], start=(zc == 0), stop=(zc == 3))
                tmp = f32p.get(); rs = f32p.get()
                act(tmp[:], pm[:, 0:TM], AF.Ln, bias=epsb[:, 0:1], scale=1.0 / 512)
                act(rs[:], tmp[:], AF.Exp, scale=-0.5)
                for zc in range(4):
                    stt("dve", yc[:, zc, :], yz[:, zc, :], vec[:, VC["ssmn"] + zc:VC["ssmn"] + zc + 1], rs[:], ALU.mult, ALU.mult)
                Ys = [ya, ob, yc, od]
                for oc in range(8):
                    macc = f32p.get()
                    for n_ in range(4):
                        pg = pp.get()
                        for kc in range(8):
                            mm(pg[:, 0:TM], wG[:, kc, n_ * 1024 + oc * 128:n_ * 1024 + (oc + 1) * 128], hb[:, kc, :], start=(kc == 0), stop=(kc == 7))
                        sg = sgp.get()
                        act(sg[:], pg[:, 0:TM], AF.Sigmoid)
                        pb = pp.get()
                        for kc in range(4):
                            mm(pb[:, 0:TM], wbr[:, n_ * 4 + kc, oc * 128:(oc + 1) * 128], Ys[n_][:, kc, :], start=(kc == 0), stop=(kc == 3))
                        if n_ == 0:
                            tt("dve", macc[:], pb[:, 0:TM], sg[:], ALU.mult)
                        else:
                            t2 = f32p.get()
                            tt("dve", t2[:], pb[:, 0:TM], sg[:], ALU.mult)
                            if n_ < 3:
                                tt("pool", macc[:], macc[:], t2[:], ALU.add)
                            else:
                                tt("pool", mrg[:, oc, :], macc[:], t2[:], ALU.add)
                pm = pst.get()
                for oc in range(8):
                    po = pp.get()
                    for kc in range(8):
                        mm(po[:, 0:TM], wo[:, kc, oc * 128:(oc + 1) * 128], mrg[:, kc, :], start=(kc == 0), stop=(kc == 7))
                    cp("act", outf[:, oc, :], po[:, 0:TM])
                    sq = f32p.get()
                    tt("pool", sq[:], outf[:, oc, :], outf[:, oc, :], ALU.mult)
                    mm(pm[:, 0:TM], cmat(CM_ONES), sq[:], start=(oc == 0), stop=(oc == 7))
                tmp = f32p.get(); rs = rsb
                act(tmp[:], pm[:, 0:TM], AF.Ln, bias=epsb[:, 0:1], scale=1.0 / D_MODEL)
                act(rs[:], tmp[:], AF.Exp, scale=-0.5)
                tt("dve", rs[:], rs[:], tm[:], ALU.mult)
                for oc in range(8):
                    t2 = f32p.get()
                    stt("dve", t2[:], outf[:, oc, :], vec[:, VC["post0"] + oc:VC["post0"] + oc + 1], rs[:], ALU.mult, ALU.mult)
                    tt("pool", xt[:, oc, :], xt[:, oc, :], t2[:], ALU.add)
                T.dma(xres.v(k_, kcv(xres.ap)[:, :, sl]), xt[:], own="in")
            S.close()

        if want("MEM"):
            S = Stage("MM%d" % l)
            wq = alloc_w(S, D_MODEL, 512, "wq"); wmo = alloc_w(S, 512, D_MODEL, "wmo")
            vec = S.sb([128, NV], F32, "vec")
            onesb = S.sb([128, 128], BF16, "onesb")
            Kmem = S.sb([128, 4, 256], BF16, "Kmem"); Vmem = S.sb([128, 2, 512], BF16, "Vmem")
            pp = S.pspool(3, [128, 512], "pp")
            pst = S.pspool(1, [128, 512], "pst")
            pS2 = S.pspool(2, [128, 1024], "pS2")
            f32p = S.pool(6, [128, 512], F32, "f")
            W = Stage("MMw%d" % l)
            stg = W.pool(2, [128, 2048], F32, "stg")
            load_w_bf16(S, wmq_in, wmq_in.h[l], D_MODEL, 512, stg, "wq", w=wq)
            load_w_bf16(S, wmo_in, wmo_in.h[l], 512, D_MODEL, stg, "wmo", w=wmo)
            wk = load_w_bf16(W, wmk_in, wmk_in.h[l], D_MODEL, 512, stg, "wk")
            wv = load_w_bf16(W, wmv_in, wmv_in.h[l], D_MODEL, 512, stg, "wv")
            T.dma(vec[:], vecs_in[l])
            cp("dve", onesb[:], cmat(CM_ONES))
            memf = W.sb([128, 8, 256], F32, "memf"); mn = W.sb([128, 8, 256], BF16, "mn")
            T.dma(memf[:], View(memT_in, memT_in.h.rearrange("(kc p) n -> p kc n", p=128)))
            pm = pst.get()
            for kc in range(8):
                sq = f32p.get()
                tt("dve", sq[:, 0:256], memf[:, kc, :], memf[:, kc, :], ALU.mult)
                mm(pm[:, 0:256], cmat(CM_ONES), sq[:, 0:256], start=(kc == 0), stop=(kc == 7))
            tmp = f32p.get(); rs = f32p.get()
            act(tmp[:, 0:256], pm[:, 0:256], AF.Ln, bias=epsb[:, 0:1], scale=1.0 / D_MODEL)
            act(rs[:, 0:256], tmp[:, 0:256], AF.Exp, scale=-0.5)
            for kc in range(8):
                stt("dve", mn[:, kc, :], memf[:, kc, :], vec[:, VC["memn"] + kc:VC["memn"] + kc + 1], rs[:, 0:256], ALU.mult, ALU.mult)
            for h in range(4):
                pk = pp.get()
                for kc in range(8):
                    mm(pk[:, 0:256], wk[:, kc, h * 128:(h + 1) * 128], mn[:, kc, :], start=(kc == 0), stop=(kc == 7))
                cp("act", Kmem[:, h, :], pk[:, 0:256])
            for mb in range(2):
                pv = pp.get()
                for kc in range(8):
                    mm(pv[:, :], mn[:, kc, mb * 128:(mb + 1) * 128], wv[:, kc, :], start=(kc == 0), stop=(kc == 7))
                cp("act", Vmem[:, mb, :], pv[:, :])
            W.close()
            xts = S.pool(2, [128, 8, 512], F32, "xt")
            tms = S.pool(2, [128, 512], F32, "tm")
            h1 = S.sb([128, 8, 512], BF16, "h1")
            h2s = S.pool(2, [128, 8, 512], BF16, "h2")
            oh = S.sb([128, 4, 512], BF16, "oh")
            outf = S.sb([128, 8, 512], F32, "outf")
            qhs = S.pool(2, [128, 512], BF16, "qh")
            rsb = S.sb([128, 512], F32, "rsb")
            pts = S.pool(2, [128, 1024], BF16, "pt")
            kcv = lambda a: a.rearrange("(kc p) n -> p kc n", p=128)

            def norm_to(xt_, col0, dst):
                pm_ = pst.get()
                for kc in range(8):
                    sq = f32p.get()
                    if kc % 2 == 0:
                        act(sq[:], xt_[:, kc, :], AF.Square)
                    else:
                        tt("pool", sq[:], xt_[:, kc, :], xt_[:, kc, :], ALU.mult)
                    mm(pm_[:], cmat(CM_ONES), sq[:], start=(kc == 0), stop=(kc == 7))
                tmp_ = f32p.get(); rs_ = f32p.get()
                act(tmp_[:], pm_[:], AF.Ln, bias=epsb[:, 0:1], scale=1.0 / D_MODEL)
                act(rs_[:], tmp_[:], AF.Exp, scale=-0.5)
                for kc in range(8):
                    stt("dve", dst[:, kc, :], xt_[:, kc, :], vec[:, col0 + kc:col0 + kc + 1], rs_[:], ALU.mult, ALU.mult)

            def ld(t):
                sl_ = slice(t * 512, (t + 1) * 512)
                xt_ = xts.get(); tm_ = tms.get()
                T.dma(xt_[:], xres.v(("e", t), kcv(xres.ap)[:, :, sl_]))
                T.dma(tm_[:], tmask_in[:, sl_])
                return xt_, tm_

            nxt = ld(0)
            for t in range(NT):
                xt, tm = nxt
                if t + 1 < NT:
                    nxt = ld(t + 1)
                sl = slice(t * 512, (t + 1) * 512)
                norm_to(xt, VC["pre1"], h1)
                for h in range(4):
                    pq = pp.get()
                    for kc in range(8):
                        mm(pq[:], wq[:, kc, h * 128:(h + 1) * 128], h1[:, kc, :], start=(kc == 0), stop=(kc == 7))
                    qh = qhs.get()
                    cp("act", qh[:], pq[:])
                    ps2 = pS2.get()
                    for mb in range(2):
                        mm(ps2[:, mb * 512:(mb + 1) * 512], Kmem[:, h, mb * 128:(mb + 1) * 128], qh[:])
                    pt = pts.get()
                    act(pt[:], ps2[:], AF.Exp, scale=128 ** -0.5)
                    po = pp.get(); pd = pp.get()
                    for mb in range(2):
                        mm(po[:], Vmem[:, mb, h * 128:(h + 1) * 128], pt[:, mb * 512:(mb + 1) * 512], start=(mb == 0), stop=(mb == 1))
                    for mb in range(2):
                        mm(pd[:], onesb[:], pt[:, mb * 512:(mb + 1) * 512], start=(mb == 0), stop=(mb == 1))
                    rd = f32p.get()
                    recip(rd[:], pd[:])
                    tt("dve", oh[:, h, :], po[:], rd[:], ALU.mult)
                pm = pst.get()
                for oc in range(8):
                    po = pp.get()
                    for h in range(4):
                        mm(po[:], wmo[:, h, oc * 128:(oc + 1) * 128], oh[:, h, :], start=(h == 0), stop=(h == 3))
                    cp("act", outf[:, oc, :], po[:])
                    sq = f32p.get()
                    tt("pool", sq[:], outf[:, oc, :], outf[:, oc, :], ALU.mult)
                    mm(pm[:], cmat(CM_ONES), sq[:], start=(oc == 0), stop=(oc == 7))
                tmp = f32p.get(); rs = rsb
                act(tmp[:], pm[:], AF.Ln, bias=epsb[:, 0:1], scale=1.0 / D_MODEL)
                act(rs[:], tmp[:], AF.Exp, scale=-0.5)
                tt("dve", rs[:], rs[:], tm[:], ALU.mult)
                for oc in range(8):
                    t2 = f32p.get()
                    stt("dve", t2[:], outf[:, oc, :], vec[:, VC["post1"] + oc:VC["post1"] + oc + 1], rs[:], ALU.mult, ALU.mult)
                    tt("pool", xt[:, oc, :], xt[:, oc, :], t2[:], ALU.add)
                T.dma(xres.v(("e", t), kcv(xres.ap)[:, :, sl]), xt[:], own="in")
                h2 = h2s.get()
                norm_to(xt, VC["pre2"], h2)
                T.dma(H2.v(("e", t), kcv(H2.ap)[:, :, sl]), h2[:], own="in")
            S.close()

        if want("FFN"):
            S = Stage("FF%d" % l)
            wfi = alloc_w(S, D_MODEL, 2 * D_FF, "wfi"); wfo = alloc_w(S, D_FF, D_MODEL, "wfo")
            vec = S.sb([128, NV], F32, "vec")
            W = Stage("FFw%d" % l)
            stg = W.pool(2, [128, 2048], F32, "stg")
            load_w_bf16(S, wfi_in, wfi_in.h[l], D_MODEL, 2 * D_FF, stg, "wfi", w=wfi)
            load_w_bf16(S, wfo_in, wfo_in.h[l], D_FF, D_MODEL, stg, "wfo", w=wfo)
            W.close()
            T.dma(vec[:], vecs_in[l])
            h2t = S.pool(2, [128, 8, TM + 2], BF16, "h2t")
            xt = S.sb([128, 8, TM], F32, "xt"); tm = S.sb([128, TM], F32, "tm")
            actT = S.sb([128, 22, TM], BF16, "actT")
            outf = S.sb([128, 8, TM], F32, "outf")
            f32p = S.pool(8, [128, TM], F32, "f")
            rsb = S.sb([128, TM], F32, "rsb")
            pp = S.pspool(7, [128, 512], "pp")
            pst = S.pspool(1, [128, 512], "pst")
            kcv = lambda a: a.rearrange("(kc p) n -> p kc n", p=128)
            H2v = kcv(H2.ap)
            last = (l == depth - 1)

            def ldh(i):
                c0 = i * TM
                hh = h2t.get()
                lo = max(c0 - 1, 0); hi = min(c0 + TM + 1, L)
                T.dma(hh[:, :, (lo - (c0 - 1)):(hi - (c0 - 1))], H2.v(("f", 0), H2v[:, :, lo:hi]))
                if c0 == 0:
                    mset("pool", hh[:, :, 0:1], 0.0)
                if c0 + TM == L:
                    mset("pool", hh[:, :, TM + 1:TM + 2], 0.0)
                return hh

            nh = ldh(0)
            for i in range(NTM):
                sl = slice(i * TM, (i + 1) * TM)
                hh = nh
                if i + 1 < NTM:
                    nh = ldh(i + 1)
                k_ = ("f", i + 1)
                T.dma(xt[:], xres.v(k_, kcv(xres.ap)[:, :, sl]))
                T.dma(tm[:], tmask_in[:, sl])
                for j in range(22):
                    cv = []
                    for half in range(2):
                        ch = j + 22 * half
                        pu = pp.get()
                        for kc in range(8):
                            mm(pu[:, 0:TM + 2], wfi[:, kc, ch * 128:(ch + 1) * 128], hh[:, kc, :], start=(kc == 0), stop=(kc == 7))
                        a0 = f32p.get(); a1 = f32p.get()
                        ts("dve", a0[:], pu[:, 0:TM], vec[:, VC["fw0"] + ch:VC["fw0"] + ch + 1], ALU.mult, vec[:, VC["fb"] + ch:VC["fb"] + ch + 1], ALU.add)
                        stt("dve", a1[:], pu[:, 1:TM + 1], vec[:, VC["fw1"] + ch:VC["fw1"] + ch + 1], a0[:], ALU.mult, ALU.add)
                        stt("dve", a0[:], pu[:, 2:TM + 2], vec[:, VC["fw2"] + ch:VC["fw2"] + ch + 1], a1[:], ALU.mult, ALU.add)
                        cv.append(a0)
                    ga = f32p.get()
                    act(ga[:], cv[0][:], AF.Gelu)
                    tt("pool", actT[:, j, :], ga[:], cv[1][:], ALU.mult)
                pm = pst.get()
                for oc in range(8):
                    po = pp.get()
                    for j in range(22):
                        mm(po[:, 0:TM], wfo[:, j, oc * 128:(oc + 1) * 128], actT[:, j, :], start=(j == 0), stop=(j == 21))
                    cp("act", outf[:, oc, :], po[:, 0:TM])
                    sq = f32p.get()
                    tt("pool", sq[:], outf[:, oc, :], outf[:, oc, :], ALU.mult)
                    mm(pm[:, 0:TM], cmat(CM_ONES), sq[:], start=(oc == 0), stop=(oc == 7))
                tmp = f32p.get(); rs = rsb
                act(tmp[:], pm[:, 0:TM], AF.Ln, bias=epsb[:, 0:1], scale=1.0 / D_MODEL)
                act(rs[:], tmp[:], AF.Exp, scale=-0.5)
                tt("dve", rs[:], rs[:], tm[:], ALU.mult)
                for oc in range(8):
                    t2 = f32p.get()
                    stt("dve", t2[:], outf[:, oc, :], vec[:, VC["post2"] + oc:VC["post2"] + oc + 1], rs[:], ALU.mult, ALU.mult)
                    tt("pool", xt[:, oc, :], xt[:, oc, :], t2[:], ALU.add)
                dst = yT_out if last else xres
                T.dma(dst.v(k_, kcv(dst.ap)[:, :, sl]), xt[:], own="in")
            S.close()

    T.finish()
    glob.close()
    return nc


def _const_tables(L):
    c = {}
    cm = np.zeros((NCM, 128, 128), np.float32)
    k = np.arange(128)
    cm[0] = np.eye(128)
    cm[1] = 1.0
    cm[2] = (k[:, None] <= k[None, :])
    cm[3] = (k[:, None] >= k[None, :])
    cm[4] = np.where(k[:, None] <= k[None, :], 0.0, NEG)
    cm[5] = np.where(k[:, None] >= k[None, :], 0.0, NEG)
    cm[6] = (k[:, None] // 64 == k[None, :] // 64)
    for m in range(32):
        if m < 16:
            cm[7][64 + m + 16, 64 + m] = -1.0
        else:
            cm[7][64 + m - 16, 64 + m] = 1.0
    for blk in range(2):
        for sub in range(2):
            o = blk * 64 + sub * 32
            for m in range(32):
                if m < 16:
                    cm[8][o + m + 16, o + m] = -1.0
                else:
                    cm[8][o + m - 16, o + m] = 1.0
    cm[9][127, :] = 1.0
    cm[10][64, :] = 1.0
    cm[11][0, :] = 1.0
    for h_ in range(16):
        cm[12 + h_][h_, :] = 1.0
    c["cmat"] = np.ascontiguousarray(cm.transpose(1, 0, 2))
    pos = np.arange(L, dtype=np.float32)
    freqs = (np.float32(10000.0) ** (-(np.arange(16, dtype=np.float32)) / np.float32(16))).astype(np.float32)
    angB = (pos[None, :] * freqs[:, None]).astype(np.float32)
    c["cosB"] = np.concatenate([np.cos(angB), np.cos(angB)], 0).astype(np.float32)
    c["sinB"] = np.concatenate([np.sin(angB), np.sin(angB)], 0).astype(np.float32)
    rowp = (np.arange(L) // 64).astype(np.float32)
    colp = (np.arange(L) % 64).astype(np.float32)
    angR = (rowp[None, :] * freqs[:, None]).astype(np.float32)
    angC = (colp[None, :] * freqs[:, None]).astype(np.float32)
    cD = np.concatenate([np.cos(angR), np.cos(angR), np.cos(angC), np.cos(angC)], 0)
    sD = np.concatenate([np.sin(angR), np.sin(angR), np.sin(angC), np.sin(angC)], 0)
    c["cosD"] = np.concatenate([cD, cD], 0).astype(np.float32)
    c["sinD"] = np.concatenate([sD, sD], 0).astype(np.float32)
    return c


def _alibi_tables(L, Lreal):
    j = np.arange(L)
    jh = (j // 128).astype(np.float32)
    jl = (j % 128).astype(np.float32)
    kmask = np.where(j < Lreal, 0.0, NEG).astype(np.float32)
    kaug = np.zeros((4, 5, L), np.float32)
    bdiag = np.zeros((4, 128, 2048), np.float32)
    il = np.arange(512, dtype=np.float32)
    jl128 = np.arange(128, dtype=np.float32)
    for h in range(4):
        m = 2.0 ** (-2.0 * (h + 1))
        kaug[h, 0] = -8 * m * 128
        kaug[h, 1] = -8 * m
        kaug[h, 2] = 8 * m * 128 * jh
        kaug[h, 3] = 8 * m * jl
        kaug[h, 4] = kmask
        for dp in range(2):
            for hf in range(2):
                koff = 128 * (2 * dp + hf)
                c0 = dp * 1024 + hf * 512
                bdiag[h, :, c0:c0 + 512] = -8 * m * np.abs(il[None, :] - (koff + jl128[:, None]))
    qaug = np.zeros((3, 5, L), np.float32)
    qaug[0, 0] = jh; qaug[0, 1] = jl; qaug[0, 2] = 1; qaug[0, 3] = 1; qaug[0, 4] = 1
    qaug[1, 0] = -jh; qaug[1, 1] = -jl; qaug[1, 2] = -1; qaug[1, 3] = -1; qaug[1, 4] = 1
    qaug[2, 4] = 1
    return dict(kaugA=kaug.astype(NPBF), qaugA=qaug.astype(NPBF), bdiag=bdiag,
                kmask=kmask[None, :].astype(NPBF), onesrow=np.ones((1, L), NPBF))


def _weights_layout(p, depth):
    f = lambda a: np.ascontiguousarray(np.asarray(a, dtype=np.float32))
    w_in = f(p["w_in"])
    o = {}
    o["wP"] = f(np.concatenate([w_in[:, :, 0:2208], w_in[:, :, 2720:4272]], axis=2))
    o["wG"] = f(np.concatenate([w_in[:, :, G_OFF:G_OFF + 4096], w_in[:, :, C_Z:C_Z + 512]], axis=2))
    o["wuq"] = f(p["w_mla_uq"])
    wukv = f(p["w_mla_ukv"]).reshape(depth, 256, 4, 192)
    o["wukvK"] = f(wukv[..., 0:64].reshape(depth, 256, 256))
    o["wukvV"] = f(wukv[..., 64:192].reshape(depth, 256, 512))
    o["wbr"] = f(p["w_branch"]).reshape(depth, 2048, D_MODEL)
    o["wout"] = f(p["w_out"])
    o["wmq"] = f(p["w_mem_q"])
    wkv = f(p["w_mem_kv"]).reshape(depth, D_MODEL, 4, 256)
    o["wmk"] = f(wkv[..., 0:128].reshape(depth, D_MODEL, 512))
    o["wmv"] = f(wkv[..., 128:256].reshape(depth, D_MODEL, 512))
    o["wmo"] = f(p["w_mem_o"])
    o["wfi"] = f(p["w_ffn_in"])
    o["wfo"] = f(p["w_ffn_out"])
    vecs = np.zeros((depth, 128, NV), np.float32)

    def put(name, arr, width):
        a = f(arr).reshape(depth, width, 128).transpose(0, 2, 1)
        vecs[:, :, VC[name]:VC[name] + width] = a

    npre = f(p["norm_pre"]); npost = f(p["norm_post"])
    for i in range(3):
        put("pre%d" % i, npre[:, i], 8)
        put("post%d" % i, npost[:, i], 8)
    put("diffn", p["diff_norm"], 1)
    put("mlaq", p["mla_q_norm"], 3)
    put("mlakv", p["mla_kv_norm"], 2)
    cw = f(p["ssm_conv_w"])
    for k_ in range(3):
        put("cw%d" % k_, cw[:, k_], 6)
    put("cb", p["ssm_conv_b"], 6)
    put("ssmn", p["ssm_norm"], 4)
    put("gq", np.tile(f(p["gqa_q_norm"]), (1, 2)), 1)
    put("gk", np.tile(f(p["gqa_k_norm"]), (1, 2)), 1)
    put("memn", p["mem_norm"], 8)
    dtb = f(p["ssm_dt_bias"]).reshape(depth, 16)
    vecs[:, 0:16, VC["dtb"]] = dtb
    put("ssmd", np.repeat(f(p["ssm_d"]), 64, axis=1), 4)
    fw = f(p["ffn_conv_w"])
    for k_ in range(3):
        put("fw%d" % k_, fw[:, k_], 44)
    put("fb", p["ffn_conv_b"], 44)
    o["vecs"] = vecs
    rows = np.zeros((depth, 128, 272), np.float32)
    rows[:, :, 0:256] = f(p["diff_lambda"]).reshape(depth, 1, 256)
    rows[:, :, 256:272] = f(p["ssm_a_log"]).reshape(depth, 1, 16)
    o["rows"] = rows
    return o


def make_in_maps(seqs, mems, params, L, depth):
    consts = _const_tables(L)
    wl = _weights_layout(params, depth)
    maps = []
    cache = {}
    for x, mem in zip(seqs, mems):
        Lr = x.shape[0]
        if Lr not in cache:
            cache[Lr] = _alibi_tables(L, Lr)
        m = dict(consts)
        m.update(wl)
        m.update(cache[Lr])
        xT = np.zeros((D_MODEL, L), np.float32)
        xT[:, :Lr] = np.asarray(x, np.float32).T
        m["xT"] = xT
        m["memT"] = np.ascontiguousarray(np.asarray(mem, np.float32).T)
        tm = np.zeros((128, L), np.float32)
        tm[:, :Lr] = 1.0
        m["tmask"] = tm
        maps.append(m)
    return maps


PARAM_NAMES = ["w_in", "w_branch", "w_out", "diff_lambda", "diff_norm", "mla_q_norm", "mla_kv_norm",
               "w_mla_uq", "w_mla_ukv", "ssm_conv_w", "ssm_conv_b", "ssm_a_log", "ssm_dt_bias", "ssm_d",
               "ssm_norm", "gqa_q_norm", "gqa_k_norm", "mem_norm", "w_mem_q", "w_mem_kv", "w_mem_o",
               "w_ffn_in", "ffn_conv_w", "ffn_conv_b", "w_ffn_out", "norm_pre", "norm_post"]


def kernel(**inputs):
    xp = np.asarray(inputs["x_prompt"], np.float32)
    xs = np.asarray(inputs["x_sample"], np.float32)
    mp = np.asarray(inputs["mem_prompt"], np.float32)
    ms = np.asarray(inputs["mem_sample"], np.float32)
    params = {n: np.asarray(inputs[n], np.float32) for n in PARAM_NAMES}
    depth = params["w_in"].shape[0]
    L = xp.shape[1]
    seqs = [xp[0], xp[1], xs[0], xs[1], xs[2], xs[3], xs[0], xs[1]]
    mems = [mp[0], mp[1], ms[0], ms[1], ms[2], ms[3], ms[0], ms[1]]
    outs = run_trunk(seqs, mems, params, L, depth)
    y_prompt = np.stack([outs[0], outs[1]], 0).astype(np.float32)
    y_sample = np.stack([outs[2], outs[3], outs[4], outs[5]], 0).astype(np.float32)
    return (y_prompt, y_sample)


_NC_CACHE = {}


def run_trunk(seqs, mems, params, L, depth):
    key = (L, depth)
    if key not in _NC_CACHE:
        _NC_CACHE[key] = build_program(L, depth=depth)
    nc = _NC_CACHE[key]
    maps = make_in_maps(seqs, mems, params, L, depth)
    res = run_bass_kernel_spmd(nc, maps, core_ids=list(range(8)))
    outs = []
    for i, x in enumerate(seqs):
        yT = np.asarray(res.results[i]["yT"], np.float32)
        outs.append(np.ascontiguousarray(yT[:, :x.shape[0]].T))
    return outs
```

```python
import math
import numpy as np
import ml_dtypes
import concourse.bass as bass
import concourse.mybir as mybir
from concourse.bass_utils import run_bass_kernel_spmd
from contextlib import ExitStack

F32 = mybir.dt.float32
BF16 = mybir.dt.bfloat16
AF = mybir.ActivationFunctionType
ALU = mybir.AluOpType
NPBF = ml_dtypes.bfloat16

ENGS = ("pe", "act", "dve", "pool")
D_MODEL = 1024
EPS = 1e-6
MEM_LEN = 256
D_FF = 2816
NEG = -30000.0
NCM = 28


class Buf:
    __slots__ = ("h", "name", "w", "r", "ld", "st")

    def __init__(self, h, name):
        self.h = h
        self.name = name
        self.w = None
        self.r = {}
        self.ld = None
        self.st = None

    def __getitem__(self, idx):
        return View(self, self.h[idx])


class View:
    __slots__ = ("b", "ap")

    def __init__(self, b, ap):
        self.b = b
        self.ap = ap

    def __getitem__(self, idx):
        return View(self.b, self.ap[idx])

    def re(self, pat, **kw):
        return View(self.b, self.ap.rearrange(pat, **kw))


class Trk:
    def __init__(self, nc, es, n_dma_sems=80):
        self.nc = nc
        self.eng = {"pe": nc.tensor, "act": nc.scalar, "dve": nc.vector, "pool": nc.gpsimd,
                    "sp": nc.sync}
        self.sem = {e: es.enter_context(nc.semaphore("c_" + e)) for e in ENGS}
        self.cnt = {e: 0 for e in ENGS}
        self.dsem = [es.enter_context(nc.semaphore("d%d" % i)) for i in range(n_dma_sems)]
        self.dcnt = [0] * n_dma_sems
        self.dfree = list(range(n_dma_sems))
        self.issuers = list(ENGS) + ["sp"]
        self.known = {e: {} for e in self.issuers}
        self.nins = 0

    def _handle(self, key):
        return self.sem[key] if isinstance(key, str) else self.dsem[key]

    def _wait(self, issuer, dep):
        if dep is None:
            return
        key, val = dep
        if issuer == "pe" and key == "pe":
            return
        if self.known[issuer].get(key, 0) >= val:
            return
        self.eng[issuer].wait_ge(self._handle(key), val)
        self.known[issuer][key] = val
        self.nins += 1

    def _deps(self, issuer, reads, writes):
        for v in reads:
            self._wait(issuer, v.b.w)
        for v in writes:
            self._wait(issuer, v.b.w)
            for k, val in v.b.r.items():
                self._wait(issuer, (k, val))

    def op(self, e, fn, reads=(), writes=()):
        self._deps(e, reads, writes)
        ins = fn()
        self.cnt[e] += 1
        self.nins += 1
        ins.then_inc(self.sem[e], 1)
        c = self.cnt[e]
        for v in reads:
            v.b.r[e] = c
        for v in writes:
            v.b.w = (e, c)
            v.b.r = {}
        return ins

    def dma(self, out, in_, own="out", q="sp", extra_reads=(), extra_writes=(), **kw):
        self._deps(q, [in_] + list(extra_reads), [out] + list(extra_writes))
        if own == "out":
            if out.b.ld is None:
                out.b.ld = self.dfree.pop()
            slot = out.b.ld
        else:
            if in_.b.st is None:
                in_.b.st = self.dfree.pop()
            slot = in_.b.st
        ins = self.eng[q].dma_start(out=out.ap, in_=in_.ap, **kw)
        self.nins += 1
        self.dcnt[slot] += 16
        ins.then_inc(self.dsem[slot], 16)
        c = self.dcnt[slot]
        for v in [in_] + list(extra_reads):
            v.b.r[slot] = c
        for v in [out] + list(extra_writes):
            v.b.w = (slot, c)
            v.b.r = {}
        return ins

    def full_barrier(self, release=()):
        for issuer in self.issuers:
            for e in ENGS:
                if self.cnt[e]:
                    self._wait(issuer, (e, self.cnt[e]))
            for s in range(len(self.dsem)):
                if self.dcnt[s]:
                    self._wait(issuer, (s, self.dcnt[s]))
        for b in release:
            for s in (b.ld, b.st):
                if s is not None:
                    self.dfree.append(s)
            b.ld = b.st = None

    def finish(self):
        for s in range(len(self.dsem)):
            if self.dcnt[s]:
                self._wait("sp", (s, self.dcnt[s]))


A_Q, A_K, A_V = 0, 512, 1024
B_CQ, B_CKV, B_KR = 1536, 1920, 2176
C_Z, C_XBC, C_DTF, C_DTB = 2208, 2720, 3488, 3496
D_Q, D_K, D_V = 3504, 4016, 4144
G_OFF = 4272
PC_XBC = 2208
PC_DT = 2208 + 768
PC_DQ = PC_DT + 16
PC_DK = PC_DQ + 512
PC_DV = PC_DK + 128
NPC = PC_DV + 128

VC = {}
_o = 0
for _n, _w in [("pre0", 8), ("pre1", 8), ("pre2", 8), ("post0", 8), ("post1", 8), ("post2", 8),
               ("diffn", 1), ("mlaq", 3), ("mlakv", 2), ("cw0", 6), ("cw1", 6), ("cw2", 6), ("cb", 6),
               ("ssmn", 4), ("gq", 1), ("gk", 1), ("memn", 8), ("dtb", 1), ("ssmd", 4),
               ("fw0", 44), ("fw1", 44), ("fw2", 44), ("fb", 44)]:
    VC[_n] = _o
    _o += _w
NV = _o


def build_program(L, depth=2, dbg=(), cut_scale=60.0, stages=None):
    nc = bass.Bass("TRN2", target_bir_lowering=False)
    NT = L // 512
    NCH = L // 128
    glob = ExitStack()
    T = Trk(nc, glob)
    V, A, P, G = nc.vector, nc.scalar, nc.tensor, nc.gpsimd
    ENG = {"dve": V, "act": A, "pool": G}

    def dram(name, shape, dt, kind="Internal"):
        if name in dbg:
            kind = "ExternalOutput"
        return nc.dram_tensor(name, shape, dt, kind=kind).ap()

    class DT:
        def __init__(self, name, shape, dt, kind="Internal", tok_axis=-1):
            self.ap = dram(name, shape, dt, kind)
            self.name = name
            self.tiles = {}

        def t(self, i):
            if i not in self.tiles:
                self.tiles[i] = Buf(self.ap, "%s.%s" % (self.name, i))
            return self.tiles[i]

        def v(self, i, ap):
            return View(self.t(i), ap)

    def ext(name, shape, dt=F32):
        return Buf(nc.dram_tensor(name, shape, dt, kind="ExternalInput").ap(), name)

    xT_in = ext("xT", [D_MODEL, L])
    memT_in = ext("memT", [D_MODEL, MEM_LEN])
    tmask_in = ext("tmask", [128, L])
    kaugA_in = ext("kaugA", [4, 5, L], BF16)
    qaugA_in = ext("qaugA", [3, 5, L], BF16)
    kmask_in = ext("kmask", [1, L], BF16)
    onesrow_in = ext("onesrow", [1, L], BF16)
    bdiag_in = ext("bdiag", [4, 128, 2048])
    cosB_in = ext("cosB", [32, L]); sinB_in = ext("sinB", [32, L])
    cosD_in = ext("cosD", [128, L]); sinD_in = ext("sinD", [128, L])
    cmat_in = ext("cmat", [128, NCM, 128])
    wP_in = ext("wP", [depth, D_MODEL, NPC])
    wG_in = ext("wG", [depth, D_MODEL, 4096 + 512])
    wuq_in = ext("wuq", [depth, 384, 384])
    wukvK_in = ext("wukvK", [depth, 256, 256])
    wukvV_in = ext("wukvV", [depth, 256, 512])
    wbr_in = ext("wbr", [depth, 2048, D_MODEL])
    wout_in = ext("wout", [depth, D_MODEL, D_MODEL])
    wmq_in = ext("wmq", [depth, D_MODEL, 512])
    wmk_in = ext("wmk", [depth, D_MODEL, 512])
    wmv_in = ext("wmv", [depth, D_MODEL, 512])
    wmo_in = ext("wmo", [depth, 512, D_MODEL])
    wfi_in = ext("wfi", [depth, D_MODEL, 2 * D_FF])
    wfo_in = ext("wfo", [depth, D_FF, D_MODEL])
    vecs_in = ext("vecs", [depth, 128, NV])
    rows_in = ext("rows", [depth, 128, 256 + 16])
    yT_out = DT("yT", [D_MODEL, L], F32, kind="ExternalOutput")

    xres = DT("xres", [D_MODEL, L], F32)
    hT = DT("hT", [D_MODEL, L], BF16)
    QA = DT("QA", [8, 64, L], BF16); KA = DT("KA", [8, 64, L], BF16); VA = DT("VA", [4, L, 128], BF16)
    QB = DT("QB", [4, 96, L], BF16); KB = DT("KB", [4, 96, L], BF16); VB = DT("VB", [4, L, 128], BF16)
    QD = DT("QD", [8, 64, L], BF16); KD = DT("KD", [2, 64, L], BF16); VD = DT("VD", [2, L, 64], BF16)
    XBC = DT("XBC", [768, L], F32); DTR = DT("DTR", [16, L], F32)
    XBCP = DT("XBCP", [768, L], F32); DTP = DT("DTP", [16, L], F32)
    OA = DT("OA", [8, 128, L], F32); OB = DT("OB", [512, L], BF16); OD = DT("OD", [512, L], BF16)
    YF = DT("YF", [512, L], F32); YS = DT("YS", [512, L], F32)
    H2 = DT("H2", [D_MODEL, L], BF16)

    def mm(out, lhsT, rhs, start=True, stop=True):
        T.op("pe", lambda: P.matmul(out.ap, lhsT=lhsT.ap, rhs=rhs.ap, start=start, stop=stop),
             reads=[lhsT, rhs], writes=[out])

    def trp(out, in_, ident):
        T.op("pe", lambda: P.matmul(out.ap, lhsT=in_.ap, rhs=ident.ap, start=True, stop=True), reads=[in_, ident], writes=[out])

    def act(out, in_, func, bias=None, scale=1.0):
        rd = [in_]
        kw = {}
        if isinstance(scale, View):
            rd.append(scale)
            scale = scale.ap
        if isinstance(bias, View):
            rd.append(bias); kw["bias"] = bias.ap
        elif bias is not None:
            kw["bias"] = bias
        T.op("act", lambda: A.activation(out=out.ap, in_=in_.ap, func=func, scale=scale, **kw),
             reads=rd, writes=[out])

    def _s(x, rd):
        if isinstance(x, View):
            rd.append(x)
            return x.ap
        return x

    def ts(e, out, in0, s1, op0, s2=None, op1=None):
        rd = [in0]
        a1 = _s(s1, rd); a2 = _s(s2, rd)
        kw = {} if op1 is None else {"op1": op1}
        T.op(e, lambda: ENG[e].tensor_scalar(out=out.ap, in0=in0.ap, scalar1=a1, scalar2=a2, op0=op0, **kw),
             reads=rd, writes=[out])

    def tt(e, out, in0, in1, op):
        T.op(e, lambda: ENG[e].tensor_tensor(out=out.ap, in0=in0.ap, in1=in1.ap, op=op),
             reads=[in0, in1], writes=[out])

    def stt(e, out, in0, scalar, in1, op0, op1):
        e = "dve"
        rd = [in0, in1]
        a = _s(scalar, rd)
        T.op(e, lambda: ENG[e].scalar_tensor_tensor(out=out.ap, in0=in0.ap, scalar=a, in1=in1.ap, op0=op0, op1=op1),
             reads=rd, writes=[out])

    def cp(e, out, in_):
        if e == "act":
            T.op("act", lambda: A.copy(out=out.ap, in_=in_.ap), reads=[in_], writes=[out])
        else:
            T.op(e, lambda: ENG[e].tensor_copy(out=out.ap, in_=in_.ap), reads=[in_], writes=[out])

    def mset(e, out, val):
        T.op(e, lambda: ENG[e].memset(out.ap, val), writes=[out])

    def recip(out, in_):
        T.op("dve", lambda: V.reciprocal(out=out.ap, in_=in_.ap), reads=[in_], writes=[out])

    class Stage:
        def __init__(self, name):
            self.es = ExitStack()
            self.bufs = []
            self.name = name
            self.n = 0

        def sb(self, shape, dt, name=None):
            self.n += 1
            nm = "%s_%d_%s" % (self.name, self.n, name or "t")
            b = Buf(self.es.enter_context(nc.sbuf_tensor(nm, list(shape), dt)), nm)
            self.bufs.append(b)
            return b

        def ps(self, shape, dt=F32, name=None):
            self.n += 1
            nm = "%s_%d_%s" % (self.name, self.n, name or "p")
            b = Buf(self.es.enter_context(nc.psum_tensor(nm, list(shape), dt)), nm)
            self.bufs.append(b)
            return b

        def pool(self, n, shape, dt, name="pl"):
            return Ring([self.sb(shape, dt, name) for _ in range(n)])

        def pspool(self, n, shape, name="pp"):
            return Ring([self.ps(shape, F32, name) for _ in range(n)])

        def close(self):
            T.full_barrier(release=self.bufs)
            self.es.close()

    class Ring:
        def __init__(self, items):
            self.items = items
            self.i = 0

        def get(self):
            b = self.items[self.i % len(self.items)]
            self.i += 1
            return b

    def alloc_w(S, K, N, name):
        return S.sb([128, (K + 127) // 128, N], BF16, name)

    def load_w_bf16(S, src_buf, src_ap, K, N, stg_ring, name, engs=("dve", "pool"), w=None):
        kc_n = (K + 127) // 128
        if w is None:
            w = S.sb([128, kc_n, N], BF16, name)
        step = stg_ring.items[0].h.shape[1]
        i = 0
        for kc in range(kc_n):
            rows = min(128, K - kc * 128)
            for c0 in range(0, N, step):
                cw = min(step, N - c0)
                st = stg_ring.get()
                T.dma(st[0:rows, 0:cw], View(src_buf, src_ap[kc * 128:kc * 128 + rows, c0:c0 + cw]))
                cp(engs[i % len(engs)], w[0:rows, kc, c0:c0 + cw], st[0:rows, 0:cw])
                i += 1
        return w

    cm = Buf(glob.enter_context(nc.sbuf_tensor("cmat_sb", [128, NCM, 128], F32)), "cmat_sb")
    T.dma(cm[:], cmat_in[:, :, :])
    CM_ID, CM_ONES, CM_U, CM_LO, CM_NEGF, CM_NEGB, CM_BD64, CM_R32, CM_RD, CM_SEL127, CM_SEL64, CM_SEL0, CM_SELH = range(13)
    ident = cm[:, CM_ID, :]

    def cmat(i, rows=128, cols=128):
        return cm[0:rows, i, 0:cols]

    def rms_stats(S, pp, sq_chunks, n_feat, out_rstd, tmp, blockmat=None, extra_scale=None):
        pm = pp.get()
        n = len(sq_chunks)
        for i, sq in enumerate(sq_chunks):
            rows = sq.ap.shape[0]
            lhs = (blockmat if blockmat is not None else cmat(CM_ONES, rows, 128))
            mm(pm[:, 0:512], lhs, sq, start=(i == 0), stop=(i == n - 1))
        act(tmp, pm[:, 0:512], AF.Ln, bias=epsb[:, 0:1], scale=1.0 / n_feat)
        act(out_rstd, tmp, AF.Exp, scale=-0.5)
        if extra_scale is not None:
            ts("dve", out_rstd, out_rstd, extra_scale, ALU.mult)

    epsb = Buf(glob.enter_context(nc.sbuf_tensor("epsb", [128, 1], F32)), "epsb")
    mset("dve", epsb[:], EPS)

    def want(s):
        return stages is None or s in stages

    for l in range(depth):
        lam_init = 0.8 - 0.6 * math.exp(-0.3 * l)
        xsrc = None

        def xin_view(t):
            if l == 0:
                return View(xT_in, xT_in.h.rearrange("(kc p) n -> p kc n", p=128)[:, :, t * 512:(t + 1) * 512])
            return xres.v(t, xres.ap.rearrange("(kc p) n -> p kc n", p=128)[:, :, t * 512:(t + 1) * 512])

        if want("P"):
            S = Stage("P%d" % l)
            stg = S.pool(2, [128, 1880], F32, "stg")
            wP = load_w_bf16(S, wP_in, wP_in.h[l], D_MODEL, NPC, stg, "wP")
            wuq = load_w_bf16(S, wuq_in, wuq_in.h[l], 384, 384, stg, "wuq")
            wkK = load_w_bf16(S, wukvK_in, wukvK_in.h[l], 256, 256, stg, "wkK")
            wkV = load_w_bf16(S, wukvV_in, wukvV_in.h[l], 256, 512, stg, "wkV")
            vec = S.sb([128, NV], F32, "vec")
            T.dma(vec[:], vecs_in[l])
            xts = S.pool(2, [128, 8, 512], F32, "xt")
            hbs = S.pool(2, [128, 8, 512], BF16, "hb")
            f32p = S.pool(6, [128, 512], F32, "f")
            bfp = S.pool(6, [128, 512], BF16, "b")
            cqf = S.sb([128, 3, 512], F32, "cqf"); cqn = S.sb([128, 3, 512], BF16, "cqn")
            ckf = S.sb([128, 2, 512], F32, "ckf"); ckn = S.sb([128, 2, 512], BF16, "ckn")
            tabs = S.pool(2, [128, 4, 512], F32, "tab")
            krope = S.sb([128, 512], BF16, "krope")
            pp = S.pspool(6, [128, 512], "pp")

            def load_x(t):
                xt = xts.get()
                T.dma(xt[:], xin_view(t))
                return xt

            nxt = load_x(0)
            for t in range(NT):
                xt = nxt
                tb = tabs.get()
                sl = slice(t * 512, (t + 1) * 512)
                T.dma(tb[64:96, 0, :], cosB_in[:, sl]); T.dma(tb[64:96, 1, :], sinB_in[:, sl])
                T.dma(tb[:, 2, :], cosD_in[:, sl]); T.dma(tb[:, 3, :], sinD_in[:, sl])
                if t + 1 < NT:
                    nxt = load_x(t + 1)
                sqs = []
                pm = pp.get()
                for kc in range(8):
                    sq = f32p.get()
                    if kc % 2 == 0:
                        act(sq[:], xt[:, kc, :], AF.Square)
                    else:
                        tt("pool", sq[:], xt[:, kc, :], xt[:, kc, :], ALU.mult)
                    mm(pm[:], cmat(CM_ONES), sq[:], start=(kc == 0), stop=(kc == 7))
                tmp = f32p.get(); rstd = f32p.get()
                act(tmp[:], pm[:], AF.Ln, bias=epsb[:, 0:1], scale=1.0 / D_MODEL)
                act(rstd[:], tmp[:], AF.Exp, scale=-0.5)
                hb = hbs.get()
                for kc in range(8):
                    stt("dve" if kc % 2 == 0 else "pool", hb[:, kc, :], xt[:, kc, :],
                        vec[:, VC["pre0"] + kc:VC["pre0"] + kc + 1], rstd[:], ALU.mult, ALU.mult)
                T.dma(hT.v(t, hT.ap.rearrange("(kc p) n -> p kc n", p=128)[:, :, sl]), hb[:], own="in")
                if l == 0:
                    T.dma(xres.v(t, xres.ap.rearrange("(kc p) n -> p kc n", p=128)[:, :, sl]), xt[:], own="in")

                def proj(c0, m, n=512):
                    pm_ = pp.get()
                    for kc in range(8):
                        mm(pm_[0:m, 0:n], wP[:, kc, c0:c0 + m], hb[:, kc, 0:n], start=(kc == 0), stop=(kc == 7))
                    return pm_

                for which, c0, dst in (("q", A_Q, QA), ("k", A_K, KA)):
                    for i in range(4):
                        pm_ = proj(c0 + i * 128, 128)
                        ob = bfp.get()
                        cp("act" if i % 2 == 0 else "dve", ob[:], pm_[:])
                        T.dma(dst.v(t, dst.ap[2 * i:2 * i + 2, :, sl].rearrange("a p n -> (a p) n")), ob[:], own="in")
                for tk in range(4):
                    pm_ = pp.get()
                    for kc in range(8):
                        mm(pm_[:], hb[:, kc, tk * 128:(tk + 1) * 128], wP[:, kc, A_V:A_V + 512], start=(kc == 0), stop=(kc == 7))
                    ob = bfp.get()
                    cp("act" if tk % 2 == 0 else "dve", ob[:], pm_[:])
                    r0 = t * 512 + tk * 128
                    T.dma(VA.v(t, VA.ap[:, r0:r0 + 128, :].rearrange("h p d -> p h d")), ob[:].re("p (h d) -> p h d", h=4), own="in")
                sqs = []
                for i in range(3):
                    pm_ = proj(B_CQ + i * 128, 128)
                    cp("act", cqf[:, i, :], pm_[:])
                    sq = f32p.get()
                    tt("dve", sq[:], cqf[:, i, :], cqf[:, i, :], ALU.mult)
                    sqs.append(sq[:])
                rs = f32p.get(); tmp = f32p.get()
                rms_stats(S, pp, sqs, 384, rs[:], tmp[:])
                for i in range(3):
                    stt("dve", cqn[:, i, :], cqf[:, i, :], vec[:, VC["mlaq"] + i:VC["mlaq"] + i + 1], rs[:], ALU.mult, ALU.mult)
                sqs = []
                for i in range(2):
                    pm_ = proj(B_CKV + i * 128, 128)
                    cp("act", ckf[:, i, :], pm_[:])
                    sq = f32p.get()
                    tt("pool", sq[:], ckf[:, i, :], ckf[:, i, :], ALU.mult)
                    sqs.append(sq[:])
                rs = f32p.get(); tmp = f32p.get()
                rms_stats(S, pp, sqs, 256, rs[:], tmp[:])
                for i in range(2):
                    stt("dve", ckn[:, i, :], ckf[:, i, :], vec[:, VC["mlakv"] + i:VC["mlakv"] + i + 1], rs[:], ALU.mult, ALU.mult)

                def rope32(src_ps, dst_bf):
                    xs_ = f32p.get()
                    cp("act", xs_[0:96, :], View(src_ps.b, src_ps.b.h[0:96, :]))
                    pr = pp.get()
                    mm(pr[0:96, :], cm[0:96, CM_R32, 0:96], xs_[0:96, :])
                    t1 = f32p.get(); t2 = f32p.get()
                    tt("dve", t1[64:96, :], xs_[64:96, :], tb[64:96, 0, :], ALU.mult)
                    tt("dve", t2[64:96, :], pr[64:96, :], tb[64:96, 1, :], ALU.mult)
                    tt("pool", dst_bf, t1[64:96, :], t2[64:96, :], ALU.add)

                pm_ = proj(B_KR - 64, 96)
                rope32(pm_[64:96, :], krope[64:96, :])
                for h in range(4):
                    pq = pp.get()
                    for i in range(3):
                        mm(pq[0:96, :], wuq[:, i, h * 96:(h + 1) * 96], cqn[:, i, :], start=(i == 0), stop=(i == 2))
                    ob = bfp.get()
                    rope32(pq[64:96, :], ob[64:96, :])
                    cp("act", ob[0:64, :], pq[0:64, :])
                    T.dma(QB.v(t, QB.ap[h, :, sl]), ob[0:96, :], own="in")
                    pk = pp.get()
                    for i in range(2):
                        mm(pk[0:64, :], wkK[:, i, h * 64:(h + 1) * 64], ckn[:, i, :], start=(i == 0), stop=(i == 1))
                    ob = bfp.get()
                    cp("dve", ob[64:96, :], krope[64:96, :])
                    cp("act", ob[0:64, :], pk[0:64, :])
                    T.dma(KB.v(t, KB.ap[h, :, sl]), ob[0:96, :], own="in")
                for tk in range(4):
                    pm_ = pp.get()
                    for i in range(2):
                        mm(pm_[:], ckn[:, i, tk * 128:(tk + 1) * 128], wkV[:, i, :], start=(i == 0), stop=(i == 1))
                    ob = bfp.get()
                    cp("act" if tk % 2 == 0 else "dve", ob[:], pm_[:])
                    r0 = t * 512 + tk * 128
                    T.dma(VB.v(t, VB.ap[:, r0:r0 + 128, :].rearrange("h p d -> p h d")), ob[:].re("p (h d) -> p h d", h=4), own="in")
                for i in range(5):
                    isq = i < 4
                    pm_ = proj((PC_DQ + i * 128) if isq else PC_DK, 128)
                    xf = f32p.get()
                    cp("act", xf[:], pm_[:])
                    sq = f32p.get()
                    tt("pool", sq[:], xf[:], xf[:], ALU.mult)
                    rs = f32p.get(); tmp = f32p.get()
                    rms_stats(S, pp, [sq[:]], 64, rs[:], tmp[:], blockmat=cmat(CM_BD64))
                    gcol = VC["gq"] if isq else VC["gk"]
                    xn = f32p.get()
                    stt("dve", xn[:], xf[:], vec[:, gcol:gcol + 1], rs[:], ALU.mult, ALU.mult)
                    pr = pp.get()
                    mm(pr[:], cmat(CM_RD), xn[:])
                    t1 = f32p.get(); t2 = f32p.get()
                    tt("dve", t1[:], xn[:], tb[:, 2, :], ALU.mult)
                    tt("dve", t2[:], pr[:], tb[:, 3, :], ALU.mult)
                    ob = bfp.get()
                    tt("pool", ob[:], t1[:], t2[:], ALU.add)
                    dst = QD if isq else KD
                    a0 = 2 * i if isq else 0
                    T.dma(dst.v(t, dst.ap[a0:a0 + 2, :, sl].rearrange("a p n -> (a p) n")), ob[:], own="in")
                for tk in range(4):
                    pm_ = pp.get()
                    for kc in range(8):
                        mm(pm_[:, 0:128], hb[:, kc, tk * 128:(tk + 1) * 128], wP[:, kc, PC_DV:PC_DV + 128], start=(kc == 0), stop=(kc == 7))
                    ob = bfp.get()
                    cp("act" if tk % 2 == 0 else "dve", ob[:, 0:128], pm_[:, 0:128])
                    r0 = t * 512 + tk * 128
                    T.dma(VD.v(t, VD.ap[:, r0:r0 + 128, :].rearrange("h p d -> p h d")), ob[:, 0:128].re("p (h d) -> p h d", h=2), own="in")
                for i in range(6):
                    pm_ = proj(PC_XBC + i * 128, 128)
                    of = f32p.get()
                    cp("act" if i % 2 == 0 else "dve", of[:], pm_[:])
                    T.dma(XBC.v(t, XBC.ap[i * 128:(i + 1) * 128, sl]), of[:], own="in")
                pm_ = proj(PC_DT, 16)
                of = f32p.get()
                cp("act", of[0:16, :], pm_[0:16, :])
                T.dma(DTR.v(t, DTR.ap[:, sl]), of[0:16, :], own="in")
            S.close()

        if want("ATT"):
            S = Stage("AT%d" % l)
            Ktr = S.pool(2, [128, L], BF16, "Kt")
            Vtr = S.pool(2, [128, NCH, 128], BF16, "Vt")
            Qp = S.pool(6, [128, 512], BF16, "Q")
            PTp = S.pool(3, [128, 1024], BF16, "PT")
            accDp = S.pool(2, [128, 512], F32, "accD")
            accPp = S.pool(2, [128, 512], F32, "accP")
            accD2p = S.pool(2, [128, 512], F32, "accD2")
            tmpSp = S.pool(2, [128, 1024], F32, "tmpS")
            bd = S.sb([128, 2048], F32, "bd")
            osbp = S.pool(2, [128, 512], F32, "osb")
            obfp = S.pool(2, [128, 512], BF16, "obf")
            rdp = S.pool(2, [128, 512], F32, "rd")
            psS = S.pspool(2, [128, 1024], "psS")
            psO = S.pspool(2, [128, 512], "psO")
            psX = S.pspool(2, [128, 512], "psX")
            onesb = S.sb([128, 128], BF16, "onesb")
            cp("dve", onesb[:], cmat(CM_ONES))

            class Cache:
                def __init__(self, ring):
                    self.ring = ring
                    self.map = {}

                def get(self, key, loader):
                    if key in self.map:
                        return self.map[key]
                    b = self.ring.get()
                    for k_ in [k_ for k_, v_ in self.map.items() if v_ is b]:
                        del self.map[k_]
                    loader(b)
                    self.map[key] = b
                    return b

            kc_ = Cache(Ktr)
            vc_ = Cache(Vtr)
            bd_state = {"h": None}

            def all_tiles(dt_):
                return [dt_.t(i) for i in range(NT)]

            def dep_views(dt_):
                return [View(b_, dt_.ap) for b_ in all_tiles(dt_)]

            units = []
            for h in range(4):
                for m in range(2):
                    units.append(("A", h, m))
            for h in range(4):
                units.append(("B", h, 0))
            for g in range(2):
                for r_ in range(4):
                    units.append(("D", g, r_))

            def load_kv(u):
                kind, a, b2 = u
                if kind == "A":
                    hm = 2 * a + b2

                    def lk(kt):
                        T.dma(kt[0:64, :], View(KA.t(0), KA.ap[hm]), extra_reads=dep_views(KA)[1:])
                        T.dma(kt[64:69, :], kaugA_in[a])
                    def lv(vt):
                        for c0 in range(0, NCH, 16):
                            c1 = min(c0 + 16, NCH)
                            T.dma(vt[:, c0:c1, :], View(VA.t(0), VA.ap[a].rearrange("(c p) d -> p c d", p=128)[:, c0:c1, :]), extra_reads=dep_views(VA)[1:])
                    return kc_.get(("A", hm), lk), vc_.get(("A", a), lv)
                if kind == "B":
                    def lk(kt):
                        T.dma(kt[0:96, :], View(KB.t(0), KB.ap[a]), extra_reads=dep_views(KB)[1:])
                        T.dma(kt[96:97, :], kmask_in[:, :])
                    def lv(vt):
                        for c0 in range(0, NCH, 16):
                            c1 = min(c0 + 16, NCH)
                            T.dma(vt[:, c0:c1, :], View(VB.t(0), VB.ap[a].rearrange("(c p) d -> p c d", p=128)[:, c0:c1, :]), extra_reads=dep_views(VB)[1:])
                    return kc_.get(("B", a), lk), vc_.get(("B", a), lv)

                def lk(kt):
                    T.dma(kt[0:64, :], View(KD.t(0), KD.ap[a]), extra_reads=dep_views(KD)[1:])
                    T.dma(kt[64:65, :], kmask_in[:, :])
                def lv(vt):
                    for c0 in range(0, NCH, 16):
                        c1 = min(c0 + 16, NCH)
                        T.dma(vt[:, c0:c1, 0:64], View(VD.t(0), VD.ap[a].rearrange("(c p) d -> p c d", p=128)[:, c0:c1, :]), extra_reads=dep_views(VD)[1:])
                    mset("pool", vt[:, :, 64:65], 1.0)
                return kc_.get(("D", a), lk), vc_.get(("D", a), lv)

            def load_q(u, qt):
                kind, a, b2 = u
                sl = slice(qt * 512, (qt + 1) * 512)
                if kind == "A":
                    hm = 2 * a + b2
                    qs = []
                    for ver in range(3):
                        q = Qp.get()
                        T.dma(q[0:64, :], QA.v(qt, QA.ap[hm, :, sl]))
                        T.dma(q[64:69, :], qaugA_in[ver, :, sl])
                        qs.append(q[0:69, :])
                    return qs
                q = Qp.get()
                if kind == "B":
                    T.dma(q[0:96, :], QB.v(qt, QB.ap[a, :, sl]))
                    T.dma(q[96:97, :], onesrow_in[:, sl])
                    return [q[0:97, :]]
                hq = 4 * a + b2
                T.dma(q[0:64, :], QD.v(qt, QD.ap[hq, :, sl]))
                T.dma(q[64:65, :], onesrow_in[:, sl])
                return [q[0:65, :]]

            def pairs_for(u, qt):
                kind, a, _ = u
                out_ = []
                for j in range(NCH // 2):
                    c0, c1 = 2 * j, 2 * j + 1
                    if kind != "A":
                        out_.append((j, 0, None))
                        continue
                    m_ = 2.0 ** (-2.0 * (a + 1))
                    if c1 < 4 * qt:
                        mind = 512 * qt - (128 * c1 + 127)
                        if mind * m_ >= cut_scale:
                            continue
                        out_.append((j, 0, None))
                    elif c0 > 4 * qt + 3:
                        mind = 128 * c0 - (512 * qt + 511)
                        if mind * m_ >= cut_scale:
                            continue
                        out_.append((j, 1, None))
                    else:
                        out_.append((j, 2, j - 2 * qt))
                return out_

            def compute(u, kt, vt):
                kind, a, b2 = u
                DK = {"A": 69, "B": 97, "D": 65}[kind]
                DVa = {"A": 128, "B": 128, "D": 65}[kind]
                scale = {"A": 0.125, "B": 96 ** -0.5, "D": 0.125}[kind]
                if kind == "A" and bd_state["h"] != a:
                    T.dma(bd[:, :], bdiag_in[a])
                    bd_state["h"] = a
                nq = load_q(u, 0)
                for qt in range(NT):
                    qs = nq
                    if qt + 1 < NT:
                        nq = load_q(u, qt + 1)
                    sl = slice(qt * 512, (qt + 1) * 512)
                    prs = pairs_for(u, qt)
                    po = psO.get()
                    px = psX.get()
                    if kind != "D":
                        accD = accDp.get(); accP = accPp.get(); accD2 = accD2p.get()
                        mset("dve", accD[:], 0.0)
                        mset("dve", accD2[:], 0.0)
                        mset("pool", accP[:], 0.0)
                    n = len(prs)

                    def emit_S(j, ver):
                        ps = psS.get()
                        q = qs[ver] if kind == "A" else qs[0]
                        mm(ps[:, 0:512], kt[0:DK, (2 * j) * 128:(2 * j + 1) * 128], q)
                        mm(ps[:, 512:1024], kt[0:DK, (2 * j + 1) * 128:(2 * j + 2) * 128], q)
                        return ps

                    def emit_rest(ps, j, ver, dp, idx):
                        src = ps
                        if ver == 2:
                            tmp = tmpSp.get()
                            tt("dve", tmp[:, 0:512], ps[:, 0:512], bd[:, dp * 1024:dp * 1024 + 512], ALU.add)
                            tt("dve", tmp[:, 512:1024], ps[:, 512:1024], bd[:, dp * 1024 + 512:(dp + 1) * 1024], ALU.add)
                            src = tmp
                        pt = PTp.get()
                        act(pt[:], src[:], AF.Exp, scale=scale)
                        mm(po[0:DVa, :], vt[:, 2 * j, 0:DVa], pt[:, 0:512], start=(idx == 0), stop=False)
                        mm(po[0:DVa, :], vt[:, 2 * j + 1, 0:DVa], pt[:, 512:1024], start=False, stop=(idx == n - 1))
                        if kind != "D":
                            if idx % 2 == 0:
                                mm(px[:], onesb[:], pt[:, 0:512], start=(idx == 0), stop=False)
                                tt("dve", accD2[:], accD2[:], pt[:, 512:1024], ALU.add)
                            else:
                                tt("dve", accD[:], accD[:], pt[:, 0:512], ALU.add)
                                tt("pool", accP[:], accP[:], pt[:, 512:1024], ALU.add)

                    prev = None
                    for idx, (j, ver, dp) in enumerate(prs):
                        ps = emit_S(j, ver)
                        if prev is not None:
                            emit_rest(*prev)
                        prev = (ps, j, ver, dp, idx)
                    emit_rest(*prev)
                    rd = rdp.get()
                    if kind != "D":
                        mm(px[:], cmat(CM_ONES), accD[:], start=False, stop=False)
                        mm(px[:], cmat(CM_ONES), accD2[:], start=False, stop=False)
                        mm(px[:], cmat(CM_ONES), accP[:], start=False, stop=True)
                        ts("dve", rd[:], px[:], 1e-30, ALU.max)
                        act(rd[:], rd[:], AF.Ln)
                        act(rd[:], rd[:], AF.Exp, scale=-1.0)
                        if kind == "A":
                            o = osbp.get()
                            tt("dve", o[:], po[:], rd[:], ALU.mult)
                            T.dma(OA.v(qt, OA.ap[2 * a + b2, :, sl]), o[:], own="in")
                        else:
                            o = obfp.get()
                            tt("dve", o[:], po[:], rd[:], ALU.mult)
                            T.dma(OB.v(qt, OB.ap[a * 128:(a + 1) * 128, sl]), o[:], own="in")
                    else:
                        o65 = osbp.get()
                        cp("act", o65[0:65, :], po[0:65, :])
                        mm(px[0:64, :], cm[0:65, CM_SEL64, 0:64], o65[0:65, :])
                        ts("dve", rd[0:64, :], px[0:64, :], 1e-30, ALU.max)
                        recip(rd[0:64, :], rd[0:64, :])
                        o = obfp.get()
                        tt("dve", o[0:64, :], o65[0:64, :], rd[0:64, :], ALU.mult)
                        hq = 4 * a + b2
                        T.dma(OD.v(qt, OD.ap[hq * 64:(hq + 1) * 64, sl]), o[0:64, :], own="in")

            import os as _os
            _kinds = _os.environ.get("ATT_KINDS", "ABD")
            units = [u for u in units if u[0] in _kinds]
            cur = load_kv(units[0])
            for ui, u in enumerate(units):
                kt, vt = cur
                if ui + 1 < len(units):
                    cur = load_kv(units[ui + 1])
                compute(u, kt, vt)
            S.close()

        if want("SSD"):
            S = Stage("SD%d" % l)
            vec = S.sb([128, NV], F32, "vec")
            T.dma(vec[:], vecs_in[l])
            rws = S.sb([128, 272], F32, "rows")
            T.dma(rws[:], rows_in[l])
            aneg = S.sb([128, 16], F32, "aneg")
            act(aneg[:], rws[:, 256:272], AF.Exp)
            ts("dve", aneg[:], aneg[:], -1.0, ALU.mult)
            xins = S.pool(2, [128, 6, 514], F32, "xin")
            xps = S.pool(2, [128, 6, 512], F32, "xp")
            f32p = S.pool(4, [128, 512], F32, "f")
            dts = S.pool(2, [16, 512], F32, "dt")
            tms = S.pool(2, [16, 512], F32, "tm")
            XBCv = XBC.ap.rearrange("(c p) n -> p c n", p=128)
            XBCPv = XBCP.ap.rearrange("(c p) n -> p c n", p=128)
            for t in range(NT):
                sl = slice(t * 512, (t + 1) * 512)
                xin = xins.get()
                T.dma(xin[:, :, 1:513], XBC.v(t, XBCv[:, :, sl]))
                if t > 0:
                    T.dma(xin[:, :, 0:1], XBC.v(t - 1, XBCv[:, :, t * 512 - 1:t * 512]), allow_slow_non_contiguous=True)
                else:
                    mset("pool", xin[:, :, 0:1], 0.0)
                if t < NT - 1:
                    T.dma(xin[:, :, 513:514], XBC.v(t + 1, XBCv[:, :, (t + 1) * 512:(t + 1) * 512 + 1]), allow_slow_non_contiguous=True)
                else:
                    mset("pool", xin[:, :, 513:514], 0.0)
                xp = xps.get()
                for c in range(6):
                    a0 = f32p.get(); a1 = f32p.get()
                    ts("dve" if c % 2 == 0 else "pool", a0[:], xin[:, c, 0:512], vec[:, VC["cw0"] + c:VC["cw0"] + c + 1], ALU.mult,
                       vec[:, VC["cb"] + c:VC["cb"] + c + 1], ALU.add)
                    stt("dve", a1[:], xin[:, c, 1:513], vec[:, VC["cw1"] + c:VC["cw1"] + c + 1], a0[:], ALU.mult, ALU.add)
                    stt("dve", a0[:], xin[:, c, 2:514], vec[:, VC["cw2"] + c:VC["cw2"] + c + 1], a1[:], ALU.mult, ALU.add)
                    act(xp[:, c, :], a0[:], AF.Silu)
                T.dma(XBCP.v(t, XBCPv[:, :, sl]), xp[:], own="in")
                dtt = dts.get(); tm = tms.get()
                T.dma(dtt[:], DTR.v(t, DTR.ap[:, sl]))
                T.dma(tm[:], tmask_in[0:16, sl])
                e1 = f32p.get()
                act(e1[0:16, :], dtt[:], AF.Exp, bias=vec[0:16, VC["dtb"]:VC["dtb"] + 1])
                act(e1[0:16, :], e1[0:16, :], AF.Ln, bias=1.0)
                tt("dve", dtt[:], e1[0:16, :], tm[:], ALU.mult)
                T.dma(DTP.v(t, DTP.ap[:, sl]), dtt[:], own="in")

            import os as _os
            _ndir = int(_os.environ.get("SSD_NDIR", "2"))
            _cut = int(_os.environ.get("SSD_CUT", "99"))
            st32 = S.sb([128, 4, 64], F32, "st32")
            stbf = S.sb([128, 4, 64], BF16, "stbf")
            xtoks = S.pool(2, [128, 512], F32, "xtok")
            xdts = S.pool(2, [128, 512], BF16, "xdt")
            smalls = S.pool(2, [128, 512], F32, "small")
            csrows = S.pool(2, [8, 128], F32, "csrow")
            bbfs = S.pool(2, [128, 128], BF16, "bbf")
            cbfz = [S.pool(2, [128, 128], BF16, "cbfz%d" % g_) for g_ in range(2)]
            cdecz = [S.pool(3, [128, 128], BF16, "cdecz%d" % g_) for g_ in range(2)]
            for g_ in range(2):
                for b_ in cbfz[g_].items + cdecz[g_].items:
                    mset("pool", b_[:], 0.0)
            cbsbs = S.pool(2, [128, 256], F32, "cbsb")
            args = S.pool(3, [128, 128], F32, "arg")
            decs = S.pool(3, [128, 128], F32, "dec")
            mts = S.pool(3, [128, 128], BF16, "mt")
            ecss = S.pool(3, [128, 128], F32, "ecs")
            bws = S.pool(3, [128, 128], BF16, "bw")
            yfs = S.pool(2, [128, 4, 512], F32, "yf")
            yos = S.pool(2, [128, 4, 512], F32, "yo")
            dtps = S.pool(2, [16, 512], F32, "dtp")
            pT = S.ps([128, 512], F32, "pT")
            pmisc = S.pspool(2, [128, 512], "pmisc")
            pcb = S.ps([128, 512], F32, "pcb")
            pbcs = S.pspool(2, [128, 512], "pbc")
            pys = S.pspool(2, [128, 512], "py")
            YFv = YF.ap.rearrange("(c p) n -> p c n", p=128)
            YSv = YS.ap.rearrange("(c p) n -> p c n", p=128)
            for d in range(_ndir):
                mset("dve", st32[:], 0.0)
                mset("pool", stbf[:], 0.0)
                tri = CM_U if d == 0 else CM_LO
                neg = CM_NEGF if d == 0 else CM_NEGB
                sel = CM_SEL127 if d == 0 else CM_SEL0
                torder = list(range(NT)) if d == 0 else list(range(NT - 1, -1, -1))
                corder = list(range(4)) if d == 0 else [3, 2, 1, 0]

                def load_tile(t):
                    sl_ = slice(t * 512, (t + 1) * 512)
                    xp_ = xps.get(); dtp_ = dtps.get()
                    T.dma(xp_[:], XBCP.v(t, XBCPv[:, :, sl_]))
                    T.dma(dtp_[:], DTP.v(t, DTP.ap[:, sl_]))
                    yf_ = None
                    if d == 1:
                        yf_ = yfs.get()
                        T.dma(yf_[:], YF.v(t, YFv[:, :, sl_]))
                    return xp_, dtp_, yf_

                items = [(ti, t, c) for ti, t in enumerate(torder) for c in corder]
                tiles = {}

                def get_tile(ti):
                    if ti not in tiles:
                        tiles[ti] = load_tile(torder[ti]) + (yos.get(),)
                    return tiles[ti]

                def preamble(item):
                    ti, t, c = item
                    xp, dtp, yf, yo = get_tile(ti)
                    if c == corder[1] and ti + 1 < NT:
                        get_tile(ti + 1)
                    cs = slice(c * 128, (c + 1) * 128)
                    for i in range(4):
                        trp(pT[:, i * 128:(i + 1) * 128], xp[:, i, cs], ident)
                    xtok = xtoks.get()
                    cp("act", xtok[:], pT[:])
                    pm = pmisc.get()
                    sm = smalls.get()
                    trp(pm[:, 0:128], xp[:, 4, cs], ident)
                    trp(pm[:, 128:144], dtp[0:16, cs], cm[0:16, CM_ID, 0:16])
                    cp("dve", sm[:, 0:144], pm[:, 0:144])
                    btok = sm[:, 0:128]
                    dttok = sm[:, 128 + d * 8:128 + d * 8 + 8]
                    dA = sm[:, 144:152]
                    tt("dve", dA, dttok, aneg[:, d * 8:d * 8 + 8], ALU.mult)
                    pm2 = pmisc.get()
                    mm(pm2[:, 0:8], cmat(tri), dA)
                    cscol = sm[:, 152:160]
                    cp("dve", cscol, pm2[:, 0:8])
                    mm(pm2[:, 8:16], cmat(sel), cscol)
                    trp(pm2[0:8, 128:256], cscol, ident)
                    csrow = csrows.get()
                    cp("act", csrow[:], pm2[0:8, 128:256])
                    cslast = sm[:, 160:168]
                    cp("dve", cslast, pm2[:, 8:16])
                    warg = sm[:, 168:176]
                    tt("dve", warg, cslast, cscol, ALU.subtract)
                    wall = sm[:, 176:184]
                    act(wall, warg, AF.Exp)
                    cdall = sm[:, 184:192]
                    act(cdall, cslast, AF.Exp)
                    bbf = bbfs.get()
                    cp("pool", bbf[:], xp[:, 4, cs])
                    for g in range(2):
                        gs = slice(g * 64, (g + 1) * 64)
                        cbf = cbfz[g].get()
                        cp("pool", cbf[gs, :], xp[gs, 5, cs])
                        mm(pcb[:, g * 128:(g + 1) * 128], bbf[:, :], cbf[:, :])
                    cbsb = cbsbs.get()
                    cp("act", cbsb[:], pcb[:, 0:256])
                    xdt = xdts.get()
                    for h in range(8):
                        if h % 2 == 0:
                            act(xdt[:, h * 64:(h + 1) * 64], xtok[:, h * 64:(h + 1) * 64], AF.Copy, scale=dttok[:, h:h + 1])
                        else:
                            ts("dve", xdt[:, h * 64:(h + 1) * 64], xtok[:, h * 64:(h + 1) * 64], dttok[:, h:h + 1], ALU.mult)
                    return dict(xp=xp, yf=yf, yo=yo, cs=cs, btok=btok, cscol=cscol, csrow=csrow, wall=wall,
                                cdall=cdall, cbsb=cbsb, xdt=xdt)

                def heads(cx):
                    xp, yf, yo, cs = cx["xp"], cx["yf"], cx["yo"], cx["cs"]
                    btok, cscol, csrow, wall, cdall, cbsb, xdt = (cx[k_] for k_ in ("btok", "cscol", "csrow", "wall", "cdall", "cbsb", "xdt"))
                    for h in range(8):
                        g = h // 4
                        gs = slice(g * 64, (g + 1) * 64)
                        hs = slice((h % 2) * 64, (h % 2) * 64 + 64)
                        pr = (h % 4) // 2
                        pbc = pbcs.get()
                        mm(pbc[:, 0:128], cm[0:8, CM_SELH + h, :], csrow[:])
                        arg = args.get()
                        stt("dve", arg[:], pbc[:, 0:128], cscol[:, h:h + 1], cmat(neg), ALU.subtract, ALU.add)
                        dec = decs.get()
                        act(dec[:], arg[:], AF.Exp)
                        mt = mts.get()
                        tt("dve", mt[:], cbsb[:, g * 128:(g + 1) * 128], dec[:], ALU.mult)
                        ecs = ecss.get()
                        act(ecs[gs, :], pbc[gs, 0:128], AF.Exp)
                        cdec = cdecz[g].get()
                        tt("pool", cdec[gs, :], xp[gs, 5, cs], ecs[gs, :], ALU.mult)
                        py = pys.get()
                        pair_cols = slice((h // 2) * 128, (h // 2) * 128 + 128)
                        mm(py[:, 0:128], xdt[:, pair_cols], mt[:], start=True, stop=False)
                        mm(py[:, 0:128], stbf[:, pr * 2:pr * 2 + 2, :].re("p a b -> p (a b)"), cdec[:, :], start=False, stop=True)
                        bw = bws.get()
                        act(bw[:], btok, AF.Copy, scale=wall[:, h:h + 1])
                        mm(py[:, 128:192], bw[:], xdt[:, h * 64:(h + 1) * 64])
                        if d == 0:
                            stt("dve", yo[hs, h // 2, cs], xp[hs, h // 2, cs], vec[hs, VC["ssmd"] + h // 2:VC["ssmd"] + h // 2 + 1], py[hs, 0:128], ALU.mult, ALU.add)
                        else:
                            tt("dve", yo[hs, h // 2, cs], py[hs, 0:128], yf[hs, h // 2, cs], ALU.add)
                        stt("dve", st32[gs, h % 4, :], st32[gs, h % 4, :], cdall[gs, h:h + 1], py[gs, 128:192], ALU.mult, ALU.add)
                    cp("act", stbf[:], st32[:])

                cxn = preamble(items[0])
                for ii, item in enumerate(items):
                    cx = cxn
                    if ii + 1 < len(items):
                        cxn = preamble(items[ii + 1])
                    heads(cx)
                    ti, t, c = item
                    if c == corder[-1]:
                        sl = slice(t * 512, (t + 1) * 512)
                        if d == 0:
                            T.dma(YF.v(t, YFv[:, :, sl]), cx["yo"][:], own="in")
                        else:
                            T.dma(YS.v(t, YSv[:, :, sl]), cx["yo"][:], own="in")
            S.close()

        TM = 256
        NTM = L // TM
        if want("MERGE"):
            S = Stage("MG%d" % l)
            wG = alloc_w(S, D_MODEL, 4608, "wG"); wbr = alloc_w(S, 2048, D_MODEL, "wbr"); wo = alloc_w(S, D_MODEL, D_MODEL, "wo")
            vec = S.sb([128, NV], F32, "vec")
            rws = S.sb([128, 272], F32, "rows")
            lam = S.sb([128, 8], F32, "lam")
            ltmp = S.sb([128, 128], F32, "ltmp")
            W = Stage("MGw%d" % l)
            stg = W.pool(2, [128, 2048], F32, "stg")
            load_w_bf16(S, wG_in, wG_in.h[l], D_MODEL, 4608, stg, "wG", w=wG)
            load_w_bf16(S, wbr_in, wbr_in.h[l], 2048, D_MODEL, stg, "wbr", w=wbr)
            load_w_bf16(S, wout_in, wout_in.h[l], D_MODEL, D_MODEL, stg, "wo", w=wo)
            W.close()
            T.dma(vec[:], vecs_in[l])
            T.dma(rws[:], rows_in[l])
            tt("dve", ltmp[:, 0:64], rws[:, 0:64], rws[:, 64:128], ALU.mult)
            tt("dve", ltmp[:, 64:128], rws[:, 128:192], rws[:, 192:256], ALU.mult)
            T.op("dve", lambda: V.reduce_sum(out=lam.h[:, 0:1], in_=ltmp.h[:, 0:64], axis=mybir.AxisListType.X), reads=[ltmp[:]], writes=[lam[:]])
            T.op("dve", lambda: V.reduce_sum(out=lam.h[:, 1:2], in_=ltmp.h[:, 64:128], axis=mybir.AxisListType.X), reads=[ltmp[:]], writes=[lam[:]])
            act(lam[:, 2:4], lam[:, 0:2], AF.Exp)
            tt("dve", lam[:, 4:5], lam[:, 3:4], lam[:, 2:3], ALU.subtract)
            ts("dve", lam[:, 5:6], lam[:, 4:5], -lam_init, ALU.add)
            neglam = lam[:, 5:6]
            hb = S.sb([128, 8, TM], BF16, "hb"); xt = S.sb([128, 8, TM], F32, "xt")
            oa = S.sb([128, 8, TM], F32, "oa"); ob = S.sb([128, 4, TM], BF16, "ob"); od = S.sb([128, 4, TM], BF16, "od")
            ys = S.sb([128, 4, TM], F32, "ys"); tm = S.sb([128, TM], F32, "tm")
            ya = S.sb([128, 4, TM], BF16, "ya"); yc = S.sb([128, 4, TM], BF16, "yc"); yz = S.sb([128, 4, TM], F32, "yz")
            mrg = S.sb([128, 8, TM], BF16, "mrg"); outf = S.sb([128, 8, TM], F32, "outf")
            f32p = S.pool(6, [128, TM], F32, "f")
            sgp = S.pool(3, [128, TM], F32, "sg")
            rsb = S.sb([128, TM], F32, "rsb")
            pp = S.pspool(6, [128, 512], "pp")
            pst = S.pspool(2, [128, 512], "pst")
            kcv = lambda a: a.rearrange("(kc p) n -> p kc n", p=128)
            for i in range(NTM):
                sl = slice(i * TM, (i + 1) * TM)
                k_ = ("m", i)
                T.dma(hb[:], hT.v(k_, kcv(hT.ap)[:, :, sl]))
                T.dma(xt[:], xres.v(k_, kcv(xres.ap)[:, :, sl]))
                T.dma(oa[:], OA.v(k_, OA.ap[:, :, sl].rearrange("a p n -> p a n")))
                T.dma(ob[:], OB.v(k_, kcv(OB.ap)[:, :, sl]))
                T.dma(od[:], OD.v(k_, kcv(OD.ap)[:, :, sl]))
                T.dma(ys[:], YS.v(k_, kcv(YS.ap)[:, :, sl]))
                T.dma(tm[:], tmask_in[:, sl])
                for h in range(4):
                    o = f32p.get(); sq = f32p.get(); rs = f32p.get(); tmp = f32p.get()
                    stt("dve", o[:], oa[:, 2 * h + 1, :], neglam, oa[:, 2 * h, :], ALU.mult, ALU.add)
                    tt("pool", sq[:], o[:], o[:], ALU.mult)
                    pm = pp.get()
                    mm(pm[:, 0:TM], cmat(CM_ONES), sq[:])
                    act(tmp[:], pm[:, 0:TM], AF.Ln, bias=epsb[:, 0:1], scale=1.0 / 128)
                    act(rs[:], tmp[:], AF.Exp, scale=-0.5)
                    ts("dve", rs[:], rs[:], 1.0 - lam_init, ALU.mult)
                    stt("dve", ya[:, h, :], o[:], vec[:, VC["diffn"]:VC["diffn"] + 1], rs[:], ALU.mult, ALU.mult)
                pm = pst.get()
                for zc in range(4):
                    pz = pp.get()
                    for kc in range(8):
                        mm(pz[:, 0:TM], wG[:, kc, 4096 + zc * 128:4096 + (zc + 1) * 128], hb[:, kc, :], start=(kc == 0), stop=(kc == 7))
                    zs = f32p.get()
                    act(zs[:], pz[:, 0:TM], AF.Silu)
                    tt("dve", yz[:, zc, :], zs[:], ys[:, zc, :], ALU.mult)
                    sq = f32p.get()
                    tt("pool", sq[:], yz[:, zc, :], yz[:, zc, :], ALU.mult)
                    mm(pm[:, 0:TM], cmat(CM_ONES), sq[:], start=(zc == 0), stop=(zc == 3))
                tmp = f32p.get(); rs = f32p.get()
                act(tmp[:], pm[:, 0:TM], AF.Ln, bias=epsb[:, 0:1], scale=1.0 / 512)
                act(rs[:], tmp[:], AF.Exp, scale=-0.5)
                for zc in range(4):
                    stt("dve", yc[:, zc, :], yz[:, zc, :], vec[:, VC["ssmn"] + zc:VC["ssmn"] + zc + 1], rs[:], ALU.mult, ALU.mult)
                Ys = [ya, ob, yc, od]
                for oc in range(8):
                    macc = f32p.get()
                    for n_ in range(4):
                        pg = pp.get()
                        for kc in range(8):
                            mm(pg[:, 0:TM], wG[:, kc, n_ * 1024 + oc * 128:n_ * 1024 + (oc + 1) * 128], hb[:, kc, :], start=(kc == 0), stop=(kc == 7))
                        sg = sgp.get()
                        act(sg[:], pg[:, 0:TM], AF.Sigmoid)
                        pb = pp.get()
                        for kc in range(4):
                            mm(pb[:, 0:TM], wbr[:, n_ * 4 + kc, oc * 128:(oc + 1) * 128], Ys[n_][:, kc, :], start=(kc == 0), stop=(kc == 3))
                        if n_ == 0:
                            tt("dve", macc[:], pb[:, 0:TM], sg[:], ALU.mult)
                        else:
                            t2 = f32p.get()
                            tt("dve", t2[:], pb[:, 0:TM], sg[:], ALU.mult)
                            if n_ < 3:
                                tt("pool", macc[:], macc[:], t2[:], ALU.add)
                            else:
                                tt("pool", mrg[:, oc, :], macc[:], t2[:], ALU.add)
                pm = pst.get()
                for oc in range(8):
                    po = pp.get()
                    for kc in range(8):
                        mm(po[:, 0:TM], wo[:, kc, oc * 128:(oc + 1) * 128], mrg[:, kc, :], start=(kc == 0), stop=(kc == 7))
                    cp("act", outf[:, oc, :], po[:, 0:TM])
                    sq = f32p.get()
                    tt("pool", sq[:], outf[:, oc, :], outf[:, oc, :], ALU.mult)
                    mm(pm[:, 0:TM], cmat(CM_ONES), sq[:], start=(oc == 0), stop=(oc == 7))
                tmp = f32p.get(); rs = rsb
                act(tmp[:], pm[:, 0:TM], AF.Ln, bias=epsb[:, 0:1], scale=1.0 / D_MODEL)
                act(rs[:], tmp[:], AF.Exp, scale=-0.5)
                tt("dve", rs[:], rs[:], tm[:], ALU.mult)
                for oc in range(8):
                    t2 = f32p.get()
                    stt("dve", t2[:], outf[:, oc, :], vec[:, VC["post0"] + oc:VC["post0"] + oc + 1], rs[:], ALU.mult, ALU.mult)
                    tt("pool", xt[:, oc, :], xt[:, oc, :], t2[:], ALU.add)
                T.dma(xres.v(k_, kcv(xres.ap)[:, :, sl]), xt[:], own="in")
            S.close()

        if want("MEM"):
            S = Stage("MM%d" % l)
            wq = alloc_w(S, D_MODEL, 512, "wq"); wmo = alloc_w(S, 512, D_MODEL, "wmo")
            vec = S.sb([128, NV], F32, "vec")
            onesb = S.sb([128, 128], BF16, "onesb")
            Kmem = S.sb([128, 4, 256], BF16, "Kmem"); Vmem = S.sb([128, 2, 512], BF16, "Vmem")
            pp = S.pspool(3, [128, 512], "pp")
            pst = S.pspool(1, [128, 512], "pst")
            pS2 = S.pspool(2, [128, 1024], "pS2")
            f32p = S.pool(6, [128, 512], F32, "f")
            W = Stage("MMw%d" % l)
            stg = W.pool(2, [128, 2048], F32, "stg")
            load_w_bf16(S, wmq_in, wmq_in.h[l], D_MODEL, 512, stg, "wq", w=wq)
            load_w_bf16(S, wmo_in, wmo_in.h[l], 512, D_MODEL, stg, "wmo", w=wmo)
            wk = load_w_bf16(W, wmk_in, wmk_in.h[l], D_MODEL, 512, stg, "wk")
            wv = load_w_bf16(W, wmv_in, wmv_in.h[l], D_MODEL, 512, stg, "wv")
            T.dma(vec[:], vecs_in[l])
            cp("dve", onesb[:], cmat(CM_ONES))
            memf = W.sb([128, 8, 256], F32, "memf"); mn = W.sb([128, 8, 256], BF16, "mn")
            T.dma(memf[:], View(memT_in, memT_in.h.rearrange("(kc p) n -> p kc n", p=128)))
            pm = pst.get()
            for kc in range(8):
                sq = f32p.get()
                tt("dve", sq[:, 0:256], memf[:, kc, :], memf[:, kc, :], ALU.mult)
                mm(pm[:, 0:256], cmat(CM_ONES), sq[:, 0:256], start=(kc == 0), stop=(kc == 7))
            tmp = f32p.get(); rs = f32p.get()
            act(tmp[:, 0:256], pm[:, 0:256], AF.Ln, bias=epsb[:, 0:1], scale=1.0 / D_MODEL)
            act(rs[:, 0:256], tmp[:, 0:256], AF.Exp, scale=-0.5)
            for kc in range(8):
                stt("dve", mn[:, kc, :], memf[:, kc, :], vec[:, VC["memn"] + kc:VC["memn"] + kc + 1], rs[:, 0:256], ALU.mult, ALU.mult)
            for h in range(4):
                pk = pp.get()
                for kc in range(8):
                    mm(pk[:, 0:256], wk[:, kc, h * 128:(h + 1) * 128], mn[:, kc, :], start=(kc == 0), stop=(kc == 7))
                cp("act", Kmem[:, h, :], pk[:, 0:256])
            for mb in range(2):
                pv = pp.get()
                for kc in range(8):
                    mm(pv[:, :], mn[:, kc, mb * 128:(mb + 1) * 128], wv[:, kc, :], start=(kc == 0), stop=(kc == 7))
                cp("act", Vmem[:, mb, :], pv[:, :])
            W.close()
            xts = S.pool(2, [128, 8, 512], F32, "xt")
            tms = S.pool(2, [128, 512], F32, "tm")
            h1 = S.sb([128, 8, 512], BF16, "h1")
            h2s = S.pool(2, [128, 8, 512], BF16, "h2")
            oh = S.sb([128, 4, 512], BF16, "oh")
            outf = S.sb([128, 8, 512], F32, "outf")
            qhs = S.pool(2, [128, 512], BF16, "qh")
            rsb = S.sb([128, 512], F32, "rsb")
            pts = S.pool(2, [128, 1024], BF16, "pt")
            kcv = lambda a: a.rearrange("(kc p) n -> p kc n", p=128)

            def norm_to(xt_, col0, dst):
                pm_ = pst.get()
                for kc in range(8):
                    sq = f32p.get()
                    if kc % 2 == 0:
                        act(sq[:], xt_[:, kc, :], AF.Square)
                    else:
                        tt("pool", sq[:], xt_[:, kc, :], xt_[:, kc, :], ALU.mult)
                    mm(pm_[:], cmat(CM_ONES), sq[:], start=(kc == 0), stop=(kc == 7))
                tmp_ = f32p.get(); rs_ = f32p.get()
                act(tmp_[:], pm_[:], AF.Ln, bias=epsb[:, 0:1], scale=1.0 / D_MODEL)
                act(rs_[:], tmp_[:], AF.Exp, scale=-0.5)
                for kc in range(8):
                    stt("dve", dst[:, kc, :], xt_[:, kc, :], vec[:, col0 + kc:col0 + kc + 1], rs_[:], ALU.mult, ALU.mult)

            def ld(t):
                sl_ = slice(t * 512, (t + 1) * 512)
                xt_ = xts.get(); tm_ = tms.get()
                T.dma(xt_[:], xres.v(("e", t), kcv(xres.ap)[:, :, sl_]))
                T.dma(tm_[:], tmask_in[:, sl_])
                return xt_, tm_

            nxt = ld(0)
            for t in range(NT):
                xt, tm = nxt
                if t + 1 < NT:
                    nxt = ld(t + 1)
                sl = slice(t * 512, (t + 1) * 512)
                norm_to(xt, VC["pre1"], h1)
                for h in range(4):
                    pq = pp.get()
                    for kc in range(8):
                        mm(pq[:], wq[:, kc, h * 128:(h + 1) * 128], h1[:, kc, :], start=(kc == 0), stop=(kc == 7))
                    qh = qhs.get()
                    cp("act", qh[:], pq[:])
                    ps2 = pS2.get()
                    for mb in range(2):
                        mm(ps2[:, mb * 512:(mb + 1) * 512], Kmem[:, h, mb * 128:(mb + 1) * 128], qh[:])
                    pt = pts.get()
                    act(pt[:], ps2[:], AF.Exp, scale=128 ** -0.5)
                    po = pp.get(); pd = pp.get()
                    for mb in range(2):
                        mm(po[:], Vmem[:, mb, h * 128:(h + 1) * 128], pt[:, mb * 512:(mb + 1) * 512], start=(mb == 0), stop=(mb == 1))
                    for mb in range(2):
                        mm(pd[:], onesb[:], pt[:, mb * 512:(mb + 1) * 512], start=(mb == 0), stop=(mb == 1))
                    rd = f32p.get()
                    act(rd[:], pd[:], AF.Ln)
                    act(rd[:], rd[:], AF.Exp, scale=-1.0)
                    tt("dve", oh[:, h, :], po[:], rd[:], ALU.mult)
                pm = pst.get()
                for oc in range(8):
                    po = pp.get()
                    for h in range(4):
                        mm(po[:], wmo[:, h, oc * 128:(oc + 1) * 128], oh[:, h, :], start=(h == 0), stop=(h == 3))
                    cp("act", outf[:, oc, :], po[:])
                    sq = f32p.get()
                    tt("pool", sq[:], outf[:, oc, :], outf[:, oc, :], ALU.mult)
                    mm(pm[:], cmat(CM_ONES), sq[:], start=(oc == 0), stop=(oc == 7))
                tmp = f32p.get(); rs = rsb
                act(tmp[:], pm[:], AF.Ln, bias=epsb[:, 0:1], scale=1.0 / D_MODEL)
                act(rs[:], tmp[:], AF.Exp, scale=-0.5)
                tt("dve", rs[:], rs[:], tm[:], ALU.mult)
                for oc in range(8):
                    t2 = f32p.get()
                    stt("dve", t2[:], outf[:, oc, :], vec[:, VC["post1"] + oc:VC["post1"] + oc + 1], rs[:], ALU.mult, ALU.mult)
                    tt("pool", xt[:, oc, :], xt[:, oc, :], t2[:], ALU.add)
                T.dma(xres.v(("e", t), kcv(xres.ap)[:, :, sl]), xt[:], own="in")
                h2 = h2s.get()
                norm_to(xt, VC["pre2"], h2)
                T.dma(H2.v(("e", t), kcv(H2.ap)[:, :, sl]), h2[:], own="in")
            S.close()

        if want("FFN"):
            S = Stage("FF%d" % l)
            wfi = alloc_w(S, D_MODEL, 2 * D_FF, "wfi"); wfo = alloc_w(S, D_FF, D_MODEL, "wfo")
            vec = S.sb([128, NV], F32, "vec")
            W = Stage("FFw%d" % l)
            stg = W.pool(2, [128, 2048], F32, "stg")
            load_w_bf16(S, wfi_in, wfi_in.h[l], D_MODEL, 2 * D_FF, stg, "wfi", w=wfi)
            load_w_bf16(S, wfo_in, wfo_in.h[l], D_FF, D_MODEL, stg, "wfo", w=wfo)
            W.close()
            T.dma(vec[:], vecs_in[l])
            h2t = S.pool(2, [128, 8, TM + 2], BF16, "h2t")
            xt = S.sb([128, 8, TM], F32, "xt"); tm = S.sb([128, TM], F32, "tm")
            actT = S.sb([128, 22, TM], BF16, "actT")
            outf = S.sb([128, 8, TM], F32, "outf")
            f32p = S.pool(8, [128, TM], F32, "f")
            rsb = S.sb([128, TM], F32, "rsb")
            pp = S.pspool(7, [128, 512], "pp")
            pst = S.pspool(1, [128, 512], "pst")
            kcv = lambda a: a.rearrange("(kc p) n -> p kc n", p=128)
            H2v = kcv(H2.ap)
            last = (l == depth - 1)

            def ldh(i):
                c0 = i * TM
                hh = h2t.get()
                lo = max(c0 - 1, 0); hi = min(c0 + TM + 1, L)
                T.dma(hh[:, :, (lo - (c0 - 1)):(hi - (c0 - 1))], H2.v(("f", 0), H2v[:, :, lo:hi]))
                if c0 == 0:
                    mset("pool", hh[:, :, 0:1], 0.0)
                if c0 + TM == L:
                    mset("pool", hh[:, :, TM + 1:TM + 2], 0.0)
                return hh

            nh = ldh(0)
            for i in range(NTM):
                sl = slice(i * TM, (i + 1) * TM)
                hh = nh
                if i + 1 < NTM:
                    nh = ldh(i + 1)
                k_ = ("f", i + 1)
                T.dma(xt[:], xres.v(k_, kcv(xres.ap)[:, :, sl]))
                T.dma(tm[:], tmask_in[:, sl])
                for j in range(22):
                    cv = []
                    for half in range(2):
                        ch = j + 22 * half
                        pu = pp.get()
                        for kc in range(8):
                            mm(pu[:, 0:TM + 2], wfi[:, kc, ch * 128:(ch + 1) * 128], hh[:, kc, :], start=(kc == 0), stop=(kc == 7))
                        a0 = f32p.get(); a1 = f32p.get()
                        ts("dve", a0[:], pu[:, 0:TM], vec[:, VC["fw0"] + ch:VC["fw0"] + ch + 1], ALU.mult, vec[:, VC["fb"] + ch:VC["fb"] + ch + 1], ALU.add)
                        stt("dve", a1[:], pu[:, 1:TM + 1], vec[:, VC["fw1"] + ch:VC["fw1"] + ch + 1], a0[:], ALU.mult, ALU.add)
                        stt("dve", a0[:], pu[:, 2:TM + 2], vec[:, VC["fw2"] + ch:VC["fw2"] + ch + 1], a1[:], ALU.mult, ALU.add)
                        cv.append(a0)
                    ga = f32p.get()
                    act(ga[:], cv[0][:], AF.Gelu)
                    tt("pool", actT[:, j, :], ga[:], cv[1][:], ALU.mult)
                pm = pst.get()
                for oc in range(8):
                    po = pp.get()
                    for j in range(22):
                        mm(po[:, 0:TM], wfo[:, j, oc * 128:(oc + 1) * 128], actT[:, j, :], start=(j == 0), stop=(j == 21))
                    cp("act", outf[:, oc, :], po[:, 0:TM])
                    sq = f32p.get()
                    tt("pool", sq[:], outf[:, oc, :], outf[:, oc, :], ALU.mult)
                    mm(pm[:, 0:TM], cmat(CM_ONES), sq[:], start=(oc == 0), stop=(oc == 7))
                tmp = f32p.get(); rs = rsb
                act(tmp[:], pm[:, 0:TM], AF.Ln, bias=epsb[:, 0:1], scale=1.0 / D_MODEL)
                act(rs[:], tmp[:], AF.Exp, scale=-0.5)
                tt("dve", rs[:], rs[:], tm[:], ALU.mult)
                for oc in range(8):
                    t2 = f32p.get()
                    stt("dve", t2[:], outf[:, oc, :], vec[:, VC["post2"] + oc:VC["post2"] + oc + 1], rs[:], ALU.mult, ALU.mult)
                    tt("pool", xt[:, oc, :], xt[:, oc, :], t2[:], ALU.add)
                dst = yT_out if last else xres
                T.dma(dst.v(k_, kcv(dst.ap)[:, :, sl]), xt[:], own="in")
            S.close()

    T.finish()
    glob.close()
    return nc


def _const_tables(L):
    c = {}
    cm = np.zeros((NCM, 128, 128), np.float32)
    k = np.arange(128)
    cm[0] = np.eye(128)
    cm[1] = 1.0
    cm[2] = (k[:, None] <= k[None, :])
    cm[3] = (k[:, None] >= k[None, :])
    cm[4] = np.where(k[:, None] <= k[None, :], 0.0, NEG)
    cm[5] = np.where(k[:, None] >= k[None, :], 0.0, NEG)
    cm[6] = (k[:, None] // 64 == k[None, :] // 64)
    for m in range(32):
        if m < 16:
            cm[7][64 + m + 16, 64 + m] = -1.0
        else:
            cm[7][64 + m - 16, 64 + m] = 1.0
    for blk in range(2):
        for sub in range(2):
            o = blk * 64 + sub * 32
            for m in range(32):
                if m < 16:
                    cm[8][o + m + 16, o + m] = -1.0
                else:
                    cm[8][o + m - 16, o + m] = 1.0
    cm[9][127, :] = 1.0
    cm[10][64, :] = 1.0
    cm[11][0, :] = 1.0
    for h_ in range(16):
        cm[12 + h_][h_, :] = 1.0
    c["cmat"] = np.ascontiguousarray(cm.transpose(1, 0, 2))
    pos = np.arange(L, dtype=np.float32)
    freqs = (np.float32(10000.0) ** (-(np.arange(16, dtype=np.float32)) / np.float32(16))).astype(np.float32)
    angB = (pos[None, :] * freqs[:, None]).astype(np.float32)
    c["cosB"] = np.concatenate([np.cos(angB), np.cos(angB)], 0).astype(np.float32)
    c["sinB"] = np.concatenate([np.sin(angB), np.sin(angB)], 0).astype(np.float32)
    rowp = (np.arange(L) // 64).astype(np.float32)
    colp = (np.arange(L) % 64).astype(np.float32)
    angR = (rowp[None, :] * freqs[:, None]).astype(np.float32)
    angC = (colp[None, :] * freqs[:, None]).astype(np.float32)
    cD = np.concatenate([np.cos(angR), np.cos(angR), np.cos(angC), np.cos(angC)], 0)
    sD = np.concatenate([np.sin(angR), np.sin(angR), np.sin(angC), np.sin(angC)], 0)
    c["cosD"] = np.concatenate([cD, cD], 0).astype(np.float32)
    c["sinD"] = np.concatenate([sD, sD], 0).astype(np.float32)
    return c


def _alibi_tables(L, Lreal):
    j = np.arange(L)
    jh = (j // 128).astype(np.float32)
    jl = (j % 128).astype(np.float32)
    kmask = np.where(j < Lreal, 0.0, NEG).astype(np.float32)
    kaug = np.zeros((4, 5, L), np.float32)
    bdiag = np.zeros((4, 128, 2048), np.float32)
    il = np.arange(512, dtype=np.float32)
    jl128 = np.arange(128, dtype=np.float32)
    for h in range(4):
        m = 2.0 ** (-2.0 * (h + 1))
        kaug[h, 0] = -8 * m * 128
        kaug[h, 1] = -8 * m
        kaug[h, 2] = 8 * m * 128 * jh
        kaug[h, 3] = 8 * m * jl
        kaug[h, 4] = kmask
        for dp in range(2):
            for hf in range(2):
                koff = 128 * (2 * dp + hf)
                c0 = dp * 1024 + hf * 512
                bdiag[h, :, c0:c0 + 512] = -8 * m * np.abs(il[None, :] - (koff + jl128[:, None]))
    qaug = np.zeros((3, 5, L), np.float32)
    qaug[0, 0] = jh; qaug[0, 1] = jl; qaug[0, 2] = 1; qaug[0, 3] = 1; qaug[0, 4] = 1
    qaug[1, 0] = -jh; qaug[1, 1] = -jl; qaug[1, 2] = -1; qaug[1, 3] = -1; qaug[1, 4] = 1
    qaug[2, 4] = 1
    return dict(kaugA=kaug.astype(NPBF), qaugA=qaug.astype(NPBF), bdiag=bdiag,
                kmask=kmask[None, :].astype(NPBF), onesrow=np.ones((1, L), NPBF))


def _weights_layout(p, depth):
    f = lambda a: np.ascontiguousarray(np.asarray(a, dtype=np.float32))
    w_in = f(p["w_in"])
    o = {}
    o["wP"] = f(np.concatenate([w_in[:, :, 0:2208], w_in[:, :, 2720:4272]], axis=2))
    o["wG"] = f(np.concatenate([w_in[:, :, G_OFF:G_OFF + 4096], w_in[:, :, C_Z:C_Z + 512]], axis=2))
    o["wuq"] = f(p["w_mla_uq"])
    wukv = f(p["w_mla_ukv"]).reshape(depth, 256, 4, 192)
    o["wukvK"] = f(wukv[..., 0:64].reshape(depth, 256, 256))
    o["wukvV"] = f(wukv[..., 64:192].reshape(depth, 256, 512))
    o["wbr"] = f(p["w_branch"]).reshape(depth, 2048, D_MODEL)
    o["wout"] = f(p["w_out"])
    o["wmq"] = f(p["w_mem_q"])
    wkv = f(p["w_mem_kv"]).reshape(depth, D_MODEL, 4, 256)
    o["wmk"] = f(wkv[..., 0:128].reshape(depth, D_MODEL, 512))
    o["wmv"] = f(wkv[..., 128:256].reshape(depth, D_MODEL, 512))
    o["wmo"] = f(p["w_mem_o"])
    o["wfi"] = f(p["w_ffn_in"])
    o["wfo"] = f(p["w_ffn_out"])
    vecs = np.zeros((depth, 128, NV), np.float32)

    def put(name, arr, width):
        a = f(arr).reshape(depth, width, 128).transpose(0, 2, 1)
        vecs[:, :, VC[name]:VC[name] + width] = a

    npre = f(p["norm_pre"]); npost = f(p["norm_post"])
    for i in range(3):
        put("pre%d" % i, npre[:, i], 8)
        put("post%d" % i, npost[:, i], 8)
    put("diffn", p["diff_norm"], 1)
    put("mlaq", p["mla_q_norm"], 3)
    put("mlakv", p["mla_kv_norm"], 2)
    cw = f(p["ssm_conv_w"])
    for k_ in range(3):
        put("cw%d" % k_, cw[:, k_], 6)
    put("cb", p["ssm_conv_b"], 6)
    put("ssmn", p["ssm_norm"], 4)
    put("gq", np.tile(f(p["gqa_q_norm"]), (1, 2)), 1)
    put("gk", np.tile(f(p["gqa_k_norm"]), (1, 2)), 1)
    put("memn", p["mem_norm"], 8)
    dtb = f(p["ssm_dt_bias"]).reshape(depth, 16)
    vecs[:, 0:16, VC["dtb"]] = dtb
    put("ssmd", np.repeat(f(p["ssm_d"]), 64, axis=1), 4)
    fw = f(p["ffn_conv_w"])
    for k_ in range(3):
        put("fw%d" % k_, fw[:, k_], 44)
    put("fb", p["ffn_conv_b"], 44)
    o["vecs"] = vecs
    rows = np.zeros((depth, 128, 272), np.float32)
    rows[:, :, 0:256] = f(p["diff_lambda"]).reshape(depth, 1, 256)
    rows[:, :, 256:272] = f(p["ssm_a_log"]).reshape(depth, 1, 16)
    o["rows"] = rows
    return o


def make_in_maps(seqs, mems, params, L, depth):
    consts = _const_tables(L)
    wl = _weights_layout(params, depth)
    maps = []
    cache = {}
    for x, mem in zip(seqs, mems):
        Lr = x.shape[0]
        if Lr not in cache:
            cache[Lr] = _alibi_tables(L, Lr)
        m = dict(consts)
        m.update(wl)
        m.update(cache[Lr])
        xT = np.zeros((D_MODEL, L), np.float32)
        xT[:, :Lr] = np.asarray(x, np.float32).T
        m["xT"] = xT
        m["memT"] = np.ascontiguousarray(np.asarray(mem, np.float32).T)
        tm = np.zeros((128, L), np.float32)
        tm[:, :Lr] = 1.0
        m["tmask"] = tm
        maps.append(m)
    return maps


PARAM_NAMES = ["w_in", "w_branch", "w_out", "diff_lambda", "diff_norm", "mla_q_norm", "mla_kv_norm",
               "w_mla_uq", "w_mla_ukv", "ssm_conv_w", "ssm_conv_b", "ssm_a_log", "ssm_dt_bias", "ssm_d",
               "ssm_norm", "gqa_q_norm", "gqa_k_norm", "mem_norm", "w_mem_q", "w_mem_kv", "w_mem_o",
               "w_ffn_in", "ffn_conv_w", "ffn_conv_b", "w_ffn_out", "norm_pre", "norm_post"]


def kernel(**inputs):
    xp = np.asarray(inputs["x_prompt"], np.float32)
    xs = np.asarray(inputs["x_sample"], np.float32)
    mp = np.asarray(inputs["mem_prompt"], np.float32)
    ms = np.asarray(inputs["mem_sample"], np.float32)
    params = {n: np.asarray(inputs[n], np.float32) for n in PARAM_NAMES}
    depth = params["w_in"].shape[0]
    L = xp.shape[1]
    seqs = [xp[0], xp[1], xs[0], xs[1], xs[2], xs[3], xs[0], xs[1]]
    mems = [mp[0], mp[1], ms[0], ms[1], ms[2], ms[3], ms[0], ms[1]]
    outs = run_trunk(seqs, mems, params, L, depth)
    y_prompt = np.stack([outs[0], outs[1]], 0).astype(np.float32)
    y_sample = np.stack([outs[2], outs[3], outs[4], outs[5]], 0).astype(np.float32)
    return (y_prompt, y_sample)


_NC_CACHE = {}


def run_trunk(seqs, mems, params, L, depth):
    key = (L, depth)
    if key not in _NC_CACHE:
        _NC_CACHE[key] = build_program(L, depth=depth)
    nc = _NC_CACHE[key]
    maps = make_in_maps(seqs, mems, params, L, depth)
    res = run_bass_kernel_spmd(nc, maps, core_ids=list(range(8)))
    outs = []
    for i, x in enumerate(seqs):
        yT = np.asarray(res.results[i]["yT"], np.float32)
        outs.append(np.ascontiguousarray(yT[:, :x.shape[0]].T))
    return outs
```
